# Optimizing a Trainium2 kernel written in Bass

```python
import jax, jax.numpy as jnp
from jax import lax
import numpy as np

D_MODEL = 1024
BATCH = 4
SEQ = 4096
DEPTH = 1

HEAD_DIM = 64
ATTN_Q_HEADS = D_MODEL // (2 * HEAD_DIM)
ATTN_KV_HEADS = ATTN_Q_HEADS // 4
ATTN_GROUP = ATTN_Q_HEADS // ATTN_KV_HEADS
ATTN_WIDTH = ATTN_Q_HEADS * HEAD_DIM
KV_WIDTH = ATTN_KV_HEADS * HEAD_DIM
WINDOW = 128
BLOCK = 128
ROPE_THETA = 500000.0
ROT_DIM = HEAD_DIM // 4
RWKV_HEADS = D_MODEL // (2 * HEAD_DIM)
RWKV_WIDTH = RWKV_HEADS * HEAD_DIM
DECAY_RANK = 64
ICLR_RANK = 64
GATE_RANK = 128
N_BRANCH = 2
D_FF = 4 * D_MODEL
PLE_DIM = 256
NORM_EPS = 1e-6
GN_EPS = 64e-5
O_Q = 0
O_K = O_Q + ATTN_WIDTH
O_V = O_K + KV_WIDTH
O_RWKV = O_V + KV_WIDTH
RWKV_COLS = 3 * RWKV_WIDTH + DECAY_RANK + ICLR_RANK + GATE_RANK
O_GATE = O_RWKV + RWKV_COLS
IN_COLS = O_GATE + N_BRANCH * D_MODEL

kernel_name = 'hybrid_swa_rwkv7_encoder'


def rms_norm(x, g):
    xf = x.astype(jnp.float32)
    y = xf * lax.rsqrt(jnp.mean(xf * xf, axis=-1, keepdims=True) + NORM_EPS)
    return (y * g.astype(jnp.float32)).astype(x.dtype)


def rotary_partial(x, positions):
    half = ROT_DIM // 2
    inv_freq = jnp.power(jnp.float32(ROPE_THETA), -jnp.arange(half, dtype=jnp.float32) * 2.0 / ROT_DIM)
    ang = positions.astype(jnp.float32)[:, None] * inv_freq[None, :]
    cos = jnp.cos(ang)[None, :, None, :]
    sin = jnp.sin(ang)[None, :, None, :]
    xf = x[..., :ROT_DIM].astype(jnp.float32)
    x1, x2 = xf[..., :half], xf[..., half:]
    rot = jnp.concatenate([x1 * cos - x2 * sin, x2 * cos + x1 * sin], axis=-1).astype(x.dtype)
    return jnp.concatenate([rot, x[..., ROT_DIM:]], axis=-1)


def band_windows(t, n_blocks):
    b, _, h, d = t.shape
    tp = jnp.pad(t, ((0, 0), (BLOCK, BLOCK), (0, 0), (0, 0))).reshape(b, n_blocks + 2, BLOCK, h, d)
    return jnp.concatenate([tp[:, :-2], tp[:, 1:-1], tp[:, 2:]], axis=2)


def windowed_gqa(q, k, v, sink):
    b, s = q.shape[0], q.shape[1]
    nb = s // BLOCK
    qb = q.reshape(b, nb, BLOCK, ATTN_KV_HEADS, ATTN_GROUP, HEAD_DIM)
    kw = band_windows(k, nb)
    vw = band_windows(v, nb)
    scores = jnp.einsum('bnqhgd,bnkhd->bnhgqk', qb, kw).astype(jnp.float32) * (HEAD_DIM ** -0.5)
    qpos = jnp.arange(nb)[:, None] * BLOCK + jnp.arange(BLOCK)[None, :]
    kpos = (jnp.arange(nb)[:, None] - 1) * BLOCK + jnp.arange(3 * BLOCK)[None, :]
    valid = ((jnp.abs(qpos[:, :, None] - kpos[:, None, :]) <= WINDOW)
             & (kpos[:, None, :] >= 0) & (kpos[:, None, :] < s))
    scores = jnp.where(valid[None, :, None, None], scores, -1e30)
    sink_logit = jnp.broadcast_to(
        sink.astype(jnp.float32).reshape(ATTN_KV_HEADS, ATTN_GROUP)[None, None, :, :, None, None],
        scores.shape[:-1] + (1,))
    probs = jax.nn.softmax(jnp.concatenate([scores, sink_logit], axis=-1), axis=-1)[..., :-1]
    out = jnp.einsum('bnhgqk,bnkhd->bnqhgd', probs.astype(v.dtype), vw)
    return out.reshape(b, s, ATTN_WIDTH)


def centred_shift(z):
    prev = jnp.pad(z[:, :-1], ((0, 0), (1, 0), (0, 0)))
    nxt = jnp.pad(z[:, 1:], ((0, 0), (0, 1), (0, 0)))
    return 0.5 * (prev + nxt)


def rwkv7_bidir_scan(r, w, k, v, a, b):
    xs = (jnp.moveaxis(r, 2, 0), jnp.moveaxis(w, 2, 0), jnp.moveaxis(k, 2, 0),
          jnp.moveaxis(v, 2, 0), jnp.moveaxis(a, 2, 0), jnp.moveaxis(b, 2, 0))

    def step(state, inp):
        r_t, w_t, k_t, v_t, a_t, b_t = inp
        sa = jnp.einsum('dbhij,dbhj->dbhi', state, a_t)
        state = (state * w_t[..., None, :] + sa[..., :, None] * b_t[..., None, :]
                 + v_t[..., :, None] * k_t[..., None, :])
        y = jnp.einsum('dbhij,dbhj->dbhi', state, r_t)
        return state, y

    d, bsz, _, h, n = r.shape
    s0 = jnp.zeros((d, bsz, h, n, n), jnp.float32)
    _, ys = lax.scan(step, s0, xs)
    return jnp.moveaxis(ys, 0, 2)


def rwkv7_time_mix(z, w0, w2, a0, a2, g2, k_k, k_a, r_k, lnx_w, lnx_b):
    f32 = jnp.float32
    bsz, t = z.shape[0], z.shape[1]
    c = RWKV_WIDTH
    hn = (RWKV_HEADS, HEAD_DIM)
    r = z[..., :c]
    k = z[..., c:2 * c]
    v = z[..., 2 * c:3 * c]
    o = 3 * c
    wl = z[..., o:o + DECAY_RANK]
    o += DECAY_RANK
    al = z[..., o:o + ICLR_RANK]
    o += ICLR_RANK
    gl = z[..., o:o + GATE_RANK]
    w_raw = (w0[:, None, None, :] + jnp.einsum('btr,drc->dbtc', jnp.tanh(wl), w2)).astype(f32)
    decay = jnp.exp(-jnp.exp(-jax.nn.softplus(-w_raw) - 0.5))
    a = jax.nn.sigmoid((a0[:, None, None, :] + jnp.einsum('btr,drc->dbtc', al, a2)).astype(f32))
    g = (jax.nn.sigmoid(gl) @ g2).astype(f32)
    kk = (k * k_k).astype(f32).reshape(bsz, t, *hn)
    kk = kk / jnp.maximum(jnp.sqrt(jnp.sum(kk * kk, axis=-1, keepdims=True)), 1e-12)
    k_dir = (k.astype(f32)[None] * (1.0 + (a - 1.0) * k_a.astype(f32))).reshape(2, bsz, t, *hn)
    a_h = a.reshape(2, bsz, t, *hn)
    rh = r.astype(f32).reshape(bsz, t, *hn)
    vh = v.astype(f32).reshape(bsz, t, *hn)

    def orient(u):
        return jnp.stack([u[0], jnp.flip(u[1], axis=1)])

    def both(u):
        return jnp.stack([u, jnp.flip(u, axis=1)])

    ys = rwkv7_bidir_scan(both(rh), orient(decay.reshape(2, bsz, t, *hn)), orient(k_dir),
                          both(vh), both(-kk), orient(kk[None] * a_h))
    y = ys[0] + jnp.flip(ys[1], axis=1)
    mu = jnp.mean(y, axis=-1, keepdims=True)
    var = jnp.mean(jnp.square(y - mu), axis=-1, keepdims=True)
    y = ((y - mu) * lax.rsqrt(var + GN_EPS)).reshape(bsz, t, c) * lnx_w.astype(f32) + lnx_b.astype(f32)
    bonus = jnp.einsum('bthn,dbthn,dhn->bth', rh, k_dir, r_k.astype(f32))[..., None] * vh
    out = (y + bonus.reshape(bsz, t, c)) * g
    return out.astype(z.dtype)


def setup_inputs(seed: int = 0) -> dict:
    key = jax.random.key(seed)
    ks = jax.random.split(key, 32)
    f32 = jnp.float32
    L = DEPTH

    def nrm(k, shape, scale):
        return jax.random.normal(k, shape, f32) * scale

    def gain(k, shape):
        return 1.0 + 0.05 * jax.random.normal(k, shape, f32)

    return {
        'x': nrm(ks[0], (BATCH, SEQ, D_MODEL), 1.0),
        'p': nrm(ks[1], (DEPTH, BATCH, SEQ, PLE_DIM), 1.0),
        'norm_mix': gain(ks[2], (L, D_MODEL)),
        'w_in': nrm(ks[3], (L, D_MODEL, IN_COLS), D_MODEL ** -0.5),
        'shift_mu': jax.random.uniform(ks[4], (L, RWKV_COLS), f32),
        'q_norm': gain(ks[5], (L, HEAD_DIM)),
        'k_norm': gain(ks[6], (L, HEAD_DIM)),
        'sink': nrm(ks[7], (L, ATTN_Q_HEADS), 0.5),
        'w0': jax.random.uniform(ks[8], (L, 2, RWKV_WIDTH), f32, minval=-6.0, maxval=1.0),
        'w2': nrm(ks[9], (L, 2, DECAY_RANK, RWKV_WIDTH), 0.1),
        'a0': nrm(ks[10], (L, 2, RWKV_WIDTH), 0.5),
        'a2': nrm(ks[11], (L, 2, ICLR_RANK, RWKV_WIDTH), ICLR_RANK ** -0.5),
        'g2': nrm(ks[12], (L, GATE_RANK, RWKV_WIDTH), GATE_RANK ** -0.5),
        'k_k': 0.85 + 0.05 * jax.random.normal(ks[13], (L, RWKV_WIDTH), f32),
        'k_a': gain(ks[14], (L, RWKV_WIDTH)),
        'r_k': nrm(ks[15], (L, 2, RWKV_HEADS, HEAD_DIM), 0.1),
        'lnx_w': gain(ks[16], (L, RWKV_WIDTH)),
        'lnx_b': nrm(ks[17], (L, RWKV_WIDTH), 0.02),
        'w_up_attn': nrm(ks[18], (L, ATTN_WIDTH, D_MODEL), ATTN_WIDTH ** -0.5),
        'w_up_rwkv': nrm(ks[19], (L, RWKV_WIDTH, D_MODEL), RWKV_WIDTH ** -0.5),
        'w_out': nrm(ks[20], (L, D_MODEL, D_MODEL), D_MODEL ** -0.5),
        'norm_ffn': gain(ks[21], (L, D_MODEL)),
        'w_ff1': nrm(ks[22], (L, D_MODEL, D_FF), D_MODEL ** -0.5),
        'w_ff2': nrm(ks[23], (L, D_FF, D_MODEL), D_FF ** -0.5),
        'norm_ple': gain(ks[24], (L, D_MODEL)),
        'w_ple_gate': nrm(ks[25], (L, D_MODEL, D_MODEL), D_MODEL ** -0.5),
        'w_ple': nrm(ks[26], (L, PLE_DIM, D_MODEL), PLE_DIM ** -0.5),
    }


def reference(x, p, norm_mix, w_in, shift_mu, q_norm, k_norm, sink, w0, w2, a0, a2, g2,
              k_k, k_a, r_k, lnx_w, lnx_b, w_up_attn, w_up_rwkv, w_out, norm_ffn,
              w_ff1, w_ff2, norm_ple, w_ple_gate, w_ple):
    bsz, s = x.shape[0], x.shape[1]
    positions = jnp.arange(s)
    for i in range(DEPTH):
        h = rms_norm(x, norm_mix[i])
        proj = h @ w_in[i]
        q = proj[..., O_Q:O_K].reshape(bsz, s, ATTN_Q_HEADS, HEAD_DIM)
        k = proj[..., O_K:O_V].reshape(bsz, s, ATTN_KV_HEADS, HEAD_DIM)
        v = proj[..., O_V:O_RWKV].reshape(bsz, s, ATTN_KV_HEADS, HEAD_DIM)
        q = rotary_partial(rms_norm(q, q_norm[i]), positions)
        k = rotary_partial(rms_norm(k, k_norm[i]), positions)
        attn = windowed_gqa(q, k, v, sink[i])
        zr = proj[..., O_RWKV:O_GATE]
        zr = zr + shift_mu[i] * (centred_shift(zr) - zr)
        rw = rwkv7_time_mix(zr, w0[i], w2[i], a0[i], a2[i], g2[i], k_k[i], k_a[i],
                            r_k[i], lnx_w[i], lnx_b[i])
        gates = jax.nn.sigmoid(proj[..., O_GATE:].reshape(bsz, s, N_BRANCH, D_MODEL))
        merged = gates[..., 0, :] * (attn @ w_up_attn[i]) + gates[..., 1, :] * (rw @ w_up_rwkv[i])
        x = x + merged @ w_out[i]
        hf = rms_norm(x, norm_ffn[i])
        x = x + jnp.square(jax.nn.relu(hf @ w_ff1[i])) @ w_ff2[i]
        hp = rms_norm(x, norm_ple[i])
        x = x + (p[i] @ w_ple[i]) * jax.nn.sigmoid(hp @ w_ple_gate[i])
    return x
```

```python
import os
import numpy as np
from contextlib import ExitStack
import concourse.bass as bass
import concourse.mybir as mybir
from concourse.bass_utils import run_bass_kernel_spmd

F32 = mybir.dt.float32
BF16 = mybir.dt.bfloat16
ALU = mybir.AluOpType
AF = mybir.ActivationFunctionType

SEM_EPOCH = 3000
DMA_SLOTS = 8

D = 1024; SEQ = 4096; HALF = 2048; NB_OWN = 16
TT = 256; NBT = 2; NTH = 8
LWS = -0.6065306597126334


class Op:
    __slots__ = ("eng", "fn", "reads", "writes", "dma", "deps", "flag", "cnt", "slot", "slotval", "idx", "xdeps")

    def __init__(self, eng, fn, reads, writes, dma):
        self.eng = eng; self.fn = fn; self.reads = reads; self.writes = writes; self.dma = dma
        self.deps = set(); self.flag = False; self.cnt = None; self.slot = None; self.slotval = None
        self.xdeps = ()


class Prog:
    def __init__(self):
        self.ops = []
        self.last_barrier = 0

    def add(self, eng, fn, r=(), w=(), dma=False):
        op = Op(eng, fn, tuple(r), tuple(w), dma)
        if getattr(self, "disabled", False):
            op.idx = -1
            return op
        op.idx = len(self.ops)
        self.ops.append(op)
        return op

    def analyse(self):
        last_w = {}
        readers = {}
        for op in self.ops:
            for j in op.xdeps:
                op.deps.add((j, "raw"))
            for k in op.reads:
                j = last_w.get(k)
                if j is not None:
                    op.deps.add((j, "raw"))
            for k in op.writes:
                j = last_w.get(k)
                if j is not None:
                    op.deps.add((j, "waw"))
                for j in readers.get(k, ()):
                    if j != op.idx:
                        op.deps.add((j, "war"))
            for k in op.reads:
                lst = readers.setdefault(k, [])
                if not op.dma:
                    lst[:] = [j for j in lst if self.ops[j].dma or self.ops[j].eng != op.eng]
                lst.append(op.idx)
            for k in op.writes:
                last_w[k] = op.idx
                readers[k] = []
        for op in self.ops:
            need = set()
            for (j, kind) in op.deps:
                p = self.ops[j]
                if p.dma:
                    need.add(j)
                elif p.eng == op.eng and not op.dma:
                    if kind == "raw" and op.eng != "pe":
                        need.add(j); p.flag = True
                else:
                    need.add(j); p.flag = True
            op.deps = need
        cnt = {}
        for op in self.ops:
            if op.dma or op.fn is None:
                continue
            if op.flag:
                cnt[op.eng] = cnt.get(op.eng, 0) + 1
                op.cnt = cnt[op.eng]
        nd = {}
        for op in self.ops:
            if op.dma:
                n = nd.get(op.eng, 0)
                nd[op.eng] = n + 1
                op.slot = (op.eng, n % DMA_SLOTS)
                op.slotval = 16 * (n // DMA_SLOTS + 1)
        self.nflag = cnt
        self.ndma = nd

    def emit(self, nc, stack):
        self.analyse()
        engs = ["pe", "act", "dve", "pool", "sp"]
        sems = {}
        for e in engs:
            n = self.nflag.get(e, 0)
            for ep in range(n // SEM_EPOCH + 1):
                sems[(e, ep)] = stack.enter_context(nc.semaphore(f"s_{e}_{ep}"))
        dsem = {}
        for e, n in self.ndma.items():
            for s in range(min(n, DMA_SLOTS)):
                dsem[(e, s)] = stack.enter_context(nc.semaphore(f"d_{e}_{s}"))
        block = stack.enter_context(nc.Block())
        per_eng = {e: [op for op in self.ops if op.eng == e] for e in engs}
        ops = self.ops

        def sem_of(p):
            if p.dma:
                return dsem[p.slot], p.slotval
            c = p.cnt - 1
            return sems[(p.eng, c // SEM_EPOCH)], (c % SEM_EPOCH) + 1

        def run(engobj, lst):
            seen = {}
            dma_hist = {}
            for op in lst:
                waits = {}
                for j in op.deps:
                    s, v = sem_of(ops[j])
                    key = id(s)
                    if seen.get(key, 0) >= v:
                        continue
                    if key not in waits or waits[key][1] < v:
                        waits[key] = (s, v)
                if op.dma:
                    prev = dma_hist.get(op.slot)
                    if prev is not None:
                        s, v = dsem[op.slot], prev
                        key = id(s)
                        if seen.get(key, 0) < v and (key not in waits or waits[key][1] < v):
                            waits[key] = (s, v)
                    dma_hist[op.slot] = op.slotval
                for key, (s, v) in waits.items():
                    engobj.wait_ge(s, v)
                    seen[key] = v
                if op.fn is None:
                    continue
                ins = op.fn(engobj)
                if op.dma:
                    ins.then_inc(dsem[op.slot], 16)
                elif op.flag:
                    s, v = sem_of(op)
                    ins.then_inc(s, 1)

        if per_eng["pe"]:
            @block.tensor
            def _(e):
                run(e, per_eng["pe"])
        if per_eng["act"]:
            @block.scalar
            def _(e):
                run(e, per_eng["act"])
        if per_eng["dve"]:
            @block.vector
            def _(e):
                run(e, per_eng["dve"])
        if per_eng["pool"]:
            @block.gpsimd
            def _(e):
                run(e, per_eng["pool"])
        if per_eng["sp"]:
            @block.sync
            def _(e):
                run(e, per_eng["sp"])


VC = {}
_o = 0
for _n, _c in [("mu", 14), ("k_k", 4), ("k_a", 4), ("rkA", 4), ("rkB", 4), ("lnw", 4), ("lnb", 4),
               ("a0A", 4), ("a0B", 4), ("qg", 1), ("kg", 1), ("nmix", 8), ("nffn", 8), ("nple", 8),
               ("sink", 4), ("omm", 14), ("hmu", 14), ("nkk", 4), ("omka", 4), ("esink", 4), ("ha0A", 4), ("ha0B", 4)]:
    VC[_n] = _o
    _o += _c
NVEC_IN = VC["omm"]
NVEC = _o

N_SW = 20


def build_program(dbg=None, upto=99, ntiles_other=None, ntiles_own=None):
    nc = bass.Bass("TRN2", target_bir_lowering=False)
    P = Prog()

    def din(name, shape):
        return nc.dram_tensor(name, list(shape), F32, kind="ExternalInput").ap()

    xs = din("xs", [SEQ + 2, D])
    pp = din("pp", [HALF, 256])
    wsw = din("wsw", [D, N_SW * 128])
    wgt = din("wgt", [D, 2048])
    vec_d = din("vec", [128, NVEC_IN])
    w2aug_d = din("w2aug", [2, 65, 512])
    a2_d = din("a2", [2, 64, 512])
    g2_d = din("g2", [128, 512])
    wua_d = din("wua", [512, D]); wur_d = din("wur", [512, D]); wout_d = din("wout", [D, D])
    wff1_d = din("wff1", [D, 4096]); wff2_d = din("wff2", [4096, D])
    wpg_d = din("wpg", [D, D]); wple_d = din("wple", [256, D])
    rope_d = din("rope", [128, 2, HALF + 512])
    cst_d = din("cst", [128, 12, 128])
    tri_d = din("tri", [128, 2, 128])
    out_d = nc.dram_tensor("out", [HALF, D], F32, kind="ExternalOutput").ap()
    m0_s = nc.dram_tensor("m0_s", [2, 64, 128, 4, 128], BF16, kind="Internal").ap()
    n0_s = nc.dram_tensor("n0_s", [2, 64, 128, 4, 64], BF16, kind="Internal").ap()
    q_s = nc.dram_tensor("q_s", [2, 16, 128, 4, 128], BF16, kind="Internal").ap()
    x1_s = nc.dram_tensor("x1_s", [HALF, D], F32, kind="Internal").ap()
    y0_s = nc.dram_tensor("y0_s", [2, 128, 4, HALF], F32, kind="Internal").ap()
    ao_s = nc.dram_tensor("ao_s", [128, 4, HALF], BF16, kind="Internal").ap()
    bv_s = nc.dram_tensor("bv_s", [128, 4, HALF], BF16, kind="Internal").ap()
    sgl_s = nc.dram_tensor("sgl_s", [128, HALF], BF16, kind="Internal").ap()
    dbg_out = None
    if dbg is not None:
        dbg_out = nc.dram_tensor("dbg", [128, 16384], F32, kind="ExternalOutput").ap()

    def MM(out, lhsT, rhs, start=True, stop=True, r=(), w=()):
        P.add("pe", lambda e: e.matmul(out, lhsT, rhs, start=start, stop=stop), r, w)

    def TR(out, in_, idn, r=(), w=()):
        P.add("pe", lambda e: e.transpose(out, in_, idn), r, w)

    def ACT(out, in_, func, bias=None, scale=None, accum=None, r=(), w=()):
        kw = {}
        if bias is not None:
            kw["bias"] = bias
        if scale is not None:
            kw["scale"] = scale
        if accum is not None:
            kw["accum_out"] = accum
        P.add("act", lambda e: e.activation(out=out, in_=in_, func=func, **kw), r, w)

    def TTo(eng, out, a, b, op, r=(), w=()):
        P.add(eng, lambda e: e.tensor_tensor(out, a, b, op), r, w)

    def TS(eng, out, a, s1, s2, op0, op1=None, r=(), w=()):
        if op1 is None:
            P.add(eng, lambda e: e.tensor_scalar(out, a, s1, None, op0), r, w)
        else:
            P.add(eng, lambda e: e.tensor_scalar(out, a, s1, s2, op0, op1), r, w)

    def STT(eng, out, in0, sc, in1, op0, op1, r=(), w=()):
        P.add(eng, lambda e: e.scalar_tensor_tensor(out, in0, sc, in1, op0, op1), r, w)

    def CP(eng, out, in_, r=(), w=()):
        P.add(eng, lambda e: e.tensor_copy(out, in_), r, w)

    def RCP(out, in_, r=(), w=()):
        P.add("dve", lambda e: e.reciprocal(out, in_), r, w)

    def MEMSET(eng, ap, val, w=()):
        P.add(eng, lambda e: e.memset(ap, val), (), w)

    def DMA(eng, out, in_, r=(), w=()):
        P.add(eng, lambda e: e.dma_start(out=out, in_=in_), r, w, dma=True)

    with ExitStack() as gs:
        def sb(name, shape, dt=F32, st=None):
            return (st or gs).enter_context(nc.sbuf_tensor("sb_" + name, list(shape), dt))

        psA = [gs.enter_context(nc.psum_tensor(f"psA{i}", [128, 512], F32)) for i in range(6)]
        psT = [gs.enter_context(nc.psum_tensor(f"psT{i}", [128, 1024], BF16)) for i in range(2)]
        pk = lambda i: ("ps", i)
        pkT = lambda i: ("psT", i)

        vec = sb("vec", [128, NVEC])
        cstb = sb("cstb", [128, 12, 128], BF16)
        cstf = sb("cstf", [128, 2, 128])
        tri = sb("tri", [128, 2, 128])
        eps_c = sb("eps_c", [128, 4])
        nhalf = sb("nhalf", [128, TT])
        cstm = sb("cstm", [128, 4, 4, 128], BF16)
        DMA("sp", vec[:, 0:NVEC_IN], vec_d, w=["vec"])
        DMA("pool", cstb[:], cst_d, w=["cstb"])
        DMA("sp", cstf[:, 0, :], cst_d[:, 0, :], w=["cstf"])
        DMA("sp", cstf[:, 1, :], cst_d[:, 9, :], w=["cstf"])
        DMA("sp", tri[:], tri_d, w=["tri"])
        MEMSET("dve", eps_c[:, 0:1], 1e-6, w=["eps"])
        MEMSET("pool", nhalf[:], -0.5, w=["nhalf"])
        for _m in range(4):
            for _r in range(4):
                DMA("pool", cstm[:, _m, _r, :], cst_d[:, 4 + _m, :], w=["cstm"])
        MEMSET("dve", eps_c[:, 1:2], 1e-24, w=["eps"])
        MEMSET("dve", eps_c[:, 2:3], 64e-5, w=["eps"])
        ident = cstb[:, 0, :]; onesblk = cstb[:, 1, :]; perm = cstb[:, 2, :]; bdm = cstb[:, 3, :]
        identf = cstf[:, 0, :]
        vc = lambda n, i=0: vec[:, VC[n] + i:VC[n] + i + 1]
        TS("dve", vec[:, VC["omm"]:VC["omm"] + 14], vec[:, VC["mu"]:VC["mu"] + 14], -1.0, 1.0, ALU.mult, ALU.add, r=["vec"], w=["vec2"])
        TS("dve", vec[:, VC["hmu"]:VC["hmu"] + 14], vec[:, VC["mu"]:VC["mu"] + 14], 0.5, None, ALU.mult, r=["vec"], w=["vec2"])
        TS("dve", vec[:, VC["nkk"]:VC["nkk"] + 4], vec[:, VC["k_k"]:VC["k_k"] + 4], -1.0, None, ALU.mult, r=["vec"], w=["vec2"])
        TS("dve", vec[:, VC["omka"]:VC["omka"] + 4], vec[:, VC["k_a"]:VC["k_a"] + 4], -1.0, 1.0, ALU.mult, ALU.add, r=["vec"], w=["vec2"])
        ACT(vec[:, VC["esink"]:VC["esink"] + 4], vec[:, VC["sink"]:VC["sink"] + 4], AF.Exp, r=["vec"], w=["vec2"])
        TS("dve", vec[:, VC["ha0A"]:VC["ha0A"] + 8], vec[:, VC["a0A"]:VC["a0A"] + 8], 0.5, None, ALU.mult, r=["vec"], w=["vec2"])
        VK = ["vec", "vec2", "eps", "cstb", "cstf", "tri"]

        def ck(name):
            if os.environ.get("KSTOP") == name:
                P.disabled = True
        if os.environ.get("KSTOP") == "c":
            P.disabled = True

        bar_t = {e: sb(f"bar_{e}", [1, 8]) for e in ["act", "dve", "pool"]}

        def RSQ(dst, src, scale, bias_ap, n, r=(), w=()):
            ACT(dst, src, AF.Identity, bias=bias_ap, scale=scale, r=list(r) + ["eps"], w=list(w))
            TTo("pool", dst, dst, nhalf[0:dst.shape[0], 0:n], ALU.pow, r=list(w) + ["nhalf"], w=list(w))

        def barrier():
            if getattr(P, "disabled", False):
                return
            n = len(P.ops)
            pend = [op.idx for op in P.ops[P.last_barrier:n]]
            marks = []
            for e in ["act", "dve", "pool"]:
                t = bar_t[e]
                marks.append(P.add(e, (lambda tt, ee: (lambda en: en.memzero(tt[:]) if ee == 'act' else en.memset(tt[:], 0.0)))(t, e), (), [f"bar_{e}_{n}"]).idx)
            marks.append(P.add("pe", lambda e: e.matmul(psA[0][0:1, 0:1], cstb[0:1, 8, 0:1], cstb[0:1, 8, 0:1], start=True, stop=True),
                               ["cstb", pk(0)], [pk(0), f"bar_pe_{n}"]).idx)
            for e in ["pe", "act", "dve", "pool", "sp"]:
                f = P.add(e, None, (), ())
                f.xdeps = tuple(marks) + tuple(i for i in pend if P.ops[i].dma)
            P.last_barrier = len(P.ops)

        def make_hT(st_key, x_aps, nrows, hT, gain, wk, xt, hb, ssq, rstd):
            nblk = len(x_aps)
            for bi, (xap, rows, col0) in enumerate(x_aps):
                s = bi % 2
                DMA("sp", xt[0:rows, s, :], xap, w=[("xt", s)])
                ACT(hb[0:rows, s, :], xt[0:rows, s, :], AF.Square, accum=ssq[0:rows, s:s + 1], r=[("xt", s)], w=[("hb", s), ("ssq", s)])
                RSQ(rstd[0:rows, s:s + 1], ssq[0:rows, s:s + 1], 1.0 / D, eps_c[0:rows, 0:1], 1, r=[("ssq", s)], w=[("rstd", s)])
                TS("dve", hb[0:rows, s, :], xt[0:rows, s, :], rstd[0:rows, s:s + 1], None, ALU.mult, r=[("xt", s), ("rstd", s), ("hb", s)], w=[("hbo", s)])
                for kc in range(8):
                    TR(psT[kc % 2][:, (kc // 2) * 128:(kc // 2) * 128 + rows], hb[0:rows, s, kc * 128:(kc + 1) * 128], ident[0:rows, 0:rows],
                       r=[("hbo", s), "cstb"], w=[pkT(kc % 2)])
                for kc in range(8):
                    eng_act = (kc % 2 == 0)
                    src = psT[kc % 2][:, (kc // 2) * 128:(kc // 2) * 128 + rows]
                    dst = hT[:, kc, col0:col0 + rows]
                    if eng_act:
                        ACT(dst, src, AF.Copy, scale=vc(gain, kc), r=[pkT(kc % 2), "vec"], w=[wk])
                    else:
                        TS("dve", dst, src, vc(gain, kc), None, ALU.mult, r=[pkT(kc % 2), "vec"], w=[wk])

        g2t = sb("g2t", [128, 512], BF16)
        PCs = sb("PCs", [128, 2, 64, 4], F32)
        with ExitStack() as s1:
            win = sb("win", [128, 8, N_SW * 128], BF16, s1)
            for kc in range(8):
                DMA("pool", win[:, kc, :], wsw[kc * 128:(kc + 1) * 128, :], w=["win"])
            w2aug = sb("w2aug", [65, 2, 512], BF16, s1)
            a2t = sb("a2t", [128, 2, 512], BF16, s1)
            rope = sb("rope", [128, 2, HALF + 512], F32, s1)
            for d in range(2):
                DMA("pool", w2aug[:, d, :], w2aug_d[d], w=["w2aug"])
                DMA("pool", a2t[64:128, d, :], a2_d[d], w=["a2t"])
            DMA("pool", g2t[:], g2_d, w=["g2t"])
            DMA("sp", rope[:], rope_d, w=["rope"])
            if os.environ.get("KSTOP") == "w":
                P.disabled = True
            kT = sb("kT", [128, HALF + 128], BF16, s1)
            vaug = sb("vaug", [128, NB_OWN + 1, 2, 2, 64], BF16, s1)
            attn_o = sb("attn_o", [128, 4, TT], BF16, s1)
            sgl = sb("sgl", [128, TT], BF16, s1)
            bv = sb("bv", [128, 4, TT], BF16, s1)
            Y0st = sb("Y0st", [128, 4, 128], F32, s1)
            MEMSET("pool", vaug[:, :, :, 1, :], 1.0, w=["vaug1"])
            xt = sb("xt", [128, 2, D], F32, s1); hb = sb("hb", [128, 2, D], BF16, s1)
            ssq = sb("ssq", [128, 2], F32, s1); rstd = sb("rstd", [128, 2], F32, s1)
            hT = sb("hT", [128, 8, TT + 2], BF16, s1)
            zr = sb("zr", [128, 2, TT + 2], F32, s1)
            zs = sb("zs", [128, 2, TT], F32, s1)
            ztmp = sb("ztmp", [128, 2, TT], F32, s1)
            zq = sb("zq", [128, TT], F32, s1)
            qsq = sb("qsq", [128, TT], BF16, s1)
            qrs = sb("qrs", [128, TT], F32, s1)
            qn = sb("qn", [128, TT], BF16, s1)
            qt1 = sb("qt1", [128, TT], F32, s1); qt2 = sb("qt2", [128, TT], F32, s1)
            qT = sb("qT", [128, 2, 4, TT], BF16, s1)
            vb = sb("vb", [128, TT], BF16, s1)
            twl = sb("twl", [65, TT], BF16, s1); alx = sb("alx", [128, TT], BF16, s1)
            MEMSET("pool", twl[64:65, :], 1.0, w=["twl1"])
            if os.environ.get("KSTOP") == "m":
                P.disabled = True
            sgfc = sb("sgfc", [128, NBT, 128], F32, s1)
            r_f = sb("r_f", [128, TT], F32, s1); k_f = sb("k_f", [128, TT], F32, s1); v_f = sb("v_f", [128, TT], F32, s1)
            nkk = sb("nkk", [128, TT], F32, s1)
            a_f = sb("a_f", [128, TT], F32, s1); b_f = sb("b_f", [128, TT], F32, s1); kd_f = sb("kd_f", [128, TT], F32, s1)
            t1_f = sb("t1_f", [128, TT], F32, s1)
            pb = sb("pb", [128, TT], BF16, s1)
            E_f = sb("E_f", [128, TT], F32, s1); G_f = sb("G_f", [128, TT], F32, s1)
            AR = sb("AR", [128, 2, 4, NBT, 256], BF16, s1)
            bt = sb("bt", [128, 2, 4, TT], BF16, s1); kt = sb("kt", [128, 2, 4, TT], BF16, s1)
            btT = sb("btT", [128, 2, NBT, 512], BF16, s1); ktT = sb("ktT", [128, 2, NBT, 512], BF16, s1)
            atT = sb("atT", [128, 2, NBT, 512], BF16, s1)
            vT = sb("vT", [128, NBT, 512], BF16, s1)
            Aab = sb("Aab", [128, 8, 128], BF16, s1); AabT = sb("AabT", [128, 8, 128], BF16, s1)
            Aak = sb("Aak", [128, 8, 128], BF16, s1); Arb = sb("Arb", [128, 8, 128], BF16, s1); Ark = sb("Ark", [128, 8, 128], BF16, s1)
            Ap = [sb(f"Ap{i}", [128, 8, 128], BF16, s1) for i in range(2)]
            ATp = [sb(f"ATp{i}", [128, 8, 128], BF16, s1) for i in range(2)]
            Zp = [sb(f"Zp{i}", [128, 4, 2, 2, 64], BF16, s1) for i in range(2)]
            M0st = sb("M0st", [128, 2, 4, 128], BF16, s1); N0st = sb("N0st", [128, 2, 4, 64], BF16, s1)
            Qst = sb("Qst", [128, 4, 128], BF16, s1)
            probs = sb("probs", [128, 2, 384], BF16, s1)
            den = sb("den", [128, 128], F32, s1)

            def proj_chunk(ci, bank, halo_bank=None):
                if halo_bank is None:
                    for kc in range(8):
                        MM(psA[bank][:, 0:TT], win[:, kc, ci * 128:(ci + 1) * 128], hT[:, kc, 1:TT + 1], start=(kc == 0), stop=(kc == 7),
                           r=["win", "hT"], w=[pk(bank)])
                else:
                    for kc in range(8):
                        MM(psA[bank][:, 0:TT + 2], win[:, kc, ci * 128:(ci + 1) * 128], hT[:, kc, 0:TT + 2], start=(kc == 0), stop=(kc == 7),
                           r=["win", "hT"], w=[pk(bank)])

            zcount = [0]

            def rwkv_z(ci, mui, out_ap, wkey, post=None):
                s = zcount[0] % 2
                zcount[0] += 1
                bank = s
                proj_chunk(ci, bank, 2)
                ACT(zr[:, s, :], psA[bank][:, 0:TT + 2], AF.Copy, r=[pk(bank)], w=[("zr", s)])
                ACT(zs[:, s, :], psA[bank][:, 1:TT + 1], AF.Copy, scale=vc("omm", mui), r=[pk(bank), "vec2"], w=[("zs", s)])
                TTo("pool", ztmp[:, s, :], zr[:, s, 0:TT], zr[:, s, 2:TT + 2], ALU.add, r=[("zr", s)], w=[("ztmp", s)])
                STT("dve", out_ap, ztmp[:, s, :], vc("hmu", mui), zs[:, s, :], ALU.mult, ALU.add, r=[("ztmp", s), ("zs", s), "vec2"], w=[wkey])

            def sweep_tile(t0, own, first_other, pcidx0):
                dirs = [0, 1] if own else [1]
                blk0 = t0 // 128
                xaps = [(xs[t0 + 128 * i: t0 + 128 * i + 128, :], 128, 128 * i) for i in range(NBT)]
                xaps.append((xs[t0 + TT: t0 + TT + 2, :], 2, TT))
                make_hT("hT", xaps, None, hT, "nmix", "hT", xt, hb, ssq, rstd)
                ck("h")
                qs = (t0 // TT) % 2
                if own or first_other:
                    for ci in ([0, 1, 2, 3, 4] if own else [4]):
                        bank = ci % 2
                        proj_chunk(ci, bank)
                        gcol = "kg" if ci == 4 else "qg"
                        ACT(qsq[:], psA[bank][:, 0:TT], AF.Square, r=[pk(bank)], w=["qsq"])
                        MM(psA[3][:, 0:TT], onesblk, qsq[:], r=["cstb", "qsq"], w=[pk(3)])
                        RSQ(qrs[:], psA[3][:, 0:TT], 1.0 / 64, eps_c[:, 0:1], TT, r=[pk(3)], w=["qrs"])
                        STT("dve", qn[:], psA[bank][:, 0:TT], vc(gcol), qrs[:], ALU.mult, ALU.mult, r=[pk(bank), "qrs", "vec"], w=["qn"])
                        MM(psA[3][:, 0:TT], perm, qn[:], r=["cstb", "qn"], w=[pk(3)])
                        TTo("pool", qt1[:], qn[:], rope[:, 0, t0:t0 + TT], ALU.mult, r=["qn", "rope"], w=["qt1"])
                        TTo("dve", qt2[:], psA[3][:, 0:TT], rope[:, 1, t0:t0 + TT], ALU.mult, r=[pk(3), "rope"], w=["qt2"])
                        if ci < 4:
                            TTo("pool", qT[:, qs, ci, :], qt1[:], qt2[:], ALU.add, r=["qt1", "qt2"], w=[("qT", qs)])
                        else:
                            n = TT if own else 128
                            TTo("pool", kT[:, t0:t0 + n], qt1[:, 0:n], qt2[:, 0:n], ALU.add, r=["qt1", "qt2"], w=["kT"])
                    proj_chunk(5, 0)
                    ACT(vb[:], psA[0][:, 0:TT], AF.Copy, r=[pk(0)], w=["vb"])
                    nbv = NBT if own else 1
                    for bi in range(nbv):
                        TR(psT[0][:, bi * 128:(bi + 1) * 128], vb[:, bi * 128:(bi + 1) * 128], ident, r=["vb", "cstb"], w=[pkT(0)])
                    for bi in range(nbv):
                        CP("dve", vaug[:, blk0 + bi, :, 0, :], psT[0][:, bi * 128:(bi + 1) * 128].rearrange("p (g d) -> p g d", g=2),
                           r=[pkT(0)], w=["vaug"])
                ck("q")
                rwkv_z(6, 0, zq[:], "zq")
                ACT(twl[0:64, :], zq[0:64, :], AF.Tanh, r=["zq"], w=["twl"])
                CP("pool", alx[64:128, :], zq[64:128, :], r=["zq"], w=["alx"])
                ck("z")
                if own:
                    rwkv_z(7, 1, zq[:], "zq")
                    ACT(zq[:], zq[:], AF.Tanh, scale=0.5, r=["zq"], w=["zq"])
                    TS("pool", sgl[:, :], zq[:], 0.5, 0.5, ALU.mult, ALU.add, r=["zq"], w=["sgl"])
                    DMA("sp", sgl_s[:, t0:t0 + TT], sgl[:, :], r=["sgl"], w=[("sgl_s", t0 // TT)])
                for fc in range(4):
                    if own:
                        rwkv_z(8 + 3 * fc, 2 + 3 * fc, r_f[:], "r_f")
                    rwkv_z(9 + 3 * fc, 3 + 3 * fc, k_f[:], "k_f")
                    rwkv_z(10 + 3 * fc, 4 + 3 * fc, v_f[:], "v_f")
                    ACT(qsq[:], k_f[:], AF.Square, scale=vc("k_k", fc), r=["k_f", "vec"], w=["qsq"])
                    MM(psA[3][:, 0:TT], onesblk, qsq[:], r=["cstb", "qsq"], w=[pk(3)])
                    RSQ(qrs[:], psA[3][:, 0:TT], 1.0, eps_c[:, 1:2], TT, r=[pk(3)], w=["qrs"])
                    STT("dve", nkk[:], k_f[:], vc("nkk", fc), qrs[:], ALU.mult, ALU.mult, r=["k_f", "qrs", "vec2"], w=["nkk"])
                    ck("k")
                    CP("pool", vb[:], v_f[:], r=["v_f"], w=["vb"])
                    for bi in range(NBT):
                        TR(psT[0][:, bi * 128:(bi + 1) * 128], vb[:, bi * 128:(bi + 1) * 128], ident, r=["vb", "cstb"], w=[pkT(0)])
                    CP("dve", vT[:, :, fc * 128:(fc + 1) * 128], psT[0][:, 0:NBT * 128].rearrange("p (b c) -> p b c", b=NBT), r=[pkT(0)], w=["vT"])
                    for d in dirs:
                        dn = "AB"[d]
                        MM(psA[3][:, 0:TT], a2t[64:128, d, fc * 128:(fc + 1) * 128], alx[64:128, :], r=["a2t", "alx"], w=[pk(3)])
                        ACT(a_f[:], psA[3][:, 0:TT], AF.Tanh, bias=vc("ha0" + dn, fc), scale=0.5, r=[pk(3), "vec2"], w=["a_f"])
                        TS("pool", a_f[:], a_f[:], 0.5, 0.5, ALU.mult, ALU.add, r=["a_f"], w=["a_f"])
                        STT("dve", b_f[:], nkk[:], -1.0, a_f[:], ALU.mult, ALU.mult, r=["nkk", "a_f"], w=["b_f"])
                        TS("dve", t1_f[:], a_f[:], vc("k_a", fc), vc("omka", fc), ALU.mult, ALU.add, r=["a_f", "vec", "vec2"], w=["t1_f"])
                        TTo("pool", kd_f[:], k_f[:], t1_f[:], ALU.mult, r=["k_f", "t1_f"], w=["kd_f"])
                        if own:
                            STT("dve", pb[:], r_f[:], vc("rk" + dn, fc), kd_f[:], ALU.mult, ALU.mult, r=["r_f", "kd_f", "vec"], w=["pb"])
                            MM(psA[2][:, 0:TT], onesblk, pb[:], start=(d == 0), stop=(d == 1), r=["cstb", "pb"], w=[pk(2)])
                        for bi in range(NBT):
                            MM(psA[4][:, bi * 128:(bi + 1) * 128], twl[:, bi * 128:(bi + 1) * 128], w2aug[:, d, fc * 128:(fc + 1) * 128],
                               r=["twl", "twl1", "w2aug"], w=[pk(4)])
                        ACT(sgfc[:, :, :], psA[4][:, 0:NBT * 128].rearrange("p (b c) -> p b c", b=NBT), AF.Tanh, scale=0.5, r=[pk(4)], w=["sgfc"])
                        TS("pool", sgfc[:, :, :], sgfc[:, :, :], 1.0, None, ALU.add, r=["sgfc"], w=["sgfc"])
                        for bi in range(NBT):
                            MM(psA[3][:, bi * 128:(bi + 1) * 128], sgfc[:, bi, :], tri[:, d, :], r=["sgfc", "tri"], w=[pk(3)])
                        ACT(E_f[:], psA[3][:, 0:TT], AF.Exp, r=[pk(3)], w=["E_f"])
                        ACT(G_f[:], psA[3][:, 0:TT], AF.Exp, scale=-1.0, r=[pk(3)], w=["G_f"])
                        Ev = E_f[:].rearrange("p (c t) -> p c t", t=64)
                        pcol = 63 if d == 0 else 0
                        CP("pool", PCs[:, d, pcidx0:pcidx0 + TT // 64, fc], Ev[:, :, pcol], r=["E_f"], w=["PCs"])
                        ARv = AR[:, d, fc, :, :]
                        if own:
                            TTo("dve", ARv[:, :, 128:256], r_f[:].rearrange("p (b t) -> p b t", t=128), E_f[:].rearrange("p (b t) -> p b t", t=128),
                                ALU.mult, r=["r_f", "E_f"], w=["AR"])
                        ARa = AR[:, d, fc, :, 0:128].rearrange("p b (c t) -> p b c t", t=64)
                        nk4 = nkk[:].rearrange("p (b c t) -> p b c t", c=2, t=64)
                        E4 = E_f[:].rearrange("p (b c t) -> p b c t", c=2, t=64)
                        for bi in range(NBT):
                            if d == 0:
                                TTo("pool", ARa[:, bi, :, 1:64], nk4[:, bi, :, 1:64], E4[:, bi, :, 0:63], ALU.mult, r=["nkk", "E_f"], w=["AR"])
                                CP("pool", ARa[:, bi, :, 0:1], nk4[:, bi, :, 0:1], r=["nkk"], w=["AR"])
                            else:
                                TTo("pool", ARa[:, bi, :, 0:63], nk4[:, bi, :, 0:63], E4[:, bi, :, 1:64], ALU.mult, r=["nkk", "E_f"], w=["AR"])
                                CP("pool", ARa[:, bi, :, 63:64], nk4[:, bi, :, 63:64], r=["nkk"], w=["AR"])
                        TTo("dve", bt[:, d, fc, :], b_f[:], G_f[:], ALU.mult, r=["b_f", "G_f"], w=["bt"])
                        TTo("pool", kt[:, d, fc, :], kd_f[:], G_f[:], ALU.mult, r=["kd_f", "G_f"], w=["kt"])
                        for (src_key, dst, dkey, which) in (("bt", btT, "btT", 0), ("kt", ktT, "ktT", 1), ("AR", atT, "atT", 2)):
                            pb_i = which % 2
                            for bi in range(NBT):
                                if which == 0:
                                    src = bt[:, d, fc, bi * 128:(bi + 1) * 128]
                                elif which == 1:
                                    src = kt[:, d, fc, bi * 128:(bi + 1) * 128]
                                else:
                                    src = AR[:, d, fc, bi, 0:128]
                                TR(psT[pb_i][:, bi * 128:(bi + 1) * 128], src, ident, r=[src_key, "cstb"], w=[pkT(pb_i)])
                            CP("dve" if which != 1 else "act", dst[:, d, :, fc * 128:(fc + 1) * 128],
                               psT[pb_i][:, 0:NBT * 128].rearrange("p (b c) -> p b c", b=NBT), r=[pkT(pb_i)], w=[dkey]) if which != 1 else \
                                ACT(dst[:, d, :, fc * 128:(fc + 1) * 128], psT[pb_i][:, 0:NBT * 128].rearrange("p (b c) -> p b c", b=NBT), AF.Copy,
                                    r=[pkT(pb_i)], w=[dkey])
                    if own:
                        TTo("dve", bv[:, fc, :], psA[2][:, 0:TT], v_f[:], ALU.mult, r=[pk(2), "v_f"], w=["bv"])
                        DMA("sp", bv_s[:, fc, t0:t0 + TT], bv[:, fc, :], r=["bv"], w=[("bv_s", t0 // TT, fc)])
                ck("f")
                for d in dirs:
                    ms, mi = (4, 5) if d == 0 else (6, 7)
                    for bi in range(NBT):
                        gblk = (t0 // 128 + bi) if own else None
                        def amat(bank0, lh, rh, keys):
                            for h in range(8):
                                fc = h // 2; rows = slice((h % 2) * 64, (h % 2) * 64 + 64)
                                MM(psA[bank0 + h % 2][:, (h // 2) * 128:(h // 2) * 128 + 128], lh(rows, fc), rh(rows, fc), r=keys, w=[pk(bank0 + h % 2)])

                        def aevac(bank0, dst, dkey, mslot):
                            dv = dst[:].rearrange("p (i two) t -> p two i t", two=2)
                            for par in range(2):
                                TTo("dve", dv[:, par, :, :], psA[bank0 + par][:, :].rearrange("p (i t) -> p i t", i=4), cstm[:, mslot - 4, :, :], ALU.mult,
                                    r=[pk(bank0 + par), "cstm"], w=[dkey])
                        tokc = slice(bi * 128, (bi + 1) * 128)
                        amat(0, lambda rows, fc: bt[rows, d, fc, tokc], lambda rows, fc: AR[rows, d, fc, bi, 0:128], ["bt", "AR"])
                        amat(2, lambda rows, fc: kt[rows, d, fc, tokc], lambda rows, fc: AR[rows, d, fc, bi, 0:128], ["kt", "AR"])
                        amat(4, lambda rows, fc: AR[rows, d, fc, bi, 0:128], lambda rows, fc: bt[rows, d, fc, tokc], ["bt", "AR"])
                        aevac(0, Aab, "Aab", ms)
                        aevac(2, Aak, "Aak", ms)
                        aevac(4, AabT, "AabT", (6 if d == 0 else 4))
                        if own:
                            amat(0, lambda rows, fc: bt[rows, d, fc, tokc], lambda rows, fc: AR[rows, d, fc, bi, 128:256], ["bt", "AR"])
                            amat(2, lambda rows, fc: kt[rows, d, fc, tokc], lambda rows, fc: AR[rows, d, fc, bi, 128:256], ["kt", "AR"])
                            aevac(0, Arb, "Arb", mi)
                            aevac(2, Ark, "Ark", mi)
                        ck("a")
                        Z = Zp[0]
                        CP("pool", Z[:, :, 0, :, :], atT[:, d, bi, :].rearrange("p (f h j) -> p f h j", f=4, h=2), r=["atT"], w=[("Z", 0)])
                        for h in range(8):
                            MM(psA[3][:, h * 64:(h + 1) * 64], Aak[:, h, :], vT[:, bi, h * 64:(h + 1) * 64], r=["Aak", "vT"], w=[pk(3)])
                        CP("dve", Z[:, :, 1, :, :], psA[3][:, 0:512].rearrange("p (f h j) -> p f h j", f=4, h=2), r=[pk(3)], w=[("Z", 0)])
                        curA, curAT, ak, atk = Aab, AabT, "Aab", "AabT"
                        zi = 0
                        for lev in range(6):
                            Zc, Zn = Zp[zi], Zp[1 - zi]
                            for hh in range(2):
                                bank = 4 + hh
                                for q4 in range(4):
                                    h = 4 * hh + q4; fc = h // 2; h2 = h % 2
                                    o = psA[bank][:, q4 * 128:(q4 + 1) * 128].rearrange("p (a j) -> p a j", a=2)
                                    MM(o, curA[:, h, :], Zc[:, fc, :, h2, :], start=True, stop=False, r=[ak, ("Z", zi)], w=[pk(bank)])
                                    MM(o, ident, Zc[:, fc, :, h2, :], start=False, stop=True, r=["cstb", ("Z", zi)], w=[pk(bank)])
                                for f in range(2):
                                    src = psA[bank][:, f * 256:(f + 1) * 256].rearrange("p (h a j) -> p a h j", h=2, a=2)
                                    if f == 0:
                                        ACT(Zn[:, 2 * hh + f, :, :, :], src, AF.Copy, r=[pk(bank)], w=[("Z", 1 - zi)])
                                    else:
                                        CP("dve", Zn[:, 2 * hh + f, :, :, :], src, r=[pk(bank)], w=[("Z", 1 - zi)])
                            zi = 1 - zi
                            if lev < 5:
                                nA, nAT = Ap[lev % 2], ATp[lev % 2]
                                nak, natk = ("Ap", lev % 2), ("ATp", lev % 2)
                                for hh in range(2):
                                    for q4 in range(4):
                                        h = 4 * hh + q4
                                        MM(psA[0 + hh][:, q4 * 128:(q4 + 1) * 128], curAT[:, h, :], curA[:, h, :], r=[ak, atk], w=[pk(0 + hh)])
                                        MM(psA[2 + hh][:, q4 * 128:(q4 + 1) * 128], curA[:, h, :], curAT[:, h, :], r=[ak, atk], w=[pk(2 + hh)])
                                    ACT(nA[:, 4 * hh:4 * hh + 4, :], psA[0 + hh][:, :].rearrange("p (h t) -> p h t", h=4), AF.Copy, r=[pk(0 + hh)], w=[nak])
                                    CP("dve", nAT[:, 4 * hh:4 * hh + 4, :], psA[2 + hh][:, :].rearrange("p (h t) -> p h t", h=4), r=[pk(2 + hh)], w=[natk])
                                curA, curAT, ak, atk = nA, nAT, nak, natk
                        Zf = Zp[zi]; zk = ("Z", zi)
                        ck("d")
                        for c2 in range(2):
                            rs_ = slice(c2 * 64, c2 * 64 + 64)
                            for fc in range(4):
                                MM(psA[c2][:, fc * 128:(fc + 1) * 128], Zf[rs_, fc, 0, :, :], btT[rs_, d, bi, fc * 128:(fc + 1) * 128], r=[zk, "btT"], w=[pk(c2)])
                            for fc in range(4):
                                TTo("dve", M0st[:, c2, fc, :], psA[c2][:, fc * 128:(fc + 1) * 128], bdm, ALU.mult, r=[pk(c2), "cstb"], w=["M0st"])
                            for h in range(8):
                                fc = h // 2; h2 = h % 2
                                o = psA[2 + c2][h2 * 64:h2 * 64 + 64, fc * 64:fc * 64 + 64]
                                MM(o, btT[rs_, d, bi, h * 64:(h + 1) * 64], Zf[rs_, fc, 1, h2, :], start=True, stop=False, r=["btT", zk], w=[pk(2 + c2)])
                                MM(o, ktT[rs_, d, bi, h * 64:(h + 1) * 64], vT[rs_, bi, h * 64:(h + 1) * 64], start=False, stop=True, r=["ktT", "vT"], w=[pk(2 + c2)])
                        for c2 in range(2):
                            ACT(N0st[:, c2, :, :], psA[2 + c2][:, 0:256].rearrange("p (f i) -> p f i", f=4), AF.Copy, r=[pk(2 + c2)], w=["N0st"])
                        cidx = pcidx0 + 2 * bi
                        DMA("sp", m0_s[d, cidx:cidx + 2].rearrange("c p f j -> p c f j"), M0st[:], r=["M0st"], w=[("m0_s", d, cidx)])
                        DMA("sp", n0_s[d, cidx:cidx + 2].rearrange("c p f i -> p c f i"), N0st[:], r=["N0st"], w=[("n0_s", d, cidx)])
                        if own:
                            for fc in range(4):
                                o = psA[2][:, fc * 128:(fc + 1) * 128]
                                MM(o, ident, AR[:, d, fc, bi, 128:256], start=True, stop=False, r=["cstb", "AR"], w=[pk(2)])
                                for h2 in range(2):
                                    h = 2 * fc + h2
                                    MM(psA[2][h2 * 64:h2 * 64 + 64, fc * 128:(fc + 1) * 128], Zf[:, fc, 0, h2, :], Arb[:, h, :],
                                       start=False, stop=(h2 == 1), r=[zk, "Arb"], w=[pk(2)])
                                for h2 in range(2):
                                    h = 2 * fc + h2
                                    o2 = psA[3][h2 * 64:h2 * 64 + 64, fc * 128:(fc + 1) * 128]
                                    MM(o2, Zf[:, fc, 1, h2, :], Arb[:, h, :], start=True, stop=False, r=[zk, "Arb"], w=[pk(3)])
                                    MM(o2, vT[:, bi, h * 64:(h + 1) * 64], Ark[:, h, :], start=False, stop=True, r=["vT", "Ark"], w=[pk(3)])
                            ACT(Qst[:], psA[2][:, :].rearrange("p (f t) -> p f t", f=4), AF.Copy, r=[pk(2)], w=["Qst"])
                            DMA("sp", q_s[d, gblk], Qst[:], r=["Qst"], w=[("q_s", d, gblk)])
                            CP("dve", Y0st[:], psA[3][:, :].rearrange("p (f t) -> p f t", f=4), r=[pk(3)], w=["Y0st"])
                            DMA("sp", y0_s[d, :, :, t0 + bi * 128:t0 + bi * 128 + 128], Y0st[:], r=["Y0st"], w=[("y0_s", d, gblk)])

            def attention_tile(ti):
                qs = ti % 2
                for qb in range(NBT):
                    n = NBT * ti + qb
                    kbs = [kb for kb in (n - 1, n, n + 1) if kb >= 0]
                    for m in range(4):
                        for h2 in range(2):
                            h = 2 * m + h2
                            c = h % 4; g = h // 4
                            rows = slice(g * 64, g * 64 + 64)
                            pi = h2
                            for kb in kbs:
                                slot = kb - (n - 1)
                                MM(psA[pi][:, slot * 128:(slot + 1) * 128], kT[rows, kb * 128:(kb + 1) * 128], qT[rows, qs, c, qb * 128:(qb + 1) * 128],
                                   r=["kT", ("qT", qs)], w=[pk(pi)])
                            lo = (kbs[0] - (n - 1)) * 128
                            ACT(probs[:, pi, lo:384], psA[pi][:, lo:384], AF.Exp, scale=0.125, r=[pk(pi)], w=[("probs", pi)])
                            if n - 1 >= 0:
                                TTo("pool", probs[:, pi, 0:128], probs[:, pi, 0:128], cstb[:, 10, :], ALU.mult, r=[("probs", pi), "cstb"], w=[("probs", pi)])
                            TTo("pool", probs[:, pi, 256:384], probs[:, pi, 256:384], cstb[:, 11, :], ALU.mult, r=[("probs", pi), "cstb"], w=[("probs", pi)])
                            orow = slice(h2 * 64, h2 * 64 + 64)
                            for i, kb in enumerate(kbs):
                                slot = kb - (n - 1)
                                MM(psA[2][orow, 0:128], vaug[:, kb, g, 0, :], probs[:, pi, slot * 128:(slot + 1) * 128], start=(i == 0), stop=(i == len(kbs) - 1),
                                   r=["vaug", ("probs", pi)], w=[pk(2)])
                            for i, kb in enumerate(kbs):
                                slot = kb - (n - 1)
                                MM(psA[3][orow, 0:128], vaug[:, kb, g, 1, :], probs[:, pi, slot * 128:(slot + 1) * 128], start=(i == 0), stop=(i == len(kbs) - 1),
                                   r=["vaug1", ("probs", pi)], w=[pk(3)])
                        TS("dve", den[:], psA[3][:, 0:128], vc("esink", m), None, ALU.add, r=[pk(3), "vec2"], w=["den"])
                        RCP(den[:], den[:], r=["den"], w=["den"])
                        TTo("dve", attn_o[:, m, qb * 128:(qb + 1) * 128], psA[2][:, 0:128], den[:], ALU.mult, r=[pk(2), "den"], w=["attn_o"])

            def attn_and_store(ti):
                attention_tile(ti)
                DMA("sp", ao_s[:, :, ti * TT:(ti + 1) * TT], attn_o[:], r=["attn_o"], w=[("ao_s", ti)])

            oth = list(range(2 * NTH - 1, NTH - 1, -1))
            if ntiles_other is not None:
                oth = oth[len(oth) - ntiles_other:] if ntiles_other > 0 else []
            for ti in oth:
                t0 = ti * TT
                sweep_tile(t0, own=False, first_other=(ti == NTH), pcidx0=t0 // 64)
            nown = NTH if ntiles_own is None else ntiles_own
            for ti in range(nown):
                sweep_tile(ti * TT, own=True, first_other=False, pcidx0=(ti * TT) // 64)
                if ti >= 1:
                    attn_and_store(ti - 1)
            if nown == NTH:
                attn_and_store(NTH - 1)
            if dbg is not None and upto == 1:
                dbg(locals(), P, DMA, dbg_out)
            if upto == 1:
                P.disabled = True
        barrier()

        with ExitStack() as s23:
            Yb = sb("Yb", [128, 4, HALF], F32, s23)
            with ExitStack() as s2:
                S32 = sb("S32", [128, 2, 4, 64], F32, s2)
                Sb = sb("Sb", [128, 2, 4, 64], BF16, s2)
                St = sb("St", [128, 2, 4, 64], F32, s2)
                M0l = sb("M0l", [128, 4, 4, 128], BF16, s2)
                N0l = sb("N0l", [128, 4, 4, 64], BF16, s2)
                Ql = sb("Ql", [128, 4, 4, 128], BF16, s2)
                ytmp = sb("ytmp", [128, 2, 4, 512], F32, s2)
                for fc in range(4):
                    DMA("sp", Yb[:, fc, :], y0_s[0, :, fc, :], r=[("y0_s", 0, g) for g in range(NB_OWN)], w=[("Yb", fc)])
                for j in range(4):
                    s = j % 2
                    DMA("sp", ytmp[:, s, :, :], y0_s[1, :, :, j * 512:(j + 1) * 512], r=[("y0_s", 1, g) for g in range(NB_OWN)], w=[("ytmp", s)])
                    TTo("pool", Yb[:, :, j * 512:(j + 1) * 512], Yb[:, :, j * 512:(j + 1) * 512], ytmp[:, s, :, :], ALU.add,
                        r=[("ytmp", s)] + [("Yb", fc) for fc in range(4)], w=["Yb"])
                MEMSET("dve", S32[:], 0.0, w=[("S32", 0), ("S32", 1)])
                MEMSET("pool", Sb[:], 0.0, w=[("Sb", 0), ("Sb", 1)])
                stepn = [0]

                def chain_step(d, cidx, own):
                    sl = stepn[0] % 4
                    stepn[0] += 1
                    DMA("sp", M0l[:, sl, :, :], m0_s[d, cidx], r=[("m0_s", d, cidx - cidx % 2)], w=[("M0l", sl)])
                    DMA("sp", N0l[:, sl, :, :], n0_s[d, cidx], r=[("n0_s", d, cidx - cidx % 2)], w=[("N0l", sl)])
                    bank = 4 + d
                    if own:
                        blk = cidx // 2; c2 = cidx % 2
                        qsl = (blk % 2) * 2 + d
                        if (d == 0 and c2 == 0) or (d == 1 and c2 == 1):
                            DMA("sp", Ql[:, qsl, :, :], q_s[d, blk], r=[("q_s", d, blk)], w=[("Ql", qsl)])
                        yb = 0 + d
                        for h in range(8):
                            fc = h // 2; h2 = h % 2; rows = slice(h2 * 64, h2 * 64 + 64)
                            MM(psA[yb][rows, fc * 64:(fc + 1) * 64], Sb[rows, d, fc, :], Ql[rows, qsl, fc, c2 * 64:c2 * 64 + 64],
                               r=[("Sb", d), ("Ql", qsl)], w=[pk(yb)])
                        yv = Yb[:, :, cidx * 64:(cidx + 1) * 64]
                        TTo("dve", yv, yv, psA[yb][:, 0:256].rearrange("p (f t) -> p f t", f=4), ALU.add, r=[pk(yb), "Yb"], w=["Yb"])
                    for fc in range(4):
                        o = psA[bank][:, fc * 64:(fc + 1) * 64]
                        MM(o, ident, N0l[:, sl, fc, :], start=True, stop=False, r=["cstb", ("N0l", sl)], w=[pk(bank)])
                        MM(o, M0l[:, sl, fc, :], Sb[:, d, fc, :], start=False, stop=True, r=[("M0l", sl), ("Sb", d)], w=[pk(bank)])
                    TTo("dve", St[:, d, :, :], psA[bank][:, 0:256].rearrange("p (f i) -> p f i", f=4), S32[:, d, :, :], ALU.add,
                        r=[pk(bank), ("S32", d)], w=[("St", d)])
                    for fc in range(4):
                        TS("dve", S32[:, d, fc, :], St[:, d, fc, :], PCs[:, d, cidx, fc:fc + 1], None, ALU.mult, r=[("St", d), "PCs"], w=[("S32", d)])
                    ACT(Sb[:, d, :, :], S32[:, d, :, :], AF.Copy, r=[("S32", d)], w=[("Sb", d)])

                for cidx in range(63, 31, -1):
                    chain_step(1, cidx, own=False)
                for i in range(32):
                    chain_step(0, i, own=True)
                    chain_step(1, 31 - i, own=True)
            if upto == 2:
                P.disabled = True
            barrier()

            with ExitStack() as s3:
                wg = sb("wg", [128, 8, 2048], BF16, s3)
                wua = sb("wua", [128, 4, D], BF16, s3); wur = sb("wur", [128, 4, D], BF16, s3); wout = sb("wout", [128, 8, D], BF16, s3)
                for kc in range(8):
                    DMA("pool", wg[:, kc, :], wgt[kc * 128:(kc + 1) * 128, :], w=["wg"])
                    DMA("pool", wout[:, kc, :], wout_d[kc * 128:(kc + 1) * 128, :], w=["wout"])
                for kc in range(4):
                    DMA("pool", wua[:, kc, :], wua_d[kc * 128:(kc + 1) * 128, :], w=["wua"])
                    DMA("pool", wur[:, kc, :], wur_d[kc * 128:(kc + 1) * 128, :], w=["wur"])
                xt = sb("xt3", [128, 2, D], F32, s3); hb = sb("hb3", [128, 2, D], BF16, s3)
                ssq = sb("ssq3", [128, 2], F32, s3); rstd = sb("rstd3", [128, 2], F32, s3)
                hT = sb("hT3", [128, 8, TT], BF16, s3)
                yc = sb("yc", [128, TT], F32, s3); ysq = sb("ysq", [128, TT], F32, s3); yr = sb("yr", [128, TT], F32, s3)
                rwT = sb("rwT", [128, 4, TT], BF16, s3)
                gates = sb("gates", [128, 16, TT], BF16, s3)
                mg = sb("mg", [128, 8, TT], BF16, s3); m1 = sb("m1", [128, TT], F32, s3); m2 = sb("m2", [128, TT], F32, s3)
                x1b = sb("x1b", [128, 2, D], F32, s3)
                xres = sb("xres", [128, 2, D], F32, s3)
                ao_l = sb("ao_l", [128, 4, TT], BF16, s3); bv_l = sb("bv_l", [128, 4, TT], BF16, s3); sgl_l = sb("sgl_l", [128, TT], BF16, s3)
                onesf = cstf[:, 1, :]
                for ti in range(NTH):
                    t0 = ti * TT
                    DMA("sp", ao_l[:], ao_s[:, :, t0:t0 + TT], r=[("ao_s", ti)], w=["ao_l"])
                    DMA("sp", bv_l[:], bv_s[:, :, t0:t0 + TT], r=[("bv_s", ti, fc) for fc in range(4)], w=["bv_l"])
                    DMA("sp", sgl_l[:], sgl_s[:, t0:t0 + TT], r=[("sgl_s", ti)], w=["sgl_l"])
                    for fc in range(4):
                        Yv = Yb[:, fc, t0:t0 + TT]
                        MM(psA[0][:, 0:TT], onesf, Yv, r=["cstf", "Yb"], w=[pk(0)])
                        TTo("dve", yc[:], Yv, psA[0][:, 0:TT], ALU.subtract, r=["Yb", pk(0)], w=["yc"])
                        ACT(ysq[:], yc[:], AF.Square, r=["yc"], w=["ysq"])
                        MM(psA[1][:, 0:TT], onesf, ysq[:], r=["cstf", "ysq"], w=[pk(1)])
                        RSQ(yr[:], psA[1][:, 0:TT], 1.0, eps_c[:, 2:3], TT, r=[pk(1)], w=["yr"])
                        TTo("pool", yc[:], yc[:], yr[:], ALU.mult, r=["yc", "yr"], w=["yc"])
                        TS("dve", yc[:], yc[:], vc("lnw", fc), vc("lnb", fc), ALU.mult, ALU.add, r=["yc", "vec"], w=["yc"])
                        TTo("pool", yc[:], yc[:], bv_l[:, fc, :], ALU.add, r=["yc", "bv_l"], w=["yc"])
                        MM(psA[2][:, 0:TT], g2t[:, fc * 128:(fc + 1) * 128], sgl_l[:, :], r=["g2t", "sgl_l"], w=[pk(2)])
                        TTo("dve", rwT[:, fc, :], yc[:], psA[2][:, 0:TT], ALU.mult, r=["yc", pk(2)], w=["rwT"])
                    xaps = [(xs[1 + t0 + 128 * i: 1 + t0 + 128 * i + 128, :], 128, 128 * i) for i in range(NBT)]
                    make_hT("hT3", xaps, None, hT, "nmix", "hT3", xt, hb, ssq, rstd)
                    for gc in range(16):
                        bank = gc % 2
                        for kc in range(8):
                            MM(psA[bank][:, 0:TT], wg[:, kc, gc * 128:(gc + 1) * 128], hT[:, kc, :], start=(kc == 0), stop=(kc == 7), r=["wg", "hT3"], w=[pk(bank)])
                        ACT(gates[:, gc, :], psA[bank][:, 0:TT], AF.Tanh, scale=0.5, r=[pk(bank)], w=["gates"])
                    for oc in range(8):
                        for kc in range(4):
                            MM(psA[2][:, 0:TT], wua[:, kc, oc * 128:(oc + 1) * 128], ao_l[:, kc, :], start=(kc == 0), stop=(kc == 3), r=["wua", "ao_l"], w=[pk(2)])
                        for kc in range(4):
                            MM(psA[3][:, 0:TT], wur[:, kc, oc * 128:(oc + 1) * 128], rwT[:, kc, :], start=(kc == 0), stop=(kc == 3), r=["wur", "rwT"], w=[pk(3)])
                        STT("dve", m1[:], gates[:, oc, :], 1.0, psA[2][:, 0:TT], ALU.add, ALU.mult, r=[pk(2), "gates"], w=["m1"])
                        STT("dve", m2[:], gates[:, 8 + oc, :], 1.0, psA[3][:, 0:TT], ALU.add, ALU.mult, r=[pk(3), "gates"], w=["m2"])
                        TTo("pool", mg[:, oc, :], m1[:], m2[:], ALU.add, r=["m1", "m2"], w=["mg"])
                    for bi in range(NBT):
                        s = bi % 2
                        tok = t0 + bi * 128
                        DMA("sp", xres[:, s, :], xs[1 + tok:1 + tok + 128, :], w=[("xres", s)])
                        for hf in range(2):
                            bank = 4 + hf
                            for kc in range(8):
                                MM(psA[bank][:, :], mg[:, kc, bi * 128:(bi + 1) * 128], wout[:, kc, hf * 512:(hf + 1) * 512], start=(kc == 0), stop=(kc == 7), r=["mg", "wout"], w=[pk(bank)])
                            STT("dve", x1b[:, s, hf * 512:(hf + 1) * 512], psA[bank][:, :], 0.5, xres[:, s, hf * 512:(hf + 1) * 512], ALU.mult, ALU.add, r=[pk(bank), ("xres", s)], w=[("x1b", s)])
                        DMA("sp", x1_s[tok:tok + 128, :], x1b[:, s, :], r=[("x1b", s)], w=[("x1_s", tok // 128)])
        if upto == 3:
            P.disabled = True
        barrier()

        with ExitStack() as s4:
            wf1 = sb("wf1", [128, 8, 4096], BF16, s4); wf2 = sb("wf2", [128, 32, D], BF16, s4)
            wpg = sb("wpg", [128, 8, D], BF16, s4); wple = sb("wple", [128, 2, D], BF16, s4)
            for kc in range(8):
                DMA("pool", wf1[:, kc, :], wff1_d[kc * 128:(kc + 1) * 128, :], w=["wf1"])
                DMA("pool", wpg[:, kc, :], wpg_d[kc * 128:(kc + 1) * 128, :], w=["wpg"])
            for kc in range(32):
                DMA("pool", wf2[:, kc, :], wff2_d[kc * 128:(kc + 1) * 128, :], w=["wf2"])
            for kc in range(2):
                DMA("pool", wple[:, kc, :], wple_d[kc * 128:(kc + 1) * 128, :], w=["wple"])
            xt = sb("xt4", [128, 2, D], F32, s4); hb = sb("hb4", [128, D], BF16, s4)
            ssq = sb("ssq4", [128, 2], F32, s4); rstd = sb("rstd4", [128, 2], F32, s4)
            hT = sb("hT4", [128, 8, 128], BF16, s4)
            act = sb("act", [128, 32, 128], BF16, s4)
            rl = sb("rl", [128, 2, 128], F32, s4)
            x2 = sb("x2", [128, D], F32, s4)
            pt = sb("pt", [128, 256], F32, s4); pb16 = sb("pb16", [128, 256], BF16, s4); pT = sb("pT", [128, 2, 128], BF16, s4)
            sgp = sb("sgp", [128, 512], F32, s4)

            def norm_T(src, gain, skey, okey, idx):
                ACT(hb[:], src, AF.Square, accum=ssq[:, idx:idx + 1], r=[skey], w=["hb4", ("ssq4", idx)])
                RSQ(rstd[:, idx:idx + 1], ssq[:, idx:idx + 1], 1.0 / D, eps_c[:, 0:1], 1, r=[("ssq4", idx)], w=[("rstd4", idx)])
                TS("dve", hb[:], src, rstd[:, idx:idx + 1], None, ALU.mult, r=[skey, ("rstd4", idx), "hb4"], w=["hb4o"])
                for kc in range(8):
                    TR(psT[kc % 2][:, (kc // 2) * 128:(kc // 2) * 128 + 128], hb[:, kc * 128:(kc + 1) * 128], ident, r=["hb4o", "cstb"], w=[pkT(kc % 2)])
                for kc in range(8):
                    srcp = psT[kc % 2][:, (kc // 2) * 128:(kc // 2) * 128 + 128]
                    if kc % 2 == 0:
                        ACT(hT[:, kc, :], srcp, AF.Copy, scale=vc(gain, kc), r=[pkT(kc % 2), "vec"], w=[okey])
                    else:
                        TS("dve", hT[:, kc, :], srcp, vc(gain, kc), None, ALU.mult, r=[pkT(kc % 2), "vec"], w=[okey])

            for blk in range(NB_OWN):
                tok = blk * 128
                s = blk % 2
                DMA("sp", xt[:, s, :], x1_s[tok:tok + 128, :], r=[("x1_s", blk)], w=[("xt4", s)])
                norm_T(xt[:, s, :], "nffn", ("xt4", s), "hT4", 0)
                for hc in range(32):
                    bank = hc % 2
                    for kc in range(8):
                        MM(psA[bank][:, 0:128], wf1[:, kc, hc * 128:(hc + 1) * 128], hT[:, kc, :], start=(kc == 0), stop=(kc == 7), r=["wf1", "hT4"], w=[pk(bank)])
                    ACT(rl[:, bank, :], psA[bank][:, 0:128], AF.Relu, r=[pk(bank)], w=[("rl", bank)])
                    STT("dve", act[:, hc, :], psA[bank][:, 0:128], 0.0, rl[:, bank, :], ALU.max, ALU.mult, r=[pk(bank), ("rl", bank)], w=["act"])
                for hf in range(2):
                    bank = 2 + hf
                    for hc in range(32):
                        MM(psA[bank][:, :], act[:, hc, :], wf2[:, hc, hf * 512:(hf + 1) * 512], start=(hc == 0), stop=(hc == 31), r=["act", "wf2"], w=[pk(bank)])
                    TTo("dve", x2[:, hf * 512:(hf + 1) * 512], psA[bank][:, :], xt[:, s, hf * 512:(hf + 1) * 512], ALU.add, r=[pk(bank), ("xt4", s)], w=["x2"])
                norm_T(x2[:], "nple", "x2", "hT4", 1)
                DMA("sp", pt[:], pp[tok:tok + 128, :], w=["pt"])
                CP("pool", pb16[:], pt[:], r=["pt"], w=["pb16"])
                for kc in range(2):
                    TR(psT[0][:, kc * 128:(kc + 1) * 128], pb16[:, kc * 128:(kc + 1) * 128], ident, r=["pb16", "cstb"], w=[pkT(0)])
                CP("dve", pT[:, :, :], psT[0][:, 0:256].rearrange("p (k t) -> p k t", k=2), r=[pkT(0)], w=["pT"])
                for hf in range(2):
                    for kc in range(8):
                        MM(psA[4][:, :], hT[:, kc, :], wpg[:, kc, hf * 512:(hf + 1) * 512], start=(kc == 0), stop=(kc == 7), r=["hT4", "wpg"], w=[pk(4)])
                    for kc in range(2):
                        MM(psA[5][:, :], pT[:, kc, :], wple[:, kc, hf * 512:(hf + 1) * 512], start=(kc == 0), stop=(kc == 1), r=["pT", "wple"], w=[pk(5)])
                    ACT(sgp[:], psA[4][:, :], AF.Tanh, scale=0.5, r=[pk(4)], w=["sgp"])
                    STT("dve", sgp[:], sgp[:], 1.0, psA[5][:, :], ALU.add, ALU.mult, r=[pk(5), "sgp"], w=["sgp"])
                    STT("dve", xt[:, s, hf * 512:(hf + 1) * 512], sgp[:], 0.5, x2[:, hf * 512:(hf + 1) * 512], ALU.mult, ALU.add, r=["sgp", "x2", ("xt4", s)], w=[("ob", s)])
                DMA("sp", out_d[tok:tok + 128, :], xt[:, s, :], r=[("ob", s), ("xt4", s)], w=[("out", blk), ("xt4", s)])
            P.add("sp", None, [("out", i) for i in range(NB_OWN)], ())
        P.emit(nc, gs)
    return nc


def _consts():
    cst = np.zeros((128, 12, 128), np.float32)
    i = np.arange(128)
    cst[:, 0, :] = np.eye(128)
    cst[:, 1, :] = (i[:, None] // 64 == i[None, :] // 64)
    perm = np.zeros((128, 128), np.float32)
    for m in range(128):
        j = m % 64
        if j < 8:
            perm[m + 8, m] = 1.0
        elif j < 16:
            perm[m - 8, m] = 1.0
    cst[:, 2, :] = perm
    cst[:, 3, :] = cst[:, 1, :]
    same = (i[:, None] // 64 == i[None, :] // 64)
    cst[:, 4, :] = same & (i[:, None] < i[None, :])
    cst[:, 5, :] = same & (i[:, None] <= i[None, :])
    cst[:, 6, :] = same & (i[:, None] > i[None, :])
    cst[:, 7, :] = same & (i[:, None] >= i[None, :])
    cst[:, 8, :] = 1.0
    cst[:, 9, :] = cst[:, 1, :] / 64.0
    cst[:, 10, :] = (i[:, None] >= i[None, :])
    cst[:, 11, :] = (i[:, None] <= i[None, :])
    tri = np.zeros((128, 2, 128), np.float32)
    tri[:, 0, :] = 0.5 * LWS * cst[:, 5, :]
    tri[:, 1, :] = 0.5 * LWS * cst[:, 7, :]
    return cst, tri


def _rope_tables(pos):
    half = 8
    inv = np.power(np.float32(500000.0), -np.arange(half, dtype=np.float32) * 2.0 / 16.0).astype(np.float32)
    ang = pos.astype(np.float32)[None, :] * inv[:, None]
    cos = np.cos(ang).astype(np.float32); sin = np.sin(ang).astype(np.float32)
    n = pos.shape[0]
    tab = np.zeros((128, 2, n), np.float32)
    for hb in range(2):
        b = hb * 64
        tab[b:b + 64, 0, :] = 1.0
        tab[b:b + 8, 0, :] = cos; tab[b + 8:b + 16, 0, :] = cos
        tab[b:b + 8, 1, :] = -sin; tab[b + 8:b + 16, 1, :] = sin
    return tab


def _col128(v):
    return np.ascontiguousarray(np.asarray(v, np.float32).reshape(-1, 128).T)


_NC_CACHE = {}


def kernel(x, p, norm_mix, w_in, shift_mu, q_norm, k_norm, sink, w0, w2, a0, a2, g2, k_k, k_a, r_k,
           lnx_w, lnx_b, w_up_attn, w_up_rwkv, w_out, norm_ffn, w_ff1, w_ff2, norm_ple, w_ple_gate, w_ple):
    f = lambda a: np.asarray(a, np.float32)
    x = f(x); p = f(p)[0]; w_in = f(w_in)[0]
    qcols = []
    for c in range(4):
        qcols += list(range(c * 64, c * 64 + 64)) + list(range((4 + c) * 64, (4 + c) * 64 + 64))
    O_K, O_V, O_R = 512, 640, 768
    cols = qcols + list(range(O_K, O_K + 128)) + list(range(O_V, O_V + 128))
    rw = O_R
    wa = list(range(rw + 1536, rw + 1536 + 128)); gl = list(range(rw + 1664, rw + 1792))
    cols += wa + gl
    mu_idx = [12, 13]
    for fc in range(4):
        for part in range(3):
            cols += list(range(rw + part * 512 + fc * 128, rw + part * 512 + fc * 128 + 128))
            mu_idx.append(part * 4 + fc)
    wsw = np.ascontiguousarray(w_in[:, cols])
    wgt = np.ascontiguousarray(w_in[:, O_R + 1792:])
    mu_c = _col128(f(shift_mu)[0])[:, mu_idx]
    cst, tri = _consts()
    cst_in = cst.copy()
    cst_in_f = cst.copy()
    cst_in[:, 1, :] = cst[:, 1, :]
    sink_ = f(sink)[0]
    sinkcols = np.zeros((128, 4), np.float32)
    for m in range(4):
        sinkcols[0:64, m] = sink_[2 * m]; sinkcols[64:128, m] = sink_[2 * m + 1]
    in_maps = []
    for c in range(8):
        b, hf = c // 2, c % 2
        xb = x[b]; pb_ = p[b]
        if hf == 0:
            xl = xb; pl = pb_[:HALF]; pos = np.arange(0, HALF + 512); dA, dB = 0, 1
        else:
            xl = xb[::-1]; pl = pb_[HALF:][::-1]; pos = SEQ - 1 - np.arange(0, HALF + 512); dA, dB = 1, 0
        xs = np.zeros((SEQ + 2, D), np.float32); xs[1:SEQ + 1] = xl
        dd = [dA, dB]
        vec = np.zeros((128, NVEC_IN), np.float32)
        def put(n, arr):
            vec[:, VC[n]:VC[n] + arr.shape[1]] = arr
        put("mu", mu_c); put("k_k", _col128(f(k_k)[0])); put("k_a", _col128(f(k_a)[0]))
        put("rkA", _col128(f(r_k)[0, dA].reshape(-1))); put("rkB", _col128(f(r_k)[0, dB].reshape(-1)))
        put("lnw", _col128(f(lnx_w)[0])); put("lnb", _col128(f(lnx_b)[0]))
        put("a0A", _col128(f(a0)[0, dA])); put("a0B", _col128(f(a0)[0, dB]))
        put("qg", np.tile(f(q_norm)[0], 2)[:, None]); put("kg", np.tile(f(k_norm)[0], 2)[:, None])
        put("nmix", _col128(f(norm_mix)[0])); put("nffn", _col128(f(norm_ffn)[0])); put("nple", _col128(f(norm_ple)[0]))
        put("sink", sinkcols)
        w2aug = np.stack([np.concatenate([f(w2)[0, d_], f(w0)[0, d_][None, :]], 0) for d_ in dd])
        a2s = np.stack([f(a2)[0, d_] for d_ in dd])
        cst_c = cst.copy()
        m = {"xs": xs, "pp": np.ascontiguousarray(pl), "wsw": wsw, "wgt": wgt, "vec": vec, "w2aug": np.ascontiguousarray(w2aug),
             "a2": np.ascontiguousarray(a2s), "g2": f(g2)[0], "wua": f(w_up_attn)[0], "wur": f(w_up_rwkv)[0], "wout": f(w_out)[0],
             "wff1": f(w_ff1)[0], "wff2": f(w_ff2)[0], "wpg": f(w_ple_gate)[0], "wple": f(w_ple)[0],
             "rope": _rope_tables(pos), "cst": cst_c, "tri": tri}
        in_maps.append(m)
    if _NC_CACHE.get("prep_only"):
        return in_maps
    if "nc" not in _NC_CACHE:
        _NC_CACHE["nc"] = build_program()
    res = run_bass_kernel_spmd(_NC_CACHE["nc"], in_maps, core_ids=list(range(8)))
    out = np.zeros((4, SEQ, D), np.float32)
    for c in range(8):
        b, hf = c // 2, c % 2
        o = res.results[c]["out"]
        if hf == 0:
            out[b, :HALF] = o
        else:
            out[b, HALF:] = o[::-1]
    return out
```

```python
import os
import numpy as np
from contextlib import ExitStack
import concourse.bass as bass
import concourse.mybir as mybir
from concourse.bass_utils import run_bass_kernel_spmd

F32 = mybir.dt.float32
BF16 = mybir.dt.bfloat16
ALU = mybir.AluOpType
AF = mybir.ActivationFunctionType

SEM_EPOCH = 3000
DMA_SLOTS = 8

D = 1024; SEQ = 4096; HALF = 2048; NB_OWN = 16
TT = 256; NBT = 2; NTH = 8
LWS = -0.6065306597126334


class Op:
    __slots__ = ("eng", "fn", "reads", "writes", "dma", "deps", "flag", "cnt", "slot", "slotval", "idx", "xdeps")

    def __init__(self, eng, fn, reads, writes, dma):
        self.eng = eng; self.fn = fn; self.reads = reads; self.writes = writes; self.dma = dma
        self.deps = set(); self.flag = False; self.cnt = None; self.slot = None; self.slotval = None
        self.xdeps = ()


class Prog:
    def __init__(self):
        self.ops = []
        self.last_barrier = 0

    def add(self, eng, fn, r=(), w=(), dma=False):
        op = Op(eng, fn, tuple(r), tuple(w), dma)
        if getattr(self, "disabled", False):
            op.idx = -1
            return op
        op.idx = len(self.ops)
        self.ops.append(op)
        return op

    def analyse(self):
        last_w = {}
        readers = {}
        for op in self.ops:
            for j in op.xdeps:
                op.deps.add((j, "raw"))
            for k in op.reads:
                j = last_w.get(k)
                if j is not None:
                    op.deps.add((j, "raw"))
            for k in op.writes:
                j = last_w.get(k)
                if j is not None:
                    op.deps.add((j, "waw"))
                for j in readers.get(k, ()):
                    if j != op.idx:
                        op.deps.add((j, "war"))
            for k in op.reads:
                lst = readers.setdefault(k, [])
                if not op.dma:
                    lst[:] = [j for j in lst if self.ops[j].dma or self.ops[j].eng != op.eng]
                lst.append(op.idx)
            for k in op.writes:
                last_w[k] = op.idx
                readers[k] = []
        for op in self.ops:
            need = set()
            for (j, kind) in op.deps:
                p = self.ops[j]
                if p.dma:
                    need.add(j)
                elif p.eng == op.eng and not op.dma:
                    if kind == "raw" and op.eng != "pe":
                        need.add(j); p.flag = True
                else:
                    need.add(j); p.flag = True
            op.deps = need
        cnt = {}
        for op in self.ops:
            if op.dma or op.fn is None:
                continue
            if op.flag:
                cnt[op.eng] = cnt.get(op.eng, 0) + 1
                op.cnt = cnt[op.eng]
        nd = {}
        for op in self.ops:
            if op.dma:
                n = nd.get(op.eng, 0)
                nd[op.eng] = n + 1
                op.slot = (op.eng, n % DMA_SLOTS)
                op.slotval = 16 * (n // DMA_SLOTS + 1)
        self.nflag = cnt
        self.ndma = nd

    def emit(self, nc, stack):
        self.analyse()
        engs = ["pe", "act", "dve", "pool", "sp"]
        sems = {}
        for e in engs:
            n = self.nflag.get(e, 0)
            for ep in range(n // SEM_EPOCH + 1):
                sems[(e, ep)] = stack.enter_context(nc.semaphore(f"s_{e}_{ep}"))
        dsem = {}
        for e, n in self.ndma.items():
            for s in range(min(n, DMA_SLOTS)):
                dsem[(e, s)] = stack.enter_context(nc.semaphore(f"d_{e}_{s}"))
        block = stack.enter_context(nc.Block())
        per_eng = {e: [op for op in self.ops if op.eng == e] for e in engs}
        ops = self.ops

        def sem_of(p):
            if p.dma:
                return dsem[p.slot], p.slotval
            c = p.cnt - 1
            return sems[(p.eng, c // SEM_EPOCH)], (c % SEM_EPOCH) + 1

        def run(engobj, lst):
            seen = {}
            dma_hist = {}
            for op in lst:
                waits = {}
                for j in op.deps:
                    s, v = sem_of(ops[j])
                    key = id(s)
                    if seen.get(key, 0) >= v:
                        continue
                    if key not in waits or waits[key][1] < v:
                        waits[key] = (s, v)
                if op.dma:
                    prev = dma_hist.get(op.slot)
                    if prev is not None:
                        s, v = dsem[op.slot], prev
                        key = id(s)
                        if seen.get(key, 0) < v and (key not in waits or waits[key][1] < v):
                            waits[key] = (s, v)
                    dma_hist[op.slot] = op.slotval
                for key, (s, v) in waits.items():
                    engobj.wait_ge(s, v)
                    seen[key] = v
                if op.fn is None:
                    continue
                ins = op.fn(engobj)
                if op.dma:
                    ins.then_inc(dsem[op.slot], 16)
                elif op.flag:
                    s, v = sem_of(op)
                    ins.then_inc(s, 1)

        if per_eng["pe"]:
            @block.tensor
            def _(e):
                run(e, per_eng["pe"])
        if per_eng["act"]:
            @block.scalar
            def _(e):
                run(e, per_eng["act"])
        if per_eng["dve"]:
            @block.vector
            def _(e):
                run(e, per_eng["dve"])
        if per_eng["pool"]:
            @block.gpsimd
            def _(e):
                run(e, per_eng["pool"])
        if per_eng["sp"]:
            @block.sync
            def _(e):
                run(e, per_eng["sp"])


VC = {}
_o = 0
for _n, _c in [("mu", 14), ("k_k", 4), ("k_a", 4), ("rkA", 4), ("rkB", 4), ("lnw", 4), ("lnb", 4),
               ("a0A", 4), ("a0B", 4), ("qg", 1), ("kg", 1), ("nmix", 8), ("nffn", 8), ("nple", 8),
               ("sink", 4), ("omm", 14), ("hmu", 14), ("nkk", 4), ("omka", 4), ("esink", 4), ("ha0A", 4), ("ha0B", 4), ("hka", 4), ("c2ka", 4)]:
    VC[_n] = _o
    _o += _c
NVEC_IN = VC["omm"]
NVEC = _o

N_SW = 20


def build_program(dbg=None, upto=99, ntiles_other=None, ntiles_own=None):
    nc = bass.Bass("TRN2", target_bir_lowering=False)
    P = Prog()

    def din(name, shape):
        return nc.dram_tensor(name, list(shape), F32, kind="ExternalInput").ap()

    xs = din("xs", [SEQ + 2, D])
    pp = din("pp", [HALF, 256])
    wsw = din("wsw", [D, N_SW * 128])
    wgt = din("wgt", [D, 2048])
    vec_d = din("vec", [128, NVEC_IN])
    w2aug_d = din("w2aug", [2, 65, 512])
    a2_d = din("a2", [2, 64, 512])
    g2_d = din("g2", [128, 512])
    wua_d = din("wua", [512, D]); wur_d = din("wur", [512, D]); wout_d = din("wout", [D, D])
    wff1_d = din("wff1", [D, 4096]); wff2_d = din("wff2", [4096, D])
    wpg_d = din("wpg", [D, D]); wple_d = din("wple", [256, D])
    rope_d = din("rope", [128, 2, HALF + 512])
    cst_d = din("cst", [128, 12, 128])
    tri_d = din("tri", [128, 2, 128])
    crow_d = din("crow", [1, 2, 128])
    out_d = nc.dram_tensor("out", [HALF, D], F32, kind="ExternalOutput").ap()
    m0_s = nc.dram_tensor("m0_s", [2, 64, 128, 4, 128], BF16, kind="Internal").ap()
    n0_s = nc.dram_tensor("n0_s", [2, 64, 128, 4, 64], BF16, kind="Internal").ap()
    q_s = nc.dram_tensor("q_s", [2, 16, 128, 4, 128], BF16, kind="Internal").ap()
    x1_s = nc.dram_tensor("x1_s", [HALF, D], F32, kind="Internal").ap()
    y0_s = nc.dram_tensor("y0_s", [2, 128, 4, HALF], F32, kind="Internal").ap()
    ao_s = nc.dram_tensor("ao_s", [128, 4, HALF], BF16, kind="Internal").ap()
    bv_s = nc.dram_tensor("bv_s", [128, 4, HALF], BF16, kind="Internal").ap()
    sgl_s = nc.dram_tensor("sgl_s", [128, HALF], BF16, kind="Internal").ap()
    dbg_out = None
    if dbg is not None:
        dbg_out = nc.dram_tensor("dbg", [128, 16384], F32, kind="ExternalOutput").ap()

    def MM(out, lhsT, rhs, start=True, stop=True, r=(), w=()):
        P.add("pe", lambda e: e.matmul(out, lhsT, rhs, start=start, stop=stop), r, w)

    def TR(out, in_, idn, r=(), w=()):
        P.add("pe", lambda e: e.transpose(out, in_, idn), r, w)

    def ACT(out, in_, func, bias=None, scale=None, accum=None, r=(), w=()):
        kw = {}
        if bias is not None:
            kw["bias"] = bias
        if scale is not None:
            kw["scale"] = scale
        if accum is not None:
            kw["accum_out"] = accum
        P.add("act", lambda e: e.activation(out=out, in_=in_, func=func, **kw), r, w)

    def TTo(eng, out, a, b, op, r=(), w=()):
        P.add(eng, lambda e: e.tensor_tensor(out, a, b, op), r, w)

    def TS(eng, out, a, s1, s2, op0, op1=None, r=(), w=()):
        if op1 is None:
            P.add(eng, lambda e: e.tensor_scalar(out, a, s1, None, op0), r, w)
        else:
            P.add(eng, lambda e: e.tensor_scalar(out, a, s1, s2, op0, op1), r, w)

    def STT(eng, out, in0, sc, in1, op0, op1, r=(), w=()):
        P.add(eng, lambda e: e.scalar_tensor_tensor(out, in0, sc, in1, op0, op1), r, w)

    def CP(eng, out, in_, r=(), w=()):
        P.add(eng, lambda e: e.tensor_copy(out, in_), r, w)

    def RCP(out, in_, r=(), w=()):
        P.add("dve", lambda e: e.reciprocal(out, in_), r, w)

    def MEMSET(eng, ap, val, w=()):
        P.add(eng, lambda e: e.memset(ap, val), (), w)

    def DMA(eng, out, in_, r=(), w=()):
        P.add(eng, lambda e: e.dma_start(out=out, in_=in_), r, w, dma=True)

    with ExitStack() as gs:
        def sb(name, shape, dt=F32, st=None):
            return (st or gs).enter_context(nc.sbuf_tensor("sb_" + name, list(shape), dt))

        psA = [gs.enter_context(nc.psum_tensor(f"psA{i}", [128, 512], F32)) for i in range(6)]
        psT = [gs.enter_context(nc.psum_tensor(f"psT{i}", [128, 1024], BF16)) for i in range(2)]
        pk = lambda i: ("ps", i)
        pkT = lambda i: ("psT", i)

        vec = sb("vec", [128, NVEC])
        cstb = sb("cstb", [128, 12, 128], BF16)
        cstf = sb("cstf", [128, 3, 128])
        crow = sb("crow", [1, 2, 128])
        tri = sb("tri", [128, 2, 128])
        eps_c = sb("eps_c", [128, 4])
        nhalf = sb("nhalf", [128, TT])
        cstm = sb("cstm", [128, 4, 4, 128], BF16)
        DMA("sp", vec[:, 0:NVEC_IN], vec_d, w=["vec"])
        DMA("pool", cstb[:], cst_d, w=["cstb"])
        DMA("sp", cstf[:, 0, :], cst_d[:, 0, :], w=["cstf"])
        DMA("sp", cstf[:, 1, :], cst_d[:, 9, :], w=["cstf"])
        DMA("sp", cstf[:, 2, :], cst_d[:, 8, :], w=["cstf"])
        DMA("sp", crow[:], crow_d, w=["crow"])
        DMA("sp", tri[:], tri_d, w=["tri"])
        MEMSET("dve", eps_c[:, 0:1], 1e-6, w=["eps"])
        MEMSET("pool", nhalf[:], -0.5, w=["nhalf"])
        for _m in range(4):
            for _r in range(4):
                DMA("pool", cstm[:, _m, _r, :], cst_d[:, 4 + _m, :], w=["cstm"])
        MEMSET("dve", eps_c[:, 1:2], 1e-24, w=["eps"])
        MEMSET("dve", eps_c[:, 2:3], 64e-5, w=["eps"])
        ident = cstb[:, 0, :]; onesblk = cstb[:, 1, :]; perm = cstb[:, 2, :]; bdm = cstb[:, 3, :]
        identf = cstf[:, 0, :]
        vc = lambda n, i=0: vec[:, VC[n] + i:VC[n] + i + 1]
        TS("dve", vec[:, VC["omm"]:VC["omm"] + 14], vec[:, VC["mu"]:VC["mu"] + 14], -1.0, 1.0, ALU.mult, ALU.add, r=["vec"], w=["vec2"])
        TS("dve", vec[:, VC["hmu"]:VC["hmu"] + 14], vec[:, VC["mu"]:VC["mu"] + 14], 0.5, None, ALU.mult, r=["vec"], w=["vec2"])
        TS("dve", vec[:, VC["nkk"]:VC["nkk"] + 4], vec[:, VC["k_k"]:VC["k_k"] + 4], -1.0, None, ALU.mult, r=["vec"], w=["vec2"])
        TS("dve", vec[:, VC["omka"]:VC["omka"] + 4], vec[:, VC["k_a"]:VC["k_a"] + 4], -1.0, 1.0, ALU.mult, ALU.add, r=["vec"], w=["vec2"])
        ACT(vec[:, VC["esink"]:VC["esink"] + 4], vec[:, VC["sink"]:VC["sink"] + 4], AF.Exp, r=["vec"], w=["vec2"])
        TS("dve", vec[:, VC["ha0A"]:VC["ha0A"] + 8], vec[:, VC["a0A"]:VC["a0A"] + 8], 0.5, None, ALU.mult, r=["vec"], w=["vec2"])
        TS("dve", vec[:, VC["hka"]:VC["hka"] + 4], vec[:, VC["k_a"]:VC["k_a"] + 4], 0.5, None, ALU.mult, r=["vec"], w=["vec2"])
        TS("dve", vec[:, VC["c2ka"]:VC["c2ka"] + 4], vec[:, VC["k_a"]:VC["k_a"] + 4], -0.5, 1.0, ALU.mult, ALU.add, r=["vec"], w=["vec2"])
        VK = ["vec", "vec2", "eps", "cstb", "cstf", "tri"]

        def ck(name):
            if os.environ.get("KSTOP") == name:
                P.disabled = True
        if os.environ.get("KSTOP") == "c":
            P.disabled = True

        bar_t = {e: sb(f"bar_{e}", [1, 8]) for e in ["act", "dve", "pool"]}

        def RSQ(dst, src, scale, bias_ap, n, r=(), w=()):
            ACT(dst, src, AF.Ln, bias=bias_ap, scale=scale, r=list(r) + ["eps"], w=list(w))
            ACT(dst, dst, AF.Exp, scale=-0.5, r=list(w), w=list(w))

        def barrier():
            if getattr(P, "disabled", False):
                return
            n = len(P.ops)
            pend = [op.idx for op in P.ops[P.last_barrier:n]]
            marks = []
            for e in ["act", "dve", "pool"]:
                t = bar_t[e]
                marks.append(P.add(e, (lambda tt, ee: (lambda en: en.memzero(tt[:]) if ee == 'act' else en.memset(tt[:], 0.0)))(t, e), (), [f"bar_{e}_{n}"]).idx)
            marks.append(P.add("pe", lambda e: e.matmul(psA[0][0:1, 0:1], cstb[0:1, 8, 0:1], cstb[0:1, 8, 0:1], start=True, stop=True),
                               ["cstb", pk(0)], [pk(0), f"bar_pe_{n}"]).idx)
            for e in ["pe", "act", "dve", "pool", "sp"]:
                f = P.add(e, None, (), ())
                f.xdeps = tuple(marks) + tuple(i for i in pend if P.ops[i].dma)
            P.last_barrier = len(P.ops)

        def make_hT(st_key, x_aps, nrows, hT, gain, wk, xt, hb, ssq, rstd):
            nblk = len(x_aps)
            for bi, (xap, rows, col0) in enumerate(x_aps):
                s = bi % 2
                DMA("sp", xt[0:rows, s, :], xap, w=[("xt", s)])
                ACT(hb[0:rows, s, :], xt[0:rows, s, :], AF.Square, accum=ssq[0:rows, s:s + 1], r=[("xt", s)], w=[("hb", s), ("ssq", s)])
                RSQ(rstd[0:rows, s:s + 1], ssq[0:rows, s:s + 1], 1.0 / D, eps_c[0:rows, 0:1], 1, r=[("ssq", s)], w=[("rstd", s)])
                TS("dve", hb[0:rows, s, :], xt[0:rows, s, :], rstd[0:rows, s:s + 1], None, ALU.mult, r=[("xt", s), ("rstd", s), ("hb", s)], w=[("hbo", s)])
                for kc in range(8):
                    TR(psT[kc % 2][:, (kc // 2) * 128:(kc // 2) * 128 + rows], hb[0:rows, s, kc * 128:(kc + 1) * 128], ident[0:rows, 0:rows],
                       r=[("hbo", s), "cstb"], w=[pkT(kc % 2)])
                for kc in range(8):
                    eng_act = (kc % 2 == 0)
                    src = psT[kc % 2][:, (kc // 2) * 128:(kc // 2) * 128 + rows]
                    dst = hT[:, kc, col0:col0 + rows]
                    if eng_act:
                        ACT(dst, src, AF.Copy, scale=vc(gain, kc), r=[pkT(kc % 2), "vec"], w=[wk])
                    else:
                        TS("dve", dst, src, vc(gain, kc), None, ALU.mult, r=[pkT(kc % 2), "vec"], w=[wk])

        g2t = sb("g2t", [128, 512], BF16)
        PCs = sb("PCs", [128, 2, 64, 4], F32)
        with ExitStack() as s1:
            win = sb("win", [128, 8, N_SW * 128], BF16, s1)
            for kc in range(8):
                DMA("pool", win[:, kc, :], wsw[kc * 128:(kc + 1) * 128, :], w=["win"])
            w2aug = sb("w2aug", [65, 2, 512], BF16, s1)
            a2t = sb("a2t", [128, 2, 512], BF16, s1)
            rope = sb("rope", [128, 2, HALF + 512], F32, s1)
            for d in range(2):
                DMA("pool", w2aug[:, d, :], w2aug_d[d], w=["w2aug"])
                DMA("pool", a2t[64:128, d, :], a2_d[d], w=["a2t"])
            DMA("pool", g2t[:], g2_d, w=["g2t"])
            DMA("sp", rope[:], rope_d, w=["rope"])
            if os.environ.get("KSTOP") == "w":
                P.disabled = True
            kT = sb("kT", [128, HALF + 128], BF16, s1)
            vaug = sb("vaug", [128, NB_OWN + 1, 2, 2, 64], BF16, s1)
            attn_o = sb("attn_o", [128, 4, TT], BF16, s1)
            sgl = sb("sgl", [128, TT], BF16, s1)
            bv = sb("bv", [128, 4, TT], BF16, s1)
            Y0st = sb("Y0st", [128, 4, 128], F32, s1)
            MEMSET("pool", vaug[:, :, :, 1, :], 1.0, w=["vaug1"])
            xt = sb("xt", [128, 2, D], F32, s1); hb = sb("hb", [128, 2, D], BF16, s1)
            ssq = sb("ssq", [128, 2], F32, s1); rstd = sb("rstd", [128, 2], F32, s1)
            hT = sb("hT", [128, 8, TT + 2], BF16, s1)
            zr = sb("zr", [128, 2, TT + 2], F32, s1)
            zs = sb("zs", [128, 2, TT], F32, s1)
            ztmp = sb("ztmp", [128, 2, TT], F32, s1)
            zq = sb("zq", [128, TT], F32, s1)
            qsq = sb("qsq", [128, TT], BF16, s1)
            qrs = sb("qrs", [128, TT], F32, s1)
            qn = sb("qn", [128, TT], BF16, s1)
            qt1 = sb("qt1", [128, TT], F32, s1); qt2 = sb("qt2", [128, TT], F32, s1)
            qT = sb("qT", [128, 2, 4, TT], BF16, s1)
            vb = sb("vb", [128, TT], BF16, s1)
            twl = sb("twl", [65, TT], BF16, s1); alx = sb("alx", [128, TT], BF16, s1)
            MEMSET("pool", twl[64:65, :], 1.0, w=["twl1"])
            if os.environ.get("KSTOP") == "m":
                P.disabled = True
            sgfc = sb("sgfc", [128, NBT, 128], F32, s1)
            r_f = sb("r_f", [128, TT], F32, s1); k_f = sb("k_f", [128, TT], F32, s1); v_f = sb("v_f", [128, TT], F32, s1)
            nkk = sb("nkk", [128, TT], F32, s1)
            a_f = sb("a_f", [128, TT], F32, s1); b_f = sb("b_f", [128, TT], F32, s1); kd_f = sb("kd_f", [128, TT], F32, s1)
            t1_f = sb("t1_f", [128, TT], F32, s1)
            pb = sb("pb", [128, TT], BF16, s1)
            E_f = sb("E_f", [128, TT], F32, s1); G_f = sb("G_f", [128, TT], F32, s1)
            AR = sb("AR", [128, 2, 4, NBT, 256], BF16, s1)
            bt = sb("bt", [128, 2, 4, TT], BF16, s1); kt = sb("kt", [128, 2, 4, TT], BF16, s1)
            btT = sb("btT", [128, 2, NBT, 512], BF16, s1); ktT = sb("ktT", [128, 2, NBT, 512], BF16, s1)
            atT = sb("atT", [128, 2, NBT, 512], BF16, s1)
            vT = sb("vT", [128, NBT, 512], BF16, s1)
            Aab = sb("Aab", [128, 8, 128], BF16, s1); AabT = sb("AabT", [128, 8, 128], BF16, s1)
            Aak = sb("Aak", [128, 8, 128], BF16, s1); Arb = sb("Arb", [128, 8, 128], BF16, s1); Ark = sb("Ark", [128, 8, 128], BF16, s1)
            Ap = [sb(f"Ap{i}", [128, 8, 128], BF16, s1) for i in range(2)]
            ATp = [sb(f"ATp{i}", [128, 8, 128], BF16, s1) for i in range(2)]
            Zp = [sb(f"Zp{i}", [128, 4, 2, 2, 64], BF16, s1) for i in range(2)]
            M0st = sb("M0st", [128, 2, 4, 128], BF16, s1); N0st = sb("N0st", [128, 2, 4, 64], BF16, s1)
            Qst = sb("Qst", [128, 4, 128], BF16, s1)
            probs = sb("probs", [128, 2, 384], BF16, s1)
            den = sb("den", [128, 128], F32, s1)

            def proj_chunk(ci, bank, halo_bank=None):
                if halo_bank is None:
                    for kc in range(8):
                        MM(psA[bank][:, 0:TT], win[:, kc, ci * 128:(ci + 1) * 128], hT[:, kc, 1:TT + 1], start=(kc == 0), stop=(kc == 7),
                           r=["win", "hT"], w=[pk(bank)])
                else:
                    for kc in range(8):
                        MM(psA[bank][:, 0:TT + 2], win[:, kc, ci * 128:(ci + 1) * 128], hT[:, kc, 0:TT + 2], start=(kc == 0), stop=(kc == 7),
                           r=["win", "hT"], w=[pk(bank)])

            zcount = [0]

            def rwkv_z(ci, mui, out_ap, wkey, post=None):
                s = zcount[0] % 2
                zcount[0] += 1
                bank = s
                proj_chunk(ci, bank, 2)
                ACT(zr[:, s, :], psA[bank][:, 0:TT + 2], AF.Copy, r=[pk(bank)], w=[("zr", s)])
                ACT(zs[:, s, :], psA[bank][:, 1:TT + 1], AF.Copy, scale=vc("omm", mui), r=[pk(bank), "vec2"], w=[("zs", s)])
                TTo("pool", ztmp[:, s, :], zr[:, s, 0:TT], zr[:, s, 2:TT + 2], ALU.add, r=[("zr", s)], w=[("ztmp", s)])
                STT("dve", out_ap, ztmp[:, s, :], vc("hmu", mui), zs[:, s, :], ALU.mult, ALU.add, r=[("ztmp", s), ("zs", s), "vec2"], w=[wkey])

            def sweep_tile(t0, own, first_other, pcidx0):
                dirs = [0, 1] if own else [1]
                blk0 = t0 // 128
                xaps = [(xs[t0 + 128 * i: t0 + 128 * i + 128, :], 128, 128 * i) for i in range(NBT)]
                xaps.append((xs[t0 + TT: t0 + TT + 2, :], 2, TT))
                make_hT("hT", xaps, None, hT, "nmix", "hT", xt, hb, ssq, rstd)
                ck("h")
                qs = (t0 // TT) % 2
                if own or first_other:
                    for ci in ([0, 1, 2, 3, 4] if own else [4]):
                        bank = ci % 2
                        proj_chunk(ci, bank)
                        gcol = "kg" if ci == 4 else "qg"
                        ACT(qsq[:], psA[bank][:, 0:TT], AF.Square, r=[pk(bank)], w=["qsq"])
                        MM(psA[3][:, 0:TT], onesblk, qsq[:], r=["cstb", "qsq"], w=[pk(3)])
                        RSQ(qrs[:], psA[3][:, 0:TT], 1.0 / 64, eps_c[:, 0:1], TT, r=[pk(3)], w=["qrs"])
                        STT("dve", qn[:], psA[bank][:, 0:TT], vc(gcol), qrs[:], ALU.mult, ALU.mult, r=[pk(bank), "qrs", "vec"], w=["qn"])
                        MM(psA[3][:, 0:TT], perm, qn[:], r=["cstb", "qn"], w=[pk(3)])
                        TTo("pool", qt1[:], qn[:], rope[:, 0, t0:t0 + TT], ALU.mult, r=["qn", "rope"], w=["qt1"])
                        TTo("dve", qt2[:], psA[3][:, 0:TT], rope[:, 1, t0:t0 + TT], ALU.mult, r=[pk(3), "rope"], w=["qt2"])
                        if ci < 4:
                            TTo("pool", qT[:, qs, ci, :], qt1[:], qt2[:], ALU.add, r=["qt1", "qt2"], w=[("qT", qs)])
                        else:
                            n = TT if own else 128
                            TTo("pool", kT[:, t0:t0 + n], qt1[:, 0:n], qt2[:, 0:n], ALU.add, r=["qt1", "qt2"], w=["kT"])
                    proj_chunk(5, 0)
                    ACT(vb[:], psA[0][:, 0:TT], AF.Copy, r=[pk(0)], w=["vb"])
                    nbv = NBT if own else 1
                    for bi in range(nbv):
                        TR(psT[0][:, bi * 128:(bi + 1) * 128], vb[:, bi * 128:(bi + 1) * 128], ident, r=["vb", "cstb"], w=[pkT(0)])
                    for bi in range(nbv):
                        CP("dve", vaug[:, blk0 + bi, :, 0, :], psT[0][:, bi * 128:(bi + 1) * 128].rearrange("p (g d) -> p g d", g=2),
                           r=[pkT(0)], w=["vaug"])
                ck("q")
                rwkv_z(6, 0, zq[:], "zq")
                ACT(twl[0:64, :], zq[0:64, :], AF.Tanh, r=["zq"], w=["twl"])
                CP("pool", alx[64:128, :], zq[64:128, :], r=["zq"], w=["alx"])
                ck("z")
                if own:
                    rwkv_z(7, 1, zq[:], "zq")
                    ACT(zq[:], zq[:], AF.Tanh, scale=0.5, r=["zq"], w=["zq"])
                    TS("pool", sgl[:, :], zq[:], 0.5, 0.5, ALU.mult, ALU.add, r=["zq"], w=["sgl"])
                    DMA("sp", sgl_s[:, t0:t0 + TT], sgl[:, :], r=["sgl"], w=[("sgl_s", t0 // TT)])
                for fc in range(4):
                    if own:
                        rwkv_z(8 + 3 * fc, 2 + 3 * fc, r_f[:], "r_f")
                    rwkv_z(9 + 3 * fc, 3 + 3 * fc, k_f[:], "k_f")
                    rwkv_z(10 + 3 * fc, 4 + 3 * fc, v_f[:], "v_f")
                    ACT(qsq[:], k_f[:], AF.Square, scale=vc("k_k", fc), r=["k_f", "vec"], w=["qsq"])
                    MM(psA[3][:, 0:TT], onesblk, qsq[:], r=["cstb", "qsq"], w=[pk(3)])
                    RSQ(qrs[:], psA[3][:, 0:TT], 1.0, eps_c[:, 1:2], TT, r=[pk(3)], w=["qrs"])
                    STT("dve", nkk[:], k_f[:], vc("nkk", fc), qrs[:], ALU.mult, ALU.mult, r=["k_f", "qrs", "vec2"], w=["nkk"])
                    ck("k")
                    CP("pool", vb[:], v_f[:], r=["v_f"], w=["vb"])
                    for bi in range(NBT):
                        TR(psT[0][:, bi * 128:(bi + 1) * 128], vb[:, bi * 128:(bi + 1) * 128], ident, r=["vb", "cstb"], w=[pkT(0)])
                    CP("dve", vT[:, :, fc * 128:(fc + 1) * 128], psT[0][:, 0:NBT * 128].rearrange("p (b c) -> p b c", b=NBT), r=[pkT(0)], w=["vT"])
                    for d in dirs:
                        dn = "AB"[d]
                        MM(psA[3][:, 0:TT], a2t[64:128, d, fc * 128:(fc + 1) * 128], alx[64:128, :], r=["a2t", "alx"], w=[pk(3)])
                        ACT(a_f[:], psA[3][:, 0:TT], AF.Tanh, bias=vc("ha0" + dn, fc), scale=0.5, r=[pk(3), "vec2"], w=["a_f"])
                        STT("dve", b_f[:], a_f[:], 1.0, nkk[:], ALU.add, ALU.mult, r=["nkk", "a_f"], w=["b_f"])
                        TS("dve", t1_f[:], a_f[:], vc("hka", fc), vc("c2ka", fc), ALU.mult, ALU.add, r=["a_f", "vec2"], w=["t1_f"])
                        TTo("pool", kd_f[:], k_f[:], t1_f[:], ALU.mult, r=["k_f", "t1_f"], w=["kd_f"])
                        if own:
                            STT("dve", pb[:], r_f[:], vc("rk" + dn, fc), kd_f[:], ALU.mult, ALU.mult, r=["r_f", "kd_f", "vec"], w=["pb"])
                            MM(psA[2][:, 0:TT], onesblk, pb[:], start=(d == 0), stop=(d == 1), r=["cstb", "pb"], w=[pk(2)])
                        for bi in range(NBT):
                            MM(psA[4][:, bi * 128:(bi + 1) * 128], twl[:, bi * 128:(bi + 1) * 128], w2aug[:, d, fc * 128:(fc + 1) * 128],
                               r=["twl", "twl1", "w2aug"], w=[pk(4)])
                        ACT(sgfc[:, :, :], psA[4][:, 0:NBT * 128].rearrange("p (b c) -> p b c", b=NBT), AF.Tanh, scale=0.5, r=[pk(4)], w=["sgfc"])
                        for bi in range(NBT):
                            MM(psA[3][:, bi * 128:(bi + 1) * 128], sgfc[:, bi, :], tri[:, d, :], start=True, stop=False, r=["sgfc", "tri"], w=[pk(3)])
                            MM(psA[3][:, bi * 128:(bi + 1) * 128], cstf[0:1, 2, :], crow[0:1, d, :], start=False, stop=True, r=["cstf", "crow"], w=[pk(3)])
                        ACT(E_f[:], psA[3][:, 0:TT], AF.Exp, r=[pk(3)], w=["E_f"])
                        ACT(G_f[:], psA[3][:, 0:TT], AF.Exp, scale=-1.0, r=[pk(3)], w=["G_f"])
                        Ev = E_f[:].rearrange("p (c t) -> p c t", t=64)
                        pcol = 63 if d == 0 else 0
                        CP("pool", PCs[:, d, pcidx0:pcidx0 + TT // 64, fc], Ev[:, :, pcol], r=["E_f"], w=["PCs"])
                        ARv = AR[:, d, fc, :, :]
                        if own:
                            TTo("dve", ARv[:, :, 128:256], r_f[:].rearrange("p (b t) -> p b t", t=128), E_f[:].rearrange("p (b t) -> p b t", t=128),
                                ALU.mult, r=["r_f", "E_f"], w=["AR"])
                        ARa = AR[:, d, fc, :, 0:128].rearrange("p b (c t) -> p b c t", t=64)
                        nk4 = nkk[:].rearrange("p (b c t) -> p b c t", c=2, t=64)
                        E4 = E_f[:].rearrange("p (b c t) -> p b c t", c=2, t=64)
                        for bi in range(NBT):
                            if d == 0:
                                TTo("pool", ARa[:, bi, :, 1:64], nk4[:, bi, :, 1:64], E4[:, bi, :, 0:63], ALU.mult, r=["nkk", "E_f"], w=["AR"])
                                CP("pool", ARa[:, bi, :, 0:1], nk4[:, bi, :, 0:1], r=["nkk"], w=["AR"])
                            else:
                                TTo("pool", ARa[:, bi, :, 0:63], nk4[:, bi, :, 0:63], E4[:, bi, :, 1:64], ALU.mult, r=["nkk", "E_f"], w=["AR"])
                                CP("pool", ARa[:, bi, :, 63:64], nk4[:, bi, :, 63:64], r=["nkk"], w=["AR"])
                        STT("dve", bt[:, d, fc, :], b_f[:], -0.5, G_f[:], ALU.mult, ALU.mult, r=["b_f", "G_f"], w=["bt"])
                        TTo("pool", kt[:, d, fc, :], kd_f[:], G_f[:], ALU.mult, r=["kd_f", "G_f"], w=["kt"])
                        for (src_key, dst, dkey, which) in (("bt", btT, "btT", 0), ("kt", ktT, "ktT", 1), ("AR", atT, "atT", 2)):
                            pb_i = which % 2
                            for bi in range(NBT):
                                if which == 0:
                                    src = bt[:, d, fc, bi * 128:(bi + 1) * 128]
                                elif which == 1:
                                    src = kt[:, d, fc, bi * 128:(bi + 1) * 128]
                                else:
                                    src = AR[:, d, fc, bi, 0:128]
                                TR(psT[pb_i][:, bi * 128:(bi + 1) * 128], src, ident, r=[src_key, "cstb"], w=[pkT(pb_i)])
                            CP("dve" if which != 1 else "act", dst[:, d, :, fc * 128:(fc + 1) * 128],
                               psT[pb_i][:, 0:NBT * 128].rearrange("p (b c) -> p b c", b=NBT), r=[pkT(pb_i)], w=[dkey]) if which != 1 else \
                                ACT(dst[:, d, :, fc * 128:(fc + 1) * 128], psT[pb_i][:, 0:NBT * 128].rearrange("p (b c) -> p b c", b=NBT), AF.Copy,
                                    r=[pkT(pb_i)], w=[dkey])
                    if own:
                        TTo("dve", bv[:, fc, :], psA[2][:, 0:TT], v_f[:], ALU.mult, r=[pk(2), "v_f"], w=["bv"])
                        DMA("sp", bv_s[:, fc, t0:t0 + TT], bv[:, fc, :], r=["bv"], w=[("bv_s", t0 // TT, fc)])
                ck("f")
                for d in dirs:
                    ms, mi = (4, 5) if d == 0 else (6, 7)
                    for bi in range(NBT):
                        gblk = (t0 // 128 + bi) if own else None
                        def amat(bank0, lh, rh, keys):
                            for h in range(8):
                                fc = h // 2; rows = slice((h % 2) * 64, (h % 2) * 64 + 64)
                                MM(psA[bank0 + h % 2][:, (h // 2) * 128:(h // 2) * 128 + 128], lh(rows, fc), rh(rows, fc), r=keys, w=[pk(bank0 + h % 2)])

                        def aevac(bank0, dst, dkey, mslot):
                            dv = dst[:].rearrange("p (i two) t -> p two i t", two=2)
                            for par in range(2):
                                TTo("dve", dv[:, par, :, :], psA[bank0 + par][:, :].rearrange("p (i t) -> p i t", i=4), cstm[:, mslot - 4, :, :], ALU.mult,
                                    r=[pk(bank0 + par), "cstm"], w=[dkey])
                        tokc = slice(bi * 128, (bi + 1) * 128)
                        amat(0, lambda rows, fc: bt[rows, d, fc, tokc], lambda rows, fc: AR[rows, d, fc, bi, 0:128], ["bt", "AR"])
                        amat(2, lambda rows, fc: kt[rows, d, fc, tokc], lambda rows, fc: AR[rows, d, fc, bi, 0:128], ["kt", "AR"])
                        amat(4, lambda rows, fc: AR[rows, d, fc, bi, 0:128], lambda rows, fc: bt[rows, d, fc, tokc], ["bt", "AR"])
                        aevac(0, Aab, "Aab", ms)
                        aevac(2, Aak, "Aak", ms)
                        aevac(4, AabT, "AabT", (6 if d == 0 else 4))
                        if own:
                            amat(0, lambda rows, fc: bt[rows, d, fc, tokc], lambda rows, fc: AR[rows, d, fc, bi, 128:256], ["bt", "AR"])
                            amat(2, lambda rows, fc: kt[rows, d, fc, tokc], lambda rows, fc: AR[rows, d, fc, bi, 128:256], ["kt", "AR"])
                            aevac(0, Arb, "Arb", mi)
                            aevac(2, Ark, "Ark", mi)
                        ck("a")
                        Z = Zp[0]
                        CP("pool", Z[:, :, 0, :, :], atT[:, d, bi, :].rearrange("p (f h j) -> p f h j", f=4, h=2), r=["atT"], w=[("Z", 0)])
                        for h in range(8):
                            MM(psA[3][:, h * 64:(h + 1) * 64], Aak[:, h, :], vT[:, bi, h * 64:(h + 1) * 64], r=["Aak", "vT"], w=[pk(3)])
                        CP("dve", Z[:, :, 1, :, :], psA[3][:, 0:512].rearrange("p (f h j) -> p f h j", f=4, h=2), r=[pk(3)], w=[("Z", 0)])
                        curA, curAT, ak, atk = Aab, AabT, "Aab", "AabT"
                        zi = 0
                        for lev in range(6):
                            Zc, Zn = Zp[zi], Zp[1 - zi]
                            for hh in range(2):
                                bank = 4 + hh
                                for q4 in range(4):
                                    h = 4 * hh + q4; fc = h // 2; h2 = h % 2
                                    o = psA[bank][:, q4 * 128:(q4 + 1) * 128].rearrange("p (a j) -> p a j", a=2)
                                    MM(o, curA[:, h, :], Zc[:, fc, :, h2, :], start=True, stop=False, r=[ak, ("Z", zi)], w=[pk(bank)])
                                    MM(o, ident, Zc[:, fc, :, h2, :], start=False, stop=True, r=["cstb", ("Z", zi)], w=[pk(bank)])
                                for f in range(2):
                                    src = psA[bank][:, f * 256:(f + 1) * 256].rearrange("p (h a j) -> p a h j", h=2, a=2)
                                    if f == 0:
                                        ACT(Zn[:, 2 * hh + f, :, :, :], src, AF.Copy, r=[pk(bank)], w=[("Z", 1 - zi)])
                                    else:
                                        CP("dve", Zn[:, 2 * hh + f, :, :, :], src, r=[pk(bank)], w=[("Z", 1 - zi)])
                            zi = 1 - zi
                            if lev < 5:
                                nA, nAT = Ap[lev % 2], ATp[lev % 2]
                                nak, natk = ("Ap", lev % 2), ("ATp", lev % 2)
                                for hh in range(2):
                                    for q4 in range(4):
                                        h = 4 * hh + q4
                                        MM(psA[0 + hh][:, q4 * 128:(q4 + 1) * 128], curAT[:, h, :], curA[:, h, :], r=[ak, atk], w=[pk(0 + hh)])
                                        MM(psA[2 + hh][:, q4 * 128:(q4 + 1) * 128], curA[:, h, :], curAT[:, h, :], r=[ak, atk], w=[pk(2 + hh)])
                                    ACT(nA[:, 4 * hh:4 * hh + 4, :], psA[0 + hh][:, :].rearrange("p (h t) -> p h t", h=4), AF.Copy, r=[pk(0 + hh)], w=[nak])
                                    CP("dve", nAT[:, 4 * hh:4 * hh + 4, :], psA[2 + hh][:, :].rearrange("p (h t) -> p h t", h=4), r=[pk(2 + hh)], w=[natk])
                                curA, curAT, ak, atk = nA, nAT, nak, natk
                        Zf = Zp[zi]; zk = ("Z", zi)
                        ck("d")
                        for c2 in range(2):
                            rs_ = slice(c2 * 64, c2 * 64 + 64)
                            for fc in range(4):
                                MM(psA[c2][:, fc * 128:(fc + 1) * 128], Zf[rs_, fc, 0, :, :], btT[rs_, d, bi, fc * 128:(fc + 1) * 128], r=[zk, "btT"], w=[pk(c2)])
                            for fc in range(4):
                                TTo("dve", M0st[:, c2, fc, :], psA[c2][:, fc * 128:(fc + 1) * 128], bdm, ALU.mult, r=[pk(c2), "cstb"], w=["M0st"])
                            for h in range(8):
                                fc = h // 2; h2 = h % 2
                                o = psA[2 + c2][h2 * 64:h2 * 64 + 64, fc * 64:fc * 64 + 64]
                                MM(o, btT[rs_, d, bi, h * 64:(h + 1) * 64], Zf[rs_, fc, 1, h2, :], start=True, stop=False, r=["btT", zk], w=[pk(2 + c2)])
                                MM(o, ktT[rs_, d, bi, h * 64:(h + 1) * 64], vT[rs_, bi, h * 64:(h + 1) * 64], start=False, stop=True, r=["ktT", "vT"], w=[pk(2 + c2)])
                        for c2 in range(2):
                            ACT(N0st[:, c2, :, :], psA[2 + c2][:, 0:256].rearrange("p (f i) -> p f i", f=4), AF.Copy, r=[pk(2 + c2)], w=["N0st"])
                        cidx = pcidx0 + 2 * bi
                        DMA("sp", m0_s[d, cidx:cidx + 2].rearrange("c p f j -> p c f j"), M0st[:], r=["M0st"], w=[("m0_s", d, cidx)])
                        DMA("sp", n0_s[d, cidx:cidx + 2].rearrange("c p f i -> p c f i"), N0st[:], r=["N0st"], w=[("n0_s", d, cidx)])
                        if own:
                            for fc in range(4):
                                o = psA[2][:, fc * 128:(fc + 1) * 128]
                                MM(o, ident, AR[:, d, fc, bi, 128:256], start=True, stop=False, r=["cstb", "AR"], w=[pk(2)])
                                for h2 in range(2):
                                    h = 2 * fc + h2
                                    MM(psA[2][h2 * 64:h2 * 64 + 64, fc * 128:(fc + 1) * 128], Zf[:, fc, 0, h2, :], Arb[:, h, :],
                                       start=False, stop=(h2 == 1), r=[zk, "Arb"], w=[pk(2)])
                                for h2 in range(2):
                                    h = 2 * fc + h2
                                    o2 = psA[3][h2 * 64:h2 * 64 + 64, fc * 128:(fc + 1) * 128]
                                    MM(o2, Zf[:, fc, 1, h2, :], Arb[:, h, :], start=True, stop=False, r=[zk, "Arb"], w=[pk(3)])
                                    MM(o2, vT[:, bi, h * 64:(h + 1) * 64], Ark[:, h, :], start=False, stop=True, r=["vT", "Ark"], w=[pk(3)])
                            ACT(Qst[:], psA[2][:, :].rearrange("p (f t) -> p f t", f=4), AF.Copy, r=[pk(2)], w=["Qst"])
                            DMA("sp", q_s[d, gblk], Qst[:], r=["Qst"], w=[("q_s", d, gblk)])
                            CP("dve", Y0st[:], psA[3][:, :].rearrange("p (f t) -> p f t", f=4), r=[pk(3)], w=["Y0st"])
                            DMA("sp", y0_s[d, :, :, t0 + bi * 128:t0 + bi * 128 + 128], Y0st[:], r=["Y0st"], w=[("y0_s", d, gblk)])

            def attention_tile(ti):
                qs = ti % 2
                for qb in range(NBT):
                    n = NBT * ti + qb
                    kbs = [kb for kb in (n - 1, n, n + 1) if kb >= 0]
                    for m in range(4):
                        for h2 in range(2):
                            h = 2 * m + h2
                            c = h % 4; g = h // 4
                            rows = slice(g * 64, g * 64 + 64)
                            pi = h2
                            for kb in kbs:
                                slot = kb - (n - 1)
                                MM(psA[pi][:, slot * 128:(slot + 1) * 128], kT[rows, kb * 128:(kb + 1) * 128], qT[rows, qs, c, qb * 128:(qb + 1) * 128],
                                   r=["kT", ("qT", qs)], w=[pk(pi)])
                            lo = (kbs[0] - (n - 1)) * 128
                            ACT(probs[:, pi, lo:384], psA[pi][:, lo:384], AF.Exp, scale=0.125, r=[pk(pi)], w=[("probs", pi)])
                            if n - 1 >= 0:
                                TTo("pool", probs[:, pi, 0:128], probs[:, pi, 0:128], cstb[:, 10, :], ALU.mult, r=[("probs", pi), "cstb"], w=[("probs", pi)])
                            TTo("pool", probs[:, pi, 256:384], probs[:, pi, 256:384], cstb[:, 11, :], ALU.mult, r=[("probs", pi), "cstb"], w=[("probs", pi)])
                            orow = slice(h2 * 64, h2 * 64 + 64)
                            for i, kb in enumerate(kbs):
                                slot = kb - (n - 1)
                                MM(psA[2][orow, 0:128], vaug[:, kb, g, 0, :], probs[:, pi, slot * 128:(slot + 1) * 128], start=(i == 0), stop=(i == len(kbs) - 1),
                                   r=["vaug", ("probs", pi)], w=[pk(2)])
                            for i, kb in enumerate(kbs):
                                slot = kb - (n - 1)
                                MM(psA[3][orow, 0:128], vaug[:, kb, g, 1, :], probs[:, pi, slot * 128:(slot + 1) * 128], start=(i == 0), stop=(i == len(kbs) - 1),
                                   r=["vaug1", ("probs", pi)], w=[pk(3)])
                        TS("dve", den[:], psA[3][:, 0:128], vc("esink", m), None, ALU.add, r=[pk(3), "vec2"], w=["den"])
                        RCP(den[:], den[:], r=["den"], w=["den"])
                        TTo("dve", attn_o[:, m, qb * 128:(qb + 1) * 128], psA[2][:, 0:128], den[:], ALU.mult, r=[pk(2), "den"], w=["attn_o"])

            def attn_and_store(ti):
                attention_tile(ti)
                DMA("sp", ao_s[:, :, ti * TT:(ti + 1) * TT], attn_o[:], r=["attn_o"], w=[("ao_s", ti)])

            oth = list(range(2 * NTH - 1, NTH - 1, -1))
            if ntiles_other is not None:
                oth = oth[len(oth) - ntiles_other:] if ntiles_other > 0 else []
            for ti in oth:
                t0 = ti * TT
                sweep_tile(t0, own=False, first_other=(ti == NTH), pcidx0=t0 // 64)
            nown = NTH if ntiles_own is None else ntiles_own
            for ti in range(nown):
                sweep_tile(ti * TT, own=True, first_other=False, pcidx0=(ti * TT) // 64)
                if ti >= 1:
                    attn_and_store(ti - 1)
            if nown == NTH:
                attn_and_store(NTH - 1)
            if dbg is not None and upto == 1:
                dbg(locals(), P, DMA, dbg_out)
            if upto == 1:
                P.disabled = True
        barrier()

        with ExitStack() as s23:
            Yb = sb("Yb", [128, 4, HALF], F32, s23)
            with ExitStack() as s2:
                S32 = sb("S32", [128, 2, 4, 64], F32, s2)
                Sb = sb("Sb", [128, 2, 4, 64], BF16, s2)
                St = sb("St", [128, 2, 4, 64], F32, s2)
                M0l = sb("M0l", [128, 4, 4, 128], BF16, s2)
                N0l = sb("N0l", [128, 4, 4, 64], BF16, s2)
                Ql = sb("Ql", [128, 4, 4, 128], BF16, s2)
                ytmp = sb("ytmp", [128, 2, 4, 512], F32, s2)
                for fc in range(4):
                    DMA("sp", Yb[:, fc, :], y0_s[0, :, fc, :], r=[("y0_s", 0, g) for g in range(NB_OWN)], w=[("Yb", fc)])
                for j in range(4):
                    s = j % 2
                    DMA("sp", ytmp[:, s, :, :], y0_s[1, :, :, j * 512:(j + 1) * 512], r=[("y0_s", 1, g) for g in range(NB_OWN)], w=[("ytmp", s)])
                    TTo("pool", Yb[:, :, j * 512:(j + 1) * 512], Yb[:, :, j * 512:(j + 1) * 512], ytmp[:, s, :, :], ALU.add,
                        r=[("ytmp", s)] + [("Yb", fc) for fc in range(4)], w=["Yb"])
                MEMSET("dve", S32[:], 0.0, w=[("S32", 0), ("S32", 1)])
                MEMSET("pool", Sb[:], 0.0, w=[("Sb", 0), ("Sb", 1)])
                stepn = [0]

                def chain_step(d, cidx, own):
                    sl = stepn[0] % 4
                    stepn[0] += 1
                    DMA("sp", M0l[:, sl, :, :], m0_s[d, cidx], r=[("m0_s", d, cidx - cidx % 2)], w=[("M0l", sl)])
                    DMA("sp", N0l[:, sl, :, :], n0_s[d, cidx], r=[("n0_s", d, cidx - cidx % 2)], w=[("N0l", sl)])
                    bank = 4 + d
                    if own:
                        blk = cidx // 2; c2 = cidx % 2
                        qsl = (blk % 2) * 2 + d
                        if (d == 0 and c2 == 0) or (d == 1 and c2 == 1):
                            DMA("sp", Ql[:, qsl, :, :], q_s[d, blk], r=[("q_s", d, blk)], w=[("Ql", qsl)])
                        yb = 0 + d
                        for h in range(8):
                            fc = h // 2; h2 = h % 2; rows = slice(h2 * 64, h2 * 64 + 64)
                            MM(psA[yb][rows, fc * 64:(fc + 1) * 64], Sb[rows, d, fc, :], Ql[rows, qsl, fc, c2 * 64:c2 * 64 + 64],
                               r=[("Sb", d), ("Ql", qsl)], w=[pk(yb)])
                        yv = Yb[:, :, cidx * 64:(cidx + 1) * 64]
                        TTo("dve", yv, yv, psA[yb][:, 0:256].rearrange("p (f t) -> p f t", f=4), ALU.add, r=[pk(yb), "Yb"], w=["Yb"])
                    for fc in range(4):
                        o = psA[bank][:, fc * 64:(fc + 1) * 64]
                        MM(o, ident, N0l[:, sl, fc, :], start=True, stop=False, r=["cstb", ("N0l", sl)], w=[pk(bank)])
                        MM(o, M0l[:, sl, fc, :], Sb[:, d, fc, :], start=False, stop=True, r=[("M0l", sl), ("Sb", d)], w=[pk(bank)])
                    TTo("dve", St[:, d, :, :], psA[bank][:, 0:256].rearrange("p (f i) -> p f i", f=4), S32[:, d, :, :], ALU.add,
                        r=[pk(bank), ("S32", d)], w=[("St", d)])
                    for fc in range(4):
                        TS("dve", S32[:, d, fc, :], St[:, d, fc, :], PCs[:, d, cidx, fc:fc + 1], None, ALU.mult, r=[("St", d), "PCs"], w=[("S32", d)])
                    ACT(Sb[:, d, :, :], S32[:, d, :, :], AF.Copy, r=[("S32", d)], w=[("Sb", d)])

                for cidx in range(63, 31, -1):
                    chain_step(1, cidx, own=False)
                for i in range(32):
                    chain_step(0, i, own=True)
                    chain_step(1, 31 - i, own=True)
            if upto == 2:
                P.disabled = True
            barrier()

            with ExitStack() as s3:
                wg = sb("wg", [128, 8, 2048], BF16, s3)
                wua = sb("wua", [128, 4, D], BF16, s3); wur = sb("wur", [128, 4, D], BF16, s3); wout = sb("wout", [128, 8, D], BF16, s3)
                for kc in range(8):
                    DMA("pool", wg[:, kc, :], wgt[kc * 128:(kc + 1) * 128, :], w=["wg"])
                    DMA("pool", wout[:, kc, :], wout_d[kc * 128:(kc + 1) * 128, :], w=["wout"])
                for kc in range(4):
                    DMA("pool", wua[:, kc, :], wua_d[kc * 128:(kc + 1) * 128, :], w=["wua"])
                    DMA("pool", wur[:, kc, :], wur_d[kc * 128:(kc + 1) * 128, :], w=["wur"])
                xt = sb("xt3", [128, 2, D], F32, s3); hb = sb("hb3", [128, 2, D], BF16, s3)
                ssq = sb("ssq3", [128, 2], F32, s3); rstd = sb("rstd3", [128, 2], F32, s3)
                hT = sb("hT3", [128, 8, TT], BF16, s3)
                yc = sb("yc", [128, TT], F32, s3); ysq = sb("ysq", [128, TT], F32, s3); yr = sb("yr", [128, TT], F32, s3)
                rwT = sb("rwT", [128, 4, TT], BF16, s3)
                gates = sb("gates", [128, 16, TT], BF16, s3)
                mg = sb("mg", [128, 8, TT], BF16, s3); m1 = sb("m1", [128, TT], F32, s3); m2 = sb("m2", [128, TT], F32, s3)
                x1b = sb("x1b", [128, 2, D], F32, s3)
                xres = sb("xres", [128, 2, D], F32, s3)
                ao_l = sb("ao_l", [128, 4, TT], BF16, s3); bv_l = sb("bv_l", [128, 4, TT], BF16, s3); sgl_l = sb("sgl_l", [128, TT], BF16, s3)
                onesf = cstf[:, 1, :]
                for ti in range(NTH):
                    t0 = ti * TT
                    DMA("sp", ao_l[:], ao_s[:, :, t0:t0 + TT], r=[("ao_s", ti)], w=["ao_l"])
                    DMA("sp", bv_l[:], bv_s[:, :, t0:t0 + TT], r=[("bv_s", ti, fc) for fc in range(4)], w=["bv_l"])
                    DMA("sp", sgl_l[:], sgl_s[:, t0:t0 + TT], r=[("sgl_s", ti)], w=["sgl_l"])
                    for fc in range(4):
                        Yv = Yb[:, fc, t0:t0 + TT]
                        MM(psA[0][:, 0:TT], onesf, Yv, r=["cstf", "Yb"], w=[pk(0)])
                        TTo("dve", yc[:], Yv, psA[0][:, 0:TT], ALU.subtract, r=["Yb", pk(0)], w=["yc"])
                        ACT(ysq[:], yc[:], AF.Square, r=["yc"], w=["ysq"])
                        MM(psA[1][:, 0:TT], onesf, ysq[:], r=["cstf", "ysq"], w=[pk(1)])
                        RSQ(yr[:], psA[1][:, 0:TT], 1.0, eps_c[:, 2:3], TT, r=[pk(1)], w=["yr"])
                        TTo("pool", yc[:], yc[:], yr[:], ALU.mult, r=["yc", "yr"], w=["yc"])
                        TS("dve", yc[:], yc[:], vc("lnw", fc), vc("lnb", fc), ALU.mult, ALU.add, r=["yc", "vec"], w=["yc"])
                        TTo("pool", yc[:], yc[:], bv_l[:, fc, :], ALU.add, r=["yc", "bv_l"], w=["yc"])
                        MM(psA[2][:, 0:TT], g2t[:, fc * 128:(fc + 1) * 128], sgl_l[:, :], r=["g2t", "sgl_l"], w=[pk(2)])
                        TTo("dve", rwT[:, fc, :], yc[:], psA[2][:, 0:TT], ALU.mult, r=["yc", pk(2)], w=["rwT"])
                    xaps = [(xs[1 + t0 + 128 * i: 1 + t0 + 128 * i + 128, :], 128, 128 * i) for i in range(NBT)]
                    make_hT("hT3", xaps, None, hT, "nmix", "hT3", xt, hb, ssq, rstd)
                    for gc in range(16):
                        bank = gc % 2
                        for kc in range(8):
                            MM(psA[bank][:, 0:TT], wg[:, kc, gc * 128:(gc + 1) * 128], hT[:, kc, :], start=(kc == 0), stop=(kc == 7), r=["wg", "hT3"], w=[pk(bank)])
                        ACT(gates[:, gc, :], psA[bank][:, 0:TT], AF.Tanh, scale=0.5, r=[pk(bank)], w=["gates"])
                    for oc in range(8):
                        for kc in range(4):
                            MM(psA[2][:, 0:TT], wua[:, kc, oc * 128:(oc + 1) * 128], ao_l[:, kc, :], start=(kc == 0), stop=(kc == 3), r=["wua", "ao_l"], w=[pk(2)])
                        for kc in range(4):
                            MM(psA[3][:, 0:TT], wur[:, kc, oc * 128:(oc + 1) * 128], rwT[:, kc, :], start=(kc == 0), stop=(kc == 3), r=["wur", "rwT"], w=[pk(3)])
                        STT("dve", m1[:], gates[:, oc, :], 1.0, psA[2][:, 0:TT], ALU.add, ALU.mult, r=[pk(2), "gates"], w=["m1"])
                        STT("dve", m2[:], gates[:, 8 + oc, :], 1.0, psA[3][:, 0:TT], ALU.add, ALU.mult, r=[pk(3), "gates"], w=["m2"])
                        TTo("pool", mg[:, oc, :], m1[:], m2[:], ALU.add, r=["m1", "m2"], w=["mg"])
                    for bi in range(NBT):
                        s = bi % 2
                        tok = t0 + bi * 128
                        DMA("sp", xres[:, s, :], xs[1 + tok:1 + tok + 128, :], w=[("xres", s)])
                        for hf in range(2):
                            bank = 4 + hf
                            for kc in range(8):
                                MM(psA[bank][:, :], mg[:, kc, bi * 128:(bi + 1) * 128], wout[:, kc, hf * 512:(hf + 1) * 512], start=(kc == 0), stop=(kc == 7), r=["mg", "wout"], w=[pk(bank)])
                            STT("dve", x1b[:, s, hf * 512:(hf + 1) * 512], psA[bank][:, :], 0.5, xres[:, s, hf * 512:(hf + 1) * 512], ALU.mult, ALU.add, r=[pk(bank), ("xres", s)], w=[("x1b", s)])
                        DMA("sp", x1_s[tok:tok + 128, :], x1b[:, s, :], r=[("x1b", s)], w=[("x1_s", tok // 128)])
        if upto == 3:
            P.disabled = True
        barrier()

        with ExitStack() as s4:
            wf1 = sb("wf1", [128, 8, 4096], BF16, s4); wf2 = sb("wf2", [128, 32, D], BF16, s4)
            wpg = sb("wpg", [128, 8, D], BF16, s4); wple = sb("wple", [128, 2, D], BF16, s4)
            for kc in range(8):
                DMA("pool", wf1[:, kc, :], wff1_d[kc * 128:(kc + 1) * 128, :], w=["wf1"])
                DMA("pool", wpg[:, kc, :], wpg_d[kc * 128:(kc + 1) * 128, :], w=["wpg"])
            for kc in range(32):
                DMA("pool", wf2[:, kc, :], wff2_d[kc * 128:(kc + 1) * 128, :], w=["wf2"])
            for kc in range(2):
                DMA("pool", wple[:, kc, :], wple_d[kc * 128:(kc + 1) * 128, :], w=["wple"])
            xt = sb("xt4", [128, 2, D], F32, s4); hb = sb("hb4", [128, D], BF16, s4)
            ssq = sb("ssq4", [128, 2], F32, s4); rstd = sb("rstd4", [128, 2], F32, s4)
            hT = sb("hT4", [128, 8, 128], BF16, s4)
            act = sb("act", [128, 32, 128], BF16, s4)
            rl = sb("rl", [128, 2, 128], F32, s4)
            x2 = sb("x2", [128, D], F32, s4)
            pt = sb("pt", [128, 256], F32, s4); pb16 = sb("pb16", [128, 256], BF16, s4); pT = sb("pT", [128, 2, 128], BF16, s4)
            sgp = sb("sgp", [128, 512], F32, s4)

            def norm_T(src, gain, skey, okey, idx):
                ACT(hb[:], src, AF.Square, accum=ssq[:, idx:idx + 1], r=[skey], w=["hb4", ("ssq4", idx)])
                RSQ(rstd[:, idx:idx + 1], ssq[:, idx:idx + 1], 1.0 / D, eps_c[:, 0:1], 1, r=[("ssq4", idx)], w=[("rstd4", idx)])
                TS("dve", hb[:], src, rstd[:, idx:idx + 1], None, ALU.mult, r=[skey, ("rstd4", idx), "hb4"], w=["hb4o"])
                for kc in range(8):
                    TR(psT[kc % 2][:, (kc // 2) * 128:(kc // 2) * 128 + 128], hb[:, kc * 128:(kc + 1) * 128], ident, r=["hb4o", "cstb"], w=[pkT(kc % 2)])
                for kc in range(8):
                    srcp = psT[kc % 2][:, (kc // 2) * 128:(kc // 2) * 128 + 128]
                    if kc % 2 == 0:
                        ACT(hT[:, kc, :], srcp, AF.Copy, scale=vc(gain, kc), r=[pkT(kc % 2), "vec"], w=[okey])
                    else:
                        TS("dve", hT[:, kc, :], srcp, vc(gain, kc), None, ALU.mult, r=[pkT(kc % 2), "vec"], w=[okey])

            for blk in range(NB_OWN):
                tok = blk * 128
                s = blk % 2
                DMA("sp", xt[:, s, :], x1_s[tok:tok + 128, :], r=[("x1_s", blk)], w=[("xt4", s)])
                norm_T(xt[:, s, :], "nffn", ("xt4", s), "hT4", 0)
                for hc in range(32):
                    bank = hc % 2
                    for kc in range(8):
                        MM(psA[bank][:, 0:128], wf1[:, kc, hc * 128:(hc + 1) * 128], hT[:, kc, :], start=(kc == 0), stop=(kc == 7), r=["wf1", "hT4"], w=[pk(bank)])
                    ACT(rl[:, bank, :], psA[bank][:, 0:128], AF.Relu, r=[pk(bank)], w=[("rl", bank)])
                    STT("dve", act[:, hc, :], psA[bank][:, 0:128], 0.0, rl[:, bank, :], ALU.max, ALU.mult, r=[pk(bank), ("rl", bank)], w=["act"])
                for hf in range(2):
                    bank = 2 + hf
                    for hc in range(32):
                        MM(psA[bank][:, :], act[:, hc, :], wf2[:, hc, hf * 512:(hf + 1) * 512], start=(hc == 0), stop=(hc == 31), r=["act", "wf2"], w=[pk(bank)])
                    TTo("dve", x2[:, hf * 512:(hf + 1) * 512], psA[bank][:, :], xt[:, s, hf * 512:(hf + 1) * 512], ALU.add, r=[pk(bank), ("xt4", s)], w=["x2"])
                norm_T(x2[:], "nple", "x2", "hT4", 1)
                DMA("sp", pt[:], pp[tok:tok + 128, :], w=["pt"])
                CP("pool", pb16[:], pt[:], r=["pt"], w=["pb16"])
                for kc in range(2):
                    TR(psT[0][:, kc * 128:(kc + 1) * 128], pb16[:, kc * 128:(kc + 1) * 128], ident, r=["pb16", "cstb"], w=[pkT(0)])
                CP("dve", pT[:, :, :], psT[0][:, 0:256].rearrange("p (k t) -> p k t", k=2), r=[pkT(0)], w=["pT"])
                for hf in range(2):
                    for kc in range(8):
                        MM(psA[4][:, :], hT[:, kc, :], wpg[:, kc, hf * 512:(hf + 1) * 512], start=(kc == 0), stop=(kc == 7), r=["hT4", "wpg"], w=[pk(4)])
                    for kc in range(2):
                        MM(psA[5][:, :], pT[:, kc, :], wple[:, kc, hf * 512:(hf + 1) * 512], start=(kc == 0), stop=(kc == 1), r=["pT", "wple"], w=[pk(5)])
                    ACT(sgp[:], psA[4][:, :], AF.Tanh, scale=0.5, r=[pk(4)], w=["sgp"])
                    STT("dve", sgp[:], sgp[:], 1.0, psA[5][:, :], ALU.add, ALU.mult, r=[pk(5), "sgp"], w=["sgp"])
                    STT("dve", xt[:, s, hf * 512:(hf + 1) * 512], sgp[:], 0.5, x2[:, hf * 512:(hf + 1) * 512], ALU.mult, ALU.add, r=["sgp", "x2", ("xt4", s)], w=[("ob", s)])
                DMA("sp", out_d[tok:tok + 128, :], xt[:, s, :], r=[("ob", s), ("xt4", s)], w=[("out", blk), ("xt4", s)])
            P.add("sp", None, [("out", i) for i in range(NB_OWN)], ())
        P.emit(nc, gs)
    return nc


def _consts():
    cst = np.zeros((128, 12, 128), np.float32)
    i = np.arange(128)
    cst[:, 0, :] = np.eye(128)
    cst[:, 1, :] = (i[:, None] // 64 == i[None, :] // 64)
    perm = np.zeros((128, 128), np.float32)
    for m in range(128):
        j = m % 64
        if j < 8:
            perm[m + 8, m] = 1.0
        elif j < 16:
            perm[m - 8, m] = 1.0
    cst[:, 2, :] = perm
    cst[:, 3, :] = cst[:, 1, :]
    same = (i[:, None] // 64 == i[None, :] // 64)
    cst[:, 4, :] = same & (i[:, None] < i[None, :])
    cst[:, 5, :] = same & (i[:, None] <= i[None, :])
    cst[:, 6, :] = same & (i[:, None] > i[None, :])
    cst[:, 7, :] = same & (i[:, None] >= i[None, :])
    cst[:, 8, :] = 1.0
    cst[:, 9, :] = cst[:, 1, :] / 64.0
    cst[:, 10, :] = (i[:, None] >= i[None, :])
    cst[:, 11, :] = (i[:, None] <= i[None, :])
    tri = np.zeros((128, 2, 128), np.float32)
    tri[:, 0, :] = 0.5 * LWS * cst[:, 5, :]
    tri[:, 1, :] = 0.5 * LWS * cst[:, 7, :]
    return cst, tri


def _rope_tables(pos):
    half = 8
    inv = np.power(np.float32(500000.0), -np.arange(half, dtype=np.float32) * 2.0 / 16.0).astype(np.float32)
    ang = pos.astype(np.float32)[None, :] * inv[:, None]
    cos = np.cos(ang).astype(np.float32); sin = np.sin(ang).astype(np.float32)
    n = pos.shape[0]
    tab = np.zeros((128, 2, n), np.float32)
    for hb in range(2):
        b = hb * 64
        tab[b:b + 64, 0, :] = 1.0
        tab[b:b + 8, 0, :] = cos; tab[b + 8:b + 16, 0, :] = cos
        tab[b:b + 8, 1, :] = -sin; tab[b + 8:b + 16, 1, :] = sin
    return tab


def _col128(v):
    return np.ascontiguousarray(np.asarray(v, np.float32).reshape(-1, 128).T)


_NC_CACHE = {}


def kernel(x, p, norm_mix, w_in, shift_mu, q_norm, k_norm, sink, w0, w2, a0, a2, g2, k_k, k_a, r_k,
           lnx_w, lnx_b, w_up_attn, w_up_rwkv, w_out, norm_ffn, w_ff1, w_ff2, norm_ple, w_ple_gate, w_ple):
    f = lambda a: np.asarray(a, np.float32)
    x = f(x); p = f(p)[0]; w_in = f(w_in)[0]
    qcols = []
    for c in range(4):
        qcols += list(range(c * 64, c * 64 + 64)) + list(range((4 + c) * 64, (4 + c) * 64 + 64))
    O_K, O_V, O_R = 512, 640, 768
    cols = qcols + list(range(O_K, O_K + 128)) + list(range(O_V, O_V + 128))
    rw = O_R
    wa = list(range(rw + 1536, rw + 1536 + 128)); gl = list(range(rw + 1664, rw + 1792))
    cols += wa + gl
    mu_idx = [12, 13]
    for fc in range(4):
        for part in range(3):
            cols += list(range(rw + part * 512 + fc * 128, rw + part * 512 + fc * 128 + 128))
            mu_idx.append(part * 4 + fc)
    wsw = np.ascontiguousarray(w_in[:, cols])
    wgt = np.ascontiguousarray(w_in[:, O_R + 1792:])
    mu_c = _col128(f(shift_mu)[0])[:, mu_idx]
    cst, tri = _consts()
    cst_in = cst.copy()
    cst_in_f = cst.copy()
    cst_in[:, 1, :] = cst[:, 1, :]
    sink_ = f(sink)[0]
    sinkcols = np.zeros((128, 4), np.float32)
    for m in range(4):
        sinkcols[0:64, m] = sink_[2 * m]; sinkcols[64:128, m] = sink_[2 * m + 1]
    in_maps = []
    for c in range(8):
        b, hf = c // 2, c % 2
        xb = x[b]; pb_ = p[b]
        if hf == 0:
            xl = xb; pl = pb_[:HALF]; pos = np.arange(0, HALF + 512); dA, dB = 0, 1
        else:
            xl = xb[::-1]; pl = pb_[HALF:][::-1]; pos = SEQ - 1 - np.arange(0, HALF + 512); dA, dB = 1, 0
        xs = np.zeros((SEQ + 2, D), np.float32); xs[1:SEQ + 1] = xl
        dd = [dA, dB]
        vec = np.zeros((128, NVEC_IN), np.float32)
        def put(n, arr):
            vec[:, VC[n]:VC[n] + arr.shape[1]] = arr
        put("mu", mu_c); put("k_k", _col128(f(k_k)[0])); put("k_a", _col128(f(k_a)[0]))
        put("rkA", _col128(f(r_k)[0, dA].reshape(-1))); put("rkB", _col128(f(r_k)[0, dB].reshape(-1)))
        put("lnw", _col128(f(lnx_w)[0])); put("lnb", _col128(f(lnx_b)[0]))
        put("a0A", _col128(f(a0)[0, dA])); put("a0B", _col128(f(a0)[0, dB]))
        put("qg", np.tile(f(q_norm)[0], 2)[:, None]); put("kg", np.tile(f(k_norm)[0], 2)[:, None])
        put("nmix", _col128(f(norm_mix)[0])); put("nffn", _col128(f(norm_ffn)[0])); put("nple", _col128(f(norm_ple)[0]))
        put("sink", sinkcols)
        w2aug = np.stack([np.concatenate([f(w2)[0, d_], f(w0)[0, d_][None, :]], 0) for d_ in dd])
        a2s = np.stack([f(a2)[0, d_] for d_ in dd])
        cst_c = cst.copy()
        m = {"xs": xs, "pp": np.ascontiguousarray(pl), "wsw": wsw, "wgt": wgt, "vec": vec, "w2aug": np.ascontiguousarray(w2aug),
             "a2": np.ascontiguousarray(a2s), "g2": f(g2)[0], "wua": f(w_up_attn)[0], "wur": f(w_up_rwkv)[0], "wout": f(w_out)[0],
             "wff1": f(w_ff1)[0], "wff2": f(w_ff2)[0], "wpg": f(w_ple_gate)[0], "wple": f(w_ple)[0],
             "rope": _rope_tables(pos), "cst": cst_c, "tri": tri, "crow": np.ascontiguousarray(tri.sum(0)[None])}
        in_maps.append(m)
    if _NC_CACHE.get("prep_only"):
        return in_maps
    if "nc" not in _NC_CACHE:
        _NC_CACHE["nc"] = build_program()
    res = run_bass_kernel_spmd(_NC_CACHE["nc"], in_maps, core_ids=list(range(8)))
    out = np.zeros((4, SEQ, D), np.float32)
    for c in range(8):
        b, hf = c // 2, c % 2
        o = res.results[c]["out"]
        if hf == 0:
            out[b, :HALF] = o
        else:
            out[b, HALF:] = o[::-1]
    return out
```

```python
import os
import numpy as np
from contextlib import ExitStack
import concourse.bass as bass
import concourse.mybir as mybir
from concourse.bass_utils import run_bass_kernel_spmd

F32 = mybir.dt.float32
BF16 = mybir.dt.bfloat16
ALU = mybir.AluOpType
AF = mybir.ActivationFunctionType

SEM_EPOCH = 3000
DMA_SLOTS = 8

D = 1024; SEQ = 4096; HALF = 2048; NB_OWN = 16
TT = 256; NBT = 2; NTH = 8
LWS = -0.6065306597126334


class Op:
    __slots__ = ("eng", "fn", "reads", "writes", "dma", "deps", "flag", "cnt", "slot", "slotval", "idx", "xdeps",
                 "dur", "rg", "tbl", "seg", "alld", "pos")

    def __init__(self, eng, fn, reads, writes, dma):
        self.eng = eng; self.fn = fn; self.reads = reads; self.writes = writes; self.dma = dma
        self.deps = set(); self.flag = False; self.cnt = None; self.slot = None; self.slotval = None
        self.xdeps = (); self.dur = 0.2; self.rg = None; self.tbl = None; self.seg = 0


SYNC_LAT = 0.35


class Prog:
    def __init__(self):
        self.ops = []
        self.tags = []
        self.last_barrier = 0
        self.seg = 0
        self.reorder = True

    def add(self, eng, fn, r=(), w=(), dma=False, dur=0.2, rg=None, tbl=None):
        op = Op(eng, fn, tuple(r), tuple(w), dma)
        op.dur = dur; op.rg = rg; op.tbl = tbl; op.seg = self.seg
        self.tags.append(getattr(self, "tag", None)) if not getattr(self, "disabled", False) else None
        if getattr(self, "disabled", False):
            op.idx = -1
            return op
        op.idx = len(self.ops)
        self.ops.append(op)
        return op

    def all_deps(self):
        last_w = {}
        readers = {}
        ops = self.ops
        for op in ops:
            d = {}
            for j in op.xdeps:
                d[j] = "raw"
            for k in op.reads:
                j = last_w.get(k)
                if j is not None:
                    d[j] = "raw"
            for k in op.writes:
                j = last_w.get(k)
                if j is not None and j not in d:
                    d[j] = "waw"
                for j in readers.get(k, ()):
                    if j != op.idx and j not in d:
                        d[j] = "war"
            for k in op.reads:
                if isinstance(k, tuple) and k[0] in ("ps", "psT"):
                    for j in readers.get(k, ()):
                        if ops[j].eng != op.eng and j not in d:
                            d[j] = "raw"
            for k in op.reads:
                readers.setdefault(k, []).append(op.idx)
            for k in op.writes:
                last_w[k] = op.idx
                readers[k] = []
            op.alld = d

    def schedule(self):
        ops = self.ops
        n = len(ops)
        if not self.reorder:
            return list(range(n))
        fin = [0.0] * n
        self.fin = fin
        self.bind = {}; self._last_on = {}; self.start = {}
        succ = [[] for _ in range(n)]
        ndep = [0] * n
        for op in ops:
            for j in op.alld:
                succ[j].append(op.idx)
            ndep[op.idx] = len(op.alld)
        order = []
        free_t = {}
        cur_tbl = [None]
        segs = {}
        for op in ops:
            segs.setdefault(op.seg, []).append(op.idx)
        done = [False] * n
        ksegs = os.environ.get("KREORD")
        for sg in sorted(segs):
            members = segs[sg]
            if ksegs is not None and str(sg) not in ksegs.split(","):
                for i in members:
                    t0_ = max([free_t.get(ops[i].eng, 0.0)] + [fin[j] for j in ops[i].alld])
                    fin[i] = t0_ + ops[i].dur
                    free_t[ops[i].eng] = fin[i]
                    done[i] = True
                    order.append(i)
                continue
            inseg = set(members)
            ready = {}
            rt = {}
            remaining = len(members)
            cnt_un = {}
            for i in members:
                c = sum(1 for j in ops[i].alld if not done[j])
                cnt_un[i] = c
                if c == 0:
                    ready.setdefault(ops[i].eng, []).append(i)
            def ready_time(i):
                op = ops[i]
                t = 0.0
                for j, kind in op.alld.items():
                    p = ops[j]
                    lat = SYNC_LAT if (p.dma or p.eng != op.eng or kind == "raw") else 0.0
                    if p.eng == op.eng and op.eng == "pe":
                        lat = 0.0
                    t = max(t, fin[j] + lat)
                return t
            for e in ready:
                for i in ready[e]:
                    rt[i] = ready_time(i)
            while remaining:
                best = None
                for e, lst in ready.items():
                    if not lst:
                        continue
                    ft = free_t.get(e, 0.0)
                    bi = None
                    for i in lst:
                        st = max(ft, rt[i])
                        if e == "act" and ops[i].tbl is not None and cur_tbl[0] is not None and ops[i].tbl != cur_tbl[0]:
                            st += 1.3
                        key = (st, i)
                        if bi is None or key < bi:
                            bi = key
                    if best is None or bi < best[0]:
                        best = (bi, e)
                (st, i), e = best
                op = ops[i]
                ready[e].remove(i)
                if e == "act" and op.tbl is not None:
                    cur_tbl[0] = op.tbl
                fin[i] = st + op.dur
                if getattr(self, "trace_bind", False):
                    bj = None; bt_ = -1.0
                    for j in op.alld:
                        if fin[j] > bt_:
                            bt_ = fin[j]; bj = j
                    prev_e = self._last_on.get(e)
                    if prev_e is not None and free_t.get(e, 0.0) >= rt[i] - 1e-9:
                        self.bind[i] = ("eng", prev_e)
                    else:
                        self.bind[i] = ("dep", bj)
                    self._last_on[e] = i
                    self.start[i] = st
                if op.dma:
                    free_t[e] = st + 0.1
                else:
                    free_t[e] = fin[i]
                done[i] = True
                order.append(i)
                remaining -= 1
                for k in succ[i]:
                    if k in inseg:
                        cnt_un[k] -= 1
                        if cnt_un[k] == 0:
                            rt[k] = ready_time(k)
                            ready.setdefault(ops[k].eng, []).append(k)
        self.est_total = max(fin) if fin else 0.0
        self.seg_end = {sg: max(fin[i] for i in segs[sg]) for sg in segs}
        return order

    def analyse(self):
        self.all_deps()
        order = self.schedule()
        ops = self.ops
        for pos, i in enumerate(order):
            ops[i].pos = pos
        self.order = order
        pe = [i for i in order if ops[i].eng == "pe"]
        extra = {}
        for a in range(len(pe)):
            oa = ops[pe[a]]
            if oa.rg is None:
                continue
            for b in range(a + 1, a + 3):
                if b >= len(pe):
                    break
                ob = ops[pe[b]]
                if ob.rg is None:
                    continue
                if (oa.rg[1] < ob.rg[0] or ob.rg[1] < oa.rg[0]) and set(oa.writes) & set(ob.writes):
                    extra.setdefault(pe[b], set()).add(pe[a])
        for op in ops:
            need = set()
            for j, kind in op.alld.items():
                p = ops[j]
                if p.dma:
                    need.add(j)
                elif p.eng == op.eng and not op.dma:
                    if kind == "raw" and op.eng != "pe":
                        need.add(j); p.flag = True
                else:
                    need.add(j); p.flag = True
            for j in extra.get(op.idx, ()):
                need.add(j); ops[j].flag = True
            op.deps = need
        cnt = {}
        nd = {}
        for i in order:
            op = ops[i]
            if op.dma:
                k = nd.get(op.eng, 0)
                nd[op.eng] = k + 1
                op.slot = (op.eng, k % DMA_SLOTS)
                op.slotval = 16 * (k // DMA_SLOTS + 1)
            elif op.fn is not None and op.flag:
                cnt[op.eng] = cnt.get(op.eng, 0) + 1
                op.cnt = cnt[op.eng]
        self.nflag = cnt
        self.ndma = nd

    def emit(self, nc, stack):
        self.analyse()
        engs = ["pe", "act", "dve", "pool", "sp"]
        sems = {}
        for e in engs:
            n = self.nflag.get(e, 0)
            for ep in range(n // SEM_EPOCH + 1):
                sems[(e, ep)] = stack.enter_context(nc.semaphore(f"s_{e}_{ep}"))
        dsem = {}
        for e, n in self.ndma.items():
            for s in range(min(n, DMA_SLOTS)):
                dsem[(e, s)] = stack.enter_context(nc.semaphore(f"d_{e}_{s}"))
        block = stack.enter_context(nc.Block())
        ops = self.ops
        per_eng = {e: [ops[i] for i in self.order if ops[i].eng == e] for e in engs}

        def sem_of(p):
            if p.dma:
                return dsem[p.slot], p.slotval
            c = p.cnt - 1
            return sems[(p.eng, c // SEM_EPOCH)], (c % SEM_EPOCH) + 1

        def run(engobj, lst):
            seen = {}
            dma_hist = {}
            for op in lst:
                waits = {}
                for j in op.deps:
                    s, v = sem_of(ops[j])
                    key = id(s)
                    if seen.get(key, 0) >= v:
                        continue
                    if key not in waits or waits[key][1] < v:
                        waits[key] = (s, v)
                if op.dma:
                    prev = dma_hist.get(op.slot)
                    if prev is not None:
                        s, v = dsem[op.slot], prev
                        key = id(s)
                        if seen.get(key, 0) < v and (key not in waits or waits[key][1] < v):
                            waits[key] = (s, v)
                    dma_hist[op.slot] = op.slotval
                for key, (s, v) in waits.items():
                    engobj.wait_ge(s, v)
                    seen[key] = v
                if op.fn is None:
                    continue
                ins = op.fn(engobj)
                if op.dma:
                    ins.then_inc(dsem[op.slot], 16)
                elif op.flag:
                    s, v = sem_of(op)
                    ins.then_inc(s, 1)

        if per_eng["pe"]:
            @block.tensor
            def _(e):
                run(e, per_eng["pe"])
        if per_eng["act"]:
            @block.scalar
            def _(e):
                run(e, per_eng["act"])
        if per_eng["dve"]:
            @block.vector
            def _(e):
                run(e, per_eng["dve"])
        if per_eng["pool"]:
            @block.gpsimd
            def _(e):
                run(e, per_eng["pool"])
        if per_eng["sp"]:
            @block.sync
            def _(e):
                run(e, per_eng["sp"])


VC = {}
_o = 0
for _n, _c in [("mu", 14), ("k_k", 4), ("k_a", 4), ("rkA", 4), ("rkB", 4), ("lnw", 4), ("lnb", 4),
               ("a0A", 4), ("a0B", 4), ("qg", 1), ("kg", 1), ("nmix", 8), ("nffn", 8), ("nple", 8),
               ("sink", 4), ("omm", 14), ("hmu", 14), ("nkk", 4), ("omka", 4), ("esink", 4), ("ha0A", 4), ("ha0B", 4), ("hka", 4), ("c2ka", 4)]:
    VC[_n] = _o
    _o += _c
NVEC_IN = VC["omm"]
NVEC = _o

N_SW = 20


def build_program(dbg=None, upto=99, ntiles_other=None, ntiles_own=None):
    nc = bass.Bass("TRN2", target_bir_lowering=False)
    P = Prog()

    def din(name, shape):
        return nc.dram_tensor(name, list(shape), F32, kind="ExternalInput").ap()

    xs = din("xs", [SEQ + 2, D])
    pp = din("pp", [HALF, 256])
    wsw = din("wsw", [D, N_SW * 128])
    wgt = din("wgt", [D, 2048])
    vec_d = din("vec", [128, NVEC_IN])
    w2aug_d = din("w2aug", [2, 65, 512])
    a2_d = din("a2", [2, 64, 512])
    g2_d = din("g2", [128, 512])
    wua_d = din("wua", [512, D]); wur_d = din("wur", [512, D]); wout_d = din("wout", [D, D])
    wff1_d = din("wff1", [D, 4096]); wff2_d = din("wff2", [4096, D])
    wpg_d = din("wpg", [D, D]); wple_d = din("wple", [256, D])
    rope_d = din("rope", [128, 2, HALF + 512])
    cst_d = din("cst", [128, 12, 128])
    tri_d = din("tri", [128, 2, 128])
    crow_d = din("crow", [1, 2, 128])
    out_d = nc.dram_tensor("out", [HALF, D], F32, kind="ExternalOutput").ap()
    m0_s = nc.dram_tensor("m0_s", [2, 64, 128, 4, 128], BF16, kind="Internal").ap()
    n0_s = nc.dram_tensor("n0_s", [2, 64, 128, 4, 64], BF16, kind="Internal").ap()
    q_s = nc.dram_tensor("q_s", [2, 16, 128, 4, 128], BF16, kind="Internal").ap()
    x1_s = nc.dram_tensor("x1_s", [HALF, D], F32, kind="Internal").ap()
    y0_s = nc.dram_tensor("y0_s", [2, 128, 4, HALF], F32, kind="Internal").ap()
    ao_s = nc.dram_tensor("ao_s", [128, 4, HALF], BF16, kind="Internal").ap()
    bv_s = nc.dram_tensor("bv_s", [128, 4, HALF], BF16, kind="Internal").ap()
    sgl_s = nc.dram_tensor("sgl_s", [128, HALF], BF16, kind="Internal").ap()
    dbg_out = None
    if dbg is not None:
        dbg_out = nc.dram_tensor("dbg", [128, 16384], F32, kind="ExternalOutput").ap()

    def fsz(ap):
        n = 1
        for d_ in ap.shape[1:]:
            n *= d_
        return n

    def rgof(ap):
        b = ap.base_partition()
        return (b // 32, (b + ap.shape[0] - 1) // 32)

    def MM(out, lhsT, rhs, start=True, stop=True, r=(), w=()):
        dur = max(0.064, 0.02 + fsz(out) / 2000.0) * (4 if lhsT.dtype == F32 else 1)
        P.add("pe", lambda e: e.matmul(out, lhsT, rhs, start=start, stop=stop), r, w, dur=dur, rg=rgof(lhsT))

    def TR(out, in_, idn, r=(), w=()):
        P.add("pe", lambda e: e.transpose(out, in_, idn), r, w, dur=0.064, rg=rgof(in_))

    def ACT(out, in_, func, bias=None, scale=None, accum=None, r=(), w=()):
        kw = {}
        if bias is not None:
            kw["bias"] = bias
        if scale is not None:
            kw["scale"] = scale
        if accum is not None:
            kw["accum_out"] = accum
        tbl = "ln" if func == AF.Ln else ("tanh" if func == AF.Tanh else None)
        P.add("act", lambda e: e.activation(out=out, in_=in_, func=func, **kw), r, w, dur=0.2 + fsz(out) / 1400.0, tbl=tbl)

    def vdur(eng, out):
        return (0.1 + fsz(out) / 960.0) if eng == "dve" else (0.3 + fsz(out) / 500.0)

    def TTo(eng, out, a, b, op, r=(), w=()):
        P.add(eng, lambda e: e.tensor_tensor(out, a, b, op), r, w, dur=vdur(eng, out))

    def TS(eng, out, a, s1, s2, op0, op1=None, r=(), w=()):
        if op1 is None:
            P.add(eng, lambda e: e.tensor_scalar(out, a, s1, None, op0), r, w, dur=vdur(eng, out))
        else:
            P.add(eng, lambda e: e.tensor_scalar(out, a, s1, s2, op0, op1), r, w, dur=vdur(eng, out))

    def STT(eng, out, in0, sc, in1, op0, op1, r=(), w=()):
        P.add(eng, lambda e: e.scalar_tensor_tensor(out, in0, sc, in1, op0, op1), r, w, dur=vdur(eng, out))

    def CP(eng, out, in_, r=(), w=()):
        P.add(eng, lambda e: e.tensor_copy(out, in_), r, w, dur=vdur(eng, out))

    def RCP(out, in_, r=(), w=()):
        P.add("dve", lambda e: e.reciprocal(out, in_), r, w, dur=vdur("dve", out) * 2)

    def MEMSET(eng, ap, val, w=()):
        P.add(eng, lambda e: e.memset(ap, val), (), w, dur=vdur(eng, ap))

    def DMA(eng, out, in_, r=(), w=()):
        nbytes = fsz(out) * out.shape[0] * (4 if out.dtype == F32 else 2)
        P.add(eng, lambda e: e.dma_start(out=out, in_=in_), r, w, dma=True, dur=2.0 + nbytes / 100000.0)

    with ExitStack() as gs:
        def sb(name, shape, dt=F32, st=None):
            return (st or gs).enter_context(nc.sbuf_tensor("sb_" + name, list(shape), dt))

        psA = [gs.enter_context(nc.psum_tensor(f"psA{i}", [128, 512], F32)) for i in range(6)]
        psT = [gs.enter_context(nc.psum_tensor(f"psT{i}", [128, 1024], BF16)) for i in range(2)]
        pk = lambda i: ("ps", i)
        pkT = lambda i: ("psT", i)

        vec = sb("vec", [128, NVEC])
        cstb = sb("cstb", [128, 12, 128], BF16)
        cstf = sb("cstf", [128, 3, 128])
        crow = sb("crow", [1, 2, 128])
        tri = sb("tri", [128, 2, 128])
        eps_c = sb("eps_c", [128, 4])
        junk = sb("junk", [128, D], BF16)
        DMA("sp", vec[:, 0:NVEC_IN], vec_d, w=["vec"])
        DMA("pool", cstb[:], cst_d, w=["cstb"])
        DMA("sp", cstf[:, 0, :], cst_d[:, 0, :], w=["cstf"])
        DMA("sp", cstf[:, 1, :], cst_d[:, 9, :], w=["cstf"])
        DMA("sp", cstf[:, 2, :], cst_d[:, 8, :], w=["cstf"])
        DMA("sp", crow[:], crow_d, w=["crow"])
        DMA("sp", tri[:], tri_d, w=["tri"])
        MEMSET("dve", eps_c[:, 0:1], 1e-6, w=["eps"])
        MEMSET("dve", eps_c[:, 1:2], 1e-24, w=["eps"])
        MEMSET("dve", eps_c[:, 2:3], 64e-5, w=["eps"])
        ident = cstb[:, 0, :]; onesblk = cstb[:, 1, :]; perm = cstb[:, 2, :]; bdm = cstb[:, 3, :]
        identf = cstf[:, 0, :]
        vc = lambda n, i=0: vec[:, VC[n] + i:VC[n] + i + 1]
        TS("dve", vec[:, VC["omm"]:VC["omm"] + 14], vec[:, VC["mu"]:VC["mu"] + 14], -1.0, 1.0, ALU.mult, ALU.add, r=["vec"], w=["vec2"])
        TS("dve", vec[:, VC["hmu"]:VC["hmu"] + 14], vec[:, VC["mu"]:VC["mu"] + 14], 0.5, None, ALU.mult, r=["vec"], w=["vec2"])
        TS("dve", vec[:, VC["nkk"]:VC["nkk"] + 4], vec[:, VC["k_k"]:VC["k_k"] + 4], -1.0, None, ALU.mult, r=["vec"], w=["vec2"])
        TS("dve", vec[:, VC["omka"]:VC["omka"] + 4], vec[:, VC["k_a"]:VC["k_a"] + 4], -1.0, 1.0, ALU.mult, ALU.add, r=["vec"], w=["vec2"])
        ACT(vec[:, VC["esink"]:VC["esink"] + 4], vec[:, VC["sink"]:VC["sink"] + 4], AF.Exp, r=["vec"], w=["vec2"])
        TS("dve", vec[:, VC["ha0A"]:VC["ha0A"] + 8], vec[:, VC["a0A"]:VC["a0A"] + 8], 0.5, None, ALU.mult, r=["vec"], w=["vec2"])
        TS("dve", vec[:, VC["hka"]:VC["hka"] + 4], vec[:, VC["k_a"]:VC["k_a"] + 4], 0.5, None, ALU.mult, r=["vec"], w=["vec2"])
        TS("dve", vec[:, VC["c2ka"]:VC["c2ka"] + 4], vec[:, VC["k_a"]:VC["k_a"] + 4], -0.5, 1.0, ALU.mult, ALU.add, r=["vec"], w=["vec2"])
        VK = ["vec", "vec2", "eps", "cstb", "cstf", "tri"]

        def ck(name):
            if os.environ.get("KSTOP") == name:
                P.disabled = True
        if os.environ.get("KSTOP") == "c":
            P.disabled = True

        bar_t = {e: sb(f"bar_{e}", [1, 8]) for e in ["act", "dve", "pool"]}

        def RSQ(dst, src, scale, bias_ap, n, r=(), w=()):
            ACT(dst, src, AF.Ln, bias=bias_ap, scale=scale, r=list(r) + ["eps"], w=list(w))
            ACT(dst, dst, AF.Exp, scale=-0.5, r=list(w), w=list(w))

        def barrier():
            if getattr(P, "disabled", False):
                return
            P.seg += 1
            n = len(P.ops)
            pend = [op.idx for op in P.ops[P.last_barrier:n]]
            marks = []
            for e in ["act", "dve", "pool"]:
                t = bar_t[e]
                marks.append(P.add(e, (lambda tt, ee: (lambda en: en.memzero(tt[:]) if ee == 'act' else en.memset(tt[:], 0.0)))(t, e), (), [f"bar_{e}_{n}"]).idx)
            marks.append(P.add("pe", lambda e: e.matmul(psA[0][0:1, 0:1], cstb[0:1, 8, 0:1], cstb[0:1, 8, 0:1], start=True, stop=True),
                               ["cstb", pk(0)], [pk(0), f"bar_pe_{n}"]).idx)
            for e in ["pe", "act", "dve", "pool", "sp"]:
                f = P.add(e, None, (), ())
                f.xdeps = tuple(marks) + tuple(i for i in pend if P.ops[i].dma)
            P.last_barrier = len(P.ops)
            P.seg += 1

        def make_hT(st_key, x_aps, nrows, hT, gain, wk, xt, hb, ssq, rstd):
            nblk = len(x_aps)
            for bi, (xap, rows, col0) in enumerate(x_aps):
                s = bi % 2
                DMA("sp", xt[0:rows, s, :], xap, w=[("xt", s)])
                ACT(junk[0:rows, :], xt[0:rows, s, :], AF.Square, accum=ssq[0:rows, s:s + 1], r=[("xt", s)], w=["junk", ("ssq", s)])
                RSQ(rstd[0:rows, s:s + 1], ssq[0:rows, s:s + 1], 1.0 / D, eps_c[0:rows, 0:1], 1, r=[("ssq", s)], w=[("rstd", s)])
                TS("dve", hb[0:rows, 0, :], xt[0:rows, s, :], rstd[0:rows, s:s + 1], None, ALU.mult, r=[("xt", s), ("rstd", s)], w=["hbo"])
                for kc in range(8):
                    TR(psT[kc % 2][:, (kc // 2) * 128:(kc // 2) * 128 + rows], hb[0:rows, 0, kc * 128:(kc + 1) * 128], ident[0:rows, 0:rows],
                       r=["hbo", "cstb"], w=[pkT(kc % 2)])
                for kc in range(8):
                    eng_act = (kc % 2 == 0)
                    src = psT[kc % 2][:, (kc // 2) * 128:(kc // 2) * 128 + rows]
                    dst = hT[:, kc, col0:col0 + rows]
                    if eng_act:
                        ACT(dst, src, AF.Copy, scale=vc(gain, kc), r=[pkT(kc % 2), "vec"], w=[(wk, kc)])
                    else:
                        TS("dve", dst, src, vc(gain, kc), None, ALU.mult, r=[pkT(kc % 2), "vec"], w=[(wk, kc)])

        g2t = sb("g2t", [128, 512], BF16)
        PCs = sb("PCs", [128, 2, 64, 4], F32)
        with ExitStack() as s1:
            win = sb("win", [128, 8, N_SW * 128], BF16, s1)
            cstm = sb("cstm", [128, 4, 4, 128], BF16, s1)
            for _m in range(4):
                for _r in range(4):
                    DMA("pool", cstm[:, _m, _r, :], cst_d[:, 4 + _m, :], w=["cstm"])
            for kc in range(8):
                DMA("pool", win[:, kc, :], wsw[kc * 128:(kc + 1) * 128, :], w=["win"])
            w2aug = sb("w2aug", [65, 2, 512], BF16, s1)
            a2t = sb("a2t", [128, 2, 512], BF16, s1)
            rope = sb("rope", [128, 2, HALF + 256], BF16, s1)
            for d in range(2):
                DMA("pool", w2aug[:, d, :], w2aug_d[d], w=["w2aug"])
                DMA("pool", a2t[64:128, d, :], a2_d[d], w=["a2t"])
            DMA("pool", g2t[:], g2_d, w=["g2t"])
            DMA("pool", rope[:], rope_d[:, :, 0:HALF + 256], w=["rope"])
            if os.environ.get("KSTOP") == "w":
                P.disabled = True
            kT = sb("kT", [128, HALF + 128], BF16, s1)
            vaug = sb("vaug", [128, NB_OWN + 1, 2, 64], BF16, s1)
            vones = sb("vones", [128, 64], BF16, s1)
            attn_o = sb("attn_o", [128, 4, TT], BF16, s1)
            sgl = sb("sgl", [128, TT], BF16, s1)
            bv = sb("bv", [128, 4, TT], BF16, s1)
            MEMSET("pool", vones[:], 1.0, w=["vaug1"])
            xt = sb("xt", [128, 2, D], F32, s1); hb = sb("hb", [128, 1, D], BF16, s1)
            ssq = sb("ssq", [128, 2], F32, s1); rstd = sb("rstd", [128, 2], F32, s1)
            hT = sb("hT", [128, 8, TT + 2], BF16, s1)
            zr = sb("zr", [128, 2, TT + 2], F32, s1)
            zs = sb("zs", [128, 2, TT], F32, s1)
            ztmp = sb("ztmp", [128, 2, TT], F32, s1)
            zq = sb("zq", [128, TT], F32, s1)
            qsq = sb("qsq", [128, TT], BF16, s1)
            qrs = sb("qrs", [128, TT], F32, s1)
            qn = sb("qn", [128, TT], BF16, s1)
            qt1 = sb("qt1", [128, TT], F32, s1); qt2 = sb("qt2", [128, TT], F32, s1)
            qT = sb("qT", [128, 2, 4, TT], BF16, s1)
            vb = sb("vb", [128, TT], BF16, s1)
            twl = sb("twl", [65, TT], BF16, s1); alx = sb("alx", [128, TT], BF16, s1)
            MEMSET("pool", twl[64:65, :], 1.0, w=["twl1"])
            if os.environ.get("KSTOP") == "m":
                P.disabled = True
            sgfc = sb("sgfc", [128, NBT, 128], F32, s1)
            r_f = sb("r_f", [128, TT], F32, s1); k_f = sb("k_f", [128, TT], F32, s1); v_f = sb("v_f", [128, TT], F32, s1)
            nkk = sb("nkk", [128, TT], F32, s1)
            a_f = sb("a_f", [128, TT], F32, s1); b_f = sb("b_f", [128, TT], F32, s1); kd_f = sb("kd_f", [128, TT], F32, s1)
            t1_f = sb("t1_f", [128, TT], F32, s1)
            pb = sb("pb", [128, TT], BF16, s1)
            E_f = sb("E_f", [128, TT], F32, s1); G_f = sb("G_f", [128, TT], F32, s1)
            AR = sb("AR", [128, 2, 4, NBT, 256], BF16, s1)
            bt = sb("bt", [128, 2, 4, TT], BF16, s1); kt = sb("kt", [128, 2, 4, TT], BF16, s1)
            btT = sb("btT", [128, 2, NBT, 512], BF16, s1); ktT = sb("ktT", [128, 2, NBT, 512], BF16, s1)
            atT = sb("atT", [128, 2, NBT, 512], BF16, s1)
            vT = sb("vT", [128, NBT, 512], BF16, s1)
            Aab = sb("Aab", [128, 8, 128], BF16, s1); AabT = sb("AabT", [128, 8, 128], BF16, s1)
            Aak = sb("Aak", [128, 8, 128], BF16, s1); Arb = sb("Arb", [128, 8, 128], BF16, s1); Ark = sb("Ark", [128, 8, 128], BF16, s1)
            ApI = [[sb(f"Ap{j}_{i}", [128, 8, 128], BF16, s1) for i in range(2)] for j in range(2)]
            ATpI = [[sb(f"ATp{j}_{i}", [128, 8, 128], BF16, s1) for i in range(2)] for j in range(2)]
            ZpI = [[sb(f"Zp{j}_{i}", [128, 4, 2, 2, 64], BF16, s1) for i in range(2)] for j in range(2)]
            M0stI = [sb(f"M0st{j}", [128, 2, 4, 128], BF16, s1) for j in range(2)]
            N0stI = [sb(f"N0st{j}", [128, 2, 4, 64], BF16, s1) for j in range(2)]
            QstI = [sb(f"Qst{j}", [128, 4, 128], BF16, s1) for j in range(2)]
            Y0stI = [sb(f"Y0st{j}", [128, 4, 128], F32, s1) for j in range(2)]
            hlcount = [0]
            probs = sb("probs", [128, 2, 384], BF16, s1)
            den = sb("den", [128, 128], F32, s1)

            def proj_chunk(ci, bank, halo_bank=None):
                if halo_bank is None:
                    for kc in range(8):
                        MM(psA[bank][:, 0:TT], win[:, kc, ci * 128:(ci + 1) * 128], hT[:, kc, 1:TT + 1], start=(kc == 0), stop=(kc == 7),
                           r=["win", ("hT", kc)], w=[pk(bank)])
                else:
                    for kc in range(8):
                        MM(psA[bank][:, 0:TT + 2], win[:, kc, ci * 128:(ci + 1) * 128], hT[:, kc, 0:TT + 2], start=(kc == 0), stop=(kc == 7),
                           r=["win", ("hT", kc)], w=[pk(bank)])

            zcount = [0]

            def rwkv_z(ci, mui, out_ap, wkey, post=None):
                s = zcount[0] % 2
                zcount[0] += 1
                bank = s
                proj_chunk(ci, bank, 2)
                ACT(zr[:, s, :], psA[bank][:, 0:TT + 2], AF.Copy, r=[pk(bank)], w=[("zr", s)])
                ACT(zs[:, s, :], psA[bank][:, 1:TT + 1], AF.Copy, scale=vc("omm", mui), r=[pk(bank), "vec2"], w=[("zs", s)])
                TTo("pool", ztmp[:, s, :], zr[:, s, 0:TT], zr[:, s, 2:TT + 2], ALU.add, r=[("zr", s)], w=[("ztmp", s)])
                STT("dve", out_ap, ztmp[:, s, :], vc("hmu", mui), zs[:, s, :], ALU.mult, ALU.add, r=[("ztmp", s), ("zs", s), "vec2"], w=[wkey])

            def sweep_tile(t0, own, first_other, pcidx0):
                dirs = [0, 1] if own else [1]
                blk0 = t0 // 128
                xaps = [(xs[t0 + 128 * i: t0 + 128 * i + 128, :], 128, 128 * i) for i in range(NBT)]
                xaps.append((xs[t0 + TT: t0 + TT + 2, :], 2, TT))
                P.tag = f"t{t0}-hT"
                make_hT("hT", xaps, None, hT, "nmix", "hT", xt, hb, ssq, rstd)
                ck("h")
                P.tag = f"t{t0}-qkv"
                qs = (t0 // TT) % 2
                if own or first_other:
                    for ci in ([0, 1, 2, 3, 4] if own else [4]):
                        bank = ci % 2
                        proj_chunk(ci, bank)
                        gcol = "kg" if ci == 4 else "qg"
                        ACT(qsq[:], psA[bank][:, 0:TT], AF.Square, r=[pk(bank)], w=["qsq"])
                        MM(psA[3][:, 0:TT], onesblk, qsq[:], r=["cstb", "qsq"], w=[pk(3)])
                        RSQ(qrs[:], psA[3][:, 0:TT], 1.0 / 64, eps_c[:, 0:1], TT, r=[pk(3)], w=["qrs"])
                        STT("dve", qn[:], psA[bank][:, 0:TT], vc(gcol), qrs[:], ALU.mult, ALU.mult, r=[pk(bank), "qrs", "vec"], w=["qn"])
                        MM(psA[3][:, 0:TT], perm, qn[:], r=["cstb", "qn"], w=[pk(3)])
                        TTo("pool", qt1[:], qn[:], rope[:, 0, t0:t0 + TT], ALU.mult, r=["qn", "rope"], w=["qt1"])
                        TTo("dve", qt2[:], psA[3][:, 0:TT], rope[:, 1, t0:t0 + TT], ALU.mult, r=[pk(3), "rope"], w=["qt2"])
                        if ci < 4:
                            TTo("pool", qT[:, qs, ci, :], qt1[:], qt2[:], ALU.add, r=["qt1", "qt2"], w=[("qT", qs)])
                        else:
                            n = TT if own else 128
                            TTo("pool", kT[:, t0:t0 + n], qt1[:, 0:n], qt2[:, 0:n], ALU.add, r=["qt1", "qt2"], w=["kT"])
                    proj_chunk(5, 0)
                    ACT(vb[:], psA[0][:, 0:TT], AF.Copy, r=[pk(0)], w=["vb"])
                    nbv = NBT if own else 1
                    for bi in range(nbv):
                        TR(psT[0][:, bi * 128:(bi + 1) * 128], vb[:, bi * 128:(bi + 1) * 128], ident, r=["vb", "cstb"], w=[pkT(0)])
                    for bi in range(nbv):
                        CP("dve", vaug[:, blk0 + bi, :, :], psT[0][:, bi * 128:(bi + 1) * 128].rearrange("p (g d) -> p g d", g=2),
                           r=[pkT(0)], w=["vaug"])
                P.tag = f"t{t0}-wa"
                ck("q")
                rwkv_z(6, 0, zq[:], "zq")
                ACT(twl[0:64, :], zq[0:64, :], AF.Tanh, r=["zq"], w=["twl"])
                CP("pool", alx[64:128, :], zq[64:128, :], r=["zq"], w=["alx"])
                ck("z")
                if own:
                    rwkv_z(7, 1, zq[:], "zq")
                    ACT(zq[:], zq[:], AF.Tanh, scale=0.5, r=["zq"], w=["zq"])
                    TS("pool", sgl[:, :], zq[:], 0.5, 0.5, ALU.mult, ALU.add, r=["zq"], w=["sgl"])
                    DMA("sp", sgl_s[:, t0:t0 + TT], sgl[:, :], r=["sgl"], w=[("sgl_s", t0 // TT)])
                for fc in range(4):
                    P.tag = f"t{t0}-fc{fc}"
                    if own:
                        rwkv_z(8 + 3 * fc, 2 + 3 * fc, r_f[:], "r_f")
                    rwkv_z(9 + 3 * fc, 3 + 3 * fc, k_f[:], "k_f")
                    rwkv_z(10 + 3 * fc, 4 + 3 * fc, v_f[:], "v_f")
                    ACT(qsq[:], k_f[:], AF.Square, scale=vc("k_k", fc), r=["k_f", "vec"], w=["qsq"])
                    MM(psA[3][:, 0:TT], onesblk, qsq[:], r=["cstb", "qsq"], w=[pk(3)])
                    RSQ(qrs[:], psA[3][:, 0:TT], 1.0, eps_c[:, 1:2], TT, r=[pk(3)], w=["qrs"])
                    STT("dve", nkk[:], k_f[:], vc("nkk", fc), qrs[:], ALU.mult, ALU.mult, r=["k_f", "qrs", "vec2"], w=["nkk"])
                    ck("k")
                    CP("pool", vb[:], v_f[:], r=["v_f"], w=["vb"])
                    for bi in range(NBT):
                        TR(psT[0][:, bi * 128:(bi + 1) * 128], vb[:, bi * 128:(bi + 1) * 128], ident, r=["vb", "cstb"], w=[pkT(0)])
                    CP("dve", vT[:, :, fc * 128:(fc + 1) * 128], psT[0][:, 0:NBT * 128].rearrange("p (b c) -> p b c", b=NBT), r=[pkT(0)], w=[("vT", fc)])
                    for d in dirs:
                        dn = "AB"[d]
                        MM(psA[3][:, 0:TT], a2t[64:128, d, fc * 128:(fc + 1) * 128], alx[64:128, :], r=["a2t", "alx"], w=[pk(3)])
                        ACT(a_f[:], psA[3][:, 0:TT], AF.Tanh, bias=vc("ha0" + dn, fc), scale=0.5, r=[pk(3), "vec2"], w=["a_f"])
                        STT("dve", b_f[:], a_f[:], 1.0, nkk[:], ALU.add, ALU.mult, r=["nkk", "a_f"], w=["b_f"])
                        TS("dve", t1_f[:], a_f[:], vc("hka", fc), vc("c2ka", fc), ALU.mult, ALU.add, r=["a_f", "vec2"], w=["t1_f"])
                        TTo("pool", kd_f[:], k_f[:], t1_f[:], ALU.mult, r=["k_f", "t1_f"], w=["kd_f"])
                        if own:
                            STT("dve", pb[:], r_f[:], vc("rk" + dn, fc), kd_f[:], ALU.mult, ALU.mult, r=["r_f", "kd_f", "vec"], w=["pb"])
                            MM(psA[2][:, 0:TT], onesblk, pb[:], start=(d == 0), stop=(d == 1), r=["cstb", "pb"], w=[pk(2)])
                        for bi in range(NBT):
                            MM(psA[4][:, bi * 128:(bi + 1) * 128], twl[:, bi * 128:(bi + 1) * 128], w2aug[:, d, fc * 128:(fc + 1) * 128],
                               r=["twl", "twl1", "w2aug"], w=[pk(4)])
                        ACT(sgfc[:, :, :], psA[4][:, 0:NBT * 128].rearrange("p (b c) -> p b c", b=NBT), AF.Tanh, scale=0.5, r=[pk(4)], w=["sgfc"])
                        for bi in range(NBT):
                            MM(psA[3][:, bi * 128:(bi + 1) * 128], sgfc[:, bi, :], tri[:, d, :], start=True, stop=False, r=["sgfc", "tri"], w=[pk(3)])
                            MM(psA[3][:, bi * 128:(bi + 1) * 128], cstf[0:1, 2, :], crow[0:1, d, :], start=False, stop=True, r=["cstf", "crow"], w=[pk(3)])
                        ACT(E_f[:], psA[3][:, 0:TT], AF.Exp, r=[pk(3)], w=["E_f"])
                        ACT(G_f[:], psA[3][:, 0:TT], AF.Exp, scale=-1.0, r=[pk(3)], w=["G_f"])
                        Ev = E_f[:].rearrange("p (c t) -> p c t", t=64)
                        pcol = 63 if d == 0 else 0
                        CP("pool", PCs[:, d, pcidx0:pcidx0 + TT // 64, fc], Ev[:, :, pcol], r=["E_f"], w=["PCs"])
                        ARv = AR[:, d, fc, :, :]
                        if own:
                            TTo("dve", ARv[:, :, 128:256], r_f[:].rearrange("p (b t) -> p b t", t=128), E_f[:].rearrange("p (b t) -> p b t", t=128),
                                ALU.mult, r=["r_f", "E_f"], w=[("AR", d, fc)])
                        ARa = AR[:, d, fc, :, 0:128].rearrange("p b (c t) -> p b c t", t=64)
                        nk4 = nkk[:].rearrange("p (b c t) -> p b c t", c=2, t=64)
                        E4 = E_f[:].rearrange("p (b c t) -> p b c t", c=2, t=64)
                        for bi in range(NBT):
                            if d == 0:
                                TTo("pool", ARa[:, bi, :, 1:64], nk4[:, bi, :, 1:64], E4[:, bi, :, 0:63], ALU.mult, r=["nkk", "E_f"], w=[("AR", d, fc)])
                                CP("pool", ARa[:, bi, :, 0:1], nk4[:, bi, :, 0:1], r=["nkk"], w=[("AR", d, fc)])
                            else:
                                TTo("pool", ARa[:, bi, :, 0:63], nk4[:, bi, :, 0:63], E4[:, bi, :, 1:64], ALU.mult, r=["nkk", "E_f"], w=[("AR", d, fc)])
                                CP("pool", ARa[:, bi, :, 63:64], nk4[:, bi, :, 63:64], r=["nkk"], w=[("AR", d, fc)])
                        STT("dve", bt[:, d, fc, :], b_f[:], -0.5, G_f[:], ALU.mult, ALU.mult, r=["b_f", "G_f"], w=[("bt", d, fc)])
                        TTo("pool", kt[:, d, fc, :], kd_f[:], G_f[:], ALU.mult, r=["kd_f", "G_f"], w=[("kt", d, fc)])
                        for which, (srck, dst, dkey) in enumerate(((("bt", d, fc), btT, "btT"), (("kt", d, fc), ktT, "ktT"), (("AR", d, fc), atT, "atT"))):
                            pb_i = which % 2
                            for bi in range(NBT):
                                if which == 0:
                                    src = bt[:, d, fc, bi * 128:(bi + 1) * 128]
                                elif which == 1:
                                    src = kt[:, d, fc, bi * 128:(bi + 1) * 128]
                                else:
                                    src = AR[:, d, fc, bi, 0:128]
                                TR(psT[pb_i][:, bi * 128:(bi + 1) * 128], src, ident, r=[srck, "cstb"], w=[pkT(pb_i)])
                            pv = psT[pb_i][:, 0:NBT * 128].rearrange("p (b c) -> p b c", b=NBT)
                            if which != 1:
                                CP("dve", dst[:, d, :, fc * 128:(fc + 1) * 128], pv, r=[pkT(pb_i)], w=[(dkey, d, fc)])
                            else:
                                ACT(dst[:, d, :, fc * 128:(fc + 1) * 128], pv, AF.Copy, r=[pkT(pb_i)], w=[(dkey, d, fc)])
                    if own:
                        TTo("dve", bv[:, fc, :], psA[2][:, 0:TT], v_f[:], ALU.mult, r=[pk(2), "v_f"], w=["bv"])
                        DMA("sp", bv_s[:, fc, t0:t0 + TT], bv[:, fc, :], r=["bv"], w=[("bv_s", t0 // TT, fc)])
                ck("f")
                for d in dirs:
                    ms, mi = (4, 5) if d == 0 else (6, 7)
                    for bi in range(NBT):
                        P.tag = f"t{t0}-hl{d}{bi}"
                        ins_ = hlcount[0] % 2
                        hlcount[0] += 1
                        Ap, ATp, Zp = ApI[ins_], ATpI[ins_], ZpI[ins_]
                        B = (lambda p_: (lambda b_: (b_ + 3 * p_) % 6))(ins_)
                        M0st, N0st, Qst, Y0st = M0stI[ins_], N0stI[ins_], QstI[ins_], Y0stI[ins_]
                        gblk = (t0 // 128 + bi) if own else None
                        def amat(bank0, lh, rh, keys):
                            for h in range(8):
                                fc = h // 2; rows = slice((h % 2) * 64, (h % 2) * 64 + 64)
                                MM(psA[B(bank0 + h % 2)][:, (h // 2) * 128:(h // 2) * 128 + 128], lh(rows, fc), rh(rows, fc),
                                   r=[(k_, d, fc) for k_ in keys], w=[pk(B(bank0 + h % 2))])

                        def aevac(bank0, dst, dkey, mslot):
                            dv = dst[:].rearrange("p (i two) t -> p two i t", two=2)
                            for par in range(2):
                                TTo("dve", dv[:, par, :, :], psA[B(bank0 + par)][:, :].rearrange("p (i t) -> p i t", i=4), cstm[:, mslot - 4, :, :], ALU.mult,
                                    r=[pk(B(bank0 + par)), "cstm"], w=[dkey])
                        tokc = slice(bi * 128, (bi + 1) * 128)
                        amat(0, lambda rows, fc: bt[rows, d, fc, tokc], lambda rows, fc: AR[rows, d, fc, bi, 0:128], ["bt", "AR"])
                        amat(2, lambda rows, fc: kt[rows, d, fc, tokc], lambda rows, fc: AR[rows, d, fc, bi, 0:128], ["kt", "AR"])
                        amat(4, lambda rows, fc: AR[rows, d, fc, bi, 0:128], lambda rows, fc: bt[rows, d, fc, tokc], ["bt", "AR"])
                        aevac(0, Aab, "Aab", ms)
                        aevac(2, Aak, "Aak", ms)
                        aevac(4, AabT, "AabT", (6 if d == 0 else 4))
                        ck("a")
                        Z = Zp[0]
                        CP("pool", Z[:, :, 0, :, :], atT[:, d, bi, :].rearrange("p (f h j) -> p f h j", f=4, h=2), r=[("atT", d, f_) for f_ in range(4)], w=[("Z", ins_, 0, f_) for f_ in range(4)])
                        for h in range(8):
                            MM(psA[B(3)][:, h * 64:(h + 1) * 64], Aak[:, h, :], vT[:, bi, h * 64:(h + 1) * 64], r=["Aak", ("vT", h // 2)], w=[pk(B(3))])
                        CP("dve", Z[:, :, 1, :, :], psA[B(3)][:, 0:512].rearrange("p (f h j) -> p f h j", f=4, h=2), r=[pk(B(3))], w=[("Z", ins_, 0, f_) for f_ in range(4)])
                        curA, curAT, ak, atk = Aab, AabT, "Aab", "AabT"
                        zi = 0
                        for lev in range(6):
                            Zc, Zn = Zp[zi], Zp[1 - zi]
                            for hh in range(2):
                                bank = 4 + hh
                                for q4 in range(4):
                                    h = 4 * hh + q4; fc = h // 2; h2 = h % 2
                                    o = psA[B(bank)][:, q4 * 128:(q4 + 1) * 128].rearrange("p (a j) -> p a j", a=2)
                                    MM(o, curA[:, h, :], Zc[:, fc, :, h2, :], start=True, stop=False, r=[ak, ("Z", ins_, zi, fc)], w=[pk(B(bank))])
                                    MM(o, ident, Zc[:, fc, :, h2, :], start=False, stop=True, r=["cstb", ("Z", ins_, zi, fc)], w=[pk(B(bank))])
                                for f in range(2):
                                    src = psA[B(bank)][:, f * 256:(f + 1) * 256].rearrange("p (h a j) -> p a h j", h=2, a=2)
                                    if hh == 0:
                                        ACT(Zn[:, 2 * hh + f, :, :, :], src, AF.Copy, r=[pk(B(bank))], w=[("Z", ins_, 1 - zi, 2 * hh + f)])
                                    else:
                                        CP("dve", Zn[:, 2 * hh + f, :, :, :], src, r=[pk(B(bank))], w=[("Z", ins_, 1 - zi, 2 * hh + f)])
                            zi = 1 - zi
                            if lev < 5:
                                nA, nAT = Ap[lev % 2], ATp[lev % 2]
                                nak, natk = ("Ap", ins_, lev % 2), ("ATp", ins_, lev % 2)
                                for hh in range(2):
                                    for q4 in range(4):
                                        h = 4 * hh + q4
                                        MM(psA[B(0 + hh)][:, q4 * 128:(q4 + 1) * 128], curAT[:, h, :], curA[:, h, :], r=[ak, atk], w=[pk(B(0 + hh))])
                                        MM(psA[B(2 + hh)][:, q4 * 128:(q4 + 1) * 128], curA[:, h, :], curAT[:, h, :], r=[ak, atk], w=[pk(B(2 + hh))])
                                    ACT(nA[:, 4 * hh:4 * hh + 4, :], psA[B(0 + hh)][:, :].rearrange("p (h t) -> p h t", h=4), AF.Copy, r=[pk(B(0 + hh))], w=[nak])
                                    CP("dve", nAT[:, 4 * hh:4 * hh + 4, :], psA[B(2 + hh)][:, :].rearrange("p (h t) -> p h t", h=4), r=[pk(B(2 + hh))], w=[natk])
                                curA, curAT, ak, atk = nA, nAT, nak, natk
                        Zf = Zp[zi]; zkf = lambda f_: ("Z", ins_, zi, f_)
                        ck("d")
                        for c2 in range(2):
                            rs_ = slice(c2 * 64, c2 * 64 + 64)
                            for fc in range(4):
                                MM(psA[B(c2)][:, fc * 128:(fc + 1) * 128], Zf[rs_, fc, 0, :, :], btT[rs_, d, bi, fc * 128:(fc + 1) * 128], r=[zkf(fc), ("btT", d, fc)], w=[pk(B(c2))])
                            for fc in range(4):
                                TTo("dve", M0st[:, c2, fc, :], psA[B(c2)][:, fc * 128:(fc + 1) * 128], bdm, ALU.mult, r=[pk(B(c2)), "cstb"], w=[("M0st", ins_)])
                            for h in range(8):
                                fc = h // 2; h2 = h % 2
                                o = psA[B(2 + c2)][h2 * 64:h2 * 64 + 64, fc * 64:fc * 64 + 64]
                                MM(o, btT[rs_, d, bi, h * 64:(h + 1) * 64], Zf[rs_, fc, 1, h2, :], start=True, stop=False, r=[("btT", d, fc), zkf(fc)], w=[pk(B(2 + c2))])
                                MM(o, ktT[rs_, d, bi, h * 64:(h + 1) * 64], vT[rs_, bi, h * 64:(h + 1) * 64], start=False, stop=True, r=[("ktT", d, fc), ("vT", fc)], w=[pk(B(2 + c2))])
                        for c2 in range(2):
                            ACT(N0st[:, c2, :, :], psA[B(2 + c2)][:, 0:256].rearrange("p (f i) -> p f i", f=4), AF.Copy, r=[pk(B(2 + c2))], w=[("N0st", ins_)])
                        cidx = pcidx0 + 2 * bi
                        DMA("sp", m0_s[d, cidx:cidx + 2].rearrange("c p f j -> p c f j"), M0st[:], r=[("M0st", ins_)], w=[("m0_s", d, cidx)])
                        DMA("sp", n0_s[d, cidx:cidx + 2].rearrange("c p f i -> p c f i"), N0st[:], r=[("N0st", ins_)], w=[("n0_s", d, cidx)])
                        if own:
                            amat(0, lambda rows, fc: bt[rows, d, fc, tokc], lambda rows, fc: AR[rows, d, fc, bi, 128:256], ["bt", "AR"])
                            amat(4, lambda rows, fc: kt[rows, d, fc, tokc], lambda rows, fc: AR[rows, d, fc, bi, 128:256], ["kt", "AR"])
                            aevac(0, Arb, "Arb", mi)
                            aevac(4, Ark, "Ark", mi)
                            for fc in range(4):
                                o = psA[B(2)][:, fc * 128:(fc + 1) * 128]
                                MM(o, ident, AR[:, d, fc, bi, 128:256], start=True, stop=False, r=["cstb", ("AR", d, fc)], w=[pk(B(2))])
                                for h2 in range(2):
                                    h = 2 * fc + h2
                                    MM(psA[B(2)][h2 * 64:h2 * 64 + 64, fc * 128:(fc + 1) * 128], Zf[:, fc, 0, h2, :], Arb[:, h, :],
                                       start=False, stop=(h2 == 1), r=[zkf(fc), "Arb"], w=[pk(B(2))])
                                for h2 in range(2):
                                    h = 2 * fc + h2
                                    o2 = psA[B(3)][h2 * 64:h2 * 64 + 64, fc * 128:(fc + 1) * 128]
                                    MM(o2, Zf[:, fc, 1, h2, :], Arb[:, h, :], start=True, stop=False, r=[zkf(fc), "Arb"], w=[pk(B(3))])
                                    MM(o2, vT[:, bi, h * 64:(h + 1) * 64], Ark[:, h, :], start=False, stop=True, r=[("vT", fc), "Ark"], w=[pk(B(3))])
                            ACT(Qst[:], psA[B(2)][:, :].rearrange("p (f t) -> p f t", f=4), AF.Copy, r=[pk(B(2))], w=[("Qst", ins_)])
                            DMA("sp", q_s[d, gblk], Qst[:], r=[("Qst", ins_)], w=[("q_s", d, gblk)])
                            CP("dve", Y0st[:], psA[B(3)][:, :].rearrange("p (f t) -> p f t", f=4), r=[pk(B(3))], w=[("Y0st", ins_)])
                            DMA("sp", y0_s[d, :, :, t0 + bi * 128:t0 + bi * 128 + 128], Y0st[:], r=[("Y0st", ins_)], w=[("y0_s", d, gblk)])

            def attention_tile(ti):
                P.tag = f"attn{ti}"
                qs = ti % 2
                for qb in range(NBT):
                    n = NBT * ti + qb
                    kbs = [kb for kb in (n - 1, n, n + 1) if kb >= 0]
                    for m in range(4):
                        for h2 in range(2):
                            h = 2 * m + h2
                            c = h % 4; g = h // 4
                            rows = slice(g * 64, g * 64 + 64)
                            pi = h2
                            for kb in kbs:
                                slot = kb - (n - 1)
                                MM(psA[pi][:, slot * 128:(slot + 1) * 128], kT[rows, kb * 128:(kb + 1) * 128], qT[rows, qs, c, qb * 128:(qb + 1) * 128],
                                   r=["kT", ("qT", qs)], w=[pk(pi)])
                            lo = (kbs[0] - (n - 1)) * 128
                            ACT(probs[:, pi, lo:384], psA[pi][:, lo:384], AF.Exp, scale=0.125, r=[pk(pi)], w=[("probs", pi)])
                            if n - 1 >= 0:
                                TTo("pool", probs[:, pi, 0:128], probs[:, pi, 0:128], cstb[:, 10, :], ALU.mult, r=[("probs", pi), "cstb"], w=[("probs", pi)])
                            TTo("pool", probs[:, pi, 256:384], probs[:, pi, 256:384], cstb[:, 11, :], ALU.mult, r=[("probs", pi), "cstb"], w=[("probs", pi)])
                            orow = slice(h2 * 64, h2 * 64 + 64)
                            for i, kb in enumerate(kbs):
                                slot = kb - (n - 1)
                                MM(psA[2][orow, 0:128], vaug[:, kb, g, :], probs[:, pi, slot * 128:(slot + 1) * 128], start=(i == 0), stop=(i == len(kbs) - 1),
                                   r=["vaug", ("probs", pi)], w=[pk(2)])
                            for i, kb in enumerate(kbs):
                                slot = kb - (n - 1)
                                MM(psA[3][orow, 0:128], vones[:], probs[:, pi, slot * 128:(slot + 1) * 128], start=(i == 0), stop=(i == len(kbs) - 1),
                                   r=["vaug1", ("probs", pi)], w=[pk(3)])
                        TS("dve", den[:], psA[3][:, 0:128], vc("esink", m), None, ALU.add, r=[pk(3), "vec2"], w=["den"])
                        RCP(den[:], den[:], r=["den"], w=["den"])
                        TTo("dve", attn_o[:, m, qb * 128:(qb + 1) * 128], psA[2][:, 0:128], den[:], ALU.mult, r=[pk(2), "den"], w=["attn_o"])

            def attn_and_store(ti):
                attention_tile(ti)
                DMA("sp", ao_s[:, :, ti * TT:(ti + 1) * TT], attn_o[:], r=["attn_o"], w=[("ao_s", ti)])

            oth = list(range(2 * NTH - 1, NTH - 1, -1))
            if ntiles_other is not None:
                oth = oth[len(oth) - ntiles_other:] if ntiles_other > 0 else []
            for ti in oth:
                t0 = ti * TT
                sweep_tile(t0, own=False, first_other=(ti == NTH), pcidx0=t0 // 64)
            nown = NTH if ntiles_own is None else ntiles_own
            for ti in range(nown):
                sweep_tile(ti * TT, own=True, first_other=False, pcidx0=(ti * TT) // 64)
                if ti >= 1:
                    attn_and_store(ti - 1)
            if nown == NTH:
                attn_and_store(NTH - 1)
            if dbg is not None and upto == 1:
                dbg(locals(), P, DMA, dbg_out)
            if upto == 1:
                P.disabled = True
        barrier()

        with ExitStack() as s23:
            Yb = sb("Yb", [128, 4, HALF], F32, s23)
            with ExitStack() as s2:
                S32 = sb("S32", [128, 2, 4, 64], F32, s2)
                Sb = sb("Sb", [128, 2, 4, 64], BF16, s2)
                St = sb("St", [128, 2, 4, 64], F32, s2)
                M0l = sb("M0l", [128, 4, 4, 128], BF16, s2)
                N0l = sb("N0l", [128, 4, 4, 64], BF16, s2)
                Ql = sb("Ql", [128, 4, 4, 128], BF16, s2)
                ytmp = sb("ytmp", [128, 2, 4, 512], F32, s2)
                for fc in range(4):
                    DMA("sp", Yb[:, fc, :], y0_s[0, :, fc, :], r=[("y0_s", 0, g) for g in range(NB_OWN)], w=[("Yb", fc)])
                for j in range(4):
                    s = j % 2
                    DMA("sp", ytmp[:, s, :, :], y0_s[1, :, :, j * 512:(j + 1) * 512], r=[("y0_s", 1, g) for g in range(NB_OWN)], w=[("ytmp", s)])
                    TTo("pool", Yb[:, :, j * 512:(j + 1) * 512], Yb[:, :, j * 512:(j + 1) * 512], ytmp[:, s, :, :], ALU.add,
                        r=[("ytmp", s)] + [("Yb", fc) for fc in range(4)], w=["Yb"])
                MEMSET("dve", S32[:], 0.0, w=[("S32", 0), ("S32", 1)])
                MEMSET("pool", Sb[:], 0.0, w=[("Sb", 0), ("Sb", 1)])
                stepn = [0]

                def chain_step(d, cidx, own):
                    sl = stepn[0] % 4
                    stepn[0] += 1
                    DMA("sp", M0l[:, sl, :, :], m0_s[d, cidx], r=[("m0_s", d, cidx - cidx % 2)], w=[("M0l", sl)])
                    DMA("sp", N0l[:, sl, :, :], n0_s[d, cidx], r=[("n0_s", d, cidx - cidx % 2)], w=[("N0l", sl)])
                    bank = 4 + d
                    if own:
                        blk = cidx // 2; c2 = cidx % 2
                        qsl = (blk % 2) * 2 + d
                        if (d == 0 and c2 == 0) or (d == 1 and c2 == 1):
                            DMA("sp", Ql[:, qsl, :, :], q_s[d, blk], r=[("q_s", d, blk)], w=[("Ql", qsl)])
                        yb = 0 + d
                        for h in range(8):
                            fc = h // 2; h2 = h % 2; rows = slice(h2 * 64, h2 * 64 + 64)
                            MM(psA[yb][rows, fc * 64:(fc + 1) * 64], Sb[rows, d, fc, :], Ql[rows, qsl, fc, c2 * 64:c2 * 64 + 64],
                               r=[("Sb", d), ("Ql", qsl)], w=[pk(yb)])
                        yv = Yb[:, :, cidx * 64:(cidx + 1) * 64]
                        TTo("dve", yv, yv, psA[yb][:, 0:256].rearrange("p (f t) -> p f t", f=4), ALU.add, r=[pk(yb), "Yb"], w=["Yb"])
                    for fc in range(4):
                        o = psA[bank][:, fc * 64:(fc + 1) * 64]
                        MM(o, ident, N0l[:, sl, fc, :], start=True, stop=False, r=["cstb", ("N0l", sl)], w=[pk(bank)])
                        MM(o, M0l[:, sl, fc, :], Sb[:, d, fc, :], start=False, stop=True, r=[("M0l", sl), ("Sb", d)], w=[pk(bank)])
                    TTo("dve", St[:, d, :, :], psA[bank][:, 0:256].rearrange("p (f i) -> p f i", f=4), S32[:, d, :, :], ALU.add,
                        r=[pk(bank), ("S32", d)], w=[("St", d)])
                    for fc in range(4):
                        TS("dve", S32[:, d, fc, :], St[:, d, fc, :], PCs[:, d, cidx, fc:fc + 1], None, ALU.mult, r=[("St", d), "PCs"], w=[("S32", d)])
                    ACT(Sb[:, d, :, :], S32[:, d, :, :], AF.Copy, r=[("S32", d)], w=[("Sb", d)])

                for cidx in range(63, 31, -1):
                    chain_step(1, cidx, own=False)
                for i in range(32):
                    chain_step(0, i, own=True)
                    chain_step(1, 31 - i, own=True)
            if upto == 2:
                P.disabled = True
            barrier()

            with ExitStack() as s3:
                wg = sb("wg", [128, 8, 2048], BF16, s3)
                wua = sb("wua", [128, 4, D], BF16, s3); wur = sb("wur", [128, 4, D], BF16, s3); wout = sb("wout", [128, 8, D], BF16, s3)
                for kc in range(8):
                    DMA("pool", wg[:, kc, :], wgt[kc * 128:(kc + 1) * 128, :], w=["wg"])
                    DMA("pool", wout[:, kc, :], wout_d[kc * 128:(kc + 1) * 128, :], w=["wout"])
                for kc in range(4):
                    DMA("pool", wua[:, kc, :], wua_d[kc * 128:(kc + 1) * 128, :], w=["wua"])
                    DMA("pool", wur[:, kc, :], wur_d[kc * 128:(kc + 1) * 128, :], w=["wur"])
                xt = sb("xt3", [128, 2, D], F32, s3); hb = sb("hb3", [128, 1, D], BF16, s3)
                ssq = sb("ssq3", [128, 2], F32, s3); rstd = sb("rstd3", [128, 2], F32, s3)
                hT = sb("hT3", [128, 8, TT], BF16, s3)
                yc = sb("yc", [128, TT], F32, s3); ysq = sb("ysq", [128, TT], F32, s3); yr = sb("yr", [128, TT], F32, s3)
                rwT = sb("rwT", [128, 4, TT], BF16, s3)
                gates = sb("gates", [128, 16, TT], BF16, s3)
                mg = sb("mg", [128, 8, TT], BF16, s3); m1 = sb("m1", [128, TT], F32, s3); m2 = sb("m2", [128, TT], F32, s3)
                x1b = sb("x1b", [128, 2, D], F32, s3)
                xres = sb("xres", [128, 2, D], F32, s3)
                ao_l = sb("ao_l", [128, 4, TT], BF16, s3); bv_l = sb("bv_l", [128, 4, TT], BF16, s3); sgl_l = sb("sgl_l", [128, TT], BF16, s3)
                onesf = cstf[:, 1, :]
                for ti in range(NTH):
                    t0 = ti * TT
                    DMA("sp", ao_l[:], ao_s[:, :, t0:t0 + TT], r=[("ao_s", ti)], w=["ao_l"])
                    DMA("sp", bv_l[:], bv_s[:, :, t0:t0 + TT], r=[("bv_s", ti, fc) for fc in range(4)], w=["bv_l"])
                    DMA("sp", sgl_l[:], sgl_s[:, t0:t0 + TT], r=[("sgl_s", ti)], w=["sgl_l"])
                    for fc in range(4):
                        Yv = Yb[:, fc, t0:t0 + TT]
                        MM(psA[0][:, 0:TT], onesf, Yv, r=["cstf", "Yb"], w=[pk(0)])
                        TTo("dve", yc[:], Yv, psA[0][:, 0:TT], ALU.subtract, r=["Yb", pk(0)], w=["yc"])
                        ACT(ysq[:], yc[:], AF.Square, r=["yc"], w=["ysq"])
                        MM(psA[1][:, 0:TT], onesf, ysq[:], r=["cstf", "ysq"], w=[pk(1)])
                        RSQ(yr[:], psA[1][:, 0:TT], 1.0, eps_c[:, 2:3], TT, r=[pk(1)], w=["yr"])
                        TTo("pool", yc[:], yc[:], yr[:], ALU.mult, r=["yc", "yr"], w=["yc"])
                        TS("dve", yc[:], yc[:], vc("lnw", fc), vc("lnb", fc), ALU.mult, ALU.add, r=["yc", "vec"], w=["yc"])
                        TTo("pool", yc[:], yc[:], bv_l[:, fc, :], ALU.add, r=["yc", "bv_l"], w=["yc"])
                        MM(psA[2][:, 0:TT], g2t[:, fc * 128:(fc + 1) * 128], sgl_l[:, :], r=["g2t", "sgl_l"], w=[pk(2)])
                        TTo("dve", rwT[:, fc, :], yc[:], psA[2][:, 0:TT], ALU.mult, r=["yc", pk(2)], w=["rwT"])
                    xaps = [(xs[1 + t0 + 128 * i: 1 + t0 + 128 * i + 128, :], 128, 128 * i) for i in range(NBT)]
                    make_hT("hT3", xaps, None, hT, "nmix", "hT3", xt, hb, ssq, rstd)
                    for gc in range(16):
                        bank = gc % 2
                        for kc in range(8):
                            MM(psA[bank][:, 0:TT], wg[:, kc, gc * 128:(gc + 1) * 128], hT[:, kc, :], start=(kc == 0), stop=(kc == 7), r=["wg", ("hT3", kc)], w=[pk(bank)])
                        ACT(gates[:, gc, :], psA[bank][:, 0:TT], AF.Tanh, scale=0.5, r=[pk(bank)], w=["gates"])
                    for oc in range(8):
                        for kc in range(4):
                            MM(psA[2][:, 0:TT], wua[:, kc, oc * 128:(oc + 1) * 128], ao_l[:, kc, :], start=(kc == 0), stop=(kc == 3), r=["wua", "ao_l"], w=[pk(2)])
                        for kc in range(4):
                            MM(psA[3][:, 0:TT], wur[:, kc, oc * 128:(oc + 1) * 128], rwT[:, kc, :], start=(kc == 0), stop=(kc == 3), r=["wur", "rwT"], w=[pk(3)])
                        STT("dve", m1[:], gates[:, oc, :], 1.0, psA[2][:, 0:TT], ALU.add, ALU.mult, r=[pk(2), "gates"], w=["m1"])
                        STT("dve", m2[:], gates[:, 8 + oc, :], 1.0, psA[3][:, 0:TT], ALU.add, ALU.mult, r=[pk(3), "gates"], w=["m2"])
                        TTo("pool", mg[:, oc, :], m1[:], m2[:], ALU.add, r=["m1", "m2"], w=["mg"])
                    for bi in range(NBT):
                        s = bi % 2
                        tok = t0 + bi * 128
                        DMA("sp", xres[:, s, :], xs[1 + tok:1 + tok + 128, :], w=[("xres", s)])
                        for hf in range(2):
                            bank = 4 + hf
                            for kc in range(8):
                                MM(psA[bank][:, :], mg[:, kc, bi * 128:(bi + 1) * 128], wout[:, kc, hf * 512:(hf + 1) * 512], start=(kc == 0), stop=(kc == 7), r=["mg", "wout"], w=[pk(bank)])
                            STT("dve", x1b[:, s, hf * 512:(hf + 1) * 512], psA[bank][:, :], 0.5, xres[:, s, hf * 512:(hf + 1) * 512], ALU.mult, ALU.add, r=[pk(bank), ("xres", s)], w=[("x1b", s)])
                        DMA("sp", x1_s[tok:tok + 128, :], x1b[:, s, :], r=[("x1b", s)], w=[("x1_s", tok // 128)])
        if upto == 3:
            P.disabled = True
        barrier()

        with ExitStack() as s4:
            wf1 = sb("wf1", [128, 8, 4096], BF16, s4); wf2 = sb("wf2", [128, 32, D], BF16, s4)
            wpg = sb("wpg", [128, 8, D], BF16, s4); wple = sb("wple", [128, 2, D], BF16, s4)
            for cg in range(4):
                for kc in range(8):
                    DMA("pool", wf1[:, kc, cg * 1024:(cg + 1) * 1024], wff1_d[kc * 128:(kc + 1) * 128, cg * 1024:(cg + 1) * 1024], w=[("wf1", cg)])
            for kc in range(32):
                DMA("pool", wf2[:, kc, :], wff2_d[kc * 128:(kc + 1) * 128, :], w=[("wf2", kc // 8)])
            for kc in range(8):
                DMA("pool", wpg[:, kc, :], wpg_d[kc * 128:(kc + 1) * 128, :], w=["wpg"])
            for kc in range(2):
                DMA("pool", wple[:, kc, :], wple_d[kc * 128:(kc + 1) * 128, :], w=["wple"])
            xt = sb("xt4", [128, 2, D], F32, s4); hb = sb("hb4", [128, 1, D], BF16, s4)
            ssq = sb("ssq4", [128, 4], F32, s4); rstd = sb("rstd4", [128, 4], F32, s4)
            hTf = sb("hT4", [128, 2, 8, 128], BF16, s4)
            hTp = sb("hT4p", [128, 1, 8, 128], BF16, s4)
            act = sb("act", [128, 2, 32, 128], BF16, s4)
            rl = sb("rl", [128, 4, 128], F32, s4)
            x2 = sb("x2", [128, 1, D], F32, s4)
            pt = sb("pt", [128, 2, 256], F32, s4); pb16 = sb("pb16", [128, 2, 256], BF16, s4); pT = sb("pT", [128, 2, 2, 128], BF16, s4)
            sgp = sb("sgp", [128, 1, 512], F32, s4)
            nhb = [0]

            def norm_T(src, gain, skey, hTt, okey, idx):
                hs = 0
                nhb[0] += 1
                ACT(junk[:], src, AF.Square, accum=ssq[:, idx:idx + 1], r=[skey], w=["junk", ("ssq4", idx)])
                RSQ(rstd[:, idx:idx + 1], ssq[:, idx:idx + 1], 1.0 / D, eps_c[:, 0:1], 1, r=[("ssq4", idx)], w=[("rstd4", idx)])
                TS("dve", hb[:, hs, :], src, rstd[:, idx:idx + 1], None, ALU.mult, r=[skey, ("rstd4", idx)], w=[("hb4o", hs)])
                for kc in range(8):
                    TR(psT[kc % 2][:, (kc // 2) * 128:(kc // 2) * 128 + 128], hb[:, hs, kc * 128:(kc + 1) * 128], ident, r=[("hb4o", hs), "cstb"], w=[pkT(kc % 2)])
                for kc in range(8):
                    srcp = psT[kc % 2][:, (kc // 2) * 128:(kc // 2) * 128 + 128]
                    if kc % 2 == 0:
                        ACT(hTt[:, kc, :], srcp, AF.Copy, scale=vc(gain, kc), r=[pkT(kc % 2), "vec"], w=[(okey, kc)])
                    else:
                        TS("dve", hTt[:, kc, :], srcp, vc(gain, kc), None, ALU.mult, r=[pkT(kc % 2), "vec"], w=[(okey, kc)])

            for blk in range(NB_OWN):
                tok = blk * 128
                s = blk % 2
                DMA("sp", xt[:, s, :], x1_s[tok:tok + 128, :], r=[("x1_s", blk)], w=[("xt4", s)])
                DMA("sp", pt[:, s, :], pp[tok:tok + 128, :], w=[("pt", s)])
                hk = ("hT4", s)
                norm_T(xt[:, s, :], "nffn", ("xt4", s), hTf[:, s], hk, s)
                for hc in range(32):
                    bank = hc % 4
                    for kc in range(8):
                        MM(psA[bank][:, 0:128], wf1[:, kc, hc * 128:(hc + 1) * 128], hTf[:, s, kc, :], start=(kc == 0), stop=(kc == 7),
                           r=[("wf1", hc // 8), (hk, kc)], w=[pk(bank)])
                    ACT(rl[:, bank, :], psA[bank][:, 0:128], AF.Relu, r=[pk(bank)], w=[("rl", bank)])
                    TTo("dve", act[:, s, hc, :], rl[:, bank, :], rl[:, bank, :], ALU.mult, r=[("rl", bank)], w=[("act", s)])
                for hf in range(2):
                    bank = 4 + hf
                    for hc in range(32):
                        MM(psA[bank][:, :], act[:, s, hc, :], wf2[:, hc, hf * 512:(hf + 1) * 512], start=(hc == 0), stop=(hc == 31),
                           r=[("act", s), ("wf2", hc // 8)], w=[pk(bank)])
                    TTo("dve", x2[:, 0, hf * 512:(hf + 1) * 512], psA[bank][:, :], xt[:, s, hf * 512:(hf + 1) * 512], ALU.add, r=[pk(bank), ("xt4", s)], w=["x2"])
                hkp = "hT4p"
                norm_T(x2[:, 0, :], "nple", "x2", hTp[:, 0], hkp, 2 + s)
                CP("pool", pb16[:, s, :], pt[:, s, :], r=[("pt", s)], w=[("pb16", s)])
                for kc in range(2):
                    TR(psT[0][:, 512 + kc * 128:512 + (kc + 1) * 128], pb16[:, s, kc * 128:(kc + 1) * 128], ident, r=[("pb16", s), "cstb"], w=[pkT(0)])
                CP("dve", pT[:, s, :, :], psT[0][:, 512:768].rearrange("p (k t) -> p k t", k=2), r=[pkT(0)], w=[("pT", s)])
                for hf in range(2):
                    gb = 0 + hf
                    pbk = 2 + hf
                    for kc in range(8):
                        MM(psA[gb][:, :], hTp[:, 0, kc, :], wpg[:, kc, hf * 512:(hf + 1) * 512], start=(kc == 0), stop=(kc == 7), r=[(hkp, kc), "wpg"], w=[pk(gb)])
                    for kc in range(2):
                        MM(psA[pbk][:, :], pT[:, s, kc, :], wple[:, kc, hf * 512:(hf + 1) * 512], start=(kc == 0), stop=(kc == 1), r=[("pT", s), "wple"], w=[pk(pbk)])
                    ACT(sgp[:, 0, :], psA[gb][:, :], AF.Tanh, scale=0.5, r=[pk(gb)], w=["sgp"])
                    STT("dve", sgp[:, 0, :], sgp[:, 0, :], 1.0, psA[pbk][:, :], ALU.add, ALU.mult, r=[pk(pbk), "sgp"], w=["sgp"])
                    STT("dve", xt[:, s, hf * 512:(hf + 1) * 512], sgp[:, 0, :], 0.5, x2[:, 0, hf * 512:(hf + 1) * 512], ALU.mult, ALU.add,
                        r=["sgp", "x2", ("xt4", s)], w=[("ob", s)])
                DMA("sp", out_d[tok:tok + 128, :], xt[:, s, :], r=[("ob", s), ("xt4", s)], w=[("out", blk), ("xt4", s)])
            P.add("sp", None, [("out", i) for i in range(NB_OWN)], ())
        P.emit(nc, gs)
    return nc


def _consts():
    cst = np.zeros((128, 12, 128), np.float32)
    i = np.arange(128)
    cst[:, 0, :] = np.eye(128)
    cst[:, 1, :] = (i[:, None] // 64 == i[None, :] // 64)
    perm = np.zeros((128, 128), np.float32)
    for m in range(128):
        j = m % 64
        if j < 8:
            perm[m + 8, m] = 1.0
        elif j < 16:
            perm[m - 8, m] = 1.0
    cst[:, 2, :] = perm
    cst[:, 3, :] = cst[:, 1, :]
    same = (i[:, None] // 64 == i[None, :] // 64)
    cst[:, 4, :] = same & (i[:, None] < i[None, :])
    cst[:, 5, :] = same & (i[:, None] <= i[None, :])
    cst[:, 6, :] = same & (i[:, None] > i[None, :])
    cst[:, 7, :] = same & (i[:, None] >= i[None, :])
    cst[:, 8, :] = 1.0
    cst[:, 9, :] = cst[:, 1, :] / 64.0
    cst[:, 10, :] = (i[:, None] >= i[None, :])
    cst[:, 11, :] = (i[:, None] <= i[None, :])
    tri = np.zeros((128, 2, 128), np.float32)
    tri[:, 0, :] = 0.5 * LWS * cst[:, 5, :]
    tri[:, 1, :] = 0.5 * LWS * cst[:, 7, :]
    return cst, tri


def _rope_tables(pos):
    half = 8
    inv = np.power(np.float32(500000.0), -np.arange(half, dtype=np.float32) * 2.0 / 16.0).astype(np.float32)
    ang = pos.astype(np.float32)[None, :] * inv[:, None]
    cos = np.cos(ang).astype(np.float32); sin = np.sin(ang).astype(np.float32)
    n = pos.shape[0]
    tab = np.zeros((128, 2, n), np.float32)
    for hb in range(2):
        b = hb * 64
        tab[b:b + 64, 0, :] = 1.0
        tab[b:b + 8, 0, :] = cos; tab[b + 8:b + 16, 0, :] = cos
        tab[b:b + 8, 1, :] = -sin; tab[b + 8:b + 16, 1, :] = sin
    return tab


def _col128(v):
    return np.ascontiguousarray(np.asarray(v, np.float32).reshape(-1, 128).T)


_NC_CACHE = {}


def kernel(x, p, norm_mix, w_in, shift_mu, q_norm, k_norm, sink, w0, w2, a0, a2, g2, k_k, k_a, r_k,
           lnx_w, lnx_b, w_up_attn, w_up_rwkv, w_out, norm_ffn, w_ff1, w_ff2, norm_ple, w_ple_gate, w_ple):
    f = lambda a: np.asarray(a, np.float32)
    x = f(x); p = f(p)[0]; w_in = f(w_in)[0]
    qcols = []
    for c in range(4):
        qcols += list(range(c * 64, c * 64 + 64)) + list(range((4 + c) * 64, (4 + c) * 64 + 64))
    O_K, O_V, O_R = 512, 640, 768
    cols = qcols + list(range(O_K, O_K + 128)) + list(range(O_V, O_V + 128))
    rw = O_R
    wa = list(range(rw + 1536, rw + 1536 + 128)); gl = list(range(rw + 1664, rw + 1792))
    cols += wa + gl
    mu_idx = [12, 13]
    for fc in range(4):
        for part in range(3):
            cols += list(range(rw + part * 512 + fc * 128, rw + part * 512 + fc * 128 + 128))
            mu_idx.append(part * 4 + fc)
    wsw = np.ascontiguousarray(w_in[:, cols])
    wgt = np.ascontiguousarray(w_in[:, O_R + 1792:])
    mu_c = _col128(f(shift_mu)[0])[:, mu_idx]
    cst, tri = _consts()
    cst_in = cst.copy()
    cst_in_f = cst.copy()
    cst_in[:, 1, :] = cst[:, 1, :]
    sink_ = f(sink)[0]
    sinkcols = np.zeros((128, 4), np.float32)
    for m in range(4):
        sinkcols[0:64, m] = sink_[2 * m]; sinkcols[64:128, m] = sink_[2 * m + 1]
    in_maps = []
    for c in range(8):
        b, hf = c // 2, c % 2
        xb = x[b]; pb_ = p[b]
        if hf == 0:
            xl = xb; pl = pb_[:HALF]; pos = np.arange(0, HALF + 512); dA, dB = 0, 1
        else:
            xl = xb[::-1]; pl = pb_[HALF:][::-1]; pos = SEQ - 1 - np.arange(0, HALF + 512); dA, dB = 1, 0
        xs = np.zeros((SEQ + 2, D), np.float32); xs[1:SEQ + 1] = xl
        dd = [dA, dB]
        vec = np.zeros((128, NVEC_IN), np.float32)
        def put(n, arr):
            vec[:, VC[n]:VC[n] + arr.shape[1]] = arr
        put("mu", mu_c); put("k_k", _col128(f(k_k)[0])); put("k_a", _col128(f(k_a)[0]))
        put("rkA", _col128(f(r_k)[0, dA].reshape(-1))); put("rkB", _col128(f(r_k)[0, dB].reshape(-1)))
        put("lnw", _col128(f(lnx_w)[0])); put("lnb", _col128(f(lnx_b)[0]))
        put("a0A", _col128(f(a0)[0, dA])); put("a0B", _col128(f(a0)[0, dB]))
        put("qg", np.tile(f(q_norm)[0], 2)[:, None]); put("kg", np.tile(f(k_norm)[0], 2)[:, None])
        put("nmix", _col128(f(norm_mix)[0])); put("nffn", _col128(f(norm_ffn)[0])); put("nple", _col128(f(norm_ple)[0]))
        put("sink", sinkcols)
        w2aug = np.stack([np.concatenate([f(w2)[0, d_], f(w0)[0, d_][None, :]], 0) for d_ in dd])
        a2s = np.stack([f(a2)[0, d_] for d_ in dd])
        cst_c = cst.copy()
        m = {"xs": xs, "pp": np.ascontiguousarray(pl), "wsw": wsw, "wgt": wgt, "vec": vec, "w2aug": np.ascontiguousarray(w2aug),
             "a2": np.ascontiguousarray(a2s), "g2": f(g2)[0], "wua": f(w_up_attn)[0], "wur": f(w_up_rwkv)[0], "wout": f(w_out)[0],
             "wff1": f(w_ff1)[0], "wff2": f(w_ff2)[0], "wpg": f(w_ple_gate)[0], "wple": f(w_ple)[0],
             "rope": _rope_tables(pos), "cst": cst_c, "tri": tri, "crow": np.ascontiguousarray(tri.sum(0)[None])}
        in_maps.append(m)
    if _NC_CACHE.get("prep_only"):
        return in_maps
    if "nc" not in _NC_CACHE:
        _NC_CACHE["nc"] = build_program()
    res = run_bass_kernel_spmd(_NC_CACHE["nc"], in_maps, core_ids=list(range(8)))
    out = np.zeros((4, SEQ, D), np.float32)
    for c in range(8):
        b, hf = c // 2, c % 2
        o = res.results[c]["out"]
        if hf == 0:
            out[b, :HALF] = o
        else:
            out[b, HALF:] = o[::-1]
    return out
```

```python
import os
import numpy as np
from contextlib import ExitStack
import concourse.bass as bass
import concourse.mybir as mybir
from concourse.bass_utils import run_bass_kernel_spmd

F32 = mybir.dt.float32
BF16 = mybir.dt.bfloat16
ALU = mybir.AluOpType
AF = mybir.ActivationFunctionType

SEM_EPOCH = 3000
DMA_SLOTS = 8

D = 1024; SEQ = 4096; HALF = 2048; NB_OWN = 16
TT = 256; NBT = 2; NTH = 8
LWS = -0.6065306597126334


class Op:
    __slots__ = ("eng", "fn", "reads", "writes", "dma", "deps", "flag", "cnt", "slot", "slotval", "idx", "xdeps",
                 "dur", "rg", "tbl", "seg", "alld", "pos")

    def __init__(self, eng, fn, reads, writes, dma):
        self.eng = eng; self.fn = fn; self.reads = reads; self.writes = writes; self.dma = dma
        self.deps = set(); self.flag = False; self.cnt = None; self.slot = None; self.slotval = None
        self.xdeps = (); self.dur = 0.2; self.rg = None; self.tbl = None; self.seg = 0


SYNC_LAT = 0.35


class Prog:
    def __init__(self):
        self.ops = []
        self.tags = []
        self.last_barrier = 0
        self.seg = 0
        self.reorder = True

    def add(self, eng, fn, r=(), w=(), dma=False, dur=0.2, rg=None, tbl=None):
        op = Op(eng, fn, tuple(r), tuple(w), dma)
        op.dur = dur; op.rg = rg; op.tbl = tbl; op.seg = self.seg
        self.tags.append(getattr(self, "tag", None)) if not getattr(self, "disabled", False) else None
        if getattr(self, "disabled", False):
            op.idx = -1
            return op
        op.idx = len(self.ops)
        self.ops.append(op)
        return op

    def all_deps(self):
        last_w = {}
        readers = {}
        ops = self.ops
        for op in ops:
            d = {}
            for j in op.xdeps:
                d[j] = "raw"
            for k in op.reads:
                j = last_w.get(k)
                if j is not None:
                    d[j] = "raw"
            for k in op.writes:
                j = last_w.get(k)
                if j is not None and j not in d:
                    d[j] = "waw"
                for j in readers.get(k, ()):
                    if j != op.idx and j not in d:
                        d[j] = "war"
            for k in op.reads:
                if isinstance(k, tuple) and k[0] in ("ps", "psT"):
                    for j in readers.get(k, ()):
                        if ops[j].eng != op.eng and j not in d:
                            d[j] = "raw"
            for k in op.reads:
                readers.setdefault(k, []).append(op.idx)
            for k in op.writes:
                last_w[k] = op.idx
                readers[k] = []
            op.alld = d

    def schedule(self):
        ops = self.ops
        n = len(ops)
        if not self.reorder:
            return list(range(n))
        fin = [0.0] * n
        self.fin = fin
        self.bind = {}; self._last_on = {}; self.start = {}
        succ = [[] for _ in range(n)]
        ndep = [0] * n
        for op in ops:
            for j in op.alld:
                succ[j].append(op.idx)
            ndep[op.idx] = len(op.alld)
        order = []
        free_t = {}
        cur_tbl = [None]
        segs = {}
        for op in ops:
            segs.setdefault(op.seg, []).append(op.idx)
        done = [False] * n
        ksegs = os.environ.get("KREORD")
        for sg in sorted(segs):
            members = segs[sg]
            if ksegs is not None and str(sg) not in ksegs.split(","):
                for i in members:
                    t0_ = max([free_t.get(ops[i].eng, 0.0)] + [fin[j] for j in ops[i].alld])
                    fin[i] = t0_ + ops[i].dur
                    free_t[ops[i].eng] = fin[i]
                    done[i] = True
                    order.append(i)
                continue
            inseg = set(members)
            ready = {}
            rt = {}
            remaining = len(members)
            cnt_un = {}
            for i in members:
                c = sum(1 for j in ops[i].alld if not done[j])
                cnt_un[i] = c
                if c == 0:
                    ready.setdefault(ops[i].eng, []).append(i)
            def ready_time(i):
                op = ops[i]
                t = 0.0
                for j, kind in op.alld.items():
                    p = ops[j]
                    lat = SYNC_LAT if (p.dma or p.eng != op.eng or kind == "raw") else 0.0
                    if p.eng == op.eng and op.eng == "pe":
                        lat = 0.0
                    t = max(t, fin[j] + lat)
                return t
            for e in ready:
                for i in ready[e]:
                    rt[i] = ready_time(i)
            while remaining:
                best = None
                for e, lst in ready.items():
                    if not lst:
                        continue
                    ft = free_t.get(e, 0.0)
                    bi = None
                    for i in lst:
                        st = max(ft, rt[i])
                        if e == "act" and ops[i].tbl is not None and cur_tbl[0] is not None and ops[i].tbl != cur_tbl[0]:
                            st += 1.3
                        key = (st, i)
                        if bi is None or key < bi:
                            bi = key
                    if best is None or bi < best[0]:
                        best = (bi, e)
                (st, i), e = best
                op = ops[i]
                ready[e].remove(i)
                if e == "act" and op.tbl is not None:
                    cur_tbl[0] = op.tbl
                fin[i] = st + op.dur
                if getattr(self, "trace_bind", False):
                    bj = None; bt_ = -1.0
                    for j in op.alld:
                        if fin[j] > bt_:
                            bt_ = fin[j]; bj = j
                    prev_e = self._last_on.get(e)
                    if prev_e is not None and free_t.get(e, 0.0) >= rt[i] - 1e-9:
                        self.bind[i] = ("eng", prev_e)
                    else:
                        self.bind[i] = ("dep", bj)
                    self._last_on[e] = i
                    self.start[i] = st
                if op.dma:
                    free_t[e] = st + 0.1
                else:
                    free_t[e] = fin[i]
                done[i] = True
                order.append(i)
                remaining -= 1
                for k in succ[i]:
                    if k in inseg:
                        cnt_un[k] -= 1
                        if cnt_un[k] == 0:
                            rt[k] = ready_time(k)
                            ready.setdefault(ops[k].eng, []).append(k)
        self.est_total = max(fin) if fin else 0.0
        self.seg_end = {sg: max(fin[i] for i in segs[sg]) for sg in segs}
        return order

    def analyse(self):
        self.all_deps()
        order = self.schedule()
        ops = self.ops
        for pos, i in enumerate(order):
            ops[i].pos = pos
        self.order = order
        pe = [i for i in order if ops[i].eng == "pe"]
        extra = {}
        for a in range(len(pe)):
            oa = ops[pe[a]]
            if oa.rg is None:
                continue
            for b in range(a + 1, a + 3):
                if b >= len(pe):
                    break
                ob = ops[pe[b]]
                if ob.rg is None:
                    continue
                if (oa.rg[1] < ob.rg[0] or ob.rg[1] < oa.rg[0]) and set(oa.writes) & set(ob.writes):
                    extra.setdefault(pe[b], set()).add(pe[a])
        for op in ops:
            need = set()
            for j, kind in op.alld.items():
                p = ops[j]
                if p.dma:
                    need.add(j)
                elif p.eng == op.eng and not op.dma:
                    if kind == "raw" and op.eng != "pe":
                        need.add(j); p.flag = True
                else:
                    need.add(j); p.flag = True
            for j in extra.get(op.idx, ()):
                need.add(j); ops[j].flag = True
            op.deps = need
        cnt = {}
        nd = {}
        for i in order:
            op = ops[i]
            if op.dma:
                k = nd.get(op.eng, 0)
                nd[op.eng] = k + 1
                op.slot = (op.eng, k % DMA_SLOTS)
                op.slotval = 16 * (k // DMA_SLOTS + 1)
            elif op.fn is not None and op.flag:
                cnt[op.eng] = cnt.get(op.eng, 0) + 1
                op.cnt = cnt[op.eng]
        self.nflag = cnt
        self.ndma = nd

    def emit(self, nc, stack):
        self.analyse()
        engs = ["pe", "act", "dve", "pool", "sp"]
        sems = {}
        for e in engs:
            n = self.nflag.get(e, 0)
            for ep in range(n // SEM_EPOCH + 1):
                sems[(e, ep)] = stack.enter_context(nc.semaphore(f"s_{e}_{ep}"))
        dsem = {}
        for e, n in self.ndma.items():
            for s in range(min(n, DMA_SLOTS)):
                dsem[(e, s)] = stack.enter_context(nc.semaphore(f"d_{e}_{s}"))
        block = stack.enter_context(nc.Block())
        ops = self.ops
        per_eng = {e: [ops[i] for i in self.order if ops[i].eng == e] for e in engs}

        def sem_of(p):
            if p.dma:
                return dsem[p.slot], p.slotval
            c = p.cnt - 1
            return sems[(p.eng, c // SEM_EPOCH)], (c % SEM_EPOCH) + 1

        def run(engobj, lst):
            seen = {}
            dma_hist = {}
            for op in lst:
                waits = {}
                for j in op.deps:
                    s, v = sem_of(ops[j])
                    key = id(s)
                    if seen.get(key, 0) >= v:
                        continue
                    if key not in waits or waits[key][1] < v:
                        waits[key] = (s, v)
                if op.dma:
                    prev = dma_hist.get(op.slot)
                    if prev is not None:
                        s, v = dsem[op.slot], prev
                        key = id(s)
                        if seen.get(key, 0) < v and (key not in waits or waits[key][1] < v):
                            waits[key] = (s, v)
                    dma_hist[op.slot] = op.slotval
                for key, (s, v) in waits.items():
                    engobj.wait_ge(s, v)
                    seen[key] = v
                if op.fn is None:
                    continue
                ins = op.fn(engobj)
                if op.dma:
                    ins.then_inc(dsem[op.slot], 16)
                elif op.flag:
                    s, v = sem_of(op)
                    ins.then_inc(s, 1)

        if per_eng["pe"]:
            @block.tensor
            def _(e):
                run(e, per_eng["pe"])
        if per_eng["act"]:
            @block.scalar
            def _(e):
                run(e, per_eng["act"])
        if per_eng["dve"]:
            @block.vector
            def _(e):
                run(e, per_eng["dve"])
        if per_eng["pool"]:
            @block.gpsimd
            def _(e):
                run(e, per_eng["pool"])
        if per_eng["sp"]:
            @block.sync
            def _(e):
                run(e, per_eng["sp"])


VC = {}
_o = 0
for _n, _c in [("mu", 14), ("k_k", 4), ("k_a", 4), ("rkA", 4), ("rkB", 4), ("lnw", 4), ("lnb", 4),
               ("a0A", 4), ("a0B", 4), ("qg", 1), ("kg", 1), ("nmix", 8), ("nffn", 8), ("nple", 8),
               ("sink", 4), ("omm", 14), ("hmu", 14), ("nkk", 4), ("omka", 4), ("esink", 4), ("ha0A", 4), ("ha0B", 4), ("hka", 4), ("c2ka", 4)]:
    VC[_n] = _o
    _o += _c
NVEC_IN = VC["omm"]
NVEC = _o

N_SW = 20


def build_program(dbg=None, upto=99, ntiles_other=None, ntiles_own=None):
    nc = bass.Bass("TRN2", target_bir_lowering=False)
    P = Prog()

    def din(name, shape):
        return nc.dram_tensor(name, list(shape), F32, kind="ExternalInput").ap()

    xs = din("xs", [SEQ + 2, D])
    pp = din("pp", [HALF, 256])
    wsw = din("wsw", [D, N_SW * 128])
    wgt = din("wgt", [D, 2048])
    vec_d = din("vec", [128, NVEC_IN])
    w2aug_d = din("w2aug", [2, 65, 512])
    a2_d = din("a2", [2, 64, 512])
    g2_d = din("g2", [128, 512])
    wua_d = din("wua", [512, D]); wur_d = din("wur", [512, D]); wout_d = din("wout", [D, D])
    wff1_d = din("wff1", [D, 4096]); wff2_d = din("wff2", [4096, D])
    wpg_d = din("wpg", [D, D]); wple_d = din("wple", [256, D])
    rope_d = din("rope", [128, 2, HALF + 512])
    cst_d = din("cst", [128, 12, 128])
    tri_d = din("tri", [128, 2, 128])
    crow_d = din("crow", [1, 2, 128])
    out_d = nc.dram_tensor("out", [HALF, D], F32, kind="ExternalOutput").ap()
    m0_s = nc.dram_tensor("m0_s", [2, 64, 128, 4, 128], BF16, kind="Internal").ap()
    n0_s = nc.dram_tensor("n0_s", [2, 64, 128, 4, 64], BF16, kind="Internal").ap()
    q_s = nc.dram_tensor("q_s", [2, 16, 128, 4, 128], BF16, kind="Internal").ap()
    x1_s = nc.dram_tensor("x1_s", [HALF, D], F32, kind="Internal").ap()
    y0_s = nc.dram_tensor("y0_s", [2, 128, 4, HALF], F32, kind="Internal").ap()
    ao_s = nc.dram_tensor("ao_s", [128, 4, HALF], BF16, kind="Internal").ap()
    bv_s = nc.dram_tensor("bv_s", [128, 4, HALF], BF16, kind="Internal").ap()
    sgl_s = nc.dram_tensor("sgl_s", [128, HALF], BF16, kind="Internal").ap()
    dbg_out = None
    if dbg is not None:
        dbg_out = nc.dram_tensor("dbg", [128, 16384], F32, kind="ExternalOutput").ap()

    def fsz(ap):
        n = 1
        for d_ in ap.shape[1:]:
            n *= d_
        return n

    def rgof(ap):
        b = ap.base_partition()
        return (b // 32, (b + ap.shape[0] - 1) // 32)

    def MM(out, lhsT, rhs, start=True, stop=True, r=(), w=()):
        dur = max(0.064, 0.02 + fsz(out) / 2000.0) * (4 if lhsT.dtype == F32 else 1)
        P.add("pe", lambda e: e.matmul(out, lhsT, rhs, start=start, stop=stop), r, w, dur=dur, rg=rgof(lhsT))

    def TR(out, in_, idn, r=(), w=()):
        P.add("pe", lambda e: e.transpose(out, in_, idn), r, w, dur=0.064, rg=rgof(in_))

    def ACT(out, in_, func, bias=None, scale=None, accum=None, r=(), w=()):
        kw = {}
        if bias is not None:
            kw["bias"] = bias
        if scale is not None:
            kw["scale"] = scale
        if accum is not None:
            kw["accum_out"] = accum
        tbl = "ln" if func == AF.Ln else ("tanh" if func == AF.Tanh else None)
        P.add("act", lambda e: e.activation(out=out, in_=in_, func=func, **kw), r, w, dur=0.2 + fsz(out) / 1400.0, tbl=tbl)

    def vdur(eng, out):
        return (0.1 + fsz(out) / 960.0) if eng == "dve" else (0.3 + fsz(out) / 500.0)

    def TTo(eng, out, a, b, op, r=(), w=()):
        P.add(eng, lambda e: e.tensor_tensor(out, a, b, op), r, w, dur=vdur(eng, out))

    def TS(eng, out, a, s1, s2, op0, op1=None, r=(), w=()):
        if op1 is None:
            P.add(eng, lambda e: e.tensor_scalar(out, a, s1, None, op0), r, w, dur=vdur(eng, out))
        else:
            P.add(eng, lambda e: e.tensor_scalar(out, a, s1, s2, op0, op1), r, w, dur=vdur(eng, out))

    def STT(eng, out, in0, sc, in1, op0, op1, r=(), w=()):
        P.add(eng, lambda e: e.scalar_tensor_tensor(out, in0, sc, in1, op0, op1), r, w, dur=vdur(eng, out))

    def CP(eng, out, in_, r=(), w=()):
        P.add(eng, lambda e: e.tensor_copy(out, in_), r, w, dur=vdur(eng, out))

    def RCP(out, in_, r=(), w=()):
        P.add("dve", lambda e: e.reciprocal(out, in_), r, w, dur=vdur("dve", out) * 2)

    def MEMSET(eng, ap, val, w=()):
        P.add(eng, lambda e: e.memset(ap, val), (), w, dur=vdur(eng, ap))

    def DMA(eng, out, in_, r=(), w=()):
        nbytes = fsz(out) * out.shape[0] * (4 if out.dtype == F32 else 2)
        P.add(eng, lambda e: e.dma_start(out=out, in_=in_), r, w, dma=True, dur=2.0 + nbytes / 100000.0)

    with ExitStack() as gs:
        def sb(name, shape, dt=F32, st=None):
            return (st or gs).enter_context(nc.sbuf_tensor("sb_" + name, list(shape), dt))

        psA = [gs.enter_context(nc.psum_tensor(f"psA{i}", [128, 512], F32)) for i in range(6)]
        psT = [gs.enter_context(nc.psum_tensor(f"psT{i}", [128, 1024], BF16)) for i in range(2)]
        pk = lambda i: ("ps", i)
        pkT = lambda i: ("psT", i)

        vec = sb("vec", [128, NVEC])
        cstb = sb("cstb", [128, 12, 128], BF16)
        cstf = sb("cstf", [128, 3, 128])
        crow = sb("crow", [1, 2, 128])
        tri = sb("tri", [128, 2, 128])
        eps_c = sb("eps_c", [128, 4])
        junk = sb("junk", [128, D], BF16)
        DMA("sp", vec[:, 0:NVEC_IN], vec_d, w=["vec"])
        DMA("pool", cstb[:], cst_d, w=["cstb"])
        DMA("sp", cstf[:, 0, :], cst_d[:, 0, :], w=["cstf"])
        DMA("sp", cstf[:, 1, :], cst_d[:, 9, :], w=["cstf"])
        DMA("sp", cstf[:, 2, :], cst_d[:, 8, :], w=["cstf"])
        DMA("sp", crow[:], crow_d, w=["crow"])
        DMA("sp", tri[:], tri_d, w=["tri"])
        MEMSET("dve", eps_c[:, 0:1], 1e-6, w=["eps"])
        MEMSET("dve", eps_c[:, 1:2], 1e-24, w=["eps"])
        MEMSET("dve", eps_c[:, 2:3], 64e-5, w=["eps"])
        ident = cstb[:, 0, :]; onesblk = cstb[:, 1, :]; perm = cstb[:, 2, :]; bdm = cstb[:, 3, :]
        identf = cstf[:, 0, :]
        vc = lambda n, i=0: vec[:, VC[n] + i:VC[n] + i + 1]
        TS("dve", vec[:, VC["omm"]:VC["omm"] + 14], vec[:, VC["mu"]:VC["mu"] + 14], -1.0, 1.0, ALU.mult, ALU.add, r=["vec"], w=["vec2"])
        TS("dve", vec[:, VC["hmu"]:VC["hmu"] + 14], vec[:, VC["mu"]:VC["mu"] + 14], 0.5, None, ALU.mult, r=["vec"], w=["vec2"])
        TS("dve", vec[:, VC["nkk"]:VC["nkk"] + 4], vec[:, VC["k_k"]:VC["k_k"] + 4], -1.0, None, ALU.mult, r=["vec"], w=["vec2"])
        TS("dve", vec[:, VC["omka"]:VC["omka"] + 4], vec[:, VC["k_a"]:VC["k_a"] + 4], -1.0, 1.0, ALU.mult, ALU.add, r=["vec"], w=["vec2"])
        ACT(vec[:, VC["esink"]:VC["esink"] + 4], vec[:, VC["sink"]:VC["sink"] + 4], AF.Exp, r=["vec"], w=["vec2"])
        TS("dve", vec[:, VC["ha0A"]:VC["ha0A"] + 8], vec[:, VC["a0A"]:VC["a0A"] + 8], 0.5, None, ALU.mult, r=["vec"], w=["vec2"])
        TS("dve", vec[:, VC["hka"]:VC["hka"] + 4], vec[:, VC["k_a"]:VC["k_a"] + 4], 0.5, None, ALU.mult, r=["vec"], w=["vec2"])
        TS("dve", vec[:, VC["c2ka"]:VC["c2ka"] + 4], vec[:, VC["k_a"]:VC["k_a"] + 4], -0.5, 1.0, ALU.mult, ALU.add, r=["vec"], w=["vec2"])
        VK = ["vec", "vec2", "eps", "cstb", "cstf", "tri"]

        def ck(name):
            if os.environ.get("KSTOP") == name:
                P.disabled = True
        if os.environ.get("KSTOP") == "c":
            P.disabled = True

        bar_t = {e: sb(f"bar_{e}", [1, 8]) for e in ["act", "dve", "pool"]}

        def RSQ(dst, src, scale, bias_ap, n, r=(), w=()):
            ACT(dst, src, AF.Ln, bias=bias_ap, scale=scale, r=list(r) + ["eps"], w=list(w))
            ACT(dst, dst, AF.Exp, scale=-0.5, r=list(w), w=list(w))

        def barrier():
            if getattr(P, "disabled", False):
                return
            P.seg += 1
            n = len(P.ops)
            pend = [op.idx for op in P.ops[P.last_barrier:n]]
            marks = []
            for e in ["act", "dve", "pool"]:
                t = bar_t[e]
                marks.append(P.add(e, (lambda tt, ee: (lambda en: en.memzero(tt[:]) if ee == 'act' else en.memset(tt[:], 0.0)))(t, e), (), [f"bar_{e}_{n}"]).idx)
            marks.append(P.add("pe", lambda e: e.matmul(psA[0][0:1, 0:1], cstb[0:1, 8, 0:1], cstb[0:1, 8, 0:1], start=True, stop=True),
                               ["cstb", pk(0)], [pk(0), f"bar_pe_{n}"]).idx)
            for e in ["pe", "act", "dve", "pool", "sp"]:
                f = P.add(e, None, (), ())
                f.xdeps = tuple(marks) + tuple(i for i in pend if P.ops[i].dma)
            P.last_barrier = len(P.ops)
            P.seg += 1

        def make_hT(st_key, x_aps, nrows, hT, gain, wk, xt, hb, ssq, rstd):
            nblk = len(x_aps)
            for bi, (xap, rows, col0) in enumerate(x_aps):
                s = bi % 2
                DMA("sp", xt[0:rows, s, :], xap, w=[("xt", s)])
                ACT(junk[0:rows, :], xt[0:rows, s, :], AF.Square, accum=ssq[0:rows, s:s + 1], r=[("xt", s)], w=["junk", ("ssq", s)])
                RSQ(rstd[0:rows, s:s + 1], ssq[0:rows, s:s + 1], 1.0 / D, eps_c[0:rows, 0:1], 1, r=[("ssq", s)], w=[("rstd", s)])
                TS("dve", hb[0:rows, 0, :], xt[0:rows, s, :], rstd[0:rows, s:s + 1], None, ALU.mult, r=[("xt", s), ("rstd", s)], w=["hbo"])
                for kc in range(8):
                    TR(psT[kc % 2][:, (kc // 2) * 128:(kc // 2) * 128 + rows], hb[0:rows, 0, kc * 128:(kc + 1) * 128], ident[0:rows, 0:rows],
                       r=["hbo", "cstb"], w=[pkT(kc % 2)])
                for kc in range(8):
                    eng_act = (kc % 2 == 0)
                    src = psT[kc % 2][:, (kc // 2) * 128:(kc // 2) * 128 + rows]
                    dst = hT[:, kc, col0:col0 + rows]
                    if eng_act:
                        ACT(dst, src, AF.Copy, scale=vc(gain, kc), r=[pkT(kc % 2), "vec"], w=[(wk, kc)])
                    else:
                        TS("dve", dst, src, vc(gain, kc), None, ALU.mult, r=[pkT(kc % 2), "vec"], w=[(wk, kc)])

        g2t = sb("g2t", [128, 512], BF16)
        PCs = sb("PCs", [128, 2, 64, 4], F32)
        with ExitStack() as s1:
            win = sb("win", [128, 8, N_SW * 128], BF16, s1)
            cstm = sb("cstm", [128, 4, 4, 128], BF16, s1)
            for _m in range(4):
                for _r in range(4):
                    DMA("pool", cstm[:, _m, _r, :], cst_d[:, 4 + _m, :], w=["cstm"])
            for kc in range(8):
                DMA("pool", win[:, kc, :], wsw[kc * 128:(kc + 1) * 128, :], w=["win"])
            w2aug = sb("w2aug", [65, 2, 512], BF16, s1)
            a2t = sb("a2t", [128, 2, 512], BF16, s1)
            rope = sb("rope", [128, 2, HALF + 256], BF16, s1)
            for d in range(2):
                DMA("pool", w2aug[:, d, :], w2aug_d[d], w=["w2aug"])
                DMA("pool", a2t[64:128, d, :], a2_d[d], w=["a2t"])
            DMA("pool", g2t[:], g2_d, w=["g2t"])
            DMA("pool", rope[:], rope_d[:, :, 0:HALF + 256], w=["rope"])
            if os.environ.get("KSTOP") == "w":
                P.disabled = True
            kT = sb("kT", [128, HALF + 128], BF16, s1)
            vaug = sb("vaug", [128, NB_OWN + 1, 2, 64], BF16, s1)
            vones = sb("vones", [128, 64], BF16, s1)
            attn_o = sb("attn_o", [128, 4, TT], BF16, s1)
            sgl = sb("sgl", [128, TT], BF16, s1)
            bv = sb("bv", [128, 4, TT], BF16, s1)
            MEMSET("pool", vones[:], 1.0, w=["vaug1"])
            xt = sb("xt", [128, 2, D], F32, s1); hb = sb("hb", [128, 1, D], BF16, s1)
            ssq = sb("ssq", [128, 2], F32, s1); rstd = sb("rstd", [128, 2], F32, s1)
            hT = sb("hT", [128, 8, TT + 2], BF16, s1)
            zr = sb("zr", [128, 2, TT + 2], F32, s1)
            zs = sb("zs", [128, 2, TT], F32, s1)
            ztmp = sb("ztmp", [128, 2, TT], F32, s1)
            zq = sb("zq", [128, TT], F32, s1)
            qsq = sb("qsq", [128, TT], BF16, s1)
            qrs = sb("qrs", [128, TT], F32, s1)
            qn = sb("qn", [128, TT], BF16, s1)
            qt1 = sb("qt1", [128, TT], F32, s1); qt2 = sb("qt2", [128, TT], F32, s1)
            qT = sb("qT", [128, 2, 4, TT], BF16, s1)
            vb = sb("vb", [128, TT], BF16, s1)
            twl = sb("twl", [65, TT], BF16, s1); alx = sb("alx", [128, TT], BF16, s1)
            MEMSET("pool", twl[64:65, :], 1.0, w=["twl1"])
            if os.environ.get("KSTOP") == "m":
                P.disabled = True
            sgfc = sb("sgfc", [128, NBT, 128], F32, s1)
            r_f = sb("r_f", [128, TT], F32, s1); k_f = sb("k_f", [128, TT], F32, s1); v_f = sb("v_f", [128, TT], F32, s1)
            nkk = sb("nkk", [128, TT], F32, s1)
            a_f = sb("a_f", [128, TT], F32, s1); b_f = sb("b_f", [128, TT], F32, s1); kd_f = sb("kd_f", [128, TT], F32, s1)
            t1_f = sb("t1_f", [128, TT], F32, s1)
            pb = sb("pb", [128, TT], BF16, s1)
            E_f = sb("E_f", [128, TT], F32, s1); G_f = sb("G_f", [128, TT], F32, s1)
            AR = sb("AR", [128, 2, 4, NBT, 256], BF16, s1)
            bt = sb("bt", [128, 2, 4, TT], BF16, s1); kt = sb("kt", [128, 2, 4, TT], BF16, s1)
            btT = sb("btT", [128, 2, NBT, 512], BF16, s1); ktT = sb("ktT", [128, 2, NBT, 512], BF16, s1)
            atT = sb("atT", [128, 2, NBT, 512], BF16, s1)
            vT = sb("vT", [128, NBT, 512], BF16, s1)
            Aab = sb("Aab", [128, 8, 128], BF16, s1); AabT = sb("AabT", [128, 8, 128], BF16, s1)
            Aak = sb("Aak", [128, 8, 128], BF16, s1); Arb = sb("Arb", [128, 8, 128], BF16, s1); Ark = sb("Ark", [128, 8, 128], BF16, s1)
            ApI = [[sb(f"Ap{j}_{i}", [128, 8, 128], BF16, s1) for i in range(2)] for j in range(2)]
            ATpI = [[sb(f"ATp{j}_{i}", [128, 8, 128], BF16, s1) for i in range(2)] for j in range(2)]
            ZpI = [[sb(f"Zp{j}_{i}", [128, 4, 2, 2, 64], BF16, s1) for i in range(2)] for j in range(2)]
            M0stI = [sb(f"M0st{j}", [128, 2, 4, 128], BF16, s1) for j in range(2)]
            N0stI = [sb(f"N0st{j}", [128, 2, 4, 64], BF16, s1) for j in range(2)]
            QstI = [sb(f"Qst{j}", [128, 4, 128], BF16, s1) for j in range(2)]
            Y0stI = [sb(f"Y0st{j}", [128, 4, 128], F32, s1) for j in range(2)]
            hlcount = [0]
            probs = sb("probs", [128, 2, 384], BF16, s1)
            den = sb("den", [128, 128], F32, s1)

            def proj_chunk(ci, bank, halo_bank=None):
                if halo_bank is None:
                    for kc in range(8):
                        MM(psA[bank][:, 0:TT], win[:, kc, ci * 128:(ci + 1) * 128], hT[:, kc, 1:TT + 1], start=(kc == 0), stop=(kc == 7),
                           r=["win", ("hT", kc)], w=[pk(bank)])
                else:
                    for kc in range(8):
                        MM(psA[bank][:, 0:TT + 2], win[:, kc, ci * 128:(ci + 1) * 128], hT[:, kc, 0:TT + 2], start=(kc == 0), stop=(kc == 7),
                           r=["win", ("hT", kc)], w=[pk(bank)])

            zcount = [0]

            def rwkv_z(ci, mui, out_ap, wkey, post=None):
                s = zcount[0] % 2
                zcount[0] += 1
                bank = s
                proj_chunk(ci, bank, 2)
                ACT(zr[:, s, :], psA[bank][:, 0:TT + 2], AF.Copy, r=[pk(bank)], w=[("zr", s)])
                ACT(zs[:, s, :], psA[bank][:, 1:TT + 1], AF.Copy, scale=vc("omm", mui), r=[pk(bank), "vec2"], w=[("zs", s)])
                TTo("pool", ztmp[:, s, :], zr[:, s, 0:TT], zr[:, s, 2:TT + 2], ALU.add, r=[("zr", s)], w=[("ztmp", s)])
                STT("dve", out_ap, ztmp[:, s, :], vc("hmu", mui), zs[:, s, :], ALU.mult, ALU.add, r=[("ztmp", s), ("zs", s), "vec2"], w=[wkey])

            def sweep_tile(t0, own, first_other, pcidx0):
                dirs = [0, 1] if own else [1]
                blk0 = t0 // 128
                xaps = [(xs[t0 + 128 * i: t0 + 128 * i + 128, :], 128, 128 * i) for i in range(NBT)]
                xaps.append((xs[t0 + TT: t0 + TT + 2, :], 2, TT))
                P.tag = f"t{t0}-hT"
                make_hT("hT", xaps, None, hT, "nmix", "hT", xt, hb, ssq, rstd)
                ck("h")
                P.tag = f"t{t0}-qkv"
                qs = (t0 // TT) % 2
                if own or first_other:
                    for ci in ([0, 1, 2, 3, 4] if own else [4]):
                        bank = ci % 2
                        proj_chunk(ci, bank)
                        gcol = "kg" if ci == 4 else "qg"
                        ACT(qsq[:], psA[bank][:, 0:TT], AF.Square, r=[pk(bank)], w=["qsq"])
                        MM(psA[3][:, 0:TT], onesblk, qsq[:], r=["cstb", "qsq"], w=[pk(3)])
                        RSQ(qrs[:], psA[3][:, 0:TT], 1.0 / 64, eps_c[:, 0:1], TT, r=[pk(3)], w=["qrs"])
                        STT("dve", qn[:], psA[bank][:, 0:TT], vc(gcol), qrs[:], ALU.mult, ALU.mult, r=[pk(bank), "qrs", "vec"], w=["qn"])
                        MM(psA[3][:, 0:TT], perm, qn[:], r=["cstb", "qn"], w=[pk(3)])
                        TTo("pool", qt1[:], qn[:], rope[:, 0, t0:t0 + TT], ALU.mult, r=["qn", "rope"], w=["qt1"])
                        TTo("dve", qt2[:], psA[3][:, 0:TT], rope[:, 1, t0:t0 + TT], ALU.mult, r=[pk(3), "rope"], w=["qt2"])
                        if ci < 4:
                            TTo("pool", qT[:, qs, ci, :], qt1[:], qt2[:], ALU.add, r=["qt1", "qt2"], w=[("qT", qs)])
                        else:
                            n = TT if own else 128
                            TTo("pool", kT[:, t0:t0 + n], qt1[:, 0:n], qt2[:, 0:n], ALU.add, r=["qt1", "qt2"], w=["kT"])
                    proj_chunk(5, 0)
                    ACT(vb[:], psA[0][:, 0:TT], AF.Copy, r=[pk(0)], w=["vb"])
                    nbv = NBT if own else 1
                    for bi in range(nbv):
                        TR(psT[0][:, bi * 128:(bi + 1) * 128], vb[:, bi * 128:(bi + 1) * 128], ident, r=["vb", "cstb"], w=[pkT(0)])
                    for bi in range(nbv):
                        CP("dve", vaug[:, blk0 + bi, :, :], psT[0][:, bi * 128:(bi + 1) * 128].rearrange("p (g d) -> p g d", g=2),
                           r=[pkT(0)], w=["vaug"])
                P.tag = f"t{t0}-wa"
                ck("q")
                rwkv_z(6, 0, zq[:], "zq")
                ACT(twl[0:64, :], zq[0:64, :], AF.Tanh, r=["zq"], w=["twl"])
                CP("pool", alx[64:128, :], zq[64:128, :], r=["zq"], w=["alx"])
                ck("z")
                if own:
                    rwkv_z(7, 1, zq[:], "zq")
                    ACT(zq[:], zq[:], AF.Tanh, scale=0.5, r=["zq"], w=["zq"])
                    TS("pool", sgl[:, :], zq[:], 0.5, 0.5, ALU.mult, ALU.add, r=["zq"], w=["sgl"])
                    DMA("sp", sgl_s[:, t0:t0 + TT], sgl[:, :], r=["sgl"], w=[("sgl_s", t0 // TT)])
                for fc in range(4):
                    P.tag = f"t{t0}-fc{fc}"
                    if own:
                        rwkv_z(8 + 3 * fc, 2 + 3 * fc, r_f[:], "r_f")
                    rwkv_z(9 + 3 * fc, 3 + 3 * fc, k_f[:], "k_f")
                    rwkv_z(10 + 3 * fc, 4 + 3 * fc, v_f[:], "v_f")
                    ACT(qsq[:], k_f[:], AF.Square, scale=vc("k_k", fc), r=["k_f", "vec"], w=["qsq"])
                    MM(psA[3][:, 0:TT], onesblk, qsq[:], r=["cstb", "qsq"], w=[pk(3)])
                    RSQ(qrs[:], psA[3][:, 0:TT], 1.0, eps_c[:, 1:2], TT, r=[pk(3)], w=["qrs"])
                    STT("dve", nkk[:], k_f[:], vc("nkk", fc), qrs[:], ALU.mult, ALU.mult, r=["k_f", "qrs", "vec2"], w=["nkk"])
                    ck("k")
                    CP("pool", vb[:], v_f[:], r=["v_f"], w=["vb"])
                    for bi in range(NBT):
                        TR(psT[0][:, bi * 128:(bi + 1) * 128], vb[:, bi * 128:(bi + 1) * 128], ident, r=["vb", "cstb"], w=[pkT(0)])
                    CP("dve", vT[:, :, fc * 128:(fc + 1) * 128], psT[0][:, 0:NBT * 128].rearrange("p (b c) -> p b c", b=NBT), r=[pkT(0)], w=[("vT", fc)])
                    for d in dirs:
                        dn = "AB"[d]
                        MM(psA[3][:, 0:TT], a2t[64:128, d, fc * 128:(fc + 1) * 128], alx[64:128, :], r=["a2t", "alx"], w=[pk(3)])
                        ACT(a_f[:], psA[3][:, 0:TT], AF.Tanh, bias=vc("ha0" + dn, fc), scale=0.5, r=[pk(3), "vec2"], w=["a_f"])
                        STT("dve", b_f[:], a_f[:], 1.0, nkk[:], ALU.add, ALU.mult, r=["nkk", "a_f"], w=["b_f"])
                        TS("dve", t1_f[:], a_f[:], vc("hka", fc), vc("c2ka", fc), ALU.mult, ALU.add, r=["a_f", "vec2"], w=["t1_f"])
                        TTo("pool", kd_f[:], k_f[:], t1_f[:], ALU.mult, r=["k_f", "t1_f"], w=["kd_f"])
                        if own:
                            STT("dve", pb[:], r_f[:], vc("rk" + dn, fc), kd_f[:], ALU.mult, ALU.mult, r=["r_f", "kd_f", "vec"], w=["pb"])
                            MM(psA[2][:, 0:TT], onesblk, pb[:], start=(d == 0), stop=(d == 1), r=["cstb", "pb"], w=[pk(2)])
                        for bi in range(NBT):
                            MM(psA[4][:, bi * 128:(bi + 1) * 128], twl[:, bi * 128:(bi + 1) * 128], w2aug[:, d, fc * 128:(fc + 1) * 128],
                               r=["twl", "twl1", "w2aug"], w=[pk(4)])
                        ACT(sgfc[:, :, :], psA[4][:, 0:NBT * 128].rearrange("p (b c) -> p b c", b=NBT), AF.Tanh, scale=0.5, r=[pk(4)], w=["sgfc"])
                        for bi in range(NBT):
                            MM(psA[3][:, bi * 128:(bi + 1) * 128], sgfc[:, bi, :], tri[:, d, :], start=True, stop=False, r=["sgfc", "tri"], w=[pk(3)])
                            MM(psA[3][:, bi * 128:(bi + 1) * 128], cstf[0:1, 2, :], crow[0:1, d, :], start=False, stop=True, r=["cstf", "crow"], w=[pk(3)])
                        ACT(E_f[:], psA[3][:, 0:TT], AF.Exp, r=[pk(3)], w=["E_f"])
                        ACT(G_f[:], psA[3][:, 0:TT], AF.Exp, scale=-1.0, r=[pk(3)], w=["G_f"])
                        Ev = E_f[:].rearrange("p (c t) -> p c t", t=64)
                        pcol = 63 if d == 0 else 0
                        CP("pool", PCs[:, d, pcidx0:pcidx0 + TT // 64, fc], Ev[:, :, pcol], r=["E_f"], w=["PCs"])
                        ARv = AR[:, d, fc, :, :]
                        if own:
                            TTo("dve", ARv[:, :, 128:256], r_f[:].rearrange("p (b t) -> p b t", t=128), E_f[:].rearrange("p (b t) -> p b t", t=128),
                                ALU.mult, r=["r_f", "E_f"], w=[("AR", d, fc)])
                        ARa = AR[:, d, fc, :, 0:128].rearrange("p b (c t) -> p b c t", t=64)
                        nk4 = nkk[:].rearrange("p (b c t) -> p b c t", c=2, t=64)
                        E4 = E_f[:].rearrange("p (b c t) -> p b c t", c=2, t=64)
                        for bi in range(NBT):
                            if d == 0:
                                TTo("pool", ARa[:, bi, :, 1:64], nk4[:, bi, :, 1:64], E4[:, bi, :, 0:63], ALU.mult, r=["nkk", "E_f"], w=[("AR", d, fc)])
                                CP("pool", ARa[:, bi, :, 0:1], nk4[:, bi, :, 0:1], r=["nkk"], w=[("AR", d, fc)])
                            else:
                                TTo("pool", ARa[:, bi, :, 0:63], nk4[:, bi, :, 0:63], E4[:, bi, :, 1:64], ALU.mult, r=["nkk", "E_f"], w=[("AR", d, fc)])
                                CP("pool", ARa[:, bi, :, 63:64], nk4[:, bi, :, 63:64], r=["nkk"], w=[("AR", d, fc)])
                        STT("dve", bt[:, d, fc, :], b_f[:], -0.5, G_f[:], ALU.mult, ALU.mult, r=["b_f", "G_f"], w=[("bt", d, fc)])
                        TTo("pool", kt[:, d, fc, :], kd_f[:], G_f[:], ALU.mult, r=["kd_f", "G_f"], w=[("kt", d, fc)])
                        for which, (srck, dst, dkey) in enumerate(((("bt", d, fc), btT, "btT"), (("kt", d, fc), ktT, "ktT"), (("AR", d, fc), atT, "atT"))):
                            pb_i = which % 2
                            for bi in range(NBT):
                                if which == 0:
                                    src = bt[:, d, fc, bi * 128:(bi + 1) * 128]
                                elif which == 1:
                                    src = kt[:, d, fc, bi * 128:(bi + 1) * 128]
                                else:
                                    src = AR[:, d, fc, bi, 0:128]
                                TR(psT[pb_i][:, bi * 128:(bi + 1) * 128], src, ident, r=[srck, "cstb"], w=[pkT(pb_i)])
                            pv = psT[pb_i][:, 0:NBT * 128].rearrange("p (b c) -> p b c", b=NBT)
                            if which != 1:
                                CP("dve", dst[:, d, :, fc * 128:(fc + 1) * 128], pv, r=[pkT(pb_i)], w=[(dkey, d, fc)])
                            else:
                                ACT(dst[:, d, :, fc * 128:(fc + 1) * 128], pv, AF.Copy, r=[pkT(pb_i)], w=[(dkey, d, fc)])
                    if own:
                        TTo("dve", bv[:, fc, :], psA[2][:, 0:TT], v_f[:], ALU.mult, r=[pk(2), "v_f"], w=["bv"])
                        DMA("sp", bv_s[:, fc, t0:t0 + TT], bv[:, fc, :], r=["bv"], w=[("bv_s", t0 // TT, fc)])
                ck("f")
                for d in dirs:
                    ms, mi = (4, 5) if d == 0 else (6, 7)
                    for bi in range(NBT):
                        P.tag = f"t{t0}-hl{d}{bi}"
                        ins_ = hlcount[0] % 2
                        hlcount[0] += 1
                        Ap, ATp, Zp = ApI[ins_], ATpI[ins_], ZpI[ins_]
                        B = (lambda p_: (lambda b_: (b_ + 3 * p_) % 6))(ins_)
                        M0st, N0st, Qst, Y0st = M0stI[ins_], N0stI[ins_], QstI[ins_], Y0stI[ins_]
                        gblk = (t0 // 128 + bi) if own else None
                        def amat(bank0, lh, rh, keys):
                            for h in range(8):
                                fc = h // 2; rows = slice((h % 2) * 64, (h % 2) * 64 + 64)
                                MM(psA[B(bank0 + h % 2)][:, (h // 2) * 128:(h // 2) * 128 + 128], lh(rows, fc), rh(rows, fc),
                                   r=[(k_, d, fc) for k_ in keys], w=[pk(B(bank0 + h % 2))])

                        def aevac(bank0, dst, dkey, mslot):
                            dv = dst[:].rearrange("p (i two) t -> p two i t", two=2)
                            for par in range(2):
                                TTo("dve", dv[:, par, :, :], psA[B(bank0 + par)][:, :].rearrange("p (i t) -> p i t", i=4), cstm[:, mslot - 4, :, :], ALU.mult,
                                    r=[pk(B(bank0 + par)), "cstm"], w=[dkey])
                        tokc = slice(bi * 128, (bi + 1) * 128)
                        amat(0, lambda rows, fc: bt[rows, d, fc, tokc], lambda rows, fc: AR[rows, d, fc, bi, 0:128], ["bt", "AR"])
                        amat(2, lambda rows, fc: kt[rows, d, fc, tokc], lambda rows, fc: AR[rows, d, fc, bi, 0:128], ["kt", "AR"])
                        amat(4, lambda rows, fc: AR[rows, d, fc, bi, 0:128], lambda rows, fc: bt[rows, d, fc, tokc], ["bt", "AR"])
                        aevac(0, Aab, "Aab", ms)
                        aevac(2, Aak, "Aak", ms)
                        aevac(4, AabT, "AabT", (6 if d == 0 else 4))
                        ck("a")
                        Z = Zp[0]
                        CP("pool", Z[:, :, 0, :, :], atT[:, d, bi, :].rearrange("p (f h j) -> p f h j", f=4, h=2), r=[("atT", d, f_) for f_ in range(4)], w=[("Z", ins_, 0, f_) for f_ in range(4)])
                        for h in range(8):
                            MM(psA[B(3)][:, h * 64:(h + 1) * 64], Aak[:, h, :], vT[:, bi, h * 64:(h + 1) * 64], r=["Aak", ("vT", h // 2)], w=[pk(B(3))])
                        CP("dve", Z[:, :, 1, :, :], psA[B(3)][:, 0:512].rearrange("p (f h j) -> p f h j", f=4, h=2), r=[pk(B(3))], w=[("Z", ins_, 0, f_) for f_ in range(4)])
                        curA, curAT, ak, atk = Aab, AabT, ["Aab", "Aab"], ["AabT", "AabT"]
                        zi = 0
                        for lev in range(6):
                            Zc, Zn = Zp[zi], Zp[1 - zi]
                            for hh in range(2):
                                bank = 4 + hh
                                for q4 in range(4):
                                    h = 4 * hh + q4; fc = h // 2; h2 = h % 2
                                    o = psA[B(bank)][:, q4 * 128:(q4 + 1) * 128].rearrange("p (a j) -> p a j", a=2)
                                    MM(o, curA[:, h, :], Zc[:, fc, :, h2, :], start=True, stop=False, r=[ak[hh], ("Z", ins_, zi, fc)], w=[pk(B(bank))])
                                    MM(o, ident, Zc[:, fc, :, h2, :], start=False, stop=True, r=["cstb", ("Z", ins_, zi, fc)], w=[pk(B(bank))])
                                for f in range(2):
                                    src = psA[B(bank)][:, f * 256:(f + 1) * 256].rearrange("p (h a j) -> p a h j", h=2, a=2)
                                    if hh == 0:
                                        ACT(Zn[:, 2 * hh + f, :, :, :], src, AF.Copy, r=[pk(B(bank))], w=[("Z", ins_, 1 - zi, 2 * hh + f)])
                                    else:
                                        CP("dve", Zn[:, 2 * hh + f, :, :, :], src, r=[pk(B(bank))], w=[("Z", ins_, 1 - zi, 2 * hh + f)])
                            zi = 1 - zi
                            if lev < 5:
                                nA, nAT = Ap[lev % 2], ATp[lev % 2]
                                nak = [("Ap", ins_, lev % 2, 0), ("Ap", ins_, lev % 2, 1)]
                                natk = [("ATp", ins_, lev % 2, 0), ("ATp", ins_, lev % 2, 1)]
                                for hh in range(2):
                                    for q4 in range(4):
                                        h = 4 * hh + q4
                                        MM(psA[B(0 + hh)][:, q4 * 128:(q4 + 1) * 128], curAT[:, h, :], curA[:, h, :], r=[ak[hh], atk[hh]], w=[pk(B(0 + hh))])
                                        MM(psA[B(2 + hh)][:, q4 * 128:(q4 + 1) * 128], curA[:, h, :], curAT[:, h, :], r=[ak[hh], atk[hh]], w=[pk(B(2 + hh))])
                                    ACT(nA[:, 4 * hh:4 * hh + 4, :], psA[B(0 + hh)][:, :].rearrange("p (h t) -> p h t", h=4), AF.Copy, r=[pk(B(0 + hh))], w=[nak[hh]])
                                    CP("dve", nAT[:, 4 * hh:4 * hh + 4, :], psA[B(2 + hh)][:, :].rearrange("p (h t) -> p h t", h=4), r=[pk(B(2 + hh))], w=[natk[hh]])
                                curA, curAT, ak, atk = nA, nAT, nak, natk
                        Zf = Zp[zi]; zkf = lambda f_: ("Z", ins_, zi, f_)
                        ck("d")
                        for c2 in range(2):
                            rs_ = slice(c2 * 64, c2 * 64 + 64)
                            for fc in range(4):
                                MM(psA[B(c2)][:, fc * 128:(fc + 1) * 128], Zf[rs_, fc, 0, :, :], btT[rs_, d, bi, fc * 128:(fc + 1) * 128], r=[zkf(fc), ("btT", d, fc)], w=[pk(B(c2))])
                            for fc in range(4):
                                TTo("dve", M0st[:, c2, fc, :], psA[B(c2)][:, fc * 128:(fc + 1) * 128], bdm, ALU.mult, r=[pk(B(c2)), "cstb"], w=[("M0st", ins_)])
                            for h in range(8):
                                fc = h // 2; h2 = h % 2
                                o = psA[B(2 + c2)][h2 * 64:h2 * 64 + 64, fc * 64:fc * 64 + 64]
                                MM(o, btT[rs_, d, bi, h * 64:(h + 1) * 64], Zf[rs_, fc, 1, h2, :], start=True, stop=False, r=[("btT", d, fc), zkf(fc)], w=[pk(B(2 + c2))])
                                MM(o, ktT[rs_, d, bi, h * 64:(h + 1) * 64], vT[rs_, bi, h * 64:(h + 1) * 64], start=False, stop=True, r=[("ktT", d, fc), ("vT", fc)], w=[pk(B(2 + c2))])
                        for c2 in range(2):
                            ACT(N0st[:, c2, :, :], psA[B(2 + c2)][:, 0:256].rearrange("p (f i) -> p f i", f=4), AF.Copy, r=[pk(B(2 + c2))], w=[("N0st", ins_)])
                        cidx = pcidx0 + 2 * bi
                        DMA("sp", m0_s[d, cidx:cidx + 2].rearrange("c p f j -> p c f j"), M0st[:], r=[("M0st", ins_)], w=[("m0_s", d, cidx)])
                        DMA("sp", n0_s[d, cidx:cidx + 2].rearrange("c p f i -> p c f i"), N0st[:], r=[("N0st", ins_)], w=[("n0_s", d, cidx)])
                        if own:
                            amat(0, lambda rows, fc: bt[rows, d, fc, tokc], lambda rows, fc: AR[rows, d, fc, bi, 128:256], ["bt", "AR"])
                            amat(4, lambda rows, fc: kt[rows, d, fc, tokc], lambda rows, fc: AR[rows, d, fc, bi, 128:256], ["kt", "AR"])
                            aevac(0, Arb, "Arb", mi)
                            aevac(4, Ark, "Ark", mi)
                            for fc in range(4):
                                o = psA[B(2)][:, fc * 128:(fc + 1) * 128]
                                MM(o, ident, AR[:, d, fc, bi, 128:256], start=True, stop=False, r=["cstb", ("AR", d, fc)], w=[pk(B(2))])
                                for h2 in range(2):
                                    h = 2 * fc + h2
                                    MM(psA[B(2)][h2 * 64:h2 * 64 + 64, fc * 128:(fc + 1) * 128], Zf[:, fc, 0, h2, :], Arb[:, h, :],
                                       start=False, stop=(h2 == 1), r=[zkf(fc), "Arb"], w=[pk(B(2))])
                                for h2 in range(2):
                                    h = 2 * fc + h2
                                    o2 = psA[B(3)][h2 * 64:h2 * 64 + 64, fc * 128:(fc + 1) * 128]
                                    MM(o2, Zf[:, fc, 1, h2, :], Arb[:, h, :], start=True, stop=False, r=[zkf(fc), "Arb"], w=[pk(B(3))])
                                    MM(o2, vT[:, bi, h * 64:(h + 1) * 64], Ark[:, h, :], start=False, stop=True, r=[("vT", fc), "Ark"], w=[pk(B(3))])
                            ACT(Qst[:], psA[B(2)][:, :].rearrange("p (f t) -> p f t", f=4), AF.Copy, r=[pk(B(2))], w=[("Qst", ins_)])
                            DMA("sp", q_s[d, gblk], Qst[:], r=[("Qst", ins_)], w=[("q_s", d, gblk)])
                            CP("dve", Y0st[:], psA[B(3)][:, :].rearrange("p (f t) -> p f t", f=4), r=[pk(B(3))], w=[("Y0st", ins_)])
                            DMA("sp", y0_s[d, :, :, t0 + bi * 128:t0 + bi * 128 + 128], Y0st[:], r=[("Y0st", ins_)], w=[("y0_s", d, gblk)])

            def attention_tile(ti):
                P.tag = f"attn{ti}"
                qs = ti % 2
                for qb in range(NBT):
                    n = NBT * ti + qb
                    kbs = [kb for kb in (n - 1, n, n + 1) if kb >= 0]
                    for m in range(4):
                        for h2 in range(2):
                            h = 2 * m + h2
                            c = h % 4; g = h // 4
                            rows = slice(g * 64, g * 64 + 64)
                            pi = h2
                            for kb in kbs:
                                slot = kb - (n - 1)
                                MM(psA[pi][:, slot * 128:(slot + 1) * 128], kT[rows, kb * 128:(kb + 1) * 128], qT[rows, qs, c, qb * 128:(qb + 1) * 128],
                                   r=["kT", ("qT", qs)], w=[pk(pi)])
                            lo = (kbs[0] - (n - 1)) * 128
                            ACT(probs[:, pi, lo:384], psA[pi][:, lo:384], AF.Exp, scale=0.125, r=[pk(pi)], w=[("probs", pi)])
                            if n - 1 >= 0:
                                TTo("pool", probs[:, pi, 0:128], probs[:, pi, 0:128], cstb[:, 10, :], ALU.mult, r=[("probs", pi), "cstb"], w=[("probs", pi)])
                            TTo("pool", probs[:, pi, 256:384], probs[:, pi, 256:384], cstb[:, 11, :], ALU.mult, r=[("probs", pi), "cstb"], w=[("probs", pi)])
                            orow = slice(h2 * 64, h2 * 64 + 64)
                            for i, kb in enumerate(kbs):
                                slot = kb - (n - 1)
                                MM(psA[2][orow, 0:128], vaug[:, kb, g, :], probs[:, pi, slot * 128:(slot + 1) * 128], start=(i == 0), stop=(i == len(kbs) - 1),
                                   r=["vaug", ("probs", pi)], w=[pk(2)])
                            for i, kb in enumerate(kbs):
                                slot = kb - (n - 1)
                                MM(psA[3][orow, 0:128], vones[:], probs[:, pi, slot * 128:(slot + 1) * 128], start=(i == 0), stop=(i == len(kbs) - 1),
                                   r=["vaug1", ("probs", pi)], w=[pk(3)])
                        TS("dve", den[:], psA[3][:, 0:128], vc("esink", m), None, ALU.add, r=[pk(3), "vec2"], w=["den"])
                        RCP(den[:], den[:], r=["den"], w=["den"])
                        TTo("dve", attn_o[:, m, qb * 128:(qb + 1) * 128], psA[2][:, 0:128], den[:], ALU.mult, r=[pk(2), "den"], w=["attn_o"])

            def attn_and_store(ti):
                attention_tile(ti)
                DMA("sp", ao_s[:, :, ti * TT:(ti + 1) * TT], attn_o[:], r=["attn_o"], w=[("ao_s", ti)])

            oth = list(range(2 * NTH - 1, NTH - 1, -1))
            if ntiles_other is not None:
                oth = oth[len(oth) - ntiles_other:] if ntiles_other > 0 else []
            for ti in oth:
                t0 = ti * TT
                sweep_tile(t0, own=False, first_other=(ti == NTH), pcidx0=t0 // 64)
            nown = NTH if ntiles_own is None else ntiles_own
            for ti in range(nown):
                sweep_tile(ti * TT, own=True, first_other=False, pcidx0=(ti * TT) // 64)
                if ti >= 1:
                    attn_and_store(ti - 1)
            if nown == NTH:
                attn_and_store(NTH - 1)
            if dbg is not None and upto == 1:
                dbg(locals(), P, DMA, dbg_out)
            if upto == 1:
                P.disabled = True
        barrier()

        with ExitStack() as s23:
            Yb = sb("Yb", [128, 4, HALF], F32, s23)
            if True:
                s2 = s23
                S32 = sb("S32", [128, 2, 4, 64], F32, s2)
                Sb = sb("Sb", [128, 2, 4, 64], BF16, s2)
                St = sb("St", [128, 2, 4, 64], F32, s2)
                M0l = sb("M0l", [128, 4, 4, 128], BF16, s2)
                N0l = sb("N0l", [128, 4, 4, 64], BF16, s2)
                Ql = sb("Ql", [128, 4, 4, 128], BF16, s2)
                ytmp = sb("ytmp", [128, 2, 4, 512], F32, s2)
                for j in range(4):
                    s = j % 2
                    gs_ = range(4 * j, 4 * j + 4)
                    DMA("sp", Yb[:, :, j * 512:(j + 1) * 512], y0_s[0, :, :, j * 512:(j + 1) * 512], r=[("y0_s", 0, g) for g in gs_], w=[("Yb", 2 * j), ("Yb", 2 * j + 1)])
                    DMA("sp", ytmp[:, s, :, :], y0_s[1, :, :, j * 512:(j + 1) * 512], r=[("y0_s", 1, g) for g in gs_], w=[("ytmp", s)])
                    TTo("pool", Yb[:, :, j * 512:(j + 1) * 512], Yb[:, :, j * 512:(j + 1) * 512], ytmp[:, s, :, :], ALU.add,
                        r=[("ytmp", s), ("Yb", 2 * j), ("Yb", 2 * j + 1)], w=[("Yb", 2 * j), ("Yb", 2 * j + 1)])
                MEMSET("dve", S32[:], 0.0, w=[("S32", 0), ("S32", 1)])
                MEMSET("pool", Sb[:], 0.0, w=[("Sb", 0), ("Sb", 1)])
                stepn = [0]

                def chain_step(d, cidx, own):
                    sl = stepn[0] % 4
                    stepn[0] += 1
                    DMA("sp", M0l[:, sl, :, :], m0_s[d, cidx], r=[("m0_s", d, cidx - cidx % 2)], w=[("M0l", sl)])
                    DMA("sp", N0l[:, sl, :, :], n0_s[d, cidx], r=[("n0_s", d, cidx - cidx % 2)], w=[("N0l", sl)])
                    bank = 4 + d
                    if own:
                        blk = cidx // 2; c2 = cidx % 2
                        qsl = (blk % 2) * 2 + d
                        if (d == 0 and c2 == 0) or (d == 1 and c2 == 1):
                            DMA("sp", Ql[:, qsl, :, :], q_s[d, blk], r=[("q_s", d, blk)], w=[("Ql", qsl)])
                        yb = 0 + d
                        for h in range(8):
                            fc = h // 2; h2 = h % 2; rows = slice(h2 * 64, h2 * 64 + 64)
                            MM(psA[yb][rows, fc * 64:(fc + 1) * 64], Sb[rows, d, fc, :], Ql[rows, qsl, fc, c2 * 64:c2 * 64 + 64],
                               r=[("Sb", d), ("Ql", qsl)], w=[pk(yb)])
                        yv = Yb[:, :, cidx * 64:(cidx + 1) * 64]
                        TTo("dve", yv, yv, psA[yb][:, 0:256].rearrange("p (f t) -> p f t", f=4), ALU.add, r=[pk(yb), ("Yb", cidx // 4)], w=[("Yb", cidx // 4)])
                    for fc in range(4):
                        o = psA[bank][:, fc * 64:(fc + 1) * 64]
                        MM(o, ident, N0l[:, sl, fc, :], start=True, stop=False, r=["cstb", ("N0l", sl)], w=[pk(bank)])
                        MM(o, M0l[:, sl, fc, :], Sb[:, d, fc, :], start=False, stop=True, r=[("M0l", sl), ("Sb", d)], w=[pk(bank)])
                    TTo("dve", St[:, d, :, :], psA[bank][:, 0:256].rearrange("p (f i) -> p f i", f=4), S32[:, d, :, :], ALU.add,
                        r=[pk(bank), ("S32", d)], w=[("St", d)])
                    for fc in range(4):
                        TS("dve", S32[:, d, fc, :], St[:, d, fc, :], PCs[:, d, cidx, fc:fc + 1], None, ALU.mult, r=[("St", d), "PCs"], w=[("S32", d)])
                    ACT(Sb[:, d, :, :], S32[:, d, :, :], AF.Copy, r=[("S32", d)], w=[("Sb", d)])

                for cidx in range(63, 31, -1):
                    chain_step(1, cidx, own=False)
                for i in range(32):
                    chain_step(0, i, own=True)
                    chain_step(1, 31 - i, own=True)
            if upto == 2:
                P.disabled = True

            if True:
                s3 = s23
                wg = sb("wg", [128, 8, 2048], BF16, s3)
                wua = sb("wua", [128, 4, D], BF16, s3); wur = sb("wur", [128, 4, D], BF16, s3); wout = sb("wout", [128, 8, D], BF16, s3)
                for kc in range(8):
                    DMA("pool", wg[:, kc, :], wgt[kc * 128:(kc + 1) * 128, :], w=["wg"])
                    DMA("pool", wout[:, kc, :], wout_d[kc * 128:(kc + 1) * 128, :], w=["wout"])
                for kc in range(4):
                    DMA("pool", wua[:, kc, :], wua_d[kc * 128:(kc + 1) * 128, :], w=["wua"])
                    DMA("pool", wur[:, kc, :], wur_d[kc * 128:(kc + 1) * 128, :], w=["wur"])
                xt = sb("xt3", [128, 2, D], F32, s3); hb = sb("hb3", [128, 1, D], BF16, s3)
                ssq = sb("ssq3", [128, 2], F32, s3); rstd = sb("rstd3", [128, 2], F32, s3)
                hT = sb("hT3", [128, 8, TT], BF16, s3)
                yc = sb("yc", [128, TT], F32, s3); ysq = sb("ysq", [128, TT], F32, s3); yr = sb("yr", [128, TT], F32, s3)
                rwT = sb("rwT", [128, 4, TT], BF16, s3)
                gates = sb("gates", [128, 16, TT], BF16, s3)
                mg = sb("mg", [128, 8, TT], BF16, s3); m1 = sb("m1", [128, TT], F32, s3); m2 = sb("m2", [128, TT], F32, s3)
                x1b = sb("x1b", [128, 2, D], F32, s3)
                xres = sb("xres", [128, 2, D], F32, s3)
                ao_l = sb("ao_l", [128, 4, TT], BF16, s3); bv_l = sb("bv_l", [128, 4, TT], BF16, s3); sgl_l = sb("sgl_l", [128, TT], BF16, s3)
                onesf = cstf[:, 1, :]
                for ti in (3, 4, 2, 5, 1, 6, 0, 7):
                    t0 = ti * TT
                    DMA("sp", ao_l[:], ao_s[:, :, t0:t0 + TT], r=[("ao_s", ti)], w=["ao_l"])
                    DMA("sp", bv_l[:], bv_s[:, :, t0:t0 + TT], r=[("bv_s", ti, fc) for fc in range(4)], w=["bv_l"])
                    DMA("sp", sgl_l[:], sgl_s[:, t0:t0 + TT], r=[("sgl_s", ti)], w=["sgl_l"])
                    for fc in range(4):
                        Yv = Yb[:, fc, t0:t0 + TT]
                        MM(psA[0][:, 0:TT], onesf, Yv, r=["cstf", ("Yb", ti)], w=[pk(0)])
                        TTo("dve", yc[:], Yv, psA[0][:, 0:TT], ALU.subtract, r=[("Yb", ti), pk(0)], w=["yc"])
                        ACT(ysq[:], yc[:], AF.Square, r=["yc"], w=["ysq"])
                        MM(psA[1][:, 0:TT], onesf, ysq[:], r=["cstf", "ysq"], w=[pk(1)])
                        RSQ(yr[:], psA[1][:, 0:TT], 1.0, eps_c[:, 2:3], TT, r=[pk(1)], w=["yr"])
                        TTo("pool", yc[:], yc[:], yr[:], ALU.mult, r=["yc", "yr"], w=["yc"])
                        TS("dve", yc[:], yc[:], vc("lnw", fc), vc("lnb", fc), ALU.mult, ALU.add, r=["yc", "vec"], w=["yc"])
                        TTo("pool", yc[:], yc[:], bv_l[:, fc, :], ALU.add, r=["yc", "bv_l"], w=["yc"])
                        MM(psA[2][:, 0:TT], g2t[:, fc * 128:(fc + 1) * 128], sgl_l[:, :], r=["g2t", "sgl_l"], w=[pk(2)])
                        TTo("dve", rwT[:, fc, :], yc[:], psA[2][:, 0:TT], ALU.mult, r=["yc", pk(2)], w=["rwT"])
                    xaps = [(xs[1 + t0 + 128 * i: 1 + t0 + 128 * i + 128, :], 128, 128 * i) for i in range(NBT)]
                    make_hT("hT3", xaps, None, hT, "nmix", "hT3", xt, hb, ssq, rstd)
                    for gc in range(16):
                        bank = gc % 2
                        for kc in range(8):
                            MM(psA[bank][:, 0:TT], wg[:, kc, gc * 128:(gc + 1) * 128], hT[:, kc, :], start=(kc == 0), stop=(kc == 7), r=["wg", ("hT3", kc)], w=[pk(bank)])
                        ACT(gates[:, gc, :], psA[bank][:, 0:TT], AF.Tanh, scale=0.5, r=[pk(bank)], w=["gates"])
                    for oc in range(8):
                        for kc in range(4):
                            MM(psA[2][:, 0:TT], wua[:, kc, oc * 128:(oc + 1) * 128], ao_l[:, kc, :], start=(kc == 0), stop=(kc == 3), r=["wua", "ao_l"], w=[pk(2)])
                        for kc in range(4):
                            MM(psA[3][:, 0:TT], wur[:, kc, oc * 128:(oc + 1) * 128], rwT[:, kc, :], start=(kc == 0), stop=(kc == 3), r=["wur", "rwT"], w=[pk(3)])
                        STT("dve", m1[:], gates[:, oc, :], 1.0, psA[2][:, 0:TT], ALU.add, ALU.mult, r=[pk(2), "gates"], w=["m1"])
                        STT("dve", m2[:], gates[:, 8 + oc, :], 1.0, psA[3][:, 0:TT], ALU.add, ALU.mult, r=[pk(3), "gates"], w=["m2"])
                        TTo("pool", mg[:, oc, :], m1[:], m2[:], ALU.add, r=["m1", "m2"], w=["mg"])
                    for bi in range(NBT):
                        s = bi % 2
                        tok = t0 + bi * 128
                        DMA("sp", xres[:, s, :], xs[1 + tok:1 + tok + 128, :], w=[("xres", s)])
                        for hf in range(2):
                            bank = 4 + hf
                            for kc in range(8):
                                MM(psA[bank][:, :], mg[:, kc, bi * 128:(bi + 1) * 128], wout[:, kc, hf * 512:(hf + 1) * 512], start=(kc == 0), stop=(kc == 7), r=["mg", "wout"], w=[pk(bank)])
                            STT("dve", x1b[:, s, hf * 512:(hf + 1) * 512], psA[bank][:, :], 0.5, xres[:, s, hf * 512:(hf + 1) * 512], ALU.mult, ALU.add, r=[pk(bank), ("xres", s)], w=[("x1b", s)])
                        DMA("sp", x1_s[tok:tok + 128, :], x1b[:, s, :], r=[("x1b", s)], w=[("x1_s", tok // 128)])
        if upto == 3:
            P.disabled = True
        barrier()

        with ExitStack() as s4:
            wf1 = sb("wf1", [128, 8, 4096], BF16, s4); wf2 = sb("wf2", [128, 32, D], BF16, s4)
            wpg = sb("wpg", [128, 8, D], BF16, s4); wple = sb("wple", [128, 2, D], BF16, s4)
            for cg in range(4):
                for kc in range(8):
                    DMA("pool", wf1[:, kc, cg * 1024:(cg + 1) * 1024], wff1_d[kc * 128:(kc + 1) * 128, cg * 1024:(cg + 1) * 1024], w=[("wf1", cg)])
            for kc in range(32):
                DMA("pool", wf2[:, kc, :], wff2_d[kc * 128:(kc + 1) * 128, :], w=[("wf2", kc // 8)])
            for kc in range(8):
                DMA("pool", wpg[:, kc, :], wpg_d[kc * 128:(kc + 1) * 128, :], w=["wpg"])
            for kc in range(2):
                DMA("pool", wple[:, kc, :], wple_d[kc * 128:(kc + 1) * 128, :], w=["wple"])
            xt = sb("xt4", [128, 2, D], F32, s4); hb = sb("hb4", [128, 1, D], BF16, s4)
            ssq = sb("ssq4", [128, 4], F32, s4); rstd = sb("rstd4", [128, 4], F32, s4)
            hTf = sb("hT4", [128, 2, 8, 128], BF16, s4)
            hTp = sb("hT4p", [128, 1, 8, 128], BF16, s4)
            act = sb("act", [128, 2, 32, 128], BF16, s4)
            rl = sb("rl", [128, 4, 128], F32, s4)
            x2 = sb("x2", [128, 1, D], F32, s4)
            pt = sb("pt", [128, 2, 256], F32, s4); pb16 = sb("pb16", [128, 2, 256], BF16, s4); pT = sb("pT", [128, 2, 2, 128], BF16, s4)
            sgp = sb("sgp", [128, 1, 512], F32, s4)
            nhb = [0]

            def norm_T(src, gain, skey, hTt, okey, idx):
                hs = 0
                nhb[0] += 1
                ACT(junk[:], src, AF.Square, accum=ssq[:, idx:idx + 1], r=[skey], w=["junk", ("ssq4", idx)])
                RSQ(rstd[:, idx:idx + 1], ssq[:, idx:idx + 1], 1.0 / D, eps_c[:, 0:1], 1, r=[("ssq4", idx)], w=[("rstd4", idx)])
                TS("dve", hb[:, hs, :], src, rstd[:, idx:idx + 1], None, ALU.mult, r=[skey, ("rstd4", idx)], w=[("hb4o", hs)])
                for kc in range(8):
                    TR(psT[kc % 2][:, (kc // 2) * 128:(kc // 2) * 128 + 128], hb[:, hs, kc * 128:(kc + 1) * 128], ident, r=[("hb4o", hs), "cstb"], w=[pkT(kc % 2)])
                for kc in range(8):
                    srcp = psT[kc % 2][:, (kc // 2) * 128:(kc // 2) * 128 + 128]
                    if kc % 2 == 0:
                        ACT(hTt[:, kc, :], srcp, AF.Copy, scale=vc(gain, kc), r=[pkT(kc % 2), "vec"], w=[(okey, kc)])
                    else:
                        TS("dve", hTt[:, kc, :], srcp, vc(gain, kc), None, ALU.mult, r=[pkT(kc % 2), "vec"], w=[(okey, kc)])

            for blk in range(NB_OWN):
                tok = blk * 128
                s = blk % 2
                DMA("sp", xt[:, s, :], x1_s[tok:tok + 128, :], r=[("x1_s", blk)], w=[("xt4", s)])
                DMA("sp", pt[:, s, :], pp[tok:tok + 128, :], w=[("pt", s)])
                hk = ("hT4", s)
                norm_T(xt[:, s, :], "nffn", ("xt4", s), hTf[:, s], hk, s)
                for hc in range(32):
                    bank = hc % 4
                    for kc in range(8):
                        MM(psA[bank][:, 0:128], wf1[:, kc, hc * 128:(hc + 1) * 128], hTf[:, s, kc, :], start=(kc == 0), stop=(kc == 7),
                           r=[("wf1", hc // 8), (hk, kc)], w=[pk(bank)])
                    ACT(rl[:, bank, :], psA[bank][:, 0:128], AF.Relu, r=[pk(bank)], w=[("rl", bank)])
                    TTo("dve", act[:, s, hc, :], rl[:, bank, :], rl[:, bank, :], ALU.mult, r=[("rl", bank)], w=[("act", s)])
                for hf in range(2):
                    bank = 4 + hf
                    for hc in range(32):
                        MM(psA[bank][:, :], act[:, s, hc, :], wf2[:, hc, hf * 512:(hf + 1) * 512], start=(hc == 0), stop=(hc == 31),
                           r=[("act", s), ("wf2", hc // 8)], w=[pk(bank)])
                    TTo("dve", x2[:, 0, hf * 512:(hf + 1) * 512], psA[bank][:, :], xt[:, s, hf * 512:(hf + 1) * 512], ALU.add, r=[pk(bank), ("xt4", s)], w=["x2"])
                hkp = "hT4p"
                norm_T(x2[:, 0, :], "nple", "x2", hTp[:, 0], hkp, 2 + s)
                CP("pool", pb16[:, s, :], pt[:, s, :], r=[("pt", s)], w=[("pb16", s)])
                for kc in range(2):
                    TR(psT[0][:, 512 + kc * 128:512 + (kc + 1) * 128], pb16[:, s, kc * 128:(kc + 1) * 128], ident, r=[("pb16", s), "cstb"], w=[pkT(0)])
                CP("dve", pT[:, s, :, :], psT[0][:, 512:768].rearrange("p (k t) -> p k t", k=2), r=[pkT(0)], w=[("pT", s)])
                for hf in range(2):
                    gb = 0 + hf
                    pbk = 2 + hf
                    for kc in range(8):
                        MM(psA[gb][:, :], hTp[:, 0, kc, :], wpg[:, kc, hf * 512:(hf + 1) * 512], start=(kc == 0), stop=(kc == 7), r=[(hkp, kc), "wpg"], w=[pk(gb)])
                    for kc in range(2):
                        MM(psA[pbk][:, :], pT[:, s, kc, :], wple[:, kc, hf * 512:(hf + 1) * 512], start=(kc == 0), stop=(kc == 1), r=[("pT", s), "wple"], w=[pk(pbk)])
                    ACT(sgp[:, 0, :], psA[gb][:, :], AF.Tanh, scale=0.5, r=[pk(gb)], w=["sgp"])
                    STT("dve", sgp[:, 0, :], sgp[:, 0, :], 1.0, psA[pbk][:, :], ALU.add, ALU.mult, r=[pk(pbk), "sgp"], w=["sgp"])
                    STT("dve", xt[:, s, hf * 512:(hf + 1) * 512], sgp[:, 0, :], 0.5, x2[:, 0, hf * 512:(hf + 1) * 512], ALU.mult, ALU.add,
                        r=["sgp", "x2", ("xt4", s)], w=[("ob", s)])
                DMA("sp", out_d[tok:tok + 128, :], xt[:, s, :], r=[("ob", s), ("xt4", s)], w=[("out", blk), ("xt4", s)])
            P.add("sp", None, [("out", i) for i in range(NB_OWN)], ())
        P.emit(nc, gs)
    return nc


def _consts():
    cst = np.zeros((128, 12, 128), np.float32)
    i = np.arange(128)
    cst[:, 0, :] = np.eye(128)
    cst[:, 1, :] = (i[:, None] // 64 == i[None, :] // 64)
    perm = np.zeros((128, 128), np.float32)
    for m in range(128):
        j = m % 64
        if j < 8:
            perm[m + 8, m] = 1.0
        elif j < 16:
            perm[m - 8, m] = 1.0
    cst[:, 2, :] = perm
    cst[:, 3, :] = cst[:, 1, :]
    same = (i[:, None] // 64 == i[None, :] // 64)
    cst[:, 4, :] = same & (i[:, None] < i[None, :])
    cst[:, 5, :] = same & (i[:, None] <= i[None, :])
    cst[:, 6, :] = same & (i[:, None] > i[None, :])
    cst[:, 7, :] = same & (i[:, None] >= i[None, :])
    cst[:, 8, :] = 1.0
    cst[:, 9, :] = cst[:, 1, :] / 64.0
    cst[:, 10, :] = (i[:, None] >= i[None, :])
    cst[:, 11, :] = (i[:, None] <= i[None, :])
    tri = np.zeros((128, 2, 128), np.float32)
    tri[:, 0, :] = 0.5 * LWS * cst[:, 5, :]
    tri[:, 1, :] = 0.5 * LWS * cst[:, 7, :]
    return cst, tri


def _rope_tables(pos):
    half = 8
    inv = np.power(np.float32(500000.0), -np.arange(half, dtype=np.float32) * 2.0 / 16.0).astype(np.float32)
    ang = pos.astype(np.float32)[None, :] * inv[:, None]
    cos = np.cos(ang).astype(np.float32); sin = np.sin(ang).astype(np.float32)
    n = pos.shape[0]
    tab = np.zeros((128, 2, n), np.float32)
    for hb in range(2):
        b = hb * 64
        tab[b:b + 64, 0, :] = 1.0
        tab[b:b + 8, 0, :] = cos; tab[b + 8:b + 16, 0, :] = cos
        tab[b:b + 8, 1, :] = -sin; tab[b + 8:b + 16, 1, :] = sin
    return tab


def _col128(v):
    return np.ascontiguousarray(np.asarray(v, np.float32).reshape(-1, 128).T)


_NC_CACHE = {}


def kernel(x, p, norm_mix, w_in, shift_mu, q_norm, k_norm, sink, w0, w2, a0, a2, g2, k_k, k_a, r_k,
           lnx_w, lnx_b, w_up_attn, w_up_rwkv, w_out, norm_ffn, w_ff1, w_ff2, norm_ple, w_ple_gate, w_ple):
    f = lambda a: np.asarray(a, np.float32)
    x = f(x); p = f(p)[0]; w_in = f(w_in)[0]
    qcols = []
    for c in range(4):
        qcols += list(range(c * 64, c * 64 + 64)) + list(range((4 + c) * 64, (4 + c) * 64 + 64))
    O_K, O_V, O_R = 512, 640, 768
    cols = qcols + list(range(O_K, O_K + 128)) + list(range(O_V, O_V + 128))
    rw = O_R
    wa = list(range(rw + 1536, rw + 1536 + 128)); gl = list(range(rw + 1664, rw + 1792))
    cols += wa + gl
    mu_idx = [12, 13]
    for fc in range(4):
        for part in range(3):
            cols += list(range(rw + part * 512 + fc * 128, rw + part * 512 + fc * 128 + 128))
            mu_idx.append(part * 4 + fc)
    wsw = np.ascontiguousarray(w_in[:, cols])
    wgt = np.ascontiguousarray(w_in[:, O_R + 1792:])
    mu_c = _col128(f(shift_mu)[0])[:, mu_idx]
    cst, tri = _consts()
    cst_in = cst.copy()
    cst_in_f = cst.copy()
    cst_in[:, 1, :] = cst[:, 1, :]
    sink_ = f(sink)[0]
    sinkcols = np.zeros((128, 4), np.float32)
    for m in range(4):
        sinkcols[0:64, m] = sink_[2 * m]; sinkcols[64:128, m] = sink_[2 * m + 1]
    in_maps = []
    for c in range(8):
        b, hf = c // 2, c % 2
        xb = x[b]; pb_ = p[b]
        if hf == 0:
            xl = xb; pl = pb_[:HALF]; pos = np.arange(0, HALF + 512); dA, dB = 0, 1
        else:
            xl = xb[::-1]; pl = pb_[HALF:][::-1]; pos = SEQ - 1 - np.arange(0, HALF + 512); dA, dB = 1, 0
        xs = np.zeros((SEQ + 2, D), np.float32); xs[1:SEQ + 1] = xl
        dd = [dA, dB]
        vec = np.zeros((128, NVEC_IN), np.float32)
        def put(n, arr):
            vec[:, VC[n]:VC[n] + arr.shape[1]] = arr
        put("mu", mu_c); put("k_k", _col128(f(k_k)[0])); put("k_a", _col128(f(k_a)[0]))
        put("rkA", _col128(f(r_k)[0, dA].reshape(-1))); put("rkB", _col128(f(r_k)[0, dB].reshape(-1)))
        put("lnw", _col128(f(lnx_w)[0])); put("lnb", _col128(f(lnx_b)[0]))
        put("a0A", _col128(f(a0)[0, dA])); put("a0B", _col128(f(a0)[0, dB]))
        put("qg", np.tile(f(q_norm)[0], 2)[:, None]); put("kg", np.tile(f(k_norm)[0], 2)[:, None])
        put("nmix", _col128(f(norm_mix)[0])); put("nffn", _col128(f(norm_ffn)[0])); put("nple", _col128(f(norm_ple)[0]))
        put("sink", sinkcols)
        w2aug = np.stack([np.concatenate([f(w2)[0, d_], f(w0)[0, d_][None, :]], 0) for d_ in dd])
        a2s = np.stack([f(a2)[0, d_] for d_ in dd])
        cst_c = cst.copy()
        m = {"xs": xs, "pp": np.ascontiguousarray(pl), "wsw": wsw, "wgt": wgt, "vec": vec, "w2aug": np.ascontiguousarray(w2aug),
             "a2": np.ascontiguousarray(a2s), "g2": f(g2)[0], "wua": f(w_up_attn)[0], "wur": f(w_up_rwkv)[0], "wout": f(w_out)[0],
             "wff1": f(w_ff1)[0], "wff2": f(w_ff2)[0], "wpg": f(w_ple_gate)[0], "wple": f(w_ple)[0],
             "rope": _rope_tables(pos), "cst": cst_c, "tri": tri, "crow": np.ascontiguousarray(tri.sum(0)[None])}
        in_maps.append(m)
    if _NC_CACHE.get("prep_only"):
        return in_maps
    if "nc" not in _NC_CACHE:
        _NC_CACHE["nc"] = build_program()
    res = run_bass_kernel_spmd(_NC_CACHE["nc"], in_maps, core_ids=list(range(8)))
    out = np.zeros((4, SEQ, D), np.float32)
    for c in range(8):
        b, hf = c // 2, c % 2
        o = res.results[c]["out"]
        if hf == 0:
            out[b, :HALF] = o
        else:
            out[b, HALF:] = o[::-1]
    return out
```

```python
import os
import numpy as np
from contextlib import ExitStack
import concourse.bass as bass
import concourse.mybir as mybir
from concourse.bass_utils import run_bass_kernel_spmd

F32 = mybir.dt.float32
BF16 = mybir.dt.bfloat16
ALU = mybir.AluOpType
AF = mybir.ActivationFunctionType

SEM_EPOCH = 3000
DMA_SLOTS = 8

D = 1024; SEQ = 4096; HALF = 2048; NB_OWN = 16
TT = 256; NBT = 2; NTH = 8
LWS = -0.6065306597126334


class Op:
    __slots__ = ("eng", "fn", "reads", "writes", "dma", "deps", "flag", "cnt", "slot", "slotval", "idx", "xdeps",
                 "dur", "rg", "tbl", "seg", "alld", "pos")

    def __init__(self, eng, fn, reads, writes, dma):
        self.eng = eng; self.fn = fn; self.reads = reads; self.writes = writes; self.dma = dma
        self.deps = set(); self.flag = False; self.cnt = None; self.slot = None; self.slotval = None
        self.xdeps = (); self.dur = 0.2; self.rg = None; self.tbl = None; self.seg = 0


SYNC_LAT = 0.35


class Prog:
    def __init__(self):
        self.ops = []
        self.tags = []
        self.last_barrier = 0
        self.seg = 0
        self.reorder = True

    def add(self, eng, fn, r=(), w=(), dma=False, dur=0.2, rg=None, tbl=None):
        op = Op(eng, fn, tuple(r), tuple(w), dma)
        op.dur = dur; op.rg = rg; op.tbl = tbl; op.seg = self.seg
        self.tags.append(getattr(self, "tag", None)) if not getattr(self, "disabled", False) else None
        if getattr(self, "disabled", False):
            op.idx = -1
            return op
        op.idx = len(self.ops)
        self.ops.append(op)
        return op

    def all_deps(self):
        last_w = {}
        readers = {}
        ops = self.ops
        for op in ops:
            d = {}
            for j in op.xdeps:
                d[j] = "raw"
            for k in op.reads:
                j = last_w.get(k)
                if j is not None:
                    d[j] = "raw"
            for k in op.writes:
                j = last_w.get(k)
                if j is not None and j not in d:
                    d[j] = "waw"
                for j in readers.get(k, ()):
                    if j != op.idx and j not in d:
                        d[j] = "war"
            for k in op.reads:
                if isinstance(k, tuple) and k[0] in ("ps", "psT"):
                    for j in readers.get(k, ()):
                        if ops[j].eng != op.eng and j not in d:
                            d[j] = "raw"
            for k in op.reads:
                readers.setdefault(k, []).append(op.idx)
            for k in op.writes:
                last_w[k] = op.idx
                readers[k] = []
            op.alld = d

    def schedule(self):
        ops = self.ops
        n = len(ops)
        if not self.reorder:
            return list(range(n))
        fin = [0.0] * n
        self.fin = fin
        self.bind = {}; self._last_on = {}; self.start = {}
        succ = [[] for _ in range(n)]
        ndep = [0] * n
        for op in ops:
            for j in op.alld:
                succ[j].append(op.idx)
            ndep[op.idx] = len(op.alld)
        order = []
        free_t = {}
        cur_tbl = [None]
        segs = {}
        for op in ops:
            segs.setdefault(op.seg, []).append(op.idx)
        done = [False] * n
        ksegs = os.environ.get("KREORD")
        for sg in sorted(segs):
            members = segs[sg]
            if ksegs is not None and str(sg) not in ksegs.split(","):
                for i in members:
                    t0_ = max([free_t.get(ops[i].eng, 0.0)] + [fin[j] for j in ops[i].alld])
                    fin[i] = t0_ + ops[i].dur
                    free_t[ops[i].eng] = fin[i]
                    done[i] = True
                    order.append(i)
                continue
            inseg = set(members)
            ready = {}
            rt = {}
            remaining = len(members)
            cnt_un = {}
            for i in members:
                c = sum(1 for j in ops[i].alld if not done[j])
                cnt_un[i] = c
                if c == 0:
                    ready.setdefault(ops[i].eng, []).append(i)
            def ready_time(i):
                op = ops[i]
                t = 0.0
                for j, kind in op.alld.items():
                    p = ops[j]
                    lat = SYNC_LAT if (p.dma or p.eng != op.eng or kind == "raw") else 0.0
                    if p.eng == op.eng and op.eng == "pe":
                        lat = 0.0
                    t = max(t, fin[j] + lat)
                return t
            for e in ready:
                for i in ready[e]:
                    rt[i] = ready_time(i)
            while remaining:
                best = None
                for e, lst in ready.items():
                    if not lst:
                        continue
                    ft = free_t.get(e, 0.0)
                    bi = None
                    for i in lst:
                        st = max(ft, rt[i])
                        if e == "act" and ops[i].tbl is not None and cur_tbl[0] is not None and ops[i].tbl != cur_tbl[0]:
                            st += 1.3
                        key = (st, i)
                        if bi is None or key < bi:
                            bi = key
                    if best is None or bi < best[0]:
                        best = (bi, e)
                (st, i), e = best
                op = ops[i]
                ready[e].remove(i)
                if e == "act" and op.tbl is not None:
                    cur_tbl[0] = op.tbl
                fin[i] = st + op.dur
                if getattr(self, "trace_bind", False):
                    bj = None; bt_ = -1.0
                    for j in op.alld:
                        if fin[j] > bt_:
                            bt_ = fin[j]; bj = j
                    prev_e = self._last_on.get(e)
                    if prev_e is not None and free_t.get(e, 0.0) >= rt[i] - 1e-9:
                        self.bind[i] = ("eng", prev_e)
                    else:
                        self.bind[i] = ("dep", bj)
                    self._last_on[e] = i
                    self.start[i] = st
                if op.dma:
                    free_t[e] = st + 0.1
                else:
                    free_t[e] = fin[i]
                done[i] = True
                order.append(i)
                remaining -= 1
                for k in succ[i]:
                    if k in inseg:
                        cnt_un[k] -= 1
                        if cnt_un[k] == 0:
                            rt[k] = ready_time(k)
                            ready.setdefault(ops[k].eng, []).append(k)
        self.est_total = max(fin) if fin else 0.0
        self.seg_end = {sg: max(fin[i] for i in segs[sg]) for sg in segs}
        return order

    def analyse(self):
        self.all_deps()
        order = self.schedule()
        ops = self.ops
        for pos, i in enumerate(order):
            ops[i].pos = pos
        self.order = order
        pe = [i for i in order if ops[i].eng == "pe"]
        extra = {}
        for a in range(len(pe)):
            oa = ops[pe[a]]
            if oa.rg is None:
                continue
            for b in range(a + 1, a + 3):
                if b >= len(pe):
                    break
                ob = ops[pe[b]]
                if ob.rg is None:
                    continue
                if (oa.rg[1] < ob.rg[0] or ob.rg[1] < oa.rg[0]) and set(oa.writes) & set(ob.writes):
                    extra.setdefault(pe[b], set()).add(pe[a])
        for op in ops:
            need = set()
            for j, kind in op.alld.items():
                p = ops[j]
                if p.dma:
                    need.add(j)
                elif p.eng == op.eng and not op.dma:
                    if kind == "raw" and op.eng != "pe":
                        need.add(j); p.flag = True
                else:
                    need.add(j); p.flag = True
            for j in extra.get(op.idx, ()):
                need.add(j); ops[j].flag = True
            op.deps = need
        cnt = {}
        nd = {}
        for i in order:
            op = ops[i]
            if op.dma:
                k = nd.get(op.eng, 0)
                nd[op.eng] = k + 1
                op.slot = (op.eng, k % DMA_SLOTS)
                op.slotval = 16 * (k // DMA_SLOTS + 1)
            elif op.fn is not None and op.flag:
                cnt[op.eng] = cnt.get(op.eng, 0) + 1
                op.cnt = cnt[op.eng]
        self.nflag = cnt
        self.ndma = nd

    def emit(self, nc, stack):
        self.analyse()
        engs = ["pe", "act", "dve", "pool", "sp"]
        sems = {}
        for e in engs:
            n = self.nflag.get(e, 0)
            for ep in range(n // SEM_EPOCH + 1):
                sems[(e, ep)] = stack.enter_context(nc.semaphore(f"s_{e}_{ep}"))
        dsem = {}
        for e, n in self.ndma.items():
            for s in range(min(n, DMA_SLOTS)):
                dsem[(e, s)] = stack.enter_context(nc.semaphore(f"d_{e}_{s}"))
        block = stack.enter_context(nc.Block())
        ops = self.ops
        per_eng = {e: [ops[i] for i in self.order if ops[i].eng == e] for e in engs}

        def sem_of(p):
            if p.dma:
                return dsem[p.slot], p.slotval
            c = p.cnt - 1
            return sems[(p.eng, c // SEM_EPOCH)], (c % SEM_EPOCH) + 1

        def run(engobj, lst):
            seen = {}
            dma_hist = {}
            for op in lst:
                waits = {}
                for j in op.deps:
                    s, v = sem_of(ops[j])
                    key = id(s)
                    if seen.get(key, 0) >= v:
                        continue
                    if key not in waits or waits[key][1] < v:
                        waits[key] = (s, v)
                if op.dma:
                    prev = dma_hist.get(op.slot)
                    if prev is not None:
                        s, v = dsem[op.slot], prev
                        key = id(s)
                        if seen.get(key, 0) < v and (key not in waits or waits[key][1] < v):
                            waits[key] = (s, v)
                    dma_hist[op.slot] = op.slotval
                for key, (s, v) in waits.items():
                    engobj.wait_ge(s, v)
                    seen[key] = v
                if op.fn is None:
                    continue
                ins = op.fn(engobj)
                if op.dma:
                    ins.then_inc(dsem[op.slot], 16)
                elif op.flag:
                    s, v = sem_of(op)
                    ins.then_inc(s, 1)

        if per_eng["pe"]:
            @block.tensor
            def _(e):
                run(e, per_eng["pe"])
        if per_eng["act"]:
            @block.scalar
            def _(e):
                run(e, per_eng["act"])
        if per_eng["dve"]:
            @block.vector
            def _(e):
                run(e, per_eng["dve"])
        if per_eng["pool"]:
            @block.gpsimd
            def _(e):
                run(e, per_eng["pool"])
        if per_eng["sp"]:
            @block.sync
            def _(e):
                run(e, per_eng["sp"])


VC = {}
_o = 0
for _n, _c in [("mu", 14), ("k_k", 4), ("k_a", 4), ("rkA", 4), ("rkB", 4), ("lnw", 4), ("lnb", 4),
               ("a0A", 4), ("a0B", 4), ("qg", 1), ("kg", 1), ("nmix", 8), ("nffn", 8), ("nple", 8),
               ("sink", 4), ("omm", 14), ("hmu", 14), ("nkk", 4), ("omka", 4), ("esink", 4), ("ha0A", 4), ("ha0B", 4), ("hka", 4), ("c2ka", 4)]:
    VC[_n] = _o
    _o += _c
NVEC_IN = VC["omm"]
NVEC = _o

N_SW = 20


def build_program(dbg=None, upto=99, ntiles_other=None, ntiles_own=None):
    nc = bass.Bass("TRN2", target_bir_lowering=False)
    P = Prog()

    def din(name, shape):
        return nc.dram_tensor(name, list(shape), F32, kind="ExternalInput").ap()

    xs = din("xs", [SEQ + 2, D])
    pp = din("pp", [HALF, 256])
    wsw = din("wsw", [D, N_SW * 128])
    wgt = din("wgt", [D, 2048])
    vec_d = din("vec", [128, NVEC_IN])
    w2aug_d = din("w2aug", [2, 65, 512])
    a2_d = din("a2", [2, 64, 512])
    g2_d = din("g2", [128, 512])
    wua_d = din("wua", [512, D]); wur_d = din("wur", [512, D]); wout_d = din("wout", [D, D])
    wff1_d = din("wff1", [D, 4096]); wff2_d = din("wff2", [4096, D])
    wpg_d = din("wpg", [D, D]); wple_d = din("wple", [256, D])
    rope_d = din("rope", [128, 2, HALF + 512])
    cst_d = din("cst", [128, 12, 128])
    tri_d = din("tri", [128, 2, 128])
    crow_d = din("crow", [1, 2, 128])
    out_d = nc.dram_tensor("out", [HALF, D], F32, kind="ExternalOutput").ap()
    m0_s = nc.dram_tensor("m0_s", [2, 64, 128, 4, 128], BF16, kind="Internal").ap()
    n0_s = nc.dram_tensor("n0_s", [2, 64, 128, 4, 64], BF16, kind="Internal").ap()
    q_s = nc.dram_tensor("q_s", [2, 16, 128, 4, 128], BF16, kind="Internal").ap()
    x1_s = nc.dram_tensor("x1_s", [HALF, D], F32, kind="Internal").ap()
    y0_s = nc.dram_tensor("y0_s", [2, 128, 4, HALF], F32, kind="Internal").ap()
    ao_s = nc.dram_tensor("ao_s", [128, 4, HALF], BF16, kind="Internal").ap()
    bv_s = nc.dram_tensor("bv_s", [128, 4, HALF], BF16, kind="Internal").ap()
    sgl_s = nc.dram_tensor("sgl_s", [128, HALF], BF16, kind="Internal").ap()
    dbg_out = None
    if dbg is not None:
        dbg_out = nc.dram_tensor("dbg", [128, 16384], F32, kind="ExternalOutput").ap()

    def fsz(ap):
        n = 1
        for d_ in ap.shape[1:]:
            n *= d_
        return n

    def rgof(ap):
        b = ap.base_partition()
        return (b // 32, (b + ap.shape[0] - 1) // 32)

    def MM(out, lhsT, rhs, start=True, stop=True, r=(), w=()):
        dur = max(0.064, 0.02 + fsz(out) / 2000.0) * (4 if lhsT.dtype == F32 else 1)
        P.add("pe", lambda e: e.matmul(out, lhsT, rhs, start=start, stop=stop), r, w, dur=dur, rg=rgof(lhsT))

    def TR(out, in_, idn, r=(), w=()):
        P.add("pe", lambda e: e.transpose(out, in_, idn), r, w, dur=0.064, rg=rgof(in_))

    def ACT(out, in_, func, bias=None, scale=None, accum=None, r=(), w=()):
        kw = {}
        if bias is not None:
            kw["bias"] = bias
        if scale is not None:
            kw["scale"] = scale
        if accum is not None:
            kw["accum_out"] = accum
        tbl = "ln" if func == AF.Ln else ("tanh" if func == AF.Tanh else None)
        P.add("act", lambda e: e.activation(out=out, in_=in_, func=func, **kw), r, w, dur=0.2 + fsz(out) / 1400.0, tbl=tbl)

    def vdur(eng, out):
        return (0.1 + fsz(out) / 960.0) if eng == "dve" else (0.3 + fsz(out) / 500.0)

    def TTo(eng, out, a, b, op, r=(), w=()):
        P.add(eng, lambda e: e.tensor_tensor(out, a, b, op), r, w, dur=vdur(eng, out))

    def TS(eng, out, a, s1, s2, op0, op1=None, r=(), w=()):
        if op1 is None:
            P.add(eng, lambda e: e.tensor_scalar(out, a, s1, None, op0), r, w, dur=vdur(eng, out))
        else:
            P.add(eng, lambda e: e.tensor_scalar(out, a, s1, s2, op0, op1), r, w, dur=vdur(eng, out))

    def STT(eng, out, in0, sc, in1, op0, op1, r=(), w=()):
        P.add(eng, lambda e: e.scalar_tensor_tensor(out, in0, sc, in1, op0, op1), r, w, dur=vdur(eng, out))

    def CP(eng, out, in_, r=(), w=()):
        P.add(eng, lambda e: e.tensor_copy(out, in_), r, w, dur=vdur(eng, out))

    def RCP(out, in_, r=(), w=()):
        P.add("dve", lambda e: e.reciprocal(out, in_), r, w, dur=vdur("dve", out) * 2)

    def MEMSET(eng, ap, val, w=()):
        P.add(eng, lambda e: e.memset(ap, val), (), w, dur=vdur(eng, ap))

    def DMA(eng, out, in_, r=(), w=()):
        nbytes = fsz(out) * out.shape[0] * (4 if out.dtype == F32 else 2)
        P.add(eng, lambda e: e.dma_start(out=out, in_=in_), r, w, dma=True, dur=2.0 + nbytes / 100000.0)

    with ExitStack() as gs:
        def sb(name, shape, dt=F32, st=None):
            return (st or gs).enter_context(nc.sbuf_tensor("sb_" + name, list(shape), dt))

        psA = [gs.enter_context(nc.psum_tensor(f"psA{i}", [128, 512], F32)) for i in range(6)]
        psT = [gs.enter_context(nc.psum_tensor(f"psT{i}", [128, 1024], BF16)) for i in range(2)]
        pk = lambda i: ("ps", i)
        pkT = lambda i: ("psT", i)

        vec = sb("vec", [128, NVEC])
        cstb = sb("cstb", [128, 12, 128], BF16)
        cstf = sb("cstf", [128, 3, 128])
        crow = sb("crow", [1, 2, 128])
        tri = sb("tri", [128, 2, 128])
        eps_c = sb("eps_c", [128, 4])
        junk = sb("junk", [128, D], BF16)
        DMA("sp", vec[:, 0:NVEC_IN], vec_d, w=["vec"])
        DMA("pool", cstb[:], cst_d, w=["cstb"])
        DMA("sp", cstf[:, 0, :], cst_d[:, 0, :], w=["cstf"])
        DMA("sp", cstf[:, 1, :], cst_d[:, 9, :], w=["cstf"])
        DMA("sp", cstf[:, 2, :], cst_d[:, 8, :], w=["cstf"])
        DMA("sp", crow[:], crow_d, w=["crow"])
        DMA("sp", tri[:], tri_d, w=["tri"])
        MEMSET("dve", eps_c[:, 0:1], 1e-6, w=["eps"])
        MEMSET("dve", eps_c[:, 1:2], 1e-24, w=["eps"])
        MEMSET("dve", eps_c[:, 2:3], 64e-5, w=["eps"])
        ident = cstb[:, 0, :]; onesblk = cstb[:, 1, :]; perm = cstb[:, 2, :]; bdm = cstb[:, 3, :]
        identf = cstf[:, 0, :]
        vc = lambda n, i=0: vec[:, VC[n] + i:VC[n] + i + 1]
        TS("dve", vec[:, VC["omm"]:VC["omm"] + 14], vec[:, VC["mu"]:VC["mu"] + 14], -1.0, 1.0, ALU.mult, ALU.add, r=["vec"], w=["vec2"])
        TS("dve", vec[:, VC["hmu"]:VC["hmu"] + 14], vec[:, VC["mu"]:VC["mu"] + 14], 0.5, None, ALU.mult, r=["vec"], w=["vec2"])
        TS("dve", vec[:, VC["nkk"]:VC["nkk"] + 4], vec[:, VC["k_k"]:VC["k_k"] + 4], -1.0, None, ALU.mult, r=["vec"], w=["vec2"])
        TS("dve", vec[:, VC["omka"]:VC["omka"] + 4], vec[:, VC["k_a"]:VC["k_a"] + 4], -1.0, 1.0, ALU.mult, ALU.add, r=["vec"], w=["vec2"])
        ACT(vec[:, VC["esink"]:VC["esink"] + 4], vec[:, VC["sink"]:VC["sink"] + 4], AF.Exp, r=["vec"], w=["vec2"])
        TS("dve", vec[:, VC["ha0A"]:VC["ha0A"] + 8], vec[:, VC["a0A"]:VC["a0A"] + 8], 0.5, None, ALU.mult, r=["vec"], w=["vec2"])
        TS("dve", vec[:, VC["hka"]:VC["hka"] + 4], vec[:, VC["k_a"]:VC["k_a"] + 4], 0.5, None, ALU.mult, r=["vec"], w=["vec2"])
        TS("dve", vec[:, VC["c2ka"]:VC["c2ka"] + 4], vec[:, VC["k_a"]:VC["k_a"] + 4], -0.5, 1.0, ALU.mult, ALU.add, r=["vec"], w=["vec2"])
        VK = ["vec", "vec2", "eps", "cstb", "cstf", "tri"]

        def ck(name):
            if os.environ.get("KSTOP") == name:
                P.disabled = True
        if os.environ.get("KSTOP") == "c":
            P.disabled = True

        bar_t = {e: sb(f"bar_{e}", [1, 8]) for e in ["act", "dve", "pool"]}

        def RSQ(dst, src, scale, bias_ap, n, r=(), w=()):
            ACT(dst, src, AF.Ln, bias=bias_ap, scale=scale, r=list(r) + ["eps"], w=list(w))
            ACT(dst, dst, AF.Exp, scale=-0.5, r=list(w), w=list(w))

        def barrier():
            if getattr(P, "disabled", False):
                return
            P.seg += 1
            n = len(P.ops)
            pend = [op.idx for op in P.ops[P.last_barrier:n]]
            marks = []
            for e in ["act", "dve", "pool"]:
                t = bar_t[e]
                marks.append(P.add(e, (lambda tt, ee: (lambda en: en.memzero(tt[:]) if ee == 'act' else en.memset(tt[:], 0.0)))(t, e), (), [f"bar_{e}_{n}"]).idx)
            marks.append(P.add("pe", lambda e: e.matmul(psA[0][0:1, 0:1], cstb[0:1, 8, 0:1], cstb[0:1, 8, 0:1], start=True, stop=True),
                               ["cstb", pk(0)], [pk(0), f"bar_pe_{n}"]).idx)
            for e in ["pe", "act", "dve", "pool", "sp"]:
                f = P.add(e, None, (), ())
                f.xdeps = tuple(marks) + tuple(i for i in pend if P.ops[i].dma)
            P.last_barrier = len(P.ops)
            P.seg += 1

        def make_hT(st_key, x_aps, nrows, hT, gain, wk, xt, hb, ssq, rstd):
            nblk = len(x_aps)
            for bi, (xap, rows, col0) in enumerate(x_aps):
                s = bi % xt.shape[1]
                DMA("sp", xt[0:rows, s, :], xap, w=[("xt", s)])
                ACT(junk[0:rows, :], xt[0:rows, s, :], AF.Square, accum=ssq[0:rows, s:s + 1], r=[("xt", s)], w=["junk", ("ssq", s)])
                RSQ(rstd[0:rows, s:s + 1], ssq[0:rows, s:s + 1], 1.0 / D, eps_c[0:rows, 0:1], 1, r=[("ssq", s)], w=[("rstd", s)])
                TS("dve", hb[0:rows, 0, :], xt[0:rows, s, :], rstd[0:rows, s:s + 1], None, ALU.mult, r=[("xt", s), ("rstd", s)], w=["hbo"])
                for kc in range(8):
                    TR(psT[kc % 2][:, (kc // 2) * 128:(kc // 2) * 128 + rows], hb[0:rows, 0, kc * 128:(kc + 1) * 128], ident[0:rows, 0:rows],
                       r=["hbo", "cstb"], w=[pkT(kc % 2)])
                for kc in range(8):
                    eng_act = (kc % 2 == 0)
                    src = psT[kc % 2][:, (kc // 2) * 128:(kc // 2) * 128 + rows]
                    dst = hT[:, kc, col0:col0 + rows]
                    if eng_act:
                        ACT(dst, src, AF.Copy, scale=vc(gain, kc), r=[pkT(kc % 2), "vec"], w=[(wk, kc)])
                    else:
                        TS("dve", dst, src, vc(gain, kc), None, ALU.mult, r=[pkT(kc % 2), "vec"], w=[(wk, kc)])

        g2t = sb("g2t", [128, 512], BF16)
        PCs = sb("PCs", [128, 2, 64, 4], F32)
        with ExitStack() as s1:
            win = sb("win", [128, 8, N_SW * 128], BF16, s1)
            cstm = sb("cstm", [128, 4, 4, 128], BF16, s1)
            for _m in range(4):
                for _r in range(4):
                    DMA("pool", cstm[:, _m, _r, :], cst_d[:, 4 + _m, :], w=["cstm"])
            for kc in range(8):
                DMA("pool", win[:, kc, :], wsw[kc * 128:(kc + 1) * 128, :], w=["win"])
            w2aug = sb("w2aug", [65, 2, 512], BF16, s1)
            a2t = sb("a2t", [128, 2, 512], BF16, s1)
            rope = sb("rope", [128, 2, HALF + 256], BF16, s1)
            for d in range(2):
                DMA("pool", w2aug[:, d, :], w2aug_d[d], w=["w2aug"])
                DMA("pool", a2t[64:128, d, :], a2_d[d], w=["a2t"])
            DMA("pool", g2t[:], g2_d, w=["g2t"])
            DMA("pool", rope[:], rope_d[:, :, 0:HALF + 256], w=["rope"])
            if os.environ.get("KSTOP") == "w":
                P.disabled = True
            kT = sb("kT", [128, HALF + 128], BF16, s1)
            vaug = sb("vaug", [128, NB_OWN + 1, 2, 64], BF16, s1)
            vones = sb("vones", [128, 64], BF16, s1)
            attn_o = sb("attn_o", [128, 4, TT], BF16, s1)
            sgl = sb("sgl", [128, TT], BF16, s1)
            bv = sb("bv", [128, 4, TT], BF16, s1)
            MEMSET("pool", vones[:], 1.0, w=["vaug1"])
            xt = sb("xt", [128, 1, D], F32, s1); hb = sb("hb", [128, 1, D], BF16, s1)
            ssq = sb("ssq", [128, 2], F32, s1); rstd = sb("rstd", [128, 2], F32, s1)
            hT = sb("hT", [128, 8, TT + 2], BF16, s1)
            zr = sb("zr", [128, 2, TT + 2], F32, s1)
            zs = sb("zs", [128, 2, TT], F32, s1)
            ztmp = sb("ztmp", [128, 2, TT], F32, s1)
            zq = sb("zq", [128, TT], F32, s1)
            qsq = sb("qsq", [128, TT], BF16, s1)
            qrs = sb("qrs", [128, TT], F32, s1)
            qn = sb("qn", [128, TT], BF16, s1)
            qt1 = sb("qt1", [128, TT], F32, s1); qt2 = sb("qt2", [128, TT], F32, s1)
            qT = sb("qT", [128, 2, 4, TT], BF16, s1)
            vb = sb("vb", [128, TT], BF16, s1)
            twl = sb("twl", [65, TT], BF16, s1); alx = sb("alx", [128, TT], BF16, s1)
            MEMSET("pool", twl[64:65, :], 1.0, w=["twl1"])
            if os.environ.get("KSTOP") == "m":
                P.disabled = True
            sgfcD = [sb(f"sgfc{i}", [128, NBT, 128], F32, s1) for i in range(2)]
            r_f = sb("r_f", [128, TT], F32, s1); k_f = sb("k_f", [128, TT], F32, s1); v_f = sb("v_f", [128, TT], F32, s1)
            nkk = sb("nkk", [128, TT], F32, s1)
            a_fD = [sb(f"a_f{i}", [128, TT], F32, s1) for i in range(2)]; b_fD = [sb(f"b_f{i}", [128, TT], F32, s1) for i in range(2)]
            kd_fD = [sb(f"kd_f{i}", [128, TT], F32, s1) for i in range(2)]; t1_fD = [sb(f"t1_f{i}", [128, TT], F32, s1) for i in range(2)]
            pbD = [sb(f"pb{i}", [128, TT], BF16, s1) for i in range(2)]
            E_fD = [sb(f"E_f{i}", [128, TT], F32, s1) for i in range(2)]; G_fD = [sb(f"G_f{i}", [128, TT], F32, s1) for i in range(2)]
            AR = sb("AR", [128, 2, 4, NBT, 256], BF16, s1)
            bt = sb("bt", [128, 2, 4, TT], BF16, s1); kt = sb("kt", [128, 2, 4, TT], BF16, s1)
            btT = sb("btT", [128, 2, NBT, 512], BF16, s1); ktT = sb("ktT", [128, 2, NBT, 512], BF16, s1)
            atT = sb("atT", [128, 2, NBT, 512], BF16, s1)
            vT = sb("vT", [128, NBT, 512], BF16, s1)
            Aab = sb("Aab", [128, 8, 128], BF16, s1); AabT = sb("AabT", [128, 8, 128], BF16, s1)
            Aak = sb("Aak", [128, 8, 128], BF16, s1); Arb = sb("Arb", [128, 8, 128], BF16, s1); Ark = sb("Ark", [128, 8, 128], BF16, s1)
            ApI = [[sb(f"Ap{j}_{i}", [128, 8, 128], BF16, s1) for i in range(2)] for j in range(2)]
            ATpI = [[sb(f"ATp{j}_{i}", [128, 8, 128], BF16, s1) for i in range(2)] for j in range(2)]
            ZpI = [[sb(f"Zp{j}_{i}", [128, 4, 2, 2, 64], BF16, s1) for i in range(2)] for j in range(2)]
            M0stI = [sb(f"M0st{j}", [128, 2, 4, 128], BF16, s1) for j in range(2)]
            N0stI = [sb(f"N0st{j}", [128, 2, 4, 64], BF16, s1) for j in range(2)]
            QstI = [sb(f"Qst{j}", [128, 4, 128], BF16, s1) for j in range(2)]
            Y0stI = [sb(f"Y0st{j}", [128, 4, 128], F32, s1) for j in range(2)]
            hlcount = [0]
            probs = sb("probs", [128, 2, 384], BF16, s1)
            den = sb("den", [128, 128], F32, s1)

            def proj_chunk(ci, bank, halo_bank=None):
                if halo_bank is None:
                    for kc in range(8):
                        MM(psA[bank][:, 0:TT], win[:, kc, ci * 128:(ci + 1) * 128], hT[:, kc, 1:TT + 1], start=(kc == 0), stop=(kc == 7),
                           r=["win", ("hT", kc)], w=[pk(bank)])
                else:
                    for kc in range(8):
                        MM(psA[bank][:, 0:TT + 2], win[:, kc, ci * 128:(ci + 1) * 128], hT[:, kc, 0:TT + 2], start=(kc == 0), stop=(kc == 7),
                           r=["win", ("hT", kc)], w=[pk(bank)])

            zcount = [0]
            mbc = [0]

            def nb():
                mbc[0] += 1
                return 3 + (mbc[0] % 3)

            def rwkv_z(ci, mui, out_ap, wkey, post=None):
                s = zcount[0] % 2
                zcount[0] += 1
                bank = s
                proj_chunk(ci, bank, 2)
                ACT(zr[:, s, :], psA[bank][:, 0:TT + 2], AF.Copy, r=[pk(bank)], w=[("zr", s)])
                ACT(zs[:, s, :], psA[bank][:, 1:TT + 1], AF.Copy, scale=vc("omm", mui), r=[pk(bank), "vec2"], w=[("zs", s)])
                TTo("pool", ztmp[:, s, :], zr[:, s, 0:TT], zr[:, s, 2:TT + 2], ALU.add, r=[("zr", s)], w=[("ztmp", s)])
                STT("dve", out_ap, ztmp[:, s, :], vc("hmu", mui), zs[:, s, :], ALU.mult, ALU.add, r=[("ztmp", s), ("zs", s), "vec2"], w=[wkey])

            def sweep_tile(t0, own, first_other, pcidx0):
                dirs = [0, 1] if own else [1]
                blk0 = t0 // 128
                xaps = [(xs[t0 + 128 * i: t0 + 128 * i + 128, :], 128, 128 * i) for i in range(NBT)]
                xaps.append((xs[t0 + TT: t0 + TT + 2, :], 2, TT))
                P.tag = f"t{t0}-hT"
                make_hT("hT", xaps, None, hT, "nmix", "hT", xt, hb, ssq, rstd)
                ck("h")
                P.tag = f"t{t0}-qkv"
                qs = (t0 // TT) % 2
                if own or first_other:
                    for ci in ([0, 1, 2, 3, 4] if own else [4]):
                        bank = ci % 2
                        proj_chunk(ci, bank)
                        gcol = "kg" if ci == 4 else "qg"
                        ACT(qsq[:], psA[bank][:, 0:TT], AF.Square, r=[pk(bank)], w=["qsq"])
                        b1 = nb()
                        MM(psA[b1][:, 0:TT], onesblk, qsq[:], r=["cstb", "qsq"], w=[pk(b1)])
                        RSQ(qrs[:], psA[b1][:, 0:TT], 1.0 / 64, eps_c[:, 0:1], TT, r=[pk(b1)], w=["qrs"])
                        STT("dve", qn[:], psA[bank][:, 0:TT], vc(gcol), qrs[:], ALU.mult, ALU.mult, r=[pk(bank), "qrs", "vec"], w=["qn"])
                        b2 = nb()
                        MM(psA[b2][:, 0:TT], perm, qn[:], r=["cstb", "qn"], w=[pk(b2)])
                        TTo("pool", qt1[:], qn[:], rope[:, 0, t0:t0 + TT], ALU.mult, r=["qn", "rope"], w=["qt1"])
                        TTo("dve", qt2[:], psA[b2][:, 0:TT], rope[:, 1, t0:t0 + TT], ALU.mult, r=[pk(b2), "rope"], w=["qt2"])
                        if ci < 4:
                            TTo("pool", qT[:, qs, ci, :], qt1[:], qt2[:], ALU.add, r=["qt1", "qt2"], w=[("qT", qs)])
                        else:
                            n = TT if own else 128
                            TTo("pool", kT[:, t0:t0 + n], qt1[:, 0:n], qt2[:, 0:n], ALU.add, r=["qt1", "qt2"], w=["kT"])
                    proj_chunk(5, 0)
                    ACT(vb[:], psA[0][:, 0:TT], AF.Copy, r=[pk(0)], w=["vb"])
                    nbv = NBT if own else 1
                    for bi in range(nbv):
                        TR(psT[0][:, bi * 128:(bi + 1) * 128], vb[:, bi * 128:(bi + 1) * 128], ident, r=["vb", "cstb"], w=[pkT(0)])
                    for bi in range(nbv):
                        CP("dve", vaug[:, blk0 + bi, :, :], psT[0][:, bi * 128:(bi + 1) * 128].rearrange("p (g d) -> p g d", g=2),
                           r=[pkT(0)], w=["vaug"])
                P.tag = f"t{t0}-wa"
                ck("q")
                rwkv_z(6, 0, zq[:], "zq")
                ACT(twl[0:64, :], zq[0:64, :], AF.Tanh, r=["zq"], w=["twl"])
                CP("pool", alx[64:128, :], zq[64:128, :], r=["zq"], w=["alx"])
                ck("z")
                if own:
                    rwkv_z(7, 1, zq[:], "zq")
                    ACT(zq[:], zq[:], AF.Tanh, scale=0.5, r=["zq"], w=["zq"])
                    TS("pool", sgl[:, :], zq[:], 0.5, 0.5, ALU.mult, ALU.add, r=["zq"], w=["sgl"])
                    DMA("sp", sgl_s[:, t0:t0 + TT], sgl[:, :], r=["sgl"], w=[("sgl_s", t0 // TT)])
                for fc in range(4):
                    P.tag = f"t{t0}-fc{fc}"
                    if own:
                        rwkv_z(8 + 3 * fc, 2 + 3 * fc, r_f[:], "r_f")
                    rwkv_z(9 + 3 * fc, 3 + 3 * fc, k_f[:], "k_f")
                    rwkv_z(10 + 3 * fc, 4 + 3 * fc, v_f[:], "v_f")
                    ACT(qsq[:], k_f[:], AF.Square, scale=vc("k_k", fc), r=["k_f", "vec"], w=["qsq"])
                    b1 = nb()
                    MM(psA[b1][:, 0:TT], onesblk, qsq[:], r=["cstb", "qsq"], w=[pk(b1)])
                    RSQ(qrs[:], psA[b1][:, 0:TT], 1.0, eps_c[:, 1:2], TT, r=[pk(b1)], w=["qrs"])
                    STT("dve", nkk[:], k_f[:], vc("nkk", fc), qrs[:], ALU.mult, ALU.mult, r=["k_f", "qrs", "vec2"], w=["nkk"])
                    ck("k")
                    CP("pool", vb[:], v_f[:], r=["v_f"], w=["vb"])
                    for bi in range(NBT):
                        TR(psT[0][:, bi * 128:(bi + 1) * 128], vb[:, bi * 128:(bi + 1) * 128], ident, r=["vb", "cstb"], w=[pkT(0)])
                    CP("dve", vT[:, :, fc * 128:(fc + 1) * 128], psT[0][:, 0:NBT * 128].rearrange("p (b c) -> p b c", b=NBT), r=[pkT(0)], w=[("vT", fc)])
                    for d in dirs:
                        dn = "AB"[d]
                        a_f, b_f, kd_f, t1_f, pb, E_f, G_f, sgfc = a_fD[d], b_fD[d], kd_fD[d], t1_fD[d], pbD[d], E_fD[d], G_fD[d], sgfcD[d]
                        b1 = nb()
                        MM(psA[b1][:, 0:TT], a2t[64:128, d, fc * 128:(fc + 1) * 128], alx[64:128, :], r=["a2t", "alx"], w=[pk(b1)])
                        ACT(a_f[:], psA[b1][:, 0:TT], AF.Tanh, bias=vc("ha0" + dn, fc), scale=0.5, r=[pk(b1), "vec2"], w=[("a_f", d)])
                        STT("dve", b_f[:], a_f[:], 1.0, nkk[:], ALU.add, ALU.mult, r=["nkk", ("a_f", d)], w=[("b_f", d)])
                        TS("dve", t1_f[:], a_f[:], vc("hka", fc), vc("c2ka", fc), ALU.mult, ALU.add, r=[("a_f", d), "vec2"], w=[("t1_f", d)])
                        TTo("pool", kd_f[:], k_f[:], t1_f[:], ALU.mult, r=["k_f", ("t1_f", d)], w=[("kd_f", d)])
                        if own:
                            STT("dve", pb[:], r_f[:], vc("rk" + dn, fc), kd_f[:], ALU.mult, ALU.mult, r=["r_f", ("kd_f", d), "vec"], w=[("pb", d)])
                            MM(psA[2][:, 0:TT], onesblk, pb[:], start=(d == 0), stop=(d == 1), r=["cstb", ("pb", d)], w=[pk(2)])
                        b2 = nb()
                        for bi in range(NBT):
                            MM(psA[b2][:, bi * 128:(bi + 1) * 128], twl[:, bi * 128:(bi + 1) * 128], w2aug[:, d, fc * 128:(fc + 1) * 128],
                               r=["twl", "twl1", "w2aug"], w=[pk(b2)])
                        ACT(sgfc[:, :, :], psA[b2][:, 0:NBT * 128].rearrange("p (b c) -> p b c", b=NBT), AF.Tanh, scale=0.5, r=[pk(b2)], w=[("sgfc", d)])
                        b3 = nb()
                        for bi in range(NBT):
                            MM(psA[b3][:, bi * 128:(bi + 1) * 128], sgfc[:, bi, :], tri[:, d, :], start=True, stop=False, r=[("sgfc", d), "tri"], w=[pk(b3)])
                            MM(psA[b3][:, bi * 128:(bi + 1) * 128], cstf[0:1, 2, :], crow[0:1, d, :], start=False, stop=True, r=["cstf", "crow"], w=[pk(b3)])
                        ACT(E_f[:], psA[b3][:, 0:TT], AF.Exp, r=[pk(b3)], w=[("E_f", d)])
                        ACT(G_f[:], psA[b3][:, 0:TT], AF.Exp, scale=-1.0, r=[pk(b3)], w=[("G_f", d)])
                        Ev = E_f[:].rearrange("p (c t) -> p c t", t=64)
                        pcol = 63 if d == 0 else 0
                        CP("pool", PCs[:, d, pcidx0:pcidx0 + TT // 64, fc], Ev[:, :, pcol], r=[("E_f", d)], w=["PCs"])
                        ARv = AR[:, d, fc, :, :]
                        if own:
                            TTo("dve", ARv[:, :, 128:256], r_f[:].rearrange("p (b t) -> p b t", t=128), E_f[:].rearrange("p (b t) -> p b t", t=128),
                                ALU.mult, r=["r_f", ("E_f", d)], w=[("AR", d, fc)])
                        ARa = AR[:, d, fc, :, 0:128].rearrange("p b (c t) -> p b c t", t=64)
                        nk4 = nkk[:].rearrange("p (b c t) -> p b c t", c=2, t=64)
                        E4 = E_f[:].rearrange("p (b c t) -> p b c t", c=2, t=64)
                        if d == 0:
                            TTo("pool", ARa[:, :, :, 1:64], nk4[:, :, :, 1:64], E4[:, :, :, 0:63], ALU.mult, r=["nkk", ("E_f", d)], w=[("AR", d, fc)])
                            CP("pool", ARa[:, :, :, 0:1], nk4[:, :, :, 0:1], r=["nkk"], w=[("AR", d, fc)])
                        else:
                            TTo("pool", ARa[:, :, :, 0:63], nk4[:, :, :, 0:63], E4[:, :, :, 1:64], ALU.mult, r=["nkk", ("E_f", d)], w=[("AR", d, fc)])
                            CP("pool", ARa[:, :, :, 63:64], nk4[:, :, :, 63:64], r=["nkk"], w=[("AR", d, fc)])
                        STT("dve", bt[:, d, fc, :], b_f[:], -0.5, G_f[:], ALU.mult, ALU.mult, r=[("b_f", d), ("G_f", d)], w=[("bt", d, fc)])
                        TTo("pool", kt[:, d, fc, :], kd_f[:], G_f[:], ALU.mult, r=[("kd_f", d), ("G_f", d)], w=[("kt", d, fc)])
                        for which, (srck, dst, dkey) in enumerate(((("bt", d, fc), btT, "btT"), (("kt", d, fc), ktT, "ktT"), (("AR", d, fc), atT, "atT"))):
                            pb_i = which % 2
                            for bi in range(NBT):
                                if which == 0:
                                    src = bt[:, d, fc, bi * 128:(bi + 1) * 128]
                                elif which == 1:
                                    src = kt[:, d, fc, bi * 128:(bi + 1) * 128]
                                else:
                                    src = AR[:, d, fc, bi, 0:128]
                                TR(psT[pb_i][:, bi * 128:(bi + 1) * 128], src, ident, r=[srck, "cstb"], w=[pkT(pb_i)])
                            pv = psT[pb_i][:, 0:NBT * 128].rearrange("p (b c) -> p b c", b=NBT)
                            if which != 1:
                                CP("dve", dst[:, d, :, fc * 128:(fc + 1) * 128], pv, r=[pkT(pb_i)], w=[(dkey, d, fc)])
                            else:
                                ACT(dst[:, d, :, fc * 128:(fc + 1) * 128], pv, AF.Copy, r=[pkT(pb_i)], w=[(dkey, d, fc)])
                    if own:
                        TTo("dve", bv[:, fc, :], psA[2][:, 0:TT], v_f[:], ALU.mult, r=[pk(2), "v_f"], w=["bv"])
                        DMA("sp", bv_s[:, fc, t0:t0 + TT], bv[:, fc, :], r=["bv"], w=[("bv_s", t0 // TT, fc)])
                ck("f")
                for d in dirs:
                    ms, mi = (4, 5) if d == 0 else (6, 7)
                    for bi in range(NBT):
                        P.tag = f"t{t0}-hl{d}{bi}"
                        ins_ = hlcount[0] % 2
                        hlcount[0] += 1
                        Ap, ATp, Zp = ApI[ins_], ATpI[ins_], ZpI[ins_]
                        B = (lambda p_: (lambda b_: (b_ + 3 * p_) % 6))(ins_)
                        M0st, N0st, Qst, Y0st = M0stI[ins_], N0stI[ins_], QstI[ins_], Y0stI[ins_]
                        gblk = (t0 // 128 + bi) if own else None
                        def amat(bank0, lh, rh, keys):
                            for h in range(8):
                                fc = h // 2; rows = slice((h % 2) * 64, (h % 2) * 64 + 64)
                                MM(psA[B(bank0 + h % 2)][:, (h // 2) * 128:(h // 2) * 128 + 128], lh(rows, fc), rh(rows, fc),
                                   r=[(k_, d, fc) for k_ in keys], w=[pk(B(bank0 + h % 2))])

                        def aevac(bank0, dst, dkey, mslot):
                            dv = dst[:].rearrange("p (i two) t -> p two i t", two=2)
                            for par in range(2):
                                TTo("dve", dv[:, par, :, :], psA[B(bank0 + par)][:, :].rearrange("p (i t) -> p i t", i=4), cstm[:, mslot - 4, :, :], ALU.mult,
                                    r=[pk(B(bank0 + par)), "cstm"], w=[dkey])
                        tokc = slice(bi * 128, (bi + 1) * 128)
                        amat(0, lambda rows, fc: bt[rows, d, fc, tokc], lambda rows, fc: AR[rows, d, fc, bi, 0:128], ["bt", "AR"])
                        amat(2, lambda rows, fc: kt[rows, d, fc, tokc], lambda rows, fc: AR[rows, d, fc, bi, 0:128], ["kt", "AR"])
                        amat(4, lambda rows, fc: AR[rows, d, fc, bi, 0:128], lambda rows, fc: bt[rows, d, fc, tokc], ["bt", "AR"])
                        aevac(0, Aab, "Aab", ms)
                        aevac(2, Aak, "Aak", ms)
                        aevac(4, AabT, "AabT", (6 if d == 0 else 4))
                        ck("a")
                        Z = Zp[0]
                        CP("pool", Z[:, :, 0, :, :], atT[:, d, bi, :].rearrange("p (f h j) -> p f h j", f=4, h=2), r=[("atT", d, f_) for f_ in range(4)], w=[("Z", ins_, 0, f_) for f_ in range(4)])
                        for h in range(8):
                            MM(psA[B(3)][:, h * 64:(h + 1) * 64], Aak[:, h, :], vT[:, bi, h * 64:(h + 1) * 64], r=["Aak", ("vT", h // 2)], w=[pk(B(3))])
                        CP("dve", Z[:, :, 1, :, :], psA[B(3)][:, 0:512].rearrange("p (f h j) -> p f h j", f=4, h=2), r=[pk(B(3))], w=[("Z", ins_, 0, f_) for f_ in range(4)])
                        curA, curAT, ak, atk = Aab, AabT, ["Aab", "Aab"], ["AabT", "AabT"]
                        zi = 0
                        for lev in range(6):
                            Zc, Zn = Zp[zi], Zp[1 - zi]
                            for hh in range(2):
                                bank = 4 + hh
                                for q4 in range(4):
                                    h = 4 * hh + q4; fc = h // 2; h2 = h % 2
                                    o = psA[B(bank)][:, q4 * 128:(q4 + 1) * 128].rearrange("p (a j) -> p a j", a=2)
                                    MM(o, curA[:, h, :], Zc[:, fc, :, h2, :], start=True, stop=False, r=[ak[hh], ("Z", ins_, zi, fc)], w=[pk(B(bank))])
                                    MM(o, ident, Zc[:, fc, :, h2, :], start=False, stop=True, r=["cstb", ("Z", ins_, zi, fc)], w=[pk(B(bank))])
                                for f in range(2):
                                    src = psA[B(bank)][:, f * 256:(f + 1) * 256].rearrange("p (h a j) -> p a h j", h=2, a=2)
                                    if hh == 0:
                                        ACT(Zn[:, 2 * hh + f, :, :, :], src, AF.Copy, r=[pk(B(bank))], w=[("Z", ins_, 1 - zi, 2 * hh + f)])
                                    else:
                                        CP("dve", Zn[:, 2 * hh + f, :, :, :], src, r=[pk(B(bank))], w=[("Z", ins_, 1 - zi, 2 * hh + f)])
                            zi = 1 - zi
                            if lev < 5:
                                nA, nAT = Ap[lev % 2], ATp[lev % 2]
                                nak = [("Ap", ins_, lev % 2, 0), ("Ap", ins_, lev % 2, 1)]
                                natk = [("ATp", ins_, lev % 2, 0), ("ATp", ins_, lev % 2, 1)]
                                for hh in range(2):
                                    for q4 in range(4):
                                        h = 4 * hh + q4
                                        MM(psA[B(0 + hh)][:, q4 * 128:(q4 + 1) * 128], curAT[:, h, :], curA[:, h, :], r=[ak[hh], atk[hh]], w=[pk(B(0 + hh))])
                                        MM(psA[B(2 + hh)][:, q4 * 128:(q4 + 1) * 128], curA[:, h, :], curAT[:, h, :], r=[ak[hh], atk[hh]], w=[pk(B(2 + hh))])
                                    ACT(nA[:, 4 * hh:4 * hh + 4, :], psA[B(0 + hh)][:, :].rearrange("p (h t) -> p h t", h=4), AF.Copy, r=[pk(B(0 + hh))], w=[nak[hh]])
                                    CP("dve", nAT[:, 4 * hh:4 * hh + 4, :], psA[B(2 + hh)][:, :].rearrange("p (h t) -> p h t", h=4), r=[pk(B(2 + hh))], w=[natk[hh]])
                                curA, curAT, ak, atk = nA, nAT, nak, natk
                        Zf = Zp[zi]; zkf = lambda f_: ("Z", ins_, zi, f_)
                        ck("d")
                        for c2 in range(2):
                            rs_ = slice(c2 * 64, c2 * 64 + 64)
                            for fc in range(4):
                                MM(psA[B(c2)][:, fc * 128:(fc + 1) * 128], Zf[rs_, fc, 0, :, :], btT[rs_, d, bi, fc * 128:(fc + 1) * 128], r=[zkf(fc), ("btT", d, fc)], w=[pk(B(c2))])
                            for fc in range(4):
                                TTo("dve", M0st[:, c2, fc, :], psA[B(c2)][:, fc * 128:(fc + 1) * 128], bdm, ALU.mult, r=[pk(B(c2)), "cstb"], w=[("M0st", ins_)])
                            for h in range(8):
                                fc = h // 2; h2 = h % 2
                                o = psA[B(2 + c2)][h2 * 64:h2 * 64 + 64, fc * 64:fc * 64 + 64]
                                MM(o, btT[rs_, d, bi, h * 64:(h + 1) * 64], Zf[rs_, fc, 1, h2, :], start=True, stop=False, r=[("btT", d, fc), zkf(fc)], w=[pk(B(2 + c2))])
                                MM(o, ktT[rs_, d, bi, h * 64:(h + 1) * 64], vT[rs_, bi, h * 64:(h + 1) * 64], start=False, stop=True, r=[("ktT", d, fc), ("vT", fc)], w=[pk(B(2 + c2))])
                        for c2 in range(2):
                            ACT(N0st[:, c2, :, :], psA[B(2 + c2)][:, 0:256].rearrange("p (f i) -> p f i", f=4), AF.Copy, r=[pk(B(2 + c2))], w=[("N0st", ins_)])
                        cidx = pcidx0 + 2 * bi
                        DMA("sp", m0_s[d, cidx:cidx + 2].rearrange("c p f j -> p c f j"), M0st[:], r=[("M0st", ins_)], w=[("m0_s", d, cidx)])
                        DMA("sp", n0_s[d, cidx:cidx + 2].rearrange("c p f i -> p c f i"), N0st[:], r=[("N0st", ins_)], w=[("n0_s", d, cidx)])
                        if own:
                            amat(0, lambda rows, fc: bt[rows, d, fc, tokc], lambda rows, fc: AR[rows, d, fc, bi, 128:256], ["bt", "AR"])
                            amat(4, lambda rows, fc: kt[rows, d, fc, tokc], lambda rows, fc: AR[rows, d, fc, bi, 128:256], ["kt", "AR"])
                            aevac(0, Arb, "Arb", mi)
                            aevac(4, Ark, "Ark", mi)
                            for fc in range(4):
                                o = psA[B(2)][:, fc * 128:(fc + 1) * 128]
                                MM(o, ident, AR[:, d, fc, bi, 128:256], start=True, stop=False, r=["cstb", ("AR", d, fc)], w=[pk(B(2))])
                                for h2 in range(2):
                                    h = 2 * fc + h2
                                    MM(psA[B(2)][h2 * 64:h2 * 64 + 64, fc * 128:(fc + 1) * 128], Zf[:, fc, 0, h2, :], Arb[:, h, :],
                                       start=False, stop=(h2 == 1), r=[zkf(fc), "Arb"], w=[pk(B(2))])
                                for h2 in range(2):
                                    h = 2 * fc + h2
                                    o2 = psA[B(3)][h2 * 64:h2 * 64 + 64, fc * 128:(fc + 1) * 128]
                                    MM(o2, Zf[:, fc, 1, h2, :], Arb[:, h, :], start=True, stop=False, r=[zkf(fc), "Arb"], w=[pk(B(3))])
                                    MM(o2, vT[:, bi, h * 64:(h + 1) * 64], Ark[:, h, :], start=False, stop=True, r=[("vT", fc), "Ark"], w=[pk(B(3))])
                            ACT(Qst[:], psA[B(2)][:, :].rearrange("p (f t) -> p f t", f=4), AF.Copy, r=[pk(B(2))], w=[("Qst", ins_)])
                            DMA("sp", q_s[d, gblk], Qst[:], r=[("Qst", ins_)], w=[("q_s", d, gblk)])
                            CP("dve", Y0st[:], psA[B(3)][:, :].rearrange("p (f t) -> p f t", f=4), r=[pk(B(3))], w=[("Y0st", ins_)])
                            DMA("sp", y0_s[d, :, :, t0 + bi * 128:t0 + bi * 128 + 128], Y0st[:], r=[("Y0st", ins_)], w=[("y0_s", d, gblk)])

            def attention_tile(ti):
                P.tag = f"attn{ti}"
                qs = ti % 2
                for qb in range(NBT):
                    n = NBT * ti + qb
                    kbs = [kb for kb in (n - 1, n, n + 1) if kb >= 0]
                    for m in range(4):
                        for h2 in range(2):
                            h = 2 * m + h2
                            c = h % 4; g = h // 4
                            rows = slice(g * 64, g * 64 + 64)
                            pi = h2
                            for kb in kbs:
                                slot = kb - (n - 1)
                                MM(psA[pi][:, slot * 128:(slot + 1) * 128], kT[rows, kb * 128:(kb + 1) * 128], qT[rows, qs, c, qb * 128:(qb + 1) * 128],
                                   r=["kT", ("qT", qs)], w=[pk(pi)])
                            lo = (kbs[0] - (n - 1)) * 128
                            ACT(probs[:, pi, lo:384], psA[pi][:, lo:384], AF.Exp, scale=0.125, r=[pk(pi)], w=[("probs", pi)])
                            if n - 1 >= 0:
                                TTo("pool", probs[:, pi, 0:128], probs[:, pi, 0:128], cstb[:, 10, :], ALU.mult, r=[("probs", pi), "cstb"], w=[("probs", pi)])
                            TTo("pool", probs[:, pi, 256:384], probs[:, pi, 256:384], cstb[:, 11, :], ALU.mult, r=[("probs", pi), "cstb"], w=[("probs", pi)])
                            orow = slice(h2 * 64, h2 * 64 + 64)
                            for i, kb in enumerate(kbs):
                                slot = kb - (n - 1)
                                MM(psA[2][orow, 0:128], vaug[:, kb, g, :], probs[:, pi, slot * 128:(slot + 1) * 128], start=(i == 0), stop=(i == len(kbs) - 1),
                                   r=["vaug", ("probs", pi)], w=[pk(2)])
                            for i, kb in enumerate(kbs):
                                slot = kb - (n - 1)
                                MM(psA[3][orow, 0:128], vones[:], probs[:, pi, slot * 128:(slot + 1) * 128], start=(i == 0), stop=(i == len(kbs) - 1),
                                   r=["vaug1", ("probs", pi)], w=[pk(3)])
                        TS("dve", den[:], psA[3][:, 0:128], vc("esink", m), None, ALU.add, r=[pk(3), "vec2"], w=["den"])
                        RCP(den[:], den[:], r=["den"], w=["den"])
                        TTo("dve", attn_o[:, m, qb * 128:(qb + 1) * 128], psA[2][:, 0:128], den[:], ALU.mult, r=[pk(2), "den"], w=["attn_o"])

            def attn_and_store(ti):
                attention_tile(ti)
                DMA("sp", ao_s[:, :, ti * TT:(ti + 1) * TT], attn_o[:], r=["attn_o"], w=[("ao_s", ti)])

            oth = list(range(2 * NTH - 1, NTH - 1, -1))
            if ntiles_other is not None:
                oth = oth[len(oth) - ntiles_other:] if ntiles_other > 0 else []
            for ti in oth:
                t0 = ti * TT
                sweep_tile(t0, own=False, first_other=(ti == NTH), pcidx0=t0 // 64)
            nown = NTH if ntiles_own is None else ntiles_own
            for ti in range(nown):
                sweep_tile(ti * TT, own=True, first_other=False, pcidx0=(ti * TT) // 64)
                if ti >= 1:
                    attn_and_store(ti - 1)
            if nown == NTH:
                attn_and_store(NTH - 1)
            if dbg is not None and upto == 1:
                dbg(locals(), P, DMA, dbg_out)
            if upto == 1:
                P.disabled = True
        barrier()

        with ExitStack() as s23:
            Yb = sb("Yb", [128, 4, HALF], F32, s23)
            if True:
                s2 = s23
                S32 = sb("S32", [128, 2, 4, 64], F32, s2)
                Sb = sb("Sb", [128, 2, 4, 64], BF16, s2)
                St = sb("St", [128, 2, 4, 64], F32, s2)
                M0l = sb("M0l", [128, 4, 4, 128], BF16, s2)
                N0l = sb("N0l", [128, 4, 4, 64], BF16, s2)
                Ql = sb("Ql", [128, 4, 4, 128], BF16, s2)
                ytmp = sb("ytmp", [128, 2, 4, 512], F32, s2)
                for j in range(4):
                    s = j % 2
                    gs_ = range(4 * j, 4 * j + 4)
                    DMA("sp", Yb[:, :, j * 512:(j + 1) * 512], y0_s[0, :, :, j * 512:(j + 1) * 512], r=[("y0_s", 0, g) for g in gs_], w=[("Yb", 2 * j), ("Yb", 2 * j + 1)])
                    DMA("sp", ytmp[:, s, :, :], y0_s[1, :, :, j * 512:(j + 1) * 512], r=[("y0_s", 1, g) for g in gs_], w=[("ytmp", s)])
                    TTo("pool", Yb[:, :, j * 512:(j + 1) * 512], Yb[:, :, j * 512:(j + 1) * 512], ytmp[:, s, :, :], ALU.add,
                        r=[("ytmp", s), ("Yb", 2 * j), ("Yb", 2 * j + 1)], w=[("Yb", 2 * j), ("Yb", 2 * j + 1)])
                MEMSET("dve", S32[:], 0.0, w=[("S32", 0), ("S32", 1)])
                MEMSET("pool", Sb[:], 0.0, w=[("Sb", 0), ("Sb", 1)])
                stepn = [0]

                def chain_step(d, cidx, own):
                    sl = stepn[0] % 4
                    stepn[0] += 1
                    DMA("sp", M0l[:, sl, :, :], m0_s[d, cidx], r=[("m0_s", d, cidx - cidx % 2)], w=[("M0l", sl)])
                    DMA("sp", N0l[:, sl, :, :], n0_s[d, cidx], r=[("n0_s", d, cidx - cidx % 2)], w=[("N0l", sl)])
                    bank = 4 + d
                    if own:
                        blk = cidx // 2; c2 = cidx % 2
                        qsl = (blk % 2) * 2 + d
                        if (d == 0 and c2 == 0) or (d == 1 and c2 == 1):
                            DMA("sp", Ql[:, qsl, :, :], q_s[d, blk], r=[("q_s", d, blk)], w=[("Ql", qsl)])
                        yb = 0 + d
                        for h in range(8):
                            fc = h // 2; h2 = h % 2; rows = slice(h2 * 64, h2 * 64 + 64)
                            MM(psA[yb][rows, fc * 64:(fc + 1) * 64], Sb[rows, d, fc, :], Ql[rows, qsl, fc, c2 * 64:c2 * 64 + 64],
                               r=[("Sb", d), ("Ql", qsl)], w=[pk(yb)])
                        yv = Yb[:, :, cidx * 64:(cidx + 1) * 64]
                        TTo("dve", yv, yv, psA[yb][:, 0:256].rearrange("p (f t) -> p f t", f=4), ALU.add, r=[pk(yb), ("Yb", cidx // 4)], w=[("Yb", cidx // 4)])
                    for fc in range(4):
                        o = psA[bank][:, fc * 64:(fc + 1) * 64]
                        MM(o, ident, N0l[:, sl, fc, :], start=True, stop=False, r=["cstb", ("N0l", sl)], w=[pk(bank)])
                        MM(o, M0l[:, sl, fc, :], Sb[:, d, fc, :], start=False, stop=True, r=[("M0l", sl), ("Sb", d)], w=[pk(bank)])
                    TTo("dve", St[:, d, :, :], psA[bank][:, 0:256].rearrange("p (f i) -> p f i", f=4), S32[:, d, :, :], ALU.add,
                        r=[pk(bank), ("S32", d)], w=[("St", d)])
                    for fc in range(4):
                        TS("dve", S32[:, d, fc, :], St[:, d, fc, :], PCs[:, d, cidx, fc:fc + 1], None, ALU.mult, r=[("St", d), "PCs"], w=[("S32", d)])
                    ACT(Sb[:, d, :, :], S32[:, d, :, :], AF.Copy, r=[("S32", d)], w=[("Sb", d)])

                for cidx in range(63, 31, -1):
                    chain_step(1, cidx, own=False)
                for i in range(32):
                    chain_step(0, i, own=True)
                    chain_step(1, 31 - i, own=True)
            if upto == 2:
                P.disabled = True

            if True:
                s3 = s23
                wg = sb("wg", [128, 8, 2048], BF16, s3)
                wua = sb("wua", [128, 4, D], BF16, s3); wur = sb("wur", [128, 4, D], BF16, s3); wout = sb("wout", [128, 8, D], BF16, s3)
                for kc in range(8):
                    DMA("pool", wg[:, kc, :], wgt[kc * 128:(kc + 1) * 128, :], w=["wg"])
                    DMA("pool", wout[:, kc, :], wout_d[kc * 128:(kc + 1) * 128, :], w=["wout"])
                for kc in range(4):
                    DMA("pool", wua[:, kc, :], wua_d[kc * 128:(kc + 1) * 128, :], w=["wua"])
                    DMA("pool", wur[:, kc, :], wur_d[kc * 128:(kc + 1) * 128, :], w=["wur"])
                xt = sb("xt3", [128, 2, D], F32, s3); hb = sb("hb3", [128, 1, D], BF16, s3)
                ssq = sb("ssq3", [128, 2], F32, s3); rstd = sb("rstd3", [128, 2], F32, s3)
                hT = sb("hT3", [128, 8, TT], BF16, s3)
                yc = sb("yc", [128, TT], F32, s3); ysq = sb("ysq", [128, TT], F32, s3); yr = sb("yr", [128, TT], F32, s3)
                rwT = sb("rwT", [128, 4, TT], BF16, s3)
                gates = sb("gates", [128, 16, TT], BF16, s3)
                mg = sb("mg", [128, 8, TT], BF16, s3); m1 = sb("m1", [128, TT], F32, s3); m2 = sb("m2", [128, TT], F32, s3)
                x1b = sb("x1b", [128, 2, D], F32, s3)
                xres = sb("xres", [128, 2, D], F32, s3)
                ao_l = sb("ao_l", [128, 4, TT], BF16, s3); bv_l = sb("bv_l", [128, 4, TT], BF16, s3); sgl_l = sb("sgl_l", [128, TT], BF16, s3)
                onesf = cstf[:, 1, :]
                for ti in (3, 4, 2, 5, 1, 6, 0, 7):
                    t0 = ti * TT
                    DMA("sp", ao_l[:], ao_s[:, :, t0:t0 + TT], r=[("ao_s", ti)], w=["ao_l"])
                    DMA("sp", bv_l[:], bv_s[:, :, t0:t0 + TT], r=[("bv_s", ti, fc) for fc in range(4)], w=["bv_l"])
                    DMA("sp", sgl_l[:], sgl_s[:, t0:t0 + TT], r=[("sgl_s", ti)], w=["sgl_l"])
                    for fc in range(4):
                        Yv = Yb[:, fc, t0:t0 + TT]
                        MM(psA[0][:, 0:TT], onesf, Yv, r=["cstf", ("Yb", ti)], w=[pk(0)])
                        TTo("dve", yc[:], Yv, psA[0][:, 0:TT], ALU.subtract, r=[("Yb", ti), pk(0)], w=["yc"])
                        ACT(ysq[:], yc[:], AF.Square, r=["yc"], w=["ysq"])
                        MM(psA[1][:, 0:TT], onesf, ysq[:], r=["cstf", "ysq"], w=[pk(1)])
                        RSQ(yr[:], psA[1][:, 0:TT], 1.0, eps_c[:, 2:3], TT, r=[pk(1)], w=["yr"])
                        TTo("pool", yc[:], yc[:], yr[:], ALU.mult, r=["yc", "yr"], w=["yc"])
                        TS("dve", yc[:], yc[:], vc("lnw", fc), vc("lnb", fc), ALU.mult, ALU.add, r=["yc", "vec"], w=["yc"])
                        TTo("pool", yc[:], yc[:], bv_l[:, fc, :], ALU.add, r=["yc", "bv_l"], w=["yc"])
                        MM(psA[2][:, 0:TT], g2t[:, fc * 128:(fc + 1) * 128], sgl_l[:, :], r=["g2t", "sgl_l"], w=[pk(2)])
                        TTo("dve", rwT[:, fc, :], yc[:], psA[2][:, 0:TT], ALU.mult, r=["yc", pk(2)], w=["rwT"])
                    xaps = [(xs[1 + t0 + 128 * i: 1 + t0 + 128 * i + 128, :], 128, 128 * i) for i in range(NBT)]
                    make_hT("hT3", xaps, None, hT, "nmix", "hT3", xt, hb, ssq, rstd)
                    for gc in range(16):
                        bank = gc % 2
                        for kc in range(8):
                            MM(psA[bank][:, 0:TT], wg[:, kc, gc * 128:(gc + 1) * 128], hT[:, kc, :], start=(kc == 0), stop=(kc == 7), r=["wg", ("hT3", kc)], w=[pk(bank)])
                        ACT(gates[:, gc, :], psA[bank][:, 0:TT], AF.Tanh, scale=0.5, r=[pk(bank)], w=["gates"])
                    for oc in range(8):
                        for kc in range(4):
                            MM(psA[2][:, 0:TT], wua[:, kc, oc * 128:(oc + 1) * 128], ao_l[:, kc, :], start=(kc == 0), stop=(kc == 3), r=["wua", "ao_l"], w=[pk(2)])
                        for kc in range(4):
                            MM(psA[3][:, 0:TT], wur[:, kc, oc * 128:(oc + 1) * 128], rwT[:, kc, :], start=(kc == 0), stop=(kc == 3), r=["wur", "rwT"], w=[pk(3)])
                        STT("dve", m1[:], gates[:, oc, :], 1.0, psA[2][:, 0:TT], ALU.add, ALU.mult, r=[pk(2), "gates"], w=["m1"])
                        STT("dve", m2[:], gates[:, 8 + oc, :], 1.0, psA[3][:, 0:TT], ALU.add, ALU.mult, r=[pk(3), "gates"], w=["m2"])
                        TTo("pool", mg[:, oc, :], m1[:], m2[:], ALU.add, r=["m1", "m2"], w=["mg"])
                    for bi in range(NBT):
                        s = bi % 2
                        tok = t0 + bi * 128
                        DMA("sp", xres[:, s, :], xs[1 + tok:1 + tok + 128, :], w=[("xres", s)])
                        for hf in range(2):
                            bank = 4 + hf
                            for kc in range(8):
                                MM(psA[bank][:, :], mg[:, kc, bi * 128:(bi + 1) * 128], wout[:, kc, hf * 512:(hf + 1) * 512], start=(kc == 0), stop=(kc == 7), r=["mg", "wout"], w=[pk(bank)])
                            STT("dve", x1b[:, s, hf * 512:(hf + 1) * 512], psA[bank][:, :], 0.5, xres[:, s, hf * 512:(hf + 1) * 512], ALU.mult, ALU.add, r=[pk(bank), ("xres", s)], w=[("x1b", s)])
                        DMA("sp", x1_s[tok:tok + 128, :], x1b[:, s, :], r=[("x1b", s)], w=[("x1_s", tok // 128)])
        if upto == 3:
            P.disabled = True
        barrier()

        with ExitStack() as s4:
            wf1 = sb("wf1", [128, 8, 4096], BF16, s4); wf2 = sb("wf2", [128, 32, D], BF16, s4)
            wpg = sb("wpg", [128, 8, D], BF16, s4); wple = sb("wple", [128, 2, D], BF16, s4)
            for cg in range(4):
                for kc in range(8):
                    DMA("pool", wf1[:, kc, cg * 1024:(cg + 1) * 1024], wff1_d[kc * 128:(kc + 1) * 128, cg * 1024:(cg + 1) * 1024], w=[("wf1", cg)])
            for kc in range(32):
                DMA("pool", wf2[:, kc, :], wff2_d[kc * 128:(kc + 1) * 128, :], w=[("wf2", kc // 8)])
            for kc in range(8):
                DMA("pool", wpg[:, kc, :], wpg_d[kc * 128:(kc + 1) * 128, :], w=["wpg"])
            for kc in range(2):
                DMA("pool", wple[:, kc, :], wple_d[kc * 128:(kc + 1) * 128, :], w=["wple"])
            xt = sb("xt4", [128, 2, D], F32, s4); hb = sb("hb4", [128, 1, D], BF16, s4)
            ssq = sb("ssq4", [128, 4], F32, s4); rstd = sb("rstd4", [128, 4], F32, s4)
            hTf = sb("hT4", [128, 2, 8, 128], BF16, s4)
            hTp = sb("hT4p", [128, 1, 8, 128], BF16, s4)
            act = sb("act", [128, 2, 32, 128], BF16, s4)
            rl = sb("rl", [128, 4, 128], F32, s4)
            x2 = sb("x2", [128, 1, D], F32, s4)
            pt = sb("pt", [128, 2, 256], F32, s4); pb16 = sb("pb16", [128, 2, 256], BF16, s4); pT = sb("pT", [128, 2, 2, 128], BF16, s4)
            sgp = sb("sgp", [128, 1, 512], F32, s4)
            nhb = [0]

            def norm_T(src, gain, skey, hTt, okey, idx):
                hs = 0
                nhb[0] += 1
                ACT(junk[:], src, AF.Square, accum=ssq[:, idx:idx + 1], r=[skey], w=["junk", ("ssq4", idx)])
                RSQ(rstd[:, idx:idx + 1], ssq[:, idx:idx + 1], 1.0 / D, eps_c[:, 0:1], 1, r=[("ssq4", idx)], w=[("rstd4", idx)])
                TS("dve", hb[:, hs, :], src, rstd[:, idx:idx + 1], None, ALU.mult, r=[skey, ("rstd4", idx)], w=[("hb4o", hs)])
                for kc in range(8):
                    TR(psT[kc % 2][:, (kc // 2) * 128:(kc // 2) * 128 + 128], hb[:, hs, kc * 128:(kc + 1) * 128], ident, r=[("hb4o", hs), "cstb"], w=[pkT(kc % 2)])
                for kc in range(8):
                    srcp = psT[kc % 2][:, (kc // 2) * 128:(kc // 2) * 128 + 128]
                    if kc % 2 == 0:
                        ACT(hTt[:, kc, :], srcp, AF.Copy, scale=vc(gain, kc), r=[pkT(kc % 2), "vec"], w=[(okey, kc)])
                    else:
                        TS("dve", hTt[:, kc, :], srcp, vc(gain, kc), None, ALU.mult, r=[pkT(kc % 2), "vec"], w=[(okey, kc)])

            for blk in range(NB_OWN):
                tok = blk * 128
                s = blk % 2
                DMA("sp", xt[:, s, :], x1_s[tok:tok + 128, :], r=[("x1_s", blk)], w=[("xt4", s)])
                DMA("sp", pt[:, s, :], pp[tok:tok + 128, :], w=[("pt", s)])
                hk = ("hT4", s)
                norm_T(xt[:, s, :], "nffn", ("xt4", s), hTf[:, s], hk, s)
                for hc in range(32):
                    bank = hc % 4
                    for kc in range(8):
                        MM(psA[bank][:, 0:128], wf1[:, kc, hc * 128:(hc + 1) * 128], hTf[:, s, kc, :], start=(kc == 0), stop=(kc == 7),
                           r=[("wf1", hc // 8), (hk, kc)], w=[pk(bank)])
                    ACT(rl[:, bank, :], psA[bank][:, 0:128], AF.Relu, r=[pk(bank)], w=[("rl", bank)])
                    TTo("dve", act[:, s, hc, :], rl[:, bank, :], rl[:, bank, :], ALU.mult, r=[("rl", bank)], w=[("act", s)])
                for hf in range(2):
                    bank = 4 + hf
                    for hc in range(32):
                        MM(psA[bank][:, :], act[:, s, hc, :], wf2[:, hc, hf * 512:(hf + 1) * 512], start=(hc == 0), stop=(hc == 31),
                           r=[("act", s), ("wf2", hc // 8)], w=[pk(bank)])
                    TTo("dve", x2[:, 0, hf * 512:(hf + 1) * 512], psA[bank][:, :], xt[:, s, hf * 512:(hf + 1) * 512], ALU.add, r=[pk(bank), ("xt4", s)], w=["x2"])
                hkp = "hT4p"
                norm_T(x2[:, 0, :], "nple", "x2", hTp[:, 0], hkp, 2 + s)
                CP("pool", pb16[:, s, :], pt[:, s, :], r=[("pt", s)], w=[("pb16", s)])
                for kc in range(2):
                    TR(psT[0][:, 512 + kc * 128:512 + (kc + 1) * 128], pb16[:, s, kc * 128:(kc + 1) * 128], ident, r=[("pb16", s), "cstb"], w=[pkT(0)])
                CP("dve", pT[:, s, :, :], psT[0][:, 512:768].rearrange("p (k t) -> p k t", k=2), r=[pkT(0)], w=[("pT", s)])
                for hf in range(2):
                    gb = 0 + hf
                    pbk = 2 + hf
                    for kc in range(8):
                        MM(psA[gb][:, :], hTp[:, 0, kc, :], wpg[:, kc, hf * 512:(hf + 1) * 512], start=(kc == 0), stop=(kc == 7), r=[(hkp, kc), "wpg"], w=[pk(gb)])
                    for kc in range(2):
                        MM(psA[pbk][:, :], pT[:, s, kc, :], wple[:, kc, hf * 512:(hf + 1) * 512], start=(kc == 0), stop=(kc == 1), r=[("pT", s), "wple"], w=[pk(pbk)])
                    ACT(sgp[:, 0, :], psA[gb][:, :], AF.Tanh, scale=0.5, r=[pk(gb)], w=["sgp"])
                    STT("dve", sgp[:, 0, :], sgp[:, 0, :], 1.0, psA[pbk][:, :], ALU.add, ALU.mult, r=[pk(pbk), "sgp"], w=["sgp"])
                    STT("dve", xt[:, s, hf * 512:(hf + 1) * 512], sgp[:, 0, :], 0.5, x2[:, 0, hf * 512:(hf + 1) * 512], ALU.mult, ALU.add,
                        r=["sgp", "x2", ("xt4", s)], w=[("ob", s)])
                DMA("sp", out_d[tok:tok + 128, :], xt[:, s, :], r=[("ob", s), ("xt4", s)], w=[("out", blk), ("xt4", s)])
            P.add("sp", None, [("out", i) for i in range(NB_OWN)], ())
        P.emit(nc, gs)
    return nc


def _consts():
    cst = np.zeros((128, 12, 128), np.float32)
    i = np.arange(128)
    cst[:, 0, :] = np.eye(128)
    cst[:, 1, :] = (i[:, None] // 64 == i[None, :] // 64)
    perm = np.zeros((128, 128), np.float32)
    for m in range(128):
        j = m % 64
        if j < 8:
            perm[m + 8, m] = 1.0
        elif j < 16:
            perm[m - 8, m] = 1.0
    cst[:, 2, :] = perm
    cst[:, 3, :] = cst[:, 1, :]
    same = (i[:, None] // 64 == i[None, :] // 64)
    cst[:, 4, :] = same & (i[:, None] < i[None, :])
    cst[:, 5, :] = same & (i[:, None] <= i[None, :])
    cst[:, 6, :] = same & (i[:, None] > i[None, :])
    cst[:, 7, :] = same & (i[:, None] >= i[None, :])
    cst[:, 8, :] = 1.0
    cst[:, 9, :] = cst[:, 1, :] / 64.0
    cst[:, 10, :] = (i[:, None] >= i[None, :])
    cst[:, 11, :] = (i[:, None] <= i[None, :])
    tri = np.zeros((128, 2, 128), np.float32)
    tri[:, 0, :] = 0.5 * LWS * cst[:, 5, :]
    tri[:, 1, :] = 0.5 * LWS * cst[:, 7, :]
    return cst, tri


def _rope_tables(pos):
    half = 8
    inv = np.power(np.float32(500000.0), -np.arange(half, dtype=np.float32) * 2.0 / 16.0).astype(np.float32)
    ang = pos.astype(np.float32)[None, :] * inv[:, None]
    cos = np.cos(ang).astype(np.float32); sin = np.sin(ang).astype(np.float32)
    n = pos.shape[0]
    tab = np.zeros((128, 2, n), np.float32)
    for hb in range(2):
        b = hb * 64
        tab[b:b + 64, 0, :] = 1.0
        tab[b:b + 8, 0, :] = cos; tab[b + 8:b + 16, 0, :] = cos
        tab[b:b + 8, 1, :] = -sin; tab[b + 8:b + 16, 1, :] = sin
    return tab


def _col128(v):
    return np.ascontiguousarray(np.asarray(v, np.float32).reshape(-1, 128).T)


_NC_CACHE = {}


def kernel(x, p, norm_mix, w_in, shift_mu, q_norm, k_norm, sink, w0, w2, a0, a2, g2, k_k, k_a, r_k,
           lnx_w, lnx_b, w_up_attn, w_up_rwkv, w_out, norm_ffn, w_ff1, w_ff2, norm_ple, w_ple_gate, w_ple):
    f = lambda a: np.asarray(a, np.float32)
    x = f(x); p = f(p)[0]; w_in = f(w_in)[0]
    qcols = []
    for c in range(4):
        qcols += list(range(c * 64, c * 64 + 64)) + list(range((4 + c) * 64, (4 + c) * 64 + 64))
    O_K, O_V, O_R = 512, 640, 768
    cols = qcols + list(range(O_K, O_K + 128)) + list(range(O_V, O_V + 128))
    rw = O_R
    wa = list(range(rw + 1536, rw + 1536 + 128)); gl = list(range(rw + 1664, rw + 1792))
    cols += wa + gl
    mu_idx = [12, 13]
    for fc in range(4):
        for part in range(3):
            cols += list(range(rw + part * 512 + fc * 128, rw + part * 512 + fc * 128 + 128))
            mu_idx.append(part * 4 + fc)
    wsw = np.ascontiguousarray(w_in[:, cols])
    wgt = np.ascontiguousarray(w_in[:, O_R + 1792:])
    mu_c = _col128(f(shift_mu)[0])[:, mu_idx]
    cst, tri = _consts()
    cst_in = cst.copy()
    cst_in_f = cst.copy()
    cst_in[:, 1, :] = cst[:, 1, :]
    sink_ = f(sink)[0]
    sinkcols = np.zeros((128, 4), np.float32)
    for m in range(4):
        sinkcols[0:64, m] = sink_[2 * m]; sinkcols[64:128, m] = sink_[2 * m + 1]
    in_maps = []
    for c in range(8):
        b, hf = c // 2, c % 2
        xb = x[b]; pb_ = p[b]
        if hf == 0:
            xl = xb; pl = pb_[:HALF]; pos = np.arange(0, HALF + 512); dA, dB = 0, 1
        else:
            xl = xb[::-1]; pl = pb_[HALF:][::-1]; pos = SEQ - 1 - np.arange(0, HALF + 512); dA, dB = 1, 0
        xs = np.zeros((SEQ + 2, D), np.float32); xs[1:SEQ + 1] = xl
        dd = [dA, dB]
        vec = np.zeros((128, NVEC_IN), np.float32)
        def put(n, arr):
            vec[:, VC[n]:VC[n] + arr.shape[1]] = arr
        put("mu", mu_c); put("k_k", _col128(f(k_k)[0])); put("k_a", _col128(f(k_a)[0]))
        put("rkA", _col128(f(r_k)[0, dA].reshape(-1))); put("rkB", _col128(f(r_k)[0, dB].reshape(-1)))
        put("lnw", _col128(f(lnx_w)[0])); put("lnb", _col128(f(lnx_b)[0]))
        put("a0A", _col128(f(a0)[0, dA])); put("a0B", _col128(f(a0)[0, dB]))
        put("qg", np.tile(f(q_norm)[0], 2)[:, None]); put("kg", np.tile(f(k_norm)[0], 2)[:, None])
        put("nmix", _col128(f(norm_mix)[0])); put("nffn", _col128(f(norm_ffn)[0])); put("nple", _col128(f(norm_ple)[0]))
        put("sink", sinkcols)
        w2aug = np.stack([np.concatenate([f(w2)[0, d_], f(w0)[0, d_][None, :]], 0) for d_ in dd])
        a2s = np.stack([f(a2)[0, d_] for d_ in dd])
        cst_c = cst.copy()
        m = {"xs": xs, "pp": np.ascontiguousarray(pl), "wsw": wsw, "wgt": wgt, "vec": vec, "w2aug": np.ascontiguousarray(w2aug),
             "a2": np.ascontiguousarray(a2s), "g2": f(g2)[0], "wua": f(w_up_attn)[0], "wur": f(w_up_rwkv)[0], "wout": f(w_out)[0],
             "wff1": f(w_ff1)[0], "wff2": f(w_ff2)[0], "wpg": f(w_ple_gate)[0], "wple": f(w_ple)[0],
             "rope": _rope_tables(pos), "cst": cst_c, "tri": tri, "crow": np.ascontiguousarray(tri.sum(0)[None])}
        in_maps.append(m)
    if _NC_CACHE.get("prep_only"):
        return in_maps
    if "nc" not in _NC_CACHE:
        _NC_CACHE["nc"] = build_program()
    res = run_bass_kernel_spmd(_NC_CACHE["nc"], in_maps, core_ids=list(range(8)))
    out = np.zeros((4, SEQ, D), np.float32)
    for c in range(8):
        b, hf = c // 2, c % 2
        o = res.results[c]["out"]
        if hf == 0:
            out[b, :HALF] = o
        else:
            out[b, HALF:] = o[::-1]
    return out
```

```python
import os
import numpy as np
from contextlib import ExitStack
import concourse.bass as bass
import concourse.mybir as mybir
from concourse.bass_utils import run_bass_kernel_spmd

F32 = mybir.dt.float32
BF16 = mybir.dt.bfloat16
ALU = mybir.AluOpType
AF = mybir.ActivationFunctionType

SEM_EPOCH = 3000
DMA_SLOTS = 8

D = 1024; SEQ = 4096; HALF = 2048; NB_OWN = 16
TT = 256; NBT = 2; NTH = 8
LWS = -0.6065306597126334


class Op:
    __slots__ = ("eng", "fn", "reads", "writes", "dma", "deps", "flag", "cnt", "slot", "slotval", "idx", "xdeps",
                 "dur", "rg", "tbl", "seg", "alld", "pos")

    def __init__(self, eng, fn, reads, writes, dma):
        self.eng = eng; self.fn = fn; self.reads = reads; self.writes = writes; self.dma = dma
        self.deps = set(); self.flag = False; self.cnt = None; self.slot = None; self.slotval = None
        self.xdeps = (); self.dur = 0.2; self.rg = None; self.tbl = None; self.seg = 0


SYNC_LAT = 0.35


class Prog:
    def __init__(self):
        self.ops = []
        self.tags = []
        self.last_barrier = 0
        self.seg = 0
        self.reorder = True

    def add(self, eng, fn, r=(), w=(), dma=False, dur=0.2, rg=None, tbl=None):
        op = Op(eng, fn, tuple(r), tuple(w), dma)
        op.dur = dur; op.rg = rg; op.tbl = tbl; op.seg = self.seg
        self.tags.append(getattr(self, "tag", None)) if not getattr(self, "disabled", False) else None
        if getattr(self, "disabled", False):
            op.idx = -1
            return op
        op.idx = len(self.ops)
        self.ops.append(op)
        return op

    def all_deps(self):
        last_w = {}
        readers = {}
        ops = self.ops
        for op in ops:
            d = {}
            for j in op.xdeps:
                d[j] = "raw"
            for k in op.reads:
                j = last_w.get(k)
                if j is not None:
                    d[j] = "raw"
            for k in op.writes:
                j = last_w.get(k)
                if j is not None and j not in d:
                    d[j] = "waw"
                for j in readers.get(k, ()):
                    if j != op.idx and j not in d:
                        d[j] = "war"
            for k in op.reads:
                if isinstance(k, tuple) and k[0] in ("ps", "psT"):
                    for j in readers.get(k, ()):
                        if ops[j].eng != op.eng and j not in d:
                            d[j] = "raw"
            for k in op.reads:
                readers.setdefault(k, []).append(op.idx)
            for k in op.writes:
                last_w[k] = op.idx
                readers[k] = []
            op.alld = d

    def schedule(self):
        ops = self.ops
        n = len(ops)
        if not self.reorder:
            return list(range(n))
        fin = [0.0] * n
        self.fin = fin
        self.bind = {}; self._last_on = {}; self.start = {}
        succ = [[] for _ in range(n)]
        ndep = [0] * n
        for op in ops:
            for j in op.alld:
                succ[j].append(op.idx)
            ndep[op.idx] = len(op.alld)
        order = []
        free_t = {}
        cur_tbl = [None]
        segs = {}
        for op in ops:
            segs.setdefault(op.seg, []).append(op.idx)
        done = [False] * n
        ksegs = os.environ.get("KREORD")
        for sg in sorted(segs):
            members = segs[sg]
            if ksegs is not None and str(sg) not in ksegs.split(","):
                for i in members:
                    t0_ = max([free_t.get(ops[i].eng, 0.0)] + [fin[j] for j in ops[i].alld])
                    fin[i] = t0_ + ops[i].dur
                    free_t[ops[i].eng] = fin[i]
                    done[i] = True
                    order.append(i)
                continue
            inseg = set(members)
            ready = {}
            rt = {}
            remaining = len(members)
            cnt_un = {}
            for i in members:
                c = sum(1 for j in ops[i].alld if not done[j])
                cnt_un[i] = c
                if c == 0:
                    ready.setdefault(ops[i].eng, []).append(i)
            def ready_time(i):
                op = ops[i]
                t = 0.0
                for j, kind in op.alld.items():
                    p = ops[j]
                    lat = SYNC_LAT if (p.dma or p.eng != op.eng or kind == "raw") else 0.0
                    if p.eng == op.eng and op.eng == "pe":
                        lat = 0.0
                    t = max(t, fin[j] + lat)
                return t
            for e in ready:
                for i in ready[e]:
                    rt[i] = ready_time(i)
            while remaining:
                best = None
                for e, lst in ready.items():
                    if not lst:
                        continue
                    ft = free_t.get(e, 0.0)
                    bi = None
                    for i in lst:
                        st = max(ft, rt[i])
                        if e == "act" and ops[i].tbl is not None and cur_tbl[0] is not None and ops[i].tbl != cur_tbl[0]:
                            st += 1.3
                        key = (st, i)
                        if bi is None or key < bi:
                            bi = key
                    if best is None or bi < best[0]:
                        best = (bi, e)
                (st, i), e = best
                op = ops[i]
                ready[e].remove(i)
                if e == "act" and op.tbl is not None:
                    cur_tbl[0] = op.tbl
                fin[i] = st + op.dur
                if getattr(self, "trace_bind", False):
                    bj = None; bt_ = -1.0
                    for j in op.alld:
                        if fin[j] > bt_:
                            bt_ = fin[j]; bj = j
                    prev_e = self._last_on.get(e)
                    if prev_e is not None and free_t.get(e, 0.0) >= rt[i] - 1e-9:
                        self.bind[i] = ("eng", prev_e)
                    else:
                        self.bind[i] = ("dep", bj)
                    self._last_on[e] = i
                    self.start[i] = st
                if op.dma:
                    free_t[e] = st + 0.1
                else:
                    free_t[e] = fin[i]
                done[i] = True
                order.append(i)
                remaining -= 1
                for k in succ[i]:
                    if k in inseg:
                        cnt_un[k] -= 1
                        if cnt_un[k] == 0:
                            rt[k] = ready_time(k)
                            ready.setdefault(ops[k].eng, []).append(k)
        self.est_total = max(fin) if fin else 0.0
        self.seg_end = {sg: max(fin[i] for i in segs[sg]) for sg in segs}
        return order

    def analyse(self):
        self.all_deps()
        order = self.schedule()
        ops = self.ops
        for pos, i in enumerate(order):
            ops[i].pos = pos
        self.order = order
        pe = [i for i in order if ops[i].eng == "pe"]
        extra = {}
        for a in range(len(pe)):
            oa = ops[pe[a]]
            if oa.rg is None:
                continue
            for b in range(a + 1, a + 3):
                if b >= len(pe):
                    break
                ob = ops[pe[b]]
                if ob.rg is None:
                    continue
                if (oa.rg[1] < ob.rg[0] or ob.rg[1] < oa.rg[0]) and set(oa.writes) & set(ob.writes):
                    extra.setdefault(pe[b], set()).add(pe[a])
        for op in ops:
            need = set()
            for j, kind in op.alld.items():
                p = ops[j]
                if p.dma:
                    need.add(j)
                elif p.eng == op.eng and not op.dma:
                    if kind == "raw" and op.eng != "pe":
                        need.add(j); p.flag = True
                else:
                    need.add(j); p.flag = True
            for j in extra.get(op.idx, ()):
                need.add(j); ops[j].flag = True
            op.deps = need
        cnt = {}
        nd = {}
        for i in order:
            op = ops[i]
            if op.dma:
                k = nd.get(op.eng, 0)
                nd[op.eng] = k + 1
                op.slot = (op.eng, k % DMA_SLOTS)
                op.slotval = 16 * (k // DMA_SLOTS + 1)
            elif op.fn is not None and op.flag:
                cnt[op.eng] = cnt.get(op.eng, 0) + 1
                op.cnt = cnt[op.eng]
        self.nflag = cnt
        self.ndma = nd

    def emit(self, nc, stack):
        self.analyse()
        engs = ["pe", "act", "dve", "pool", "sp"]
        sems = {}
        for e in engs:
            n = self.nflag.get(e, 0)
            for ep in range(n // SEM_EPOCH + 1):
                sems[(e, ep)] = stack.enter_context(nc.semaphore(f"s_{e}_{ep}"))
        dsem = {}
        for e, n in self.ndma.items():
            for s in range(min(n, DMA_SLOTS)):
                dsem[(e, s)] = stack.enter_context(nc.semaphore(f"d_{e}_{s}"))
        block = stack.enter_context(nc.Block())
        ops = self.ops
        per_eng = {e: [ops[i] for i in self.order if ops[i].eng == e] for e in engs}

        def sem_of(p):
            if p.dma:
                return dsem[p.slot], p.slotval
            c = p.cnt - 1
            return sems[(p.eng, c // SEM_EPOCH)], (c % SEM_EPOCH) + 1

        def run(engobj, lst):
            seen = {}
            dma_hist = {}
            for op in lst:
                waits = {}
                for j in op.deps:
                    s, v = sem_of(ops[j])
                    key = id(s)
                    if seen.get(key, 0) >= v:
                        continue
                    if key not in waits or waits[key][1] < v:
                        waits[key] = (s, v)
                if op.dma:
                    prev = dma_hist.get(op.slot)
                    if prev is not None:
                        s, v = dsem[op.slot], prev
                        key = id(s)
                        if seen.get(key, 0) < v and (key not in waits or waits[key][1] < v):
                            waits[key] = (s, v)
                    dma_hist[op.slot] = op.slotval
                for key, (s, v) in waits.items():
                    engobj.wait_ge(s, v)
                    seen[key] = v
                if op.fn is None:
                    continue
                ins = op.fn(engobj)
                if op.dma:
                    ins.then_inc(dsem[op.slot], 16)
                elif op.flag:
                    s, v = sem_of(op)
                    ins.then_inc(s, 1)

        if per_eng["pe"]:
            @block.tensor
            def _(e):
                run(e, per_eng["pe"])
        if per_eng["act"]:
            @block.scalar
            def _(e):
                run(e, per_eng["act"])
        if per_eng["dve"]:
            @block.vector
            def _(e):
                run(e, per_eng["dve"])
        if per_eng["pool"]:
            @block.gpsimd
            def _(e):
                run(e, per_eng["pool"])
        if per_eng["sp"]:
            @block.sync
            def _(e):
                run(e, per_eng["sp"])


VC = {}
_o = 0
for _n, _c in [("mu", 14), ("k_k", 4), ("k_a", 4), ("rkA", 4), ("rkB", 4), ("lnw", 4), ("lnb", 4),
               ("a0A", 4), ("a0B", 4), ("qg", 1), ("kg", 1), ("nmix", 8), ("nffn", 8), ("nple", 8),
               ("sink", 4), ("omm", 14), ("hmu", 14), ("nkk", 4), ("omka", 4), ("esink", 4), ("ha0A", 4), ("ha0B", 4), ("hka", 4), ("c2ka", 4)]:
    VC[_n] = _o
    _o += _c
NVEC_IN = VC["omm"]
NVEC = _o

N_SW = 20


def build_program(dbg=None, upto=99, ntiles_other=None, ntiles_own=None):
    nc = bass.Bass("TRN2", target_bir_lowering=False)
    P = Prog()

    def din(name, shape):
        return nc.dram_tensor(name, list(shape), F32, kind="ExternalInput").ap()

    xs = din("xs", [SEQ + 2, D])
    pp = din("pp", [HALF, 256])
    wsw = din("wsw", [D, N_SW * 128])
    wgt = din("wgt", [D, 2048])
    vec_d = din("vec", [128, NVEC_IN])
    w2aug_d = din("w2aug", [2, 65, 512])
    a2_d = din("a2", [2, 64, 512])
    g2_d = din("g2", [128, 512])
    wua_d = din("wua", [512, D]); wur_d = din("wur", [512, D]); wout_d = din("wout", [D, D])
    wff1_d = din("wff1", [D, 4096]); wff2_d = din("wff2", [4096, D])
    wpg_d = din("wpg", [D, D]); wple_d = din("wple", [256, D])
    rope_d = din("rope", [128, 2, HALF + 512])
    cst_d = din("cst", [128, 12, 128])
    tri_d = din("tri", [128, 2, 128])
    crow_d = din("crow", [1, 2, 128])
    out_d = nc.dram_tensor("out", [HALF, D], F32, kind="ExternalOutput").ap()
    m0_s = nc.dram_tensor("m0_s", [2, 64, 128, 4, 128], BF16, kind="Internal").ap()
    n0_s = nc.dram_tensor("n0_s", [2, 64, 128, 4, 64], BF16, kind="Internal").ap()
    q_s = nc.dram_tensor("q_s", [2, 16, 128, 4, 128], BF16, kind="Internal").ap()
    x1_s = nc.dram_tensor("x1_s", [HALF, D], F32, kind="Internal").ap()
    y0_s = nc.dram_tensor("y0_s", [2, 128, 4, HALF], F32, kind="Internal").ap()
    ao_s = nc.dram_tensor("ao_s", [128, 4, HALF], BF16, kind="Internal").ap()
    bv_s = nc.dram_tensor("bv_s", [128, 4, HALF], BF16, kind="Internal").ap()
    sgl_s = nc.dram_tensor("sgl_s", [128, HALF], BF16, kind="Internal").ap()
    dbg_out = None
    if dbg is not None:
        dbg_out = nc.dram_tensor("dbg", [128, 16384], F32, kind="ExternalOutput").ap()

    def fsz(ap):
        n = 1
        for d_ in ap.shape[1:]:
            n *= d_
        return n

    def rgof(ap):
        b = ap.base_partition()
        return (b // 32, (b + ap.shape[0] - 1) // 32)

    def MM(out, lhsT, rhs, start=True, stop=True, r=(), w=()):
        dur = max(0.064, 0.02 + fsz(out) / 2000.0) * (4 if lhsT.dtype == F32 else 1)
        P.add("pe", lambda e: e.matmul(out, lhsT, rhs, start=start, stop=stop), r, w, dur=dur, rg=rgof(lhsT))

    def TR(out, in_, idn, r=(), w=()):
        P.add("pe", lambda e: e.transpose(out, in_, idn), r, w, dur=0.064, rg=rgof(in_))

    def ACT(out, in_, func, bias=None, scale=None, accum=None, r=(), w=()):
        kw = {}
        if bias is not None:
            kw["bias"] = bias
        if scale is not None:
            kw["scale"] = scale
        if accum is not None:
            kw["accum_out"] = accum
        tbl = "ln" if func == AF.Ln else ("tanh" if func == AF.Tanh else None)
        P.add("act", lambda e: e.activation(out=out, in_=in_, func=func, **kw), r, w, dur=0.2 + fsz(out) / 1400.0, tbl=tbl)

    def vdur(eng, out):
        return (0.1 + fsz(out) / 960.0) if eng == "dve" else (0.3 + fsz(out) / 500.0)

    def TTo(eng, out, a, b, op, r=(), w=()):
        P.add(eng, lambda e: e.tensor_tensor(out, a, b, op), r, w, dur=vdur(eng, out))

    def TS(eng, out, a, s1, s2, op0, op1=None, r=(), w=()):
        if op1 is None:
            P.add(eng, lambda e: e.tensor_scalar(out, a, s1, None, op0), r, w, dur=vdur(eng, out))
        else:
            P.add(eng, lambda e: e.tensor_scalar(out, a, s1, s2, op0, op1), r, w, dur=vdur(eng, out))

    def STT(eng, out, in0, sc, in1, op0, op1, r=(), w=()):
        P.add(eng, lambda e: e.scalar_tensor_tensor(out, in0, sc, in1, op0, op1), r, w, dur=vdur(eng, out))

    def CP(eng, out, in_, r=(), w=()):
        P.add(eng, lambda e: e.tensor_copy(out, in_), r, w, dur=vdur(eng, out))

    def RCP(out, in_, r=(), w=()):
        P.add("dve", lambda e: e.reciprocal(out, in_), r, w, dur=vdur("dve", out) * 2)

    def MEMSET(eng, ap, val, w=()):
        P.add(eng, lambda e: e.memset(ap, val), (), w, dur=vdur(eng, ap))

    def DMA(eng, out, in_, r=(), w=()):
        nbytes = fsz(out) * out.shape[0] * (4 if out.dtype == F32 else 2)
        P.add(eng, lambda e: e.dma_start(out=out, in_=in_), r, w, dma=True, dur=2.0 + nbytes / 100000.0)

    with ExitStack() as gs:
        def sb(name, shape, dt=F32, st=None):
            return (st or gs).enter_context(nc.sbuf_tensor("sb_" + name, list(shape), dt))

        psA = [gs.enter_context(nc.psum_tensor(f"psA{i}", [128, 512], F32)) for i in range(6)]
        psT = [gs.enter_context(nc.psum_tensor(f"psT{i}", [128, 1024], BF16)) for i in range(2)]
        pk = lambda i: ("ps", i)
        pkT = lambda i: ("psT", i)

        vec = sb("vec", [128, NVEC])
        cstb = sb("cstb", [128, 12, 128], BF16)
        cstf = sb("cstf", [128, 3, 128])
        crow = sb("crow", [1, 2, 128])
        tri = sb("tri", [128, 2, 128])
        eps_c = sb("eps_c", [128, 4])
        junk = sb("junk", [128, D], BF16)
        DMA("sp", vec[:, 0:NVEC_IN], vec_d, w=["vec"])
        DMA("pool", cstb[:], cst_d, w=["cstb"])
        DMA("sp", cstf[:, 0, :], cst_d[:, 0, :], w=["cstf"])
        DMA("sp", cstf[:, 1, :], cst_d[:, 9, :], w=["cstf"])
        DMA("sp", cstf[:, 2, :], cst_d[:, 8, :], w=["cstf"])
        DMA("sp", crow[:], crow_d, w=["crow"])
        DMA("sp", tri[:], tri_d, w=["tri"])
        MEMSET("dve", eps_c[:, 0:1], 1e-6, w=["eps"])
        MEMSET("dve", eps_c[:, 1:2], 1e-24, w=["eps"])
        MEMSET("dve", eps_c[:, 2:3], 64e-5, w=["eps"])
        ident = cstb[:, 0, :]; onesblk = cstb[:, 1, :]; perm = cstb[:, 2, :]; bdm = cstb[:, 3, :]
        identf = cstf[:, 0, :]
        vc = lambda n, i=0: vec[:, VC[n] + i:VC[n] + i + 1]
        TS("dve", vec[:, VC["omm"]:VC["omm"] + 14], vec[:, VC["mu"]:VC["mu"] + 14], -1.0, 1.0, ALU.mult, ALU.add, r=["vec"], w=["vec2"])
        TS("dve", vec[:, VC["hmu"]:VC["hmu"] + 14], vec[:, VC["mu"]:VC["mu"] + 14], 0.5, None, ALU.mult, r=["vec"], w=["vec2"])
        TS("dve", vec[:, VC["nkk"]:VC["nkk"] + 4], vec[:, VC["k_k"]:VC["k_k"] + 4], -1.0, None, ALU.mult, r=["vec"], w=["vec2"])
        TS("dve", vec[:, VC["omka"]:VC["omka"] + 4], vec[:, VC["k_a"]:VC["k_a"] + 4], -1.0, 1.0, ALU.mult, ALU.add, r=["vec"], w=["vec2"])
        ACT(vec[:, VC["esink"]:VC["esink"] + 4], vec[:, VC["sink"]:VC["sink"] + 4], AF.Exp, r=["vec"], w=["vec2"])
        TS("dve", vec[:, VC["ha0A"]:VC["ha0A"] + 8], vec[:, VC["a0A"]:VC["a0A"] + 8], 0.5, None, ALU.mult, r=["vec"], w=["vec2"])
        TS("dve", vec[:, VC["hka"]:VC["hka"] + 4], vec[:, VC["k_a"]:VC["k_a"] + 4], 0.5, None, ALU.mult, r=["vec"], w=["vec2"])
        TS("dve", vec[:, VC["c2ka"]:VC["c2ka"] + 4], vec[:, VC["k_a"]:VC["k_a"] + 4], -0.5, 1.0, ALU.mult, ALU.add, r=["vec"], w=["vec2"])
        VK = ["vec", "vec2", "eps", "cstb", "cstf", "tri"]

        def ck(name):
            if os.environ.get("KSTOP") == name:
                P.disabled = True
        if os.environ.get("KSTOP") == "c":
            P.disabled = True

        bar_t = {e: sb(f"bar_{e}", [1, 8]) for e in ["act", "dve", "pool"]}

        def RSQ(dst, src, scale, bias_ap, n, r=(), w=()):
            ACT(dst, src, AF.Ln, bias=bias_ap, scale=scale, r=list(r) + ["eps"], w=list(w))
            ACT(dst, dst, AF.Exp, scale=-0.5, r=list(w), w=list(w))

        def barrier():
            if getattr(P, "disabled", False):
                return
            P.seg += 1
            n = len(P.ops)
            pend = [op.idx for op in P.ops[P.last_barrier:n]]
            marks = []
            for e in ["act", "dve", "pool"]:
                t = bar_t[e]
                marks.append(P.add(e, (lambda tt, ee: (lambda en: en.memzero(tt[:]) if ee == 'act' else en.memset(tt[:], 0.0)))(t, e), (), [f"bar_{e}_{n}"]).idx)
            marks.append(P.add("pe", lambda e: e.matmul(psA[0][0:1, 0:1], cstb[0:1, 8, 0:1], cstb[0:1, 8, 0:1], start=True, stop=True),
                               ["cstb", pk(0)], [pk(0), f"bar_pe_{n}"]).idx)
            for e in ["pe", "act", "dve", "pool", "sp"]:
                f = P.add(e, None, (), ())
                f.xdeps = tuple(marks) + tuple(i for i in pend if P.ops[i].dma)
            P.last_barrier = len(P.ops)
            P.seg += 1

        def make_hT(st_key, x_aps, nrows, hT, gain, wk, xt, hb, ssq, rstd):
            nblk = len(x_aps)
            for bi, (xap, rows, col0) in enumerate(x_aps):
                s = bi % xt.shape[1]
                DMA("sp", xt[0:rows, s, :], xap, w=[("xt", s)])
                ACT(junk[0:rows, :], xt[0:rows, s, :], AF.Square, accum=ssq[0:rows, s:s + 1], r=[("xt", s)], w=["junk", ("ssq", s)])
                RSQ(rstd[0:rows, s:s + 1], ssq[0:rows, s:s + 1], 1.0 / D, eps_c[0:rows, 0:1], 1, r=[("ssq", s)], w=[("rstd", s)])
                TS("dve", hb[0:rows, 0, :], xt[0:rows, s, :], rstd[0:rows, s:s + 1], None, ALU.mult, r=[("xt", s), ("rstd", s)], w=["hbo"])
                for kc in range(8):
                    TR(psT[kc % 2][:, (kc // 2) * 128:(kc // 2) * 128 + rows], hb[0:rows, 0, kc * 128:(kc + 1) * 128], ident[0:rows, 0:rows],
                       r=["hbo", "cstb"], w=[pkT(kc % 2)])
                for kc in range(8):
                    eng_act = (kc % 2 == 0)
                    src = psT[kc % 2][:, (kc // 2) * 128:(kc // 2) * 128 + rows]
                    dst = hT[:, kc, col0:col0 + rows]
                    if eng_act:
                        ACT(dst, src, AF.Copy, scale=vc(gain, kc), r=[pkT(kc % 2), "vec"], w=[(wk, kc)])
                    else:
                        TS("dve", dst, src, vc(gain, kc), None, ALU.mult, r=[pkT(kc % 2), "vec"], w=[(wk, kc)])

        g2t = sb("g2t", [128, 512], BF16)
        PCs = sb("PCs", [128, 2, 64, 4], F32)
        with ExitStack() as s1:
            win = sb("win", [128, 8, N_SW * 128], BF16, s1)
            cstm = sb("cstm", [128, 4, 4, 128], BF16, s1)
            for _m in range(4):
                for _r in range(4):
                    DMA("pool", cstm[:, _m, _r, :], cst_d[:, 4 + _m, :], w=["cstm"])
            for (c0_, c1_) in ((768, 1664), (1664, 2560), (0, 768)):
                for kc in range(8):
                    DMA("pool", win[:, kc, c0_:c1_], wsw[kc * 128:(kc + 1) * 128, c0_:c1_], w=[("win", c0_)])
            w2aug = sb("w2aug", [65, 2, 512], BF16, s1)
            a2t = sb("a2t", [128, 2, 512], BF16, s1)
            rope = sb("rope", [128, 2, HALF + 256], BF16, s1)
            for d in range(2):
                DMA("pool", w2aug[:, d, :], w2aug_d[d], w=["w2aug"])
                DMA("pool", a2t[64:128, d, :], a2_d[d], w=["a2t"])
            DMA("pool", g2t[:], g2_d, w=["g2t"])
            DMA("pool", rope[:], rope_d[:, :, 0:HALF + 256], w=["rope"])
            if os.environ.get("KSTOP") == "w":
                P.disabled = True
            kT = sb("kT", [128, HALF + 128], BF16, s1)
            vaug = sb("vaug", [128, NB_OWN + 1, 2, 64], BF16, s1)
            vones = sb("vones", [128, 64], BF16, s1)
            attn_o = sb("attn_o", [128, 4, TT], BF16, s1)
            sgl = sb("sgl", [128, TT], BF16, s1)
            bv = sb("bv", [128, 4, TT], BF16, s1)
            MEMSET("pool", vones[:], 1.0, w=["vaug1"])
            xt = sb("xt", [128, 1, D], F32, s1); hb = sb("hb", [128, 1, D], BF16, s1)
            ssq = sb("ssq", [128, 2], F32, s1); rstd = sb("rstd", [128, 2], F32, s1)
            hT = sb("hT", [128, 8, TT + 2], BF16, s1)
            zr = sb("zr", [128, 2, TT + 2], F32, s1)
            zs = sb("zs", [128, 2, TT], F32, s1)
            ztmp = sb("ztmp", [128, 2, TT], F32, s1)
            zq = sb("zq", [128, TT], F32, s1)
            qsq = sb("qsq", [128, TT], BF16, s1)
            qrs = sb("qrs", [128, TT], F32, s1)
            qn = sb("qn", [128, TT], BF16, s1)
            qt1 = sb("qt1", [128, TT], F32, s1); qt2 = sb("qt2", [128, TT], F32, s1)
            qT = sb("qT", [128, 2, 4, TT], BF16, s1)
            vb = sb("vb", [128, TT], BF16, s1)
            twl = sb("twl", [65, TT], BF16, s1); alx = sb("alx", [128, TT], BF16, s1)
            MEMSET("pool", twl[64:65, :], 1.0, w=["twl1"])
            if os.environ.get("KSTOP") == "m":
                P.disabled = True
            sgfcD = [sb(f"sgfc{i}", [128, NBT, 128], F32, s1) for i in range(2)]
            r_f = sb("r_f", [128, TT], F32, s1); k_f = sb("k_f", [128, TT], F32, s1); v_f = sb("v_f", [128, TT], F32, s1)
            nkk = sb("nkk", [128, TT], F32, s1)
            a_fD = [sb(f"a_f{i}", [128, TT], F32, s1) for i in range(2)]; b_fD = [sb(f"b_f{i}", [128, TT], F32, s1) for i in range(2)]
            kd_fD = [sb(f"kd_f{i}", [128, TT], F32, s1) for i in range(2)]; t1_fD = [sb(f"t1_f{i}", [128, TT], F32, s1) for i in range(2)]
            pbD = [sb(f"pb{i}", [128, TT], BF16, s1) for i in range(2)]
            E_fD = [sb(f"E_f{i}", [128, TT], F32, s1) for i in range(2)]; G_fD = [sb(f"G_f{i}", [128, TT], F32, s1) for i in range(2)]
            AR = sb("AR", [128, 2, 4, NBT, 256], BF16, s1)
            bt = sb("bt", [128, 2, 4, TT], BF16, s1); kt = sb("kt", [128, 2, 4, TT], BF16, s1)
            btT = sb("btT", [128, 2, NBT, 512], BF16, s1); ktT = sb("ktT", [128, 2, NBT, 512], BF16, s1)
            atT = sb("atT", [128, 2, NBT, 512], BF16, s1)
            vT = sb("vT", [128, NBT, 512], BF16, s1)
            Aab = sb("Aab", [128, 8, 128], BF16, s1); AabT = sb("AabT", [128, 8, 128], BF16, s1)
            Aak = sb("Aak", [128, 8, 128], BF16, s1); Arb = sb("Arb", [128, 8, 128], BF16, s1); Ark = sb("Ark", [128, 8, 128], BF16, s1)
            ApI = [[sb(f"Ap{j}_{i}", [128, 8, 128], BF16, s1) for i in range(2)] for j in range(2)]
            ATpI = [[sb(f"ATp{j}_{i}", [128, 8, 128], BF16, s1) for i in range(2)] for j in range(2)]
            ZpI = [[sb(f"Zp{j}_{i}", [128, 4, 2, 2, 64], BF16, s1) for i in range(2)] for j in range(2)]
            M0stI = [sb(f"M0st{j}", [128, 2, 4, 128], BF16, s1) for j in range(2)]
            N0stI = [sb(f"N0st{j}", [128, 2, 4, 64], BF16, s1) for j in range(2)]
            QstI = [sb(f"Qst{j}", [128, 4, 128], BF16, s1) for j in range(2)]
            Y0stI = [sb(f"Y0st{j}", [128, 4, 128], F32, s1) for j in range(2)]
            hlcount = [0]
            probs = sb("probs", [128, 2, 384], BF16, s1)
            den = sb("den", [128, 128], F32, s1)

            def proj_chunk(ci, bank, halo_bank=None):
                if halo_bank is None:
                    for kc in range(8):
                        MM(psA[bank][:, 0:TT], win[:, kc, ci * 128:(ci + 1) * 128], hT[:, kc, 1:TT + 1], start=(kc == 0), stop=(kc == 7),
                           r=[("win", 0 if ci < 6 else (768 if ci < 13 else 1664)), ("hT", kc)], w=[pk(bank)])
                else:
                    for kc in range(8):
                        MM(psA[bank][:, 0:TT + 2], win[:, kc, ci * 128:(ci + 1) * 128], hT[:, kc, 0:TT + 2], start=(kc == 0), stop=(kc == 7),
                           r=[("win", 0 if ci < 6 else (768 if ci < 13 else 1664)), ("hT", kc)], w=[pk(bank)])

            zcount = [0]
            mbc = [0]

            def nb():
                mbc[0] += 1
                return 3 + (mbc[0] % 3)

            def rwkv_z(ci, mui, out_ap, wkey, post=None):
                s = zcount[0] % 2
                zcount[0] += 1
                bank = s
                proj_chunk(ci, bank, 2)
                ACT(zr[:, s, :], psA[bank][:, 0:TT + 2], AF.Copy, r=[pk(bank)], w=[("zr", s)])
                ACT(zs[:, s, :], psA[bank][:, 1:TT + 1], AF.Copy, scale=vc("omm", mui), r=[pk(bank), "vec2"], w=[("zs", s)])
                TTo("pool", ztmp[:, s, :], zr[:, s, 0:TT], zr[:, s, 2:TT + 2], ALU.add, r=[("zr", s)], w=[("ztmp", s)])
                STT("dve", out_ap, ztmp[:, s, :], vc("hmu", mui), zs[:, s, :], ALU.mult, ALU.add, r=[("ztmp", s), ("zs", s), "vec2"], w=[wkey])

            def sweep_tile(t0, own, first_other, pcidx0):
                dirs = [0, 1] if own else [1]
                blk0 = t0 // 128
                xaps = [(xs[t0 + 128 * i: t0 + 128 * i + 128, :], 128, 128 * i) for i in range(NBT)]
                xaps.append((xs[t0 + TT: t0 + TT + 2, :], 2, TT))
                P.tag = f"t{t0}-hT"
                make_hT("hT", xaps, None, hT, "nmix", "hT", xt, hb, ssq, rstd)
                ck("h")
                P.tag = f"t{t0}-qkv"
                qs = (t0 // TT) % 2
                if own or first_other:
                    for ci in ([0, 1, 2, 3, 4] if own else [4]):
                        bank = ci % 2
                        proj_chunk(ci, bank)
                        gcol = "kg" if ci == 4 else "qg"
                        ACT(qsq[:], psA[bank][:, 0:TT], AF.Square, r=[pk(bank)], w=["qsq"])
                        b1 = nb()
                        MM(psA[b1][:, 0:TT], onesblk, qsq[:], r=["cstb", "qsq"], w=[pk(b1)])
                        RSQ(qrs[:], psA[b1][:, 0:TT], 1.0 / 64, eps_c[:, 0:1], TT, r=[pk(b1)], w=["qrs"])
                        STT("dve", qn[:], psA[bank][:, 0:TT], vc(gcol), qrs[:], ALU.mult, ALU.mult, r=[pk(bank), "qrs", "vec"], w=["qn"])
                        b2 = nb()
                        MM(psA[b2][:, 0:TT], perm, qn[:], r=["cstb", "qn"], w=[pk(b2)])
                        TTo("pool", qt1[:], qn[:], rope[:, 0, t0:t0 + TT], ALU.mult, r=["qn", "rope"], w=["qt1"])
                        TTo("dve", qt2[:], psA[b2][:, 0:TT], rope[:, 1, t0:t0 + TT], ALU.mult, r=[pk(b2), "rope"], w=["qt2"])
                        if ci < 4:
                            TTo("pool", qT[:, qs, ci, :], qt1[:], qt2[:], ALU.add, r=["qt1", "qt2"], w=[("qT", qs)])
                        else:
                            n = TT if own else 128
                            TTo("pool", kT[:, t0:t0 + n], qt1[:, 0:n], qt2[:, 0:n], ALU.add, r=["qt1", "qt2"], w=["kT"])
                    proj_chunk(5, 0)
                    ACT(vb[:], psA[0][:, 0:TT], AF.Copy, r=[pk(0)], w=["vb"])
                    nbv = NBT if own else 1
                    for bi in range(nbv):
                        TR(psT[0][:, bi * 128:(bi + 1) * 128], vb[:, bi * 128:(bi + 1) * 128], ident, r=["vb", "cstb"], w=[pkT(0)])
                    for bi in range(nbv):
                        CP("dve", vaug[:, blk0 + bi, :, :], psT[0][:, bi * 128:(bi + 1) * 128].rearrange("p (g d) -> p g d", g=2),
                           r=[pkT(0)], w=["vaug"])
                P.tag = f"t{t0}-wa"
                ck("q")
                rwkv_z(6, 0, zq[:], "zq")
                ACT(twl[0:64, :], zq[0:64, :], AF.Tanh, r=["zq"], w=["twl"])
                CP("pool", alx[64:128, :], zq[64:128, :], r=["zq"], w=["alx"])
                ck("z")
                if own:
                    rwkv_z(7, 1, zq[:], "zq")
                    ACT(zq[:], zq[:], AF.Tanh, scale=0.5, r=["zq"], w=["zq"])
                    TS("pool", sgl[:, :], zq[:], 0.5, 0.5, ALU.mult, ALU.add, r=["zq"], w=["sgl"])
                    DMA("sp", sgl_s[:, t0:t0 + TT], sgl[:, :], r=["sgl"], w=[("sgl_s", t0 // TT)])
                for fc in range(4):
                    P.tag = f"t{t0}-fc{fc}"
                    if own:
                        rwkv_z(8 + 3 * fc, 2 + 3 * fc, r_f[:], "r_f")
                    rwkv_z(9 + 3 * fc, 3 + 3 * fc, k_f[:], "k_f")
                    rwkv_z(10 + 3 * fc, 4 + 3 * fc, v_f[:], "v_f")
                    ACT(qsq[:], k_f[:], AF.Square, scale=vc("k_k", fc), r=["k_f", "vec"], w=["qsq"])
                    b1 = nb()
                    MM(psA[b1][:, 0:TT], onesblk, qsq[:], r=["cstb", "qsq"], w=[pk(b1)])
                    RSQ(qrs[:], psA[b1][:, 0:TT], 1.0, eps_c[:, 1:2], TT, r=[pk(b1)], w=["qrs"])
                    STT("dve", nkk[:], k_f[:], vc("nkk", fc), qrs[:], ALU.mult, ALU.mult, r=["k_f", "qrs", "vec2"], w=["nkk"])
                    ck("k")
                    CP("pool", vb[:], v_f[:], r=["v_f"], w=["vb"])
                    for bi in range(NBT):
                        TR(psT[0][:, bi * 128:(bi + 1) * 128], vb[:, bi * 128:(bi + 1) * 128], ident, r=["vb", "cstb"], w=[pkT(0)])
                    CP("dve", vT[:, :, fc * 128:(fc + 1) * 128], psT[0][:, 0:NBT * 128].rearrange("p (b c) -> p b c", b=NBT), r=[pkT(0)], w=[("vT", fc)])
                    for d in dirs:
                        dn = "AB"[d]
                        a_f, b_f, kd_f, t1_f, pb, E_f, G_f, sgfc = a_fD[d], b_fD[d], kd_fD[d], t1_fD[d], pbD[d], E_fD[d], G_fD[d], sgfcD[d]
                        b1 = nb()
                        MM(psA[b1][:, 0:TT], a2t[64:128, d, fc * 128:(fc + 1) * 128], alx[64:128, :], r=["a2t", "alx"], w=[pk(b1)])
                        ACT(a_f[:], psA[b1][:, 0:TT], AF.Tanh, bias=vc("ha0" + dn, fc), scale=0.5, r=[pk(b1), "vec2"], w=[("a_f", d)])
                        STT("dve", b_f[:], a_f[:], 1.0, nkk[:], ALU.add, ALU.mult, r=["nkk", ("a_f", d)], w=[("b_f", d)])
                        TS("dve", t1_f[:], a_f[:], vc("hka", fc), vc("c2ka", fc), ALU.mult, ALU.add, r=[("a_f", d), "vec2"], w=[("t1_f", d)])
                        TTo("pool", kd_f[:], k_f[:], t1_f[:], ALU.mult, r=["k_f", ("t1_f", d)], w=[("kd_f", d)])
                        if own:
                            STT("dve", pb[:], r_f[:], vc("rk" + dn, fc), kd_f[:], ALU.mult, ALU.mult, r=["r_f", ("kd_f", d), "vec"], w=[("pb", d)])
                            MM(psA[2][:, 0:TT], onesblk, pb[:], start=(d == 0), stop=(d == 1), r=["cstb", ("pb", d)], w=[pk(2)])
                        b2 = nb()
                        for bi in range(NBT):
                            MM(psA[b2][:, bi * 128:(bi + 1) * 128], twl[:, bi * 128:(bi + 1) * 128], w2aug[:, d, fc * 128:(fc + 1) * 128],
                               r=["twl", "twl1", "w2aug"], w=[pk(b2)])
                        ACT(sgfc[:, :, :], psA[b2][:, 0:NBT * 128].rearrange("p (b c) -> p b c", b=NBT), AF.Tanh, scale=0.5, r=[pk(b2)], w=[("sgfc", d)])
                        b3 = nb()
                        for bi in range(NBT):
                            MM(psA[b3][:, bi * 128:(bi + 1) * 128], sgfc[:, bi, :], tri[:, d, :], start=True, stop=False, r=[("sgfc", d), "tri"], w=[pk(b3)])
                            MM(psA[b3][:, bi * 128:(bi + 1) * 128], cstf[0:1, 2, :], crow[0:1, d, :], start=False, stop=True, r=["cstf", "crow"], w=[pk(b3)])
                        ACT(E_f[:], psA[b3][:, 0:TT], AF.Exp, r=[pk(b3)], w=[("E_f", d)])
                        ACT(G_f[:], psA[b3][:, 0:TT], AF.Exp, scale=-1.0, r=[pk(b3)], w=[("G_f", d)])
                        Ev = E_f[:].rearrange("p (c t) -> p c t", t=64)
                        pcol = 63 if d == 0 else 0
                        CP("pool", PCs[:, d, pcidx0:pcidx0 + TT // 64, fc], Ev[:, :, pcol], r=[("E_f", d)], w=["PCs"])
                        ARv = AR[:, d, fc, :, :]
                        if own:
                            TTo("dve", ARv[:, :, 128:256], r_f[:].rearrange("p (b t) -> p b t", t=128), E_f[:].rearrange("p (b t) -> p b t", t=128),
                                ALU.mult, r=["r_f", ("E_f", d)], w=[("AR", d, fc)])
                        ARa = AR[:, d, fc, :, 0:128].rearrange("p b (c t) -> p b c t", t=64)
                        nk4 = nkk[:].rearrange("p (b c t) -> p b c t", c=2, t=64)
                        E4 = E_f[:].rearrange("p (b c t) -> p b c t", c=2, t=64)
                        if d == 0:
                            TTo("pool", ARa[:, :, :, 1:64], nk4[:, :, :, 1:64], E4[:, :, :, 0:63], ALU.mult, r=["nkk", ("E_f", d)], w=[("AR", d, fc)])
                            CP("pool", ARa[:, :, :, 0:1], nk4[:, :, :, 0:1], r=["nkk"], w=[("AR", d, fc)])
                        else:
                            TTo("pool", ARa[:, :, :, 0:63], nk4[:, :, :, 0:63], E4[:, :, :, 1:64], ALU.mult, r=["nkk", ("E_f", d)], w=[("AR", d, fc)])
                            CP("pool", ARa[:, :, :, 63:64], nk4[:, :, :, 63:64], r=["nkk"], w=[("AR", d, fc)])
                        STT("dve", bt[:, d, fc, :], b_f[:], -0.5, G_f[:], ALU.mult, ALU.mult, r=[("b_f", d), ("G_f", d)], w=[("bt", d, fc)])
                        TTo("pool", kt[:, d, fc, :], kd_f[:], G_f[:], ALU.mult, r=[("kd_f", d), ("G_f", d)], w=[("kt", d, fc)])
                        for which, (srck, dst, dkey) in enumerate(((("bt", d, fc), btT, "btT"), (("kt", d, fc), ktT, "ktT"), (("AR", d, fc), atT, "atT"))):
                            pb_i = which % 2
                            for bi in range(NBT):
                                if which == 0:
                                    src = bt[:, d, fc, bi * 128:(bi + 1) * 128]
                                elif which == 1:
                                    src = kt[:, d, fc, bi * 128:(bi + 1) * 128]
                                else:
                                    src = AR[:, d, fc, bi, 0:128]
                                TR(psT[pb_i][:, bi * 128:(bi + 1) * 128], src, ident, r=[srck, "cstb"], w=[pkT(pb_i)])
                            pv = psT[pb_i][:, 0:NBT * 128].rearrange("p (b c) -> p b c", b=NBT)
                            if which != 1:
                                CP("dve", dst[:, d, :, fc * 128:(fc + 1) * 128], pv, r=[pkT(pb_i)], w=[(dkey, d, fc)])
                            else:
                                ACT(dst[:, d, :, fc * 128:(fc + 1) * 128], pv, AF.Copy, r=[pkT(pb_i)], w=[(dkey, d, fc)])
                    if own:
                        TTo("dve", bv[:, fc, :], psA[2][:, 0:TT], v_f[:], ALU.mult, r=[pk(2), "v_f"], w=["bv"])
                        DMA("sp", bv_s[:, fc, t0:t0 + TT], bv[:, fc, :], r=["bv"], w=[("bv_s", t0 // TT, fc)])
                ck("f")
                for d in dirs:
                    ms, mi = (4, 5) if d == 0 else (6, 7)
                    for bi in range(NBT):
                        P.tag = f"t{t0}-hl{d}{bi}"
                        ins_ = hlcount[0] % 2
                        hlcount[0] += 1
                        Ap, ATp, Zp = ApI[ins_], ATpI[ins_], ZpI[ins_]
                        B = (lambda p_: (lambda b_: (b_ + 3 * p_) % 6))(ins_)
                        M0st, N0st, Qst, Y0st = M0stI[ins_], N0stI[ins_], QstI[ins_], Y0stI[ins_]
                        gblk = (t0 // 128 + bi) if own else None
                        def amat(bank0, lh, rh, keys):
                            for h in range(8):
                                fc = h // 2; rows = slice((h % 2) * 64, (h % 2) * 64 + 64)
                                MM(psA[B(bank0 + h % 2)][:, (h // 2) * 128:(h // 2) * 128 + 128], lh(rows, fc), rh(rows, fc),
                                   r=[(k_, d, fc) for k_ in keys], w=[pk(B(bank0 + h % 2))])

                        def aevac(bank0, dst, dkey, mslot):
                            dv = dst[:].rearrange("p (i two) t -> p two i t", two=2)
                            for par in range(2):
                                TTo("dve", dv[:, par, :, :], psA[B(bank0 + par)][:, :].rearrange("p (i t) -> p i t", i=4), cstm[:, mslot - 4, :, :], ALU.mult,
                                    r=[pk(B(bank0 + par)), "cstm"], w=[dkey])
                        tokc = slice(bi * 128, (bi + 1) * 128)
                        amat(0, lambda rows, fc: bt[rows, d, fc, tokc], lambda rows, fc: AR[rows, d, fc, bi, 0:128], ["bt", "AR"])
                        amat(2, lambda rows, fc: kt[rows, d, fc, tokc], lambda rows, fc: AR[rows, d, fc, bi, 0:128], ["kt", "AR"])
                        amat(4, lambda rows, fc: AR[rows, d, fc, bi, 0:128], lambda rows, fc: bt[rows, d, fc, tokc], ["bt", "AR"])
                        aevac(0, Aab, "Aab", ms)
                        aevac(2, Aak, "Aak", ms)
                        aevac(4, AabT, "AabT", (6 if d == 0 else 4))
                        ck("a")
                        Z = Zp[0]
                        CP("pool", Z[:, :, 0, :, :], atT[:, d, bi, :].rearrange("p (f h j) -> p f h j", f=4, h=2), r=[("atT", d, f_) for f_ in range(4)], w=[("Z", ins_, 0, f_) for f_ in range(4)])
                        for h in range(8):
                            MM(psA[B(3)][:, h * 64:(h + 1) * 64], Aak[:, h, :], vT[:, bi, h * 64:(h + 1) * 64], r=["Aak", ("vT", h // 2)], w=[pk(B(3))])
                        CP("dve", Z[:, :, 1, :, :], psA[B(3)][:, 0:512].rearrange("p (f h j) -> p f h j", f=4, h=2), r=[pk(B(3))], w=[("Z", ins_, 0, f_) for f_ in range(4)])
                        curA, curAT, ak, atk = Aab, AabT, ["Aab", "Aab"], ["AabT", "AabT"]
                        zi = 0
                        for lev in range(6):
                            Zc, Zn = Zp[zi], Zp[1 - zi]
                            for hh in range(2):
                                bank = 4 + hh
                                for q4 in range(4):
                                    h = 4 * hh + q4; fc = h // 2; h2 = h % 2
                                    o = psA[B(bank)][:, q4 * 128:(q4 + 1) * 128].rearrange("p (a j) -> p a j", a=2)
                                    MM(o, curA[:, h, :], Zc[:, fc, :, h2, :], start=True, stop=False, r=[ak[hh], ("Z", ins_, zi, fc)], w=[pk(B(bank))])
                                    MM(o, ident, Zc[:, fc, :, h2, :], start=False, stop=True, r=["cstb", ("Z", ins_, zi, fc)], w=[pk(B(bank))])
                                for f in range(2):
                                    src = psA[B(bank)][:, f * 256:(f + 1) * 256].rearrange("p (h a j) -> p a h j", h=2, a=2)
                                    if hh == 0:
                                        ACT(Zn[:, 2 * hh + f, :, :, :], src, AF.Copy, r=[pk(B(bank))], w=[("Z", ins_, 1 - zi, 2 * hh + f)])
                                    else:
                                        CP("dve", Zn[:, 2 * hh + f, :, :, :], src, r=[pk(B(bank))], w=[("Z", ins_, 1 - zi, 2 * hh + f)])
                            zi = 1 - zi
                            if lev < 5:
                                nA, nAT = Ap[lev % 2], ATp[lev % 2]
                                nak = [("Ap", ins_, lev % 2, 0), ("Ap", ins_, lev % 2, 1)]
                                natk = [("ATp", ins_, lev % 2, 0), ("ATp", ins_, lev % 2, 1)]
                                for hh in range(2):
                                    for q4 in range(4):
                                        h = 4 * hh + q4
                                        MM(psA[B(0 + hh)][:, q4 * 128:(q4 + 1) * 128], curAT[:, h, :], curA[:, h, :], r=[ak[hh], atk[hh]], w=[pk(B(0 + hh))])
                                        MM(psA[B(2 + hh)][:, q4 * 128:(q4 + 1) * 128], curA[:, h, :], curAT[:, h, :], r=[ak[hh], atk[hh]], w=[pk(B(2 + hh))])
                                    ACT(nA[:, 4 * hh:4 * hh + 4, :], psA[B(0 + hh)][:, :].rearrange("p (h t) -> p h t", h=4), AF.Copy, r=[pk(B(0 + hh))], w=[nak[hh]])
                                    CP("dve", nAT[:, 4 * hh:4 * hh + 4, :], psA[B(2 + hh)][:, :].rearrange("p (h t) -> p h t", h=4), r=[pk(B(2 + hh))], w=[natk[hh]])
                                curA, curAT, ak, atk = nA, nAT, nak, natk
                        Zf = Zp[zi]; zkf = lambda f_: ("Z", ins_, zi, f_)
                        ck("d")
                        for c2 in range(2):
                            rs_ = slice(c2 * 64, c2 * 64 + 64)
                            for fc in range(4):
                                MM(psA[B(c2)][:, fc * 128:(fc + 1) * 128], Zf[rs_, fc, 0, :, :], btT[rs_, d, bi, fc * 128:(fc + 1) * 128], r=[zkf(fc), ("btT", d, fc)], w=[pk(B(c2))])
                            for fc in range(4):
                                TTo("dve", M0st[:, c2, fc, :], psA[B(c2)][:, fc * 128:(fc + 1) * 128], bdm, ALU.mult, r=[pk(B(c2)), "cstb"], w=[("M0st", ins_)])
                            for h in range(8):
                                fc = h // 2; h2 = h % 2
                                o = psA[B(2 + c2)][h2 * 64:h2 * 64 + 64, fc * 64:fc * 64 + 64]
                                MM(o, btT[rs_, d, bi, h * 64:(h + 1) * 64], Zf[rs_, fc, 1, h2, :], start=True, stop=False, r=[("btT", d, fc), zkf(fc)], w=[pk(B(2 + c2))])
                                MM(o, ktT[rs_, d, bi, h * 64:(h + 1) * 64], vT[rs_, bi, h * 64:(h + 1) * 64], start=False, stop=True, r=[("ktT", d, fc), ("vT", fc)], w=[pk(B(2 + c2))])
                        for c2 in range(2):
                            ACT(N0st[:, c2, :, :], psA[B(2 + c2)][:, 0:256].rearrange("p (f i) -> p f i", f=4), AF.Copy, r=[pk(B(2 + c2))], w=[("N0st", ins_)])
                        cidx = pcidx0 + 2 * bi
                        DMA("sp", m0_s[d, cidx:cidx + 2].rearrange("c p f j -> p c f j"), M0st[:], r=[("M0st", ins_)], w=[("m0_s", d, cidx)])
                        DMA("sp", n0_s[d, cidx:cidx + 2].rearrange("c p f i -> p c f i"), N0st[:], r=[("N0st", ins_)], w=[("n0_s", d, cidx)])
                        if own:
                            amat(0, lambda rows, fc: bt[rows, d, fc, tokc], lambda rows, fc: AR[rows, d, fc, bi, 128:256], ["bt", "AR"])
                            amat(4, lambda rows, fc: kt[rows, d, fc, tokc], lambda rows, fc: AR[rows, d, fc, bi, 128:256], ["kt", "AR"])
                            aevac(0, Arb, "Arb", mi)
                            aevac(4, Ark, "Ark", mi)
                            for fc in range(4):
                                o = psA[B(2)][:, fc * 128:(fc + 1) * 128]
                                MM(o, ident, AR[:, d, fc, bi, 128:256], start=True, stop=False, r=["cstb", ("AR", d, fc)], w=[pk(B(2))])
                                for h2 in range(2):
                                    h = 2 * fc + h2
                                    MM(psA[B(2)][h2 * 64:h2 * 64 + 64, fc * 128:(fc + 1) * 128], Zf[:, fc, 0, h2, :], Arb[:, h, :],
                                       start=False, stop=(h2 == 1), r=[zkf(fc), "Arb"], w=[pk(B(2))])
                                for h2 in range(2):
                                    h = 2 * fc + h2
                                    o2 = psA[B(3)][h2 * 64:h2 * 64 + 64, fc * 128:(fc + 1) * 128]
                                    MM(o2, Zf[:, fc, 1, h2, :], Arb[:, h, :], start=True, stop=False, r=[zkf(fc), "Arb"], w=[pk(B(3))])
                                    MM(o2, vT[:, bi, h * 64:(h + 1) * 64], Ark[:, h, :], start=False, stop=True, r=[("vT", fc), "Ark"], w=[pk(B(3))])
                            ACT(Qst[:], psA[B(2)][:, :].rearrange("p (f t) -> p f t", f=4), AF.Copy, r=[pk(B(2))], w=[("Qst", ins_)])
                            DMA("sp", q_s[d, gblk], Qst[:], r=[("Qst", ins_)], w=[("q_s", d, gblk)])
                            CP("dve", Y0st[:], psA[B(3)][:, :].rearrange("p (f t) -> p f t", f=4), r=[pk(B(3))], w=[("Y0st", ins_)])
                            DMA("sp", y0_s[d, :, :, t0 + bi * 128:t0 + bi * 128 + 128], Y0st[:], r=[("Y0st", ins_)], w=[("y0_s", d, gblk)])

            def attention_tile(ti):
                P.tag = f"attn{ti}"
                qs = ti % 2
                for qb in range(NBT):
                    n = NBT * ti + qb
                    kbs = [kb for kb in (n - 1, n, n + 1) if kb >= 0]
                    for m in range(4):
                        for h2 in range(2):
                            h = 2 * m + h2
                            c = h % 4; g = h // 4
                            rows = slice(g * 64, g * 64 + 64)
                            pi = h2
                            for kb in kbs:
                                slot = kb - (n - 1)
                                MM(psA[pi][:, slot * 128:(slot + 1) * 128], kT[rows, kb * 128:(kb + 1) * 128], qT[rows, qs, c, qb * 128:(qb + 1) * 128],
                                   r=["kT", ("qT", qs)], w=[pk(pi)])
                            lo = (kbs[0] - (n - 1)) * 128
                            ACT(probs[:, pi, lo:384], psA[pi][:, lo:384], AF.Exp, scale=0.125, r=[pk(pi)], w=[("probs", pi)])
                            if n - 1 >= 0:
                                TTo("pool", probs[:, pi, 0:128], probs[:, pi, 0:128], cstb[:, 10, :], ALU.mult, r=[("probs", pi), "cstb"], w=[("probs", pi)])
                            TTo("pool", probs[:, pi, 256:384], probs[:, pi, 256:384], cstb[:, 11, :], ALU.mult, r=[("probs", pi), "cstb"], w=[("probs", pi)])
                            orow = slice(h2 * 64, h2 * 64 + 64)
                            for i, kb in enumerate(kbs):
                                slot = kb - (n - 1)
                                MM(psA[2][orow, 0:128], vaug[:, kb, g, :], probs[:, pi, slot * 128:(slot + 1) * 128], start=(i == 0), stop=(i == len(kbs) - 1),
                                   r=["vaug", ("probs", pi)], w=[pk(2)])
                            for i, kb in enumerate(kbs):
                                slot = kb - (n - 1)
                                MM(psA[3][orow, 0:128], vones[:], probs[:, pi, slot * 128:(slot + 1) * 128], start=(i == 0), stop=(i == len(kbs) - 1),
                                   r=["vaug1", ("probs", pi)], w=[pk(3)])
                        TS("dve", den[:], psA[3][:, 0:128], vc("esink", m), None, ALU.add, r=[pk(3), "vec2"], w=["den"])
                        RCP(den[:], den[:], r=["den"], w=["den"])
                        TTo("dve", attn_o[:, m, qb * 128:(qb + 1) * 128], psA[2][:, 0:128], den[:], ALU.mult, r=[pk(2), "den"], w=["attn_o"])

            def attn_and_store(ti):
                attention_tile(ti)
                DMA("sp", ao_s[:, :, ti * TT:(ti + 1) * TT], attn_o[:], r=["attn_o"], w=[("ao_s", ti)])

            oth = list(range(2 * NTH - 1, NTH - 1, -1))
            if ntiles_other is not None:
                oth = oth[len(oth) - ntiles_other:] if ntiles_other > 0 else []
            for ti in oth:
                t0 = ti * TT
                sweep_tile(t0, own=False, first_other=(ti == NTH), pcidx0=t0 // 64)
            nown = NTH if ntiles_own is None else ntiles_own
            for ti in range(nown):
                sweep_tile(ti * TT, own=True, first_other=False, pcidx0=(ti * TT) // 64)
                if ti >= 1:
                    attn_and_store(ti - 1)
            if nown == NTH:
                attn_and_store(NTH - 1)
            if dbg is not None and upto == 1:
                dbg(locals(), P, DMA, dbg_out)
            if upto == 1:
                P.disabled = True
        barrier()

        with ExitStack() as s23:
            Yb = sb("Yb", [128, 4, HALF], F32, s23)
            if True:
                s2 = s23
                S32 = sb("S32", [128, 2, 4, 64], F32, s2)
                Sb = sb("Sb", [128, 2, 4, 64], BF16, s2)
                St = sb("St", [128, 2, 4, 64], F32, s2)
                M0l = sb("M0l", [128, 4, 4, 128], BF16, s2)
                N0l = sb("N0l", [128, 4, 4, 64], BF16, s2)
                Ql = sb("Ql", [128, 4, 4, 128], BF16, s2)
                ytmp = sb("ytmp", [128, 2, 4, 512], F32, s2)
                for j in range(4):
                    s = j % 2
                    gs_ = range(4 * j, 4 * j + 4)
                    DMA("sp", Yb[:, :, j * 512:(j + 1) * 512], y0_s[0, :, :, j * 512:(j + 1) * 512], r=[("y0_s", 0, g) for g in gs_], w=[("Yb", 2 * j), ("Yb", 2 * j + 1)])
                    DMA("sp", ytmp[:, s, :, :], y0_s[1, :, :, j * 512:(j + 1) * 512], r=[("y0_s", 1, g) for g in gs_], w=[("ytmp", s)])
                    TTo("pool", Yb[:, :, j * 512:(j + 1) * 512], Yb[:, :, j * 512:(j + 1) * 512], ytmp[:, s, :, :], ALU.add,
                        r=[("ytmp", s), ("Yb", 2 * j), ("Yb", 2 * j + 1)], w=[("Yb", 2 * j), ("Yb", 2 * j + 1)])
                MEMSET("dve", S32[:], 0.0, w=[("S32", 0), ("S32", 1)])
                MEMSET("pool", Sb[:], 0.0, w=[("Sb", 0), ("Sb", 1)])
                stepn = [0]

                def chain_step(d, cidx, own):
                    sl = stepn[0] % 4
                    stepn[0] += 1
                    DMA("sp", M0l[:, sl, :, :], m0_s[d, cidx], r=[("m0_s", d, cidx - cidx % 2)], w=[("M0l", sl)])
                    DMA("sp", N0l[:, sl, :, :], n0_s[d, cidx], r=[("n0_s", d, cidx - cidx % 2)], w=[("N0l", sl)])
                    bank = 4 + d
                    if own:
                        blk = cidx // 2; c2 = cidx % 2
                        qsl = (blk % 2) * 2 + d
                        if (d == 0 and c2 == 0) or (d == 1 and c2 == 1):
                            DMA("sp", Ql[:, qsl, :, :], q_s[d, blk], r=[("q_s", d, blk)], w=[("Ql", qsl)])
                        yb = 0 + d
                        for h in range(8):
                            fc = h // 2; h2 = h % 2; rows = slice(h2 * 64, h2 * 64 + 64)
                            MM(psA[yb][rows, fc * 64:(fc + 1) * 64], Sb[rows, d, fc, :], Ql[rows, qsl, fc, c2 * 64:c2 * 64 + 64],
                               r=[("Sb", d), ("Ql", qsl)], w=[pk(yb)])
                        yv = Yb[:, :, cidx * 64:(cidx + 1) * 64]
                        TTo("dve", yv, yv, psA[yb][:, 0:256].rearrange("p (f t) -> p f t", f=4), ALU.add, r=[pk(yb), ("Yb", cidx // 4)], w=[("Yb", cidx // 4)])
                    for fc in range(4):
                        o = psA[bank][:, fc * 64:(fc + 1) * 64]
                        MM(o, ident, N0l[:, sl, fc, :], start=True, stop=False, r=["cstb", ("N0l", sl)], w=[pk(bank)])
                        MM(o, M0l[:, sl, fc, :], Sb[:, d, fc, :], start=False, stop=True, r=[("M0l", sl), ("Sb", d)], w=[pk(bank)])
                    TTo("dve", St[:, d, :, :], psA[bank][:, 0:256].rearrange("p (f i) -> p f i", f=4), S32[:, d, :, :], ALU.add,
                        r=[pk(bank), ("S32", d)], w=[("St", d)])
                    for fc in range(4):
                        TS("dve", S32[:, d, fc, :], St[:, d, fc, :], PCs[:, d, cidx, fc:fc + 1], None, ALU.mult, r=[("St", d), "PCs"], w=[("S32", d)])
                    ACT(Sb[:, d, :, :], S32[:, d, :, :], AF.Copy, r=[("S32", d)], w=[("Sb", d)])

                for cidx in range(63, 31, -1):
                    chain_step(1, cidx, own=False)
                for i in range(32):
                    chain_step(0, i, own=True)
                    chain_step(1, 31 - i, own=True)
            if upto == 2:
                P.disabled = True

            if True:
                s3 = s23
                wg = sb("wg", [128, 8, 2048], BF16, s3)
                wua = sb("wua", [128, 4, D], BF16, s3); wur = sb("wur", [128, 4, D], BF16, s3); wout = sb("wout", [128, 8, D], BF16, s3)
                for kc in range(8):
                    DMA("pool", wg[:, kc, :], wgt[kc * 128:(kc + 1) * 128, :], w=["wg"])
                    DMA("pool", wout[:, kc, :], wout_d[kc * 128:(kc + 1) * 128, :], w=["wout"])
                for kc in range(4):
                    DMA("pool", wua[:, kc, :], wua_d[kc * 128:(kc + 1) * 128, :], w=["wua"])
                    DMA("pool", wur[:, kc, :], wur_d[kc * 128:(kc + 1) * 128, :], w=["wur"])
                xt = sb("xt3", [128, 2, D], F32, s3); hb = sb("hb3", [128, 1, D], BF16, s3)
                ssq = sb("ssq3", [128, 2], F32, s3); rstd = sb("rstd3", [128, 2], F32, s3)
                hT = sb("hT3", [128, 8, TT], BF16, s3)
                ycD = [sb(f"yc{i}", [128, TT], F32, s3) for i in range(2)]; ysqD = [sb(f"ysq{i}", [128, TT], F32, s3) for i in range(2)]
                yrD = [sb(f"yr{i}", [128, TT], F32, s3) for i in range(2)]
                rwT = sb("rwT", [128, 4, TT], BF16, s3)
                gates = sb("gates", [128, 16, TT], BF16, s3)
                mg = sb("mg", [128, 8, TT], BF16, s3)
                m1D = [sb(f"m1{i}", [128, TT], F32, s3) for i in range(2)]; m2D = [sb(f"m2{i}", [128, TT], F32, s3) for i in range(2)]
                x1b = sb("x1b", [128, 2, D], F32, s3)
                xres = sb("xres", [128, 2, D], F32, s3)
                ao_l = sb("ao_l", [128, 4, TT], BF16, s3); bv_l = sb("bv_l", [128, 4, TT], BF16, s3); sgl_l = sb("sgl_l", [128, TT], BF16, s3)
                onesf = cstf[:, 1, :]
                for ti in (3, 4, 2, 5, 1, 6, 0, 7):
                    t0 = ti * TT
                    DMA("sp", ao_l[:], ao_s[:, :, t0:t0 + TT], r=[("ao_s", ti)], w=["ao_l"])
                    DMA("sp", bv_l[:], bv_s[:, :, t0:t0 + TT], r=[("bv_s", ti, fc) for fc in range(4)], w=["bv_l"])
                    DMA("sp", sgl_l[:], sgl_s[:, t0:t0 + TT], r=[("sgl_s", ti)], w=["sgl_l"])
                    for fc in range(4):
                        Yv = Yb[:, fc, t0:t0 + TT]
                        yc, ysq, yr = ycD[fc % 2], ysqD[fc % 2], yrD[fc % 2]
                        b0_ = 2 * (fc % 2)
                        MM(psA[b0_][:, 0:TT], onesf, Yv, r=["cstf", ("Yb", ti)], w=[pk(b0_)])
                        TTo("dve", yc[:], Yv, psA[b0_][:, 0:TT], ALU.subtract, r=[("Yb", ti), pk(b0_)], w=[("yc", fc % 2)])
                        ACT(ysq[:], yc[:], AF.Square, r=[("yc", fc % 2)], w=[("ysq", fc % 2)])
                        MM(psA[b0_ + 1][:, 0:TT], onesf, ysq[:], r=["cstf", ("ysq", fc % 2)], w=[pk(b0_ + 1)])
                        RSQ(yr[:], psA[b0_ + 1][:, 0:TT], 1.0, eps_c[:, 2:3], TT, r=[pk(b0_ + 1)], w=[("yr", fc % 2)])
                        TTo("pool", yc[:], yc[:], yr[:], ALU.mult, r=[("yc", fc % 2), ("yr", fc % 2)], w=[("yc", fc % 2)])
                        TS("dve", yc[:], yc[:], vc("lnw", fc), vc("lnb", fc), ALU.mult, ALU.add, r=[("yc", fc % 2), "vec"], w=[("yc", fc % 2)])
                        TTo("pool", yc[:], yc[:], bv_l[:, fc, :], ALU.add, r=[("yc", fc % 2), "bv_l"], w=[("yc", fc % 2)])
                        MM(psA[4 + fc % 2][:, 0:TT], g2t[:, fc * 128:(fc + 1) * 128], sgl_l[:, :], r=["g2t", "sgl_l"], w=[pk(4 + fc % 2)])
                        TTo("dve", rwT[:, fc, :], yc[:], psA[4 + fc % 2][:, 0:TT], ALU.mult, r=[("yc", fc % 2), pk(4 + fc % 2)], w=["rwT"])
                    xaps = [(xs[1 + t0 + 128 * i: 1 + t0 + 128 * i + 128, :], 128, 128 * i) for i in range(NBT)]
                    make_hT("hT3", xaps, None, hT, "nmix", "hT3", xt, hb, ssq, rstd)
                    for gc in range(16):
                        bank = gc % 2
                        for kc in range(8):
                            MM(psA[bank][:, 0:TT], wg[:, kc, gc * 128:(gc + 1) * 128], hT[:, kc, :], start=(kc == 0), stop=(kc == 7), r=["wg", ("hT3", kc)], w=[pk(bank)])
                        ACT(gates[:, gc, :], psA[bank][:, 0:TT], AF.Tanh, scale=0.5, r=[pk(bank)], w=["gates"])
                    for oc in range(8):
                        m1, m2 = m1D[oc % 2], m2D[oc % 2]
                        ba_ = 2 + 2 * (oc % 2)
                        for kc in range(4):
                            MM(psA[ba_][:, 0:TT], wua[:, kc, oc * 128:(oc + 1) * 128], ao_l[:, kc, :], start=(kc == 0), stop=(kc == 3), r=["wua", "ao_l"], w=[pk(ba_)])
                        for kc in range(4):
                            MM(psA[ba_ + 1][:, 0:TT], wur[:, kc, oc * 128:(oc + 1) * 128], rwT[:, kc, :], start=(kc == 0), stop=(kc == 3), r=["wur", "rwT"], w=[pk(ba_ + 1)])
                        STT("dve", m1[:], gates[:, oc, :], 1.0, psA[ba_][:, 0:TT], ALU.add, ALU.mult, r=[pk(ba_), "gates"], w=[("m1", oc % 2)])
                        STT("dve", m2[:], gates[:, 8 + oc, :], 1.0, psA[ba_ + 1][:, 0:TT], ALU.add, ALU.mult, r=[pk(ba_ + 1), "gates"], w=[("m2", oc % 2)])
                        TTo("pool", mg[:, oc, :], m1[:], m2[:], ALU.add, r=[("m1", oc % 2), ("m2", oc % 2)], w=["mg"])
                    for bi in range(NBT):
                        s = bi % 2
                        tok = t0 + bi * 128
                        DMA("sp", xres[:, s, :], xs[1 + tok:1 + tok + 128, :], w=[("xres", s)])
                        for hf in range(2):
                            bank = 4 + hf
                            for kc in range(8):
                                MM(psA[bank][:, :], mg[:, kc, bi * 128:(bi + 1) * 128], wout[:, kc, hf * 512:(hf + 1) * 512], start=(kc == 0), stop=(kc == 7), r=["mg", "wout"], w=[pk(bank)])
                            STT("dve", x1b[:, s, hf * 512:(hf + 1) * 512], psA[bank][:, :], 0.5, xres[:, s, hf * 512:(hf + 1) * 512], ALU.mult, ALU.add, r=[pk(bank), ("xres", s)], w=[("x1b", s)])
                        DMA("sp", x1_s[tok:tok + 128, :], x1b[:, s, :], r=[("x1b", s)], w=[("x1_s", tok // 128)])
        if upto == 3:
            P.disabled = True
        barrier()

        with ExitStack() as s4:
            wf1 = sb("wf1", [128, 8, 4096], BF16, s4); wf2 = sb("wf2", [128, 32, D], BF16, s4)
            wpg = sb("wpg", [128, 8, D], BF16, s4); wple = sb("wple", [128, 2, D], BF16, s4)
            for cg in range(4):
                for kc in range(8):
                    DMA("pool", wf1[:, kc, cg * 1024:(cg + 1) * 1024], wff1_d[kc * 128:(kc + 1) * 128, cg * 1024:(cg + 1) * 1024], w=[("wf1", cg)])
            for kc in range(32):
                DMA("pool", wf2[:, kc, :], wff2_d[kc * 128:(kc + 1) * 128, :], w=[("wf2", kc // 8)])
            for kc in range(8):
                DMA("pool", wpg[:, kc, :], wpg_d[kc * 128:(kc + 1) * 128, :], w=["wpg"])
            for kc in range(2):
                DMA("pool", wple[:, kc, :], wple_d[kc * 128:(kc + 1) * 128, :], w=["wple"])
            xt = sb("xt4", [128, 2, D], F32, s4); hb = sb("hb4", [128, 1, D], BF16, s4)
            ssq = sb("ssq4", [128, 4], F32, s4); rstd = sb("rstd4", [128, 4], F32, s4)
            hTf = sb("hT4", [128, 2, 8, 128], BF16, s4)
            hTp = sb("hT4p", [128, 1, 8, 128], BF16, s4)
            act = sb("act", [128, 2, 32, 128], BF16, s4)
            rl = sb("rl", [128, 4, 128], F32, s4)
            x2 = sb("x2", [128, 1, D], F32, s4)
            pt = sb("pt", [128, 2, 256], F32, s4); pb16 = sb("pb16", [128, 2, 256], BF16, s4); pT = sb("pT", [128, 2, 2, 128], BF16, s4)
            sgp = sb("sgp", [128, 1, 512], F32, s4)
            nhb = [0]

            def norm_T(src, gain, skey, hTt, okey, idx):
                hs = 0
                nhb[0] += 1
                ACT(junk[:], src, AF.Square, accum=ssq[:, idx:idx + 1], r=[skey], w=["junk", ("ssq4", idx)])
                RSQ(rstd[:, idx:idx + 1], ssq[:, idx:idx + 1], 1.0 / D, eps_c[:, 0:1], 1, r=[("ssq4", idx)], w=[("rstd4", idx)])
                TS("dve", hb[:, hs, :], src, rstd[:, idx:idx + 1], None, ALU.mult, r=[skey, ("rstd4", idx)], w=[("hb4o", hs)])
                for kc in range(8):
                    TR(psT[kc % 2][:, (kc // 2) * 128:(kc // 2) * 128 + 128], hb[:, hs, kc * 128:(kc + 1) * 128], ident, r=[("hb4o", hs), "cstb"], w=[pkT(kc % 2)])
                for kc in range(8):
                    srcp = psT[kc % 2][:, (kc // 2) * 128:(kc // 2) * 128 + 128]
                    if kc % 2 == 0:
                        ACT(hTt[:, kc, :], srcp, AF.Copy, scale=vc(gain, kc), r=[pkT(kc % 2), "vec"], w=[(okey, kc)])
                    else:
                        TS("dve", hTt[:, kc, :], srcp, vc(gain, kc), None, ALU.mult, r=[pkT(kc % 2), "vec"], w=[(okey, kc)])

            for blk in range(NB_OWN):
                tok = blk * 128
                s = blk % 2
                DMA("sp", xt[:, s, :], x1_s[tok:tok + 128, :], r=[("x1_s", blk)], w=[("xt4", s)])
                DMA("sp", pt[:, s, :], pp[tok:tok + 128, :], w=[("pt", s)])
                hk = ("hT4", s)
                norm_T(xt[:, s, :], "nffn", ("xt4", s), hTf[:, s], hk, s)
                for hc in range(32):
                    bank = hc % 4
                    for kc in range(8):
                        MM(psA[bank][:, 0:128], wf1[:, kc, hc * 128:(hc + 1) * 128], hTf[:, s, kc, :], start=(kc == 0), stop=(kc == 7),
                           r=[("wf1", hc // 8), (hk, kc)], w=[pk(bank)])
                    ACT(rl[:, bank, :], psA[bank][:, 0:128], AF.Relu, r=[pk(bank)], w=[("rl", bank)])
                    TTo("dve", act[:, s, hc, :], rl[:, bank, :], rl[:, bank, :], ALU.mult, r=[("rl", bank)], w=[("act", s)])
                for hf in range(2):
                    bank = 4 + hf
                    for hc in range(32):
                        MM(psA[bank][:, :], act[:, s, hc, :], wf2[:, hc, hf * 512:(hf + 1) * 512], start=(hc == 0), stop=(hc == 31),
                           r=[("act", s), ("wf2", hc // 8)], w=[pk(bank)])
                    TTo("dve", x2[:, 0, hf * 512:(hf + 1) * 512], psA[bank][:, :], xt[:, s, hf * 512:(hf + 1) * 512], ALU.add, r=[pk(bank), ("xt4", s)], w=["x2"])
                hkp = "hT4p"
                norm_T(x2[:, 0, :], "nple", "x2", hTp[:, 0], hkp, 2 + s)
                CP("pool", pb16[:, s, :], pt[:, s, :], r=[("pt", s)], w=[("pb16", s)])
                for kc in range(2):
                    TR(psT[0][:, 512 + kc * 128:512 + (kc + 1) * 128], pb16[:, s, kc * 128:(kc + 1) * 128], ident, r=[("pb16", s), "cstb"], w=[pkT(0)])
                CP("dve", pT[:, s, :, :], psT[0][:, 512:768].rearrange("p (k t) -> p k t", k=2), r=[pkT(0)], w=[("pT", s)])
                for hf in range(2):
                    gb = 0 + hf
                    pbk = 2 + hf
                    for kc in range(8):
                        MM(psA[gb][:, :], hTp[:, 0, kc, :], wpg[:, kc, hf * 512:(hf + 1) * 512], start=(kc == 0), stop=(kc == 7), r=[(hkp, kc), "wpg"], w=[pk(gb)])
                    for kc in range(2):
                        MM(psA[pbk][:, :], pT[:, s, kc, :], wple[:, kc, hf * 512:(hf + 1) * 512], start=(kc == 0), stop=(kc == 1), r=[("pT", s), "wple"], w=[pk(pbk)])
                    ACT(sgp[:, 0, :], psA[gb][:, :], AF.Tanh, scale=0.5, r=[pk(gb)], w=["sgp"])
                    STT("dve", sgp[:, 0, :], sgp[:, 0, :], 1.0, psA[pbk][:, :], ALU.add, ALU.mult, r=[pk(pbk), "sgp"], w=["sgp"])
                    STT("dve", xt[:, s, hf * 512:(hf + 1) * 512], sgp[:, 0, :], 0.5, x2[:, 0, hf * 512:(hf + 1) * 512], ALU.mult, ALU.add,
                        r=["sgp", "x2", ("xt4", s)], w=[("ob", s)])
                DMA("sp", out_d[tok:tok + 128, :], xt[:, s, :], r=[("ob", s), ("xt4", s)], w=[("out", blk), ("xt4", s)])
            P.add("sp", None, [("out", i) for i in range(NB_OWN)], ())
        P.emit(nc, gs)
    return nc


def _consts():
    cst = np.zeros((128, 12, 128), np.float32)
    i = np.arange(128)
    cst[:, 0, :] = np.eye(128)
    cst[:, 1, :] = (i[:, None] // 64 == i[None, :] // 64)
    perm = np.zeros((128, 128), np.float32)
    for m in range(128):
        j = m % 64
        if j < 8:
            perm[m + 8, m] = 1.0
        elif j < 16:
            perm[m - 8, m] = 1.0
    cst[:, 2, :] = perm
    cst[:, 3, :] = cst[:, 1, :]
    same = (i[:, None] // 64 == i[None, :] // 64)
    cst[:, 4, :] = same & (i[:, None] < i[None, :])
    cst[:, 5, :] = same & (i[:, None] <= i[None, :])
    cst[:, 6, :] = same & (i[:, None] > i[None, :])
    cst[:, 7, :] = same & (i[:, None] >= i[None, :])
    cst[:, 8, :] = 1.0
    cst[:, 9, :] = cst[:, 1, :] / 64.0
    cst[:, 10, :] = (i[:, None] >= i[None, :])
    cst[:, 11, :] = (i[:, None] <= i[None, :])
    tri = np.zeros((128, 2, 128), np.float32)
    tri[:, 0, :] = 0.5 * LWS * cst[:, 5, :]
    tri[:, 1, :] = 0.5 * LWS * cst[:, 7, :]
    return cst, tri


def _rope_tables(pos):
    half = 8
    inv = np.power(np.float32(500000.0), -np.arange(half, dtype=np.float32) * 2.0 / 16.0).astype(np.float32)
    ang = pos.astype(np.float32)[None, :] * inv[:, None]
    cos = np.cos(ang).astype(np.float32); sin = np.sin(ang).astype(np.float32)
    n = pos.shape[0]
    tab = np.zeros((128, 2, n), np.float32)
    for hb in range(2):
        b = hb * 64
        tab[b:b + 64, 0, :] = 1.0
        tab[b:b + 8, 0, :] = cos; tab[b + 8:b + 16, 0, :] = cos
        tab[b:b + 8, 1, :] = -sin; tab[b + 8:b + 16, 1, :] = sin
    return tab


def _col128(v):
    return np.ascontiguousarray(np.asarray(v, np.float32).reshape(-1, 128).T)


_NC_CACHE = {}


def kernel(x, p, norm_mix, w_in, shift_mu, q_norm, k_norm, sink, w0, w2, a0, a2, g2, k_k, k_a, r_k,
           lnx_w, lnx_b, w_up_attn, w_up_rwkv, w_out, norm_ffn, w_ff1, w_ff2, norm_ple, w_ple_gate, w_ple):
    f = lambda a: np.asarray(a, np.float32)
    x = f(x); p = f(p)[0]; w_in = f(w_in)[0]
    qcols = []
    for c in range(4):
        qcols += list(range(c * 64, c * 64 + 64)) + list(range((4 + c) * 64, (4 + c) * 64 + 64))
    O_K, O_V, O_R = 512, 640, 768
    cols = qcols + list(range(O_K, O_K + 128)) + list(range(O_V, O_V + 128))
    rw = O_R
    wa = list(range(rw + 1536, rw + 1536 + 128)); gl = list(range(rw + 1664, rw + 1792))
    cols += wa + gl
    mu_idx = [12, 13]
    for fc in range(4):
        for part in range(3):
            cols += list(range(rw + part * 512 + fc * 128, rw + part * 512 + fc * 128 + 128))
            mu_idx.append(part * 4 + fc)
    wsw = np.ascontiguousarray(w_in[:, cols])
    wgt = np.ascontiguousarray(w_in[:, O_R + 1792:])
    mu_c = _col128(f(shift_mu)[0])[:, mu_idx]
    cst, tri = _consts()
    cst_in = cst.copy()
    cst_in_f = cst.copy()
    cst_in[:, 1, :] = cst[:, 1, :]
    sink_ = f(sink)[0]
    sinkcols = np.zeros((128, 4), np.float32)
    for m in range(4):
        sinkcols[0:64, m] = sink_[2 * m]; sinkcols[64:128, m] = sink_[2 * m + 1]
    in_maps = []
    for c in range(8):
        b, hf = c // 2, c % 2
        xb = x[b]; pb_ = p[b]
        if hf == 0:
            xl = xb; pl = pb_[:HALF]; pos = np.arange(0, HALF + 512); dA, dB = 0, 1
        else:
            xl = xb[::-1]; pl = pb_[HALF:][::-1]; pos = SEQ - 1 - np.arange(0, HALF + 512); dA, dB = 1, 0
        xs = np.zeros((SEQ + 2, D), np.float32); xs[1:SEQ + 1] = xl
        dd = [dA, dB]
        vec = np.zeros((128, NVEC_IN), np.float32)
        def put(n, arr):
            vec[:, VC[n]:VC[n] + arr.shape[1]] = arr
        put("mu", mu_c); put("k_k", _col128(f(k_k)[0])); put("k_a", _col128(f(k_a)[0]))
        put("rkA", _col128(f(r_k)[0, dA].reshape(-1))); put("rkB", _col128(f(r_k)[0, dB].reshape(-1)))
        put("lnw", _col128(f(lnx_w)[0])); put("lnb", _col128(f(lnx_b)[0]))
        put("a0A", _col128(f(a0)[0, dA])); put("a0B", _col128(f(a0)[0, dB]))
        put("qg", np.tile(f(q_norm)[0], 2)[:, None]); put("kg", np.tile(f(k_norm)[0], 2)[:, None])
        put("nmix", _col128(f(norm_mix)[0])); put("nffn", _col128(f(norm_ffn)[0])); put("nple", _col128(f(norm_ple)[0]))
        put("sink", sinkcols)
        w2aug = np.stack([np.concatenate([f(w2)[0, d_], f(w0)[0, d_][None, :]], 0) for d_ in dd])
        a2s = np.stack([f(a2)[0, d_] for d_ in dd])
        cst_c = cst.copy()
        m = {"xs": xs, "pp": np.ascontiguousarray(pl), "wsw": wsw, "wgt": wgt, "vec": vec, "w2aug": np.ascontiguousarray(w2aug),
             "a2": np.ascontiguousarray(a2s), "g2": f(g2)[0], "wua": f(w_up_attn)[0], "wur": f(w_up_rwkv)[0], "wout": f(w_out)[0],
             "wff1": f(w_ff1)[0], "wff2": f(w_ff2)[0], "wpg": f(w_ple_gate)[0], "wple": f(w_ple)[0],
             "rope": _rope_tables(pos), "cst": cst_c, "tri": tri, "crow": np.ascontiguousarray(tri.sum(0)[None])}
        in_maps.append(m)
    if _NC_CACHE.get("prep_only"):
        return in_maps
    if "nc" not in _NC_CACHE:
        _NC_CACHE["nc"] = build_program()
    res = run_bass_kernel_spmd(_NC_CACHE["nc"], in_maps, core_ids=list(range(8)))
    out = np.zeros((4, SEQ, D), np.float32)
    for c in range(8):
        b, hf = c // 2, c % 2
        o = res.results[c]["out"]
        if hf == 0:
            out[b, :HALF] = o
        else:
            out[b, HALF:] = o[::-1]
    return out
```

```python
import os
import numpy as np
from contextlib import ExitStack
import concourse.bass as bass
import concourse.mybir as mybir
from concourse.bass_utils import run_bass_kernel_spmd

F32 = mybir.dt.float32
BF16 = mybir.dt.bfloat16
ALU = mybir.AluOpType
AF = mybir.ActivationFunctionType

SEM_EPOCH = 3000
DMA_SLOTS = 8

D = 1024; SEQ = 4096; HALF = 2048; NB_OWN = 16
TT = 256; NBT = 2; NTH = 8
LWS = -0.6065306597126334


class Op:
    __slots__ = ("eng", "fn", "reads", "writes", "dma", "deps", "flag", "cnt", "slot", "slotval", "idx", "xdeps",
                 "dur", "rg", "tbl", "seg", "alld", "pos")

    def __init__(self, eng, fn, reads, writes, dma):
        self.eng = eng; self.fn = fn; self.reads = reads; self.writes = writes; self.dma = dma
        self.deps = set(); self.flag = False; self.cnt = None; self.slot = None; self.slotval = None
        self.xdeps = (); self.dur = 0.2; self.rg = None; self.tbl = None; self.seg = 0


SYNC_LAT = 0.35


class Prog:
    def __init__(self):
        self.ops = []
        self.tags = []
        self.last_barrier = 0
        self.seg = 0
        self.reorder = True

    def add(self, eng, fn, r=(), w=(), dma=False, dur=0.2, rg=None, tbl=None):
        op = Op(eng, fn, tuple(r), tuple(w), dma)
        op.dur = dur; op.rg = rg; op.tbl = tbl; op.seg = self.seg
        self.tags.append(getattr(self, "tag", None)) if not getattr(self, "disabled", False) else None
        if getattr(self, "disabled", False):
            op.idx = -1
            return op
        op.idx = len(self.ops)
        self.ops.append(op)
        return op

    def all_deps(self):
        last_w = {}
        readers = {}
        ops = self.ops
        for op in ops:
            d = {}
            for j in op.xdeps:
                d[j] = "raw"
            for k in op.reads:
                j = last_w.get(k)
                if j is not None:
                    d[j] = "raw"
            for k in op.writes:
                j = last_w.get(k)
                if j is not None and j not in d:
                    d[j] = "waw"
                for j in readers.get(k, ()):
                    if j != op.idx and j not in d:
                        d[j] = "war"
            for k in op.reads:
                if isinstance(k, tuple) and k[0] in ("ps", "psT"):
                    for j in readers.get(k, ()):
                        if ops[j].eng != op.eng and j not in d:
                            d[j] = "raw"
            for k in op.reads:
                readers.setdefault(k, []).append(op.idx)
            for k in op.writes:
                last_w[k] = op.idx
                readers[k] = []
            op.alld = d

    def schedule(self):
        ops = self.ops
        n = len(ops)
        if not self.reorder:
            return list(range(n))
        fin = [0.0] * n
        self.fin = fin
        self.bind = {}; self._last_on = {}; self.start = {}
        succ = [[] for _ in range(n)]
        ndep = [0] * n
        for op in ops:
            for j in op.alld:
                succ[j].append(op.idx)
            ndep[op.idx] = len(op.alld)
        order = []
        free_t = {}
        cur_tbl = [None]
        segs = {}
        for op in ops:
            segs.setdefault(op.seg, []).append(op.idx)
        done = [False] * n
        ksegs = os.environ.get("KREORD")
        for sg in sorted(segs):
            members = segs[sg]
            if ksegs is not None and str(sg) not in ksegs.split(","):
                for i in members:
                    t0_ = max([free_t.get(ops[i].eng, 0.0)] + [fin[j] for j in ops[i].alld])
                    fin[i] = t0_ + ops[i].dur
                    free_t[ops[i].eng] = fin[i]
                    done[i] = True
                    order.append(i)
                continue
            inseg = set(members)
            ready = {}
            rt = {}
            remaining = len(members)
            cnt_un = {}
            for i in members:
                c = sum(1 for j in ops[i].alld if not done[j])
                cnt_un[i] = c
                if c == 0:
                    ready.setdefault(ops[i].eng, []).append(i)
            def ready_time(i):
                op = ops[i]
                t = 0.0
                for j, kind in op.alld.items():
                    p = ops[j]
                    lat = SYNC_LAT if (p.dma or p.eng != op.eng or kind == "raw") else 0.0
                    if p.eng == op.eng and op.eng == "pe":
                        lat = 0.0
                    t = max(t, fin[j] + lat)
                return t
            for e in ready:
                for i in ready[e]:
                    rt[i] = ready_time(i)
            while remaining:
                best = None
                for e, lst in ready.items():
                    if not lst:
                        continue
                    ft = free_t.get(e, 0.0)
                    bi = None
                    for i in lst:
                        st = max(ft, rt[i])
                        if e == "act" and ops[i].tbl is not None and cur_tbl[0] is not None and ops[i].tbl != cur_tbl[0]:
                            st += 1.3
                        key = (st, i)
                        if bi is None or key < bi:
                            bi = key
                    if best is None or bi < best[0]:
                        best = (bi, e)
                (st, i), e = best
                op = ops[i]
                ready[e].remove(i)
                if e == "act" and op.tbl is not None:
                    cur_tbl[0] = op.tbl
                fin[i] = st + op.dur
                if getattr(self, "trace_bind", False):
                    bj = None; bt_ = -1.0
                    for j in op.alld:
                        if fin[j] > bt_:
                            bt_ = fin[j]; bj = j
                    prev_e = self._last_on.get(e)
                    if prev_e is not None and free_t.get(e, 0.0) >= rt[i] - 1e-9:
                        self.bind[i] = ("eng", prev_e)
                    else:
                        self.bind[i] = ("dep", bj)
                    self._last_on[e] = i
                    self.start[i] = st
                if op.dma:
                    free_t[e] = st + 0.1
                else:
                    free_t[e] = fin[i]
                done[i] = True
                order.append(i)
                remaining -= 1
                for k in succ[i]:
                    if k in inseg:
                        cnt_un[k] -= 1
                        if cnt_un[k] == 0:
                            rt[k] = ready_time(k)
                            ready.setdefault(ops[k].eng, []).append(k)
        self.est_total = max(fin) if fin else 0.0
        self.seg_end = {sg: max(fin[i] for i in segs[sg]) for sg in segs}
        return order

    def analyse(self):
        self.all_deps()
        order = self.schedule()
        ops = self.ops
        for pos, i in enumerate(order):
            ops[i].pos = pos
        self.order = order
        pe = [i for i in order if ops[i].eng == "pe"]
        extra = {}
        for a in range(len(pe)):
            oa = ops[pe[a]]
            if oa.rg is None:
                continue
            for b in range(a + 1, a + 3):
                if b >= len(pe):
                    break
                ob = ops[pe[b]]
                if ob.rg is None:
                    continue
                if (oa.rg[1] < ob.rg[0] or ob.rg[1] < oa.rg[0]) and set(oa.writes) & set(ob.writes):
                    extra.setdefault(pe[b], set()).add(pe[a])
        for op in ops:
            need = set()
            for j, kind in op.alld.items():
                p = ops[j]
                if p.dma:
                    need.add(j)
                elif p.eng == op.eng and not op.dma:
                    if kind == "raw" and op.eng != "pe":
                        need.add(j); p.flag = True
                else:
                    need.add(j); p.flag = True
            for j in extra.get(op.idx, ()):
                need.add(j); ops[j].flag = True
            op.deps = need
        cnt = {}
        nd = {}
        for i in order:
            op = ops[i]
            if op.dma:
                k = nd.get(op.eng, 0)
                nd[op.eng] = k + 1
                op.slot = (op.eng, k % DMA_SLOTS)
                op.slotval = 16 * (k // DMA_SLOTS + 1)
            elif op.fn is not None and op.flag:
                cnt[op.eng] = cnt.get(op.eng, 0) + 1
                op.cnt = cnt[op.eng]
        self.nflag = cnt
        self.ndma = nd

    def emit(self, nc, stack):
        self.analyse()
        engs = ["pe", "act", "dve", "pool", "sp"]
        sems = {}
        for e in engs:
            n = self.nflag.get(e, 0)
            for ep in range(n // SEM_EPOCH + 1):
                sems[(e, ep)] = stack.enter_context(nc.semaphore(f"s_{e}_{ep}"))
        dsem = {}
        for e, n in self.ndma.items():
            for s in range(min(n, DMA_SLOTS)):
                dsem[(e, s)] = stack.enter_context(nc.semaphore(f"d_{e}_{s}"))
        block = stack.enter_context(nc.Block())
        ops = self.ops
        per_eng = {e: [ops[i] for i in self.order if ops[i].eng == e] for e in engs}

        def sem_of(p):
            if p.dma:
                return dsem[p.slot], p.slotval
            c = p.cnt - 1
            return sems[(p.eng, c // SEM_EPOCH)], (c % SEM_EPOCH) + 1

        def run(engobj, lst):
            seen = {}
            dma_hist = {}
            for op in lst:
                waits = {}
                for j in op.deps:
                    s, v = sem_of(ops[j])
                    key = id(s)
                    if seen.get(key, 0) >= v:
                        continue
                    if key not in waits or waits[key][1] < v:
                        waits[key] = (s, v)
                if op.dma:
                    prev = dma_hist.get(op.slot)
                    if prev is not None:
                        s, v = dsem[op.slot], prev
                        key = id(s)
                        if seen.get(key, 0) < v and (key not in waits or waits[key][1] < v):
                            waits[key] = (s, v)
                    dma_hist[op.slot] = op.slotval
                for key, (s, v) in waits.items():
                    engobj.wait_ge(s, v)
                    seen[key] = v
                if op.fn is None:
                    continue
                ins = op.fn(engobj)
                if op.dma:
                    ins.then_inc(dsem[op.slot], 16)
                elif op.flag:
                    s, v = sem_of(op)
                    ins.then_inc(s, 1)

        if per_eng["pe"]:
            @block.tensor
            def _(e):
                run(e, per_eng["pe"])
        if per_eng["act"]:
            @block.scalar
            def _(e):
                run(e, per_eng["act"])
        if per_eng["dve"]:
            @block.vector
            def _(e):
                run(e, per_eng["dve"])
        if per_eng["pool"]:
            @block.gpsimd
            def _(e):
                run(e, per_eng["pool"])
        if per_eng["sp"]:
            @block.sync
            def _(e):
                run(e, per_eng["sp"])


VC = {}
_o = 0
for _n, _c in [("mu", 14), ("k_k", 4), ("k_a", 4), ("rkA", 4), ("rkB", 4), ("lnw", 4), ("lnb", 4),
               ("a0A", 4), ("a0B", 4), ("qg", 1), ("kg", 1), ("nmix", 8), ("nffn", 8), ("nple", 8),
               ("sink", 4), ("omm", 14), ("hmu", 14), ("nkk", 4), ("omka", 4), ("esink", 4), ("ha0A", 4), ("ha0B", 4), ("hka", 4), ("c2ka", 4)]:
    VC[_n] = _o
    _o += _c
NVEC_IN = VC["omm"]
NVEC = _o

N_SW = 20


def build_program(dbg=None, upto=99, ntiles_other=None, ntiles_own=None):
    nc = bass.Bass("TRN2", target_bir_lowering=False)
    P = Prog()

    def din(name, shape):
        return nc.dram_tensor(name, list(shape), F32, kind="ExternalInput").ap()

    xs = din("xs", [SEQ + 2, D])
    pp = din("pp", [HALF, 256])
    wsw = din("wsw", [D, N_SW * 128])
    wgt = din("wgt", [D, 2048])
    vec_d = din("vec", [128, NVEC_IN])
    w2aug_d = din("w2aug", [2, 65, 512])
    a2_d = din("a2", [2, 64, 512])
    g2_d = din("g2", [128, 512])
    wua_d = din("wua", [512, D]); wur_d = din("wur", [512, D]); wout_d = din("wout", [D, D])
    wff1_d = din("wff1", [D, 4096]); wff2_d = din("wff2", [4096, D])
    wpg_d = din("wpg", [D, D]); wple_d = din("wple", [256, D])
    rope_d = din("rope", [128, 2, HALF + 512])
    cst_d = din("cst", [128, 12, 128])
    tri_d = din("tri", [128, 2, 128])
    crow_d = din("crow", [1, 2, 128])
    out_d = nc.dram_tensor("out", [HALF, D], F32, kind="ExternalOutput").ap()
    m0_s = nc.dram_tensor("m0_s", [2, 64, 128, 4, 128], BF16, kind="Internal").ap()
    n0_s = nc.dram_tensor("n0_s", [2, 64, 128, 4, 64], BF16, kind="Internal").ap()
    q_s = nc.dram_tensor("q_s", [2, 16, 128, 4, 128], BF16, kind="Internal").ap()
    x1_s = nc.dram_tensor("x1_s", [HALF, D], F32, kind="Internal").ap()
    y0_s = nc.dram_tensor("y0_s", [2, 128, 4, HALF], F32, kind="Internal").ap()
    ao_s = nc.dram_tensor("ao_s", [128, 4, HALF], BF16, kind="Internal").ap()
    bv_s = nc.dram_tensor("bv_s", [128, 4, HALF], BF16, kind="Internal").ap()
    sgl_s = nc.dram_tensor("sgl_s", [128, HALF], BF16, kind="Internal").ap()
    dbg_out = None
    if dbg is not None:
        dbg_out = nc.dram_tensor("dbg", [128, 16384], F32, kind="ExternalOutput").ap()

    def fsz(ap):
        n = 1
        for d_ in ap.shape[1:]:
            n *= d_
        return n

    def rgof(ap):
        b = ap.base_partition()
        return (b // 32, (b + ap.shape[0] - 1) // 32)

    def MM(out, lhsT, rhs, start=True, stop=True, r=(), w=()):
        dur = max(0.064, 0.02 + fsz(out) / 2000.0) * (4 if lhsT.dtype == F32 else 1)
        P.add("pe", lambda e: e.matmul(out, lhsT, rhs, start=start, stop=stop), r, w, dur=dur, rg=rgof(lhsT))

    def TR(out, in_, idn, r=(), w=()):
        P.add("pe", lambda e: e.transpose(out, in_, idn), r, w, dur=0.064, rg=rgof(in_))

    def ACT(out, in_, func, bias=None, scale=None, accum=None, r=(), w=()):
        kw = {}
        if bias is not None:
            kw["bias"] = bias
        if scale is not None:
            kw["scale"] = scale
        if accum is not None:
            kw["accum_out"] = accum
        tbl = "ln" if func == AF.Ln else ("tanh" if func == AF.Tanh else None)
        P.add("act", lambda e: e.activation(out=out, in_=in_, func=func, **kw), r, w, dur=0.2 + fsz(out) / 1400.0, tbl=tbl)

    def vdur(eng, out):
        return (0.1 + fsz(out) / 960.0) if eng == "dve" else (0.3 + fsz(out) / 500.0)

    def TTo(eng, out, a, b, op, r=(), w=()):
        P.add(eng, lambda e: e.tensor_tensor(out, a, b, op), r, w, dur=vdur(eng, out))

    def TS(eng, out, a, s1, s2, op0, op1=None, r=(), w=()):
        if op1 is None:
            P.add(eng, lambda e: e.tensor_scalar(out, a, s1, None, op0), r, w, dur=vdur(eng, out))
        else:
            P.add(eng, lambda e: e.tensor_scalar(out, a, s1, s2, op0, op1), r, w, dur=vdur(eng, out))

    def STT(eng, out, in0, sc, in1, op0, op1, r=(), w=()):
        P.add(eng, lambda e: e.scalar_tensor_tensor(out, in0, sc, in1, op0, op1), r, w, dur=vdur(eng, out))

    def CP(eng, out, in_, r=(), w=()):
        P.add(eng, lambda e: e.tensor_copy(out, in_), r, w, dur=vdur(eng, out))

    def RCP(out, in_, r=(), w=()):
        P.add("dve", lambda e: e.reciprocal(out, in_), r, w, dur=vdur("dve", out) * 2)

    def MEMSET(eng, ap, val, w=()):
        P.add(eng, lambda e: e.memset(ap, val), (), w, dur=vdur(eng, ap))

    def DMA(eng, out, in_, r=(), w=()):
        nbytes = fsz(out) * out.shape[0] * (4 if out.dtype == F32 else 2)
        P.add(eng, lambda e: e.dma_start(out=out, in_=in_), r, w, dma=True, dur=2.0 + nbytes / 100000.0)

    with ExitStack() as gs:
        def sb(name, shape, dt=F32, st=None):
            return (st or gs).enter_context(nc.sbuf_tensor("sb_" + name, list(shape), dt))

        psA = [gs.enter_context(nc.psum_tensor(f"psA{i}", [128, 512], F32)) for i in range(6)]
        psT = [gs.enter_context(nc.psum_tensor(f"psT{i}", [128, 1024], BF16)) for i in range(2)]
        psTf = [psT[i][:, :].bitcast(F32) for i in range(2)]
        pk = lambda i: ("ps", i)
        pkT = lambda i: ("psT", i)

        vec = sb("vec", [128, NVEC])
        cstb = sb("cstb", [128, 12, 128], BF16)
        cstf = sb("cstf", [128, 3, 128])
        crow = sb("crow", [1, 2, 128])
        tri = sb("tri", [128, 2, 128])
        eps_c = sb("eps_c", [128, 4])
        junk = sb("junk", [128, D], BF16)
        DMA("sp", vec[:, 0:NVEC_IN], vec_d, w=["vec"])
        DMA("pool", cstb[:], cst_d, w=["cstb"])
        DMA("sp", cstf[:, 0, :], cst_d[:, 0, :], w=["cstf"])
        DMA("sp", cstf[:, 1, :], cst_d[:, 9, :], w=["cstf"])
        DMA("sp", cstf[:, 2, :], cst_d[:, 8, :], w=["cstf"])
        DMA("sp", crow[:], crow_d, w=["crow"])
        DMA("sp", tri[:], tri_d, w=["tri"])
        MEMSET("dve", eps_c[:, 0:1], 1e-6, w=["eps"])
        MEMSET("dve", eps_c[:, 1:2], 1e-24, w=["eps"])
        MEMSET("dve", eps_c[:, 2:3], 64e-5, w=["eps"])
        ident = cstb[:, 0, :]; onesblk = cstb[:, 1, :]; perm = cstb[:, 2, :]; bdm = cstb[:, 3, :]
        identf = cstf[:, 0, :]
        vc = lambda n, i=0: vec[:, VC[n] + i:VC[n] + i + 1]
        TS("dve", vec[:, VC["omm"]:VC["omm"] + 14], vec[:, VC["mu"]:VC["mu"] + 14], -1.0, 1.0, ALU.mult, ALU.add, r=["vec"], w=["vec2"])
        TS("dve", vec[:, VC["hmu"]:VC["hmu"] + 14], vec[:, VC["mu"]:VC["mu"] + 14], 0.5, None, ALU.mult, r=["vec"], w=["vec2"])
        TS("dve", vec[:, VC["nkk"]:VC["nkk"] + 4], vec[:, VC["k_k"]:VC["k_k"] + 4], -1.0, None, ALU.mult, r=["vec"], w=["vec2"])
        TS("dve", vec[:, VC["omka"]:VC["omka"] + 4], vec[:, VC["k_a"]:VC["k_a"] + 4], -1.0, 1.0, ALU.mult, ALU.add, r=["vec"], w=["vec2"])
        ACT(vec[:, VC["esink"]:VC["esink"] + 4], vec[:, VC["sink"]:VC["sink"] + 4], AF.Exp, r=["vec"], w=["vec2"])
        TS("dve", vec[:, VC["ha0A"]:VC["ha0A"] + 8], vec[:, VC["a0A"]:VC["a0A"] + 8], 0.5, None, ALU.mult, r=["vec"], w=["vec2"])
        TS("dve", vec[:, VC["hka"]:VC["hka"] + 4], vec[:, VC["k_a"]:VC["k_a"] + 4], 0.5, None, ALU.mult, r=["vec"], w=["vec2"])
        TS("dve", vec[:, VC["c2ka"]:VC["c2ka"] + 4], vec[:, VC["k_a"]:VC["k_a"] + 4], -0.5, 1.0, ALU.mult, ALU.add, r=["vec"], w=["vec2"])
        VK = ["vec", "vec2", "eps", "cstb", "cstf", "tri"]

        def ck(name):
            if os.environ.get("KSTOP") == name:
                P.disabled = True
        if os.environ.get("KSTOP") == "c":
            P.disabled = True

        bar_t = {e: sb(f"bar_{e}", [1, 8]) for e in ["act", "dve", "pool"]}

        def RSQ(dst, src, scale, bias_ap, n, r=(), w=()):
            ACT(dst, src, AF.Ln, bias=bias_ap, scale=scale, r=list(r) + ["eps"], w=list(w))
            ACT(dst, dst, AF.Exp, scale=-0.5, r=list(w), w=list(w))

        def barrier():
            if getattr(P, "disabled", False):
                return
            P.seg += 1
            n = len(P.ops)
            pend = [op.idx for op in P.ops[P.last_barrier:n]]
            marks = []
            for e in ["act", "dve", "pool"]:
                t = bar_t[e]
                marks.append(P.add(e, (lambda tt, ee: (lambda en: en.memzero(tt[:]) if ee == 'act' else en.memset(tt[:], 0.0)))(t, e), (), [f"bar_{e}_{n}"]).idx)
            marks.append(P.add("pe", lambda e: e.matmul(psA[0][0:1, 0:1], cstb[0:1, 8, 0:1], cstb[0:1, 8, 0:1], start=True, stop=True),
                               ["cstb", pk(0)], [pk(0), f"bar_pe_{n}"]).idx)
            for e in ["pe", "act", "dve", "pool", "sp"]:
                f = P.add(e, None, (), ())
                f.xdeps = tuple(marks) + tuple(i for i in pend if P.ops[i].dma)
            P.last_barrier = len(P.ops)
            P.seg += 1

        def make_hT(st_key, x_aps, nrows, hT, gain, wk, xt, hb, ssq, rstd):
            nblk = len(x_aps)
            for bi, (xap, rows, col0) in enumerate(x_aps):
                s = bi % xt.shape[1]
                DMA("sp", xt[0:rows, s, :], xap, w=[("xt", s)])
                ACT(junk[0:rows, :], xt[0:rows, s, :], AF.Square, accum=ssq[0:rows, s:s + 1], r=[("xt", s)], w=["junk", ("ssq", s)])
                RSQ(rstd[0:rows, s:s + 1], ssq[0:rows, s:s + 1], 1.0 / D, eps_c[0:rows, 0:1], 1, r=[("ssq", s)], w=[("rstd", s)])
                TS("dve", hb[0:rows, 0, :], xt[0:rows, s, :], rstd[0:rows, s:s + 1], None, ALU.mult, r=[("xt", s), ("rstd", s)], w=["hbo"])
                for kc in range(8):
                    TR(psT[kc % 2][:, (kc // 2) * 128:(kc // 2) * 128 + rows], hb[0:rows, 0, kc * 128:(kc + 1) * 128], ident[0:rows, 0:rows],
                       r=["hbo", "cstb"], w=[pkT(kc % 2)])
                for kc in range(8):
                    eng_act = (kc % 2 == 0)
                    src = psT[kc % 2][:, (kc // 2) * 128:(kc // 2) * 128 + rows]
                    dst = hT[:, kc, col0:col0 + rows]
                    if eng_act:
                        ACT(dst, src, AF.Copy, scale=vc(gain, kc), r=[pkT(kc % 2), "vec"], w=[(wk, kc)])
                    else:
                        TS("dve", dst, src, vc(gain, kc), None, ALU.mult, r=[pkT(kc % 2), "vec"], w=[(wk, kc)])

        g2t = sb("g2t", [128, 512], BF16)
        PCs = sb("PCs", [128, 2, 64, 4], F32)
        with ExitStack() as s1:
            win = sb("win", [128, 8, N_SW * 128], BF16, s1)
            cstm = sb("cstm", [128, 4, 4, 128], BF16, s1)
            for _m in range(4):
                for _r in range(4):
                    DMA("pool", cstm[:, _m, _r, :], cst_d[:, 4 + _m, :], w=["cstm"])
            for (c0_, c1_) in ((768, 1664), (1664, 2560), (0, 768)):
                for kc in range(8):
                    DMA("pool", win[:, kc, c0_:c1_], wsw[kc * 128:(kc + 1) * 128, c0_:c1_], w=[("win", c0_)])
            w2aug = sb("w2aug", [65, 2, 512], BF16, s1)
            a2t = sb("a2t", [128, 2, 512], BF16, s1)
            rope = sb("rope", [128, 2, HALF + 256], BF16, s1)
            for d in range(2):
                DMA("pool", w2aug[:, d, :], w2aug_d[d], w=["w2aug"])
                DMA("pool", a2t[64:128, d, :], a2_d[d], w=["a2t"])
            DMA("pool", g2t[:], g2_d, w=["g2t"])
            DMA("pool", rope[:], rope_d[:, :, 0:HALF + 256], w=["rope"])
            if os.environ.get("KSTOP") == "w":
                P.disabled = True
            kT = sb("kT", [128, HALF + 128], BF16, s1)
            vaug = sb("vaug", [128, NB_OWN + 1, 2, 64], BF16, s1)
            vones = sb("vones", [128, 64], BF16, s1)
            attn_o = sb("attn_o", [128, 4, TT], BF16, s1)
            sgl = sb("sgl", [128, TT], BF16, s1)
            bv = sb("bv", [128, 4, TT], BF16, s1)
            MEMSET("pool", vones[:], 1.0, w=["vaug1"])
            xt = sb("xt", [128, 1, D], F32, s1); hb = sb("hb", [128, 1, D], BF16, s1)
            ssq = sb("ssq", [128, 2], F32, s1); rstd = sb("rstd", [128, 2], F32, s1)
            hT = sb("hT", [128, 8, TT + 2], BF16, s1)
            zr = sb("zr", [128, 2, TT + 2], F32, s1)
            zs = sb("zs", [128, 2, TT], F32, s1)
            ztmp = sb("ztmp", [128, 2, TT], F32, s1)
            zq = sb("zq", [128, TT], F32, s1)
            qsq = sb("qsq", [128, TT], BF16, s1)
            qrs = sb("qrs", [128, TT], F32, s1)
            qn = sb("qn", [128, TT], BF16, s1)
            qt1 = sb("qt1", [128, TT], F32, s1); qt2 = sb("qt2", [128, TT], F32, s1)
            qT = sb("qT", [128, 2, 4, TT], BF16, s1)
            vb = sb("vb", [128, TT], BF16, s1)
            twl = sb("twl", [65, TT], BF16, s1); alx = sb("alx", [128, TT], BF16, s1)
            MEMSET("pool", twl[64:65, :], 1.0, w=["twl1"])
            if os.environ.get("KSTOP") == "m":
                P.disabled = True
            sgfcD = [sb(f"sgfc{i}", [128, NBT, 128], F32, s1) for i in range(2)]
            r_f = sb("r_f", [128, TT], F32, s1); k_f = sb("k_f", [128, TT], F32, s1); v_f = sb("v_f", [128, TT], F32, s1)
            nkk = sb("nkk", [128, TT], F32, s1)
            a_fD = [sb(f"a_f{i}", [128, TT], F32, s1) for i in range(2)]; b_fD = [sb(f"b_f{i}", [128, TT], F32, s1) for i in range(2)]
            kd_fD = [sb(f"kd_f{i}", [128, TT], F32, s1) for i in range(2)]; t1_fD = [sb(f"t1_f{i}", [128, TT], F32, s1) for i in range(2)]
            pbD = [sb(f"pb{i}", [128, TT], BF16, s1) for i in range(2)]
            E_fD = [sb(f"E_f{i}", [128, TT], F32, s1) for i in range(2)]; G_fD = [sb(f"G_f{i}", [128, TT], F32, s1) for i in range(2)]
            AR = sb("AR", [128, 2, 4, NBT, 256], BF16, s1)
            bt = sb("bt", [128, 2, 4, TT], BF16, s1); kt = sb("kt", [128, 2, 4, TT], BF16, s1)
            btT = sb("btT", [128, 2, NBT, 512], BF16, s1); ktT = sb("ktT", [128, 2, NBT, 512], BF16, s1)
            atT = sb("atT", [128, 2, NBT, 512], BF16, s1)
            vT = sb("vT", [128, NBT, 512], BF16, s1)
            Aab = sb("Aab", [128, 8, 128], BF16, s1); AabT = sb("AabT", [128, 8, 128], BF16, s1)
            Aak = sb("Aak", [128, 8, 128], BF16, s1); Arb = sb("Arb", [128, 8, 128], BF16, s1); Ark = sb("Ark", [128, 8, 128], BF16, s1)
            ApI = [[sb(f"Ap{j}_{i}", [128, 8, 128], BF16, s1) for i in range(2)] for j in range(2)]
            ATpI = [[sb(f"ATp{j}_{i}", [128, 8, 128], BF16, s1) for i in range(2)] for j in range(2)]
            ZpI = [[sb(f"Zp{j}_{i}", [128, 4, 2, 2, 64], BF16, s1) for i in range(2)] for j in range(2)]
            M0stI = [sb(f"M0st{j}", [128, 2, 4, 128], BF16, s1) for j in range(2)]
            N0stI = [sb(f"N0st{j}", [128, 2, 4, 64], BF16, s1) for j in range(2)]
            QstI = [sb(f"Qst{j}", [128, 4, 128], BF16, s1) for j in range(2)]
            Y0stI = [sb(f"Y0st{j}", [128, 4, 128], F32, s1) for j in range(2)]
            hlcount = [0]
            probs = sb("probs", [128, 2, 384], BF16, s1)
            den = sb("den", [128, 128], F32, s1)

            def proj_chunk(ci, bank, halo_bank=None):
                if halo_bank is None:
                    for kc in range(8):
                        MM(psA[bank][:, 0:TT], win[:, kc, ci * 128:(ci + 1) * 128], hT[:, kc, 1:TT + 1], start=(kc == 0), stop=(kc == 7),
                           r=[("win", 0 if ci < 6 else (768 if ci < 13 else 1664)), ("hT", kc)], w=[pk(bank)])
                else:
                    for kc in range(8):
                        MM(psA[bank][:, 0:TT + 2], win[:, kc, ci * 128:(ci + 1) * 128], hT[:, kc, 0:TT + 2], start=(kc == 0), stop=(kc == 7),
                           r=[("win", 0 if ci < 6 else (768 if ci < 13 else 1664)), ("hT", kc)], w=[pk(bank)])

            zcount = [0]
            mbc = [0]

            def nb():
                mbc[0] += 1
                return 3 + (mbc[0] % 3)

            def rwkv_z(ci, mui, out_ap, wkey, post=None):
                s = zcount[0] % 2
                zcount[0] += 1
                bank = s
                proj_chunk(ci, bank, 2)
                ACT(zr[:, s, :], psA[bank][:, 0:TT + 2], AF.Copy, r=[pk(bank)], w=[("zr", s)])
                ACT(zs[:, s, :], psA[bank][:, 1:TT + 1], AF.Copy, scale=vc("omm", mui), r=[pk(bank), "vec2"], w=[("zs", s)])
                TTo("pool", ztmp[:, s, :], zr[:, s, 0:TT], zr[:, s, 2:TT + 2], ALU.add, r=[("zr", s)], w=[("ztmp", s)])
                STT("dve", out_ap, ztmp[:, s, :], vc("hmu", mui), zs[:, s, :], ALU.mult, ALU.add, r=[("ztmp", s), ("zs", s), "vec2"], w=[wkey])

            def sweep_tile(t0, own, first_other, pcidx0):
                dirs = [0, 1] if own else [1]
                blk0 = t0 // 128
                xaps = [(xs[t0 + 128 * i: t0 + 128 * i + 128, :], 128, 128 * i) for i in range(NBT)]
                xaps.append((xs[t0 + TT: t0 + TT + 2, :], 2, TT))
                P.tag = f"t{t0}-hT"
                make_hT("hT", xaps, None, hT, "nmix", "hT", xt, hb, ssq, rstd)
                ck("h")
                P.tag = f"t{t0}-qkv"
                qs = (t0 // TT) % 2
                if own or first_other:
                    for ci in ([0, 1, 2, 3, 4] if own else [4]):
                        bank = ci % 2
                        proj_chunk(ci, bank)
                        gcol = "kg" if ci == 4 else "qg"
                        ACT(qsq[:], psA[bank][:, 0:TT], AF.Square, r=[pk(bank)], w=["qsq"])
                        b1 = nb()
                        MM(psA[b1][:, 0:TT], onesblk, qsq[:], r=["cstb", "qsq"], w=[pk(b1)])
                        RSQ(qrs[:], psA[b1][:, 0:TT], 1.0 / 64, eps_c[:, 0:1], TT, r=[pk(b1)], w=["qrs"])
                        STT("dve", qn[:], psA[bank][:, 0:TT], vc(gcol), qrs[:], ALU.mult, ALU.mult, r=[pk(bank), "qrs", "vec"], w=["qn"])
                        b2 = nb()
                        MM(psA[b2][:, 0:TT], perm, qn[:], r=["cstb", "qn"], w=[pk(b2)])
                        TTo("pool", qt1[:], qn[:], rope[:, 0, t0:t0 + TT], ALU.mult, r=["qn", "rope"], w=["qt1"])
                        TTo("dve", qt2[:], psA[b2][:, 0:TT], rope[:, 1, t0:t0 + TT], ALU.mult, r=[pk(b2), "rope"], w=["qt2"])
                        if ci < 4:
                            TTo("pool", qT[:, qs, ci, :], qt1[:], qt2[:], ALU.add, r=["qt1", "qt2"], w=[("qT", qs)])
                        else:
                            n = TT if own else 128
                            TTo("pool", kT[:, t0:t0 + n], qt1[:, 0:n], qt2[:, 0:n], ALU.add, r=["qt1", "qt2"], w=["kT"])
                    proj_chunk(5, 0)
                    ACT(vb[:], psA[0][:, 0:TT], AF.Copy, r=[pk(0)], w=["vb"])
                    nbv = NBT if own else 1
                    for bi in range(nbv):
                        TR(psT[0][:, bi * 128:(bi + 1) * 128], vb[:, bi * 128:(bi + 1) * 128], ident, r=["vb", "cstb"], w=[pkT(0)])
                    for bi in range(nbv):
                        CP("dve", vaug[:, blk0 + bi, :, :], psT[0][:, bi * 128:(bi + 1) * 128].rearrange("p (g d) -> p g d", g=2),
                           r=[pkT(0)], w=["vaug"])
                P.tag = f"t{t0}-wa"
                ck("q")
                rwkv_z(6, 0, zq[:], "zq")
                ACT(twl[0:64, :], zq[0:64, :], AF.Tanh, r=["zq"], w=["twl"])
                CP("pool", alx[64:128, :], zq[64:128, :], r=["zq"], w=["alx"])
                ck("z")
                if own:
                    rwkv_z(7, 1, zq[:], "zq")
                    ACT(zq[:], zq[:], AF.Tanh, scale=0.5, r=["zq"], w=["zq"])
                    TS("pool", sgl[:, :], zq[:], 0.5, 0.5, ALU.mult, ALU.add, r=["zq"], w=["sgl"])
                    DMA("sp", sgl_s[:, t0:t0 + TT], sgl[:, :], r=["sgl"], w=[("sgl_s", t0 // TT)])
                for fc in range(4):
                    P.tag = f"t{t0}-fc{fc}"
                    if own:
                        rwkv_z(8 + 3 * fc, 2 + 3 * fc, r_f[:], "r_f")
                    rwkv_z(9 + 3 * fc, 3 + 3 * fc, k_f[:], "k_f")
                    rwkv_z(10 + 3 * fc, 4 + 3 * fc, v_f[:], "v_f")
                    ACT(qsq[:], k_f[:], AF.Square, scale=vc("k_k", fc), r=["k_f", "vec"], w=["qsq"])
                    b1 = nb()
                    MM(psA[b1][:, 0:TT], onesblk, qsq[:], r=["cstb", "qsq"], w=[pk(b1)])
                    RSQ(qrs[:], psA[b1][:, 0:TT], 1.0, eps_c[:, 1:2], TT, r=[pk(b1)], w=["qrs"])
                    STT("dve", nkk[:], k_f[:], vc("nkk", fc), qrs[:], ALU.mult, ALU.mult, r=["k_f", "qrs", "vec2"], w=["nkk"])
                    ck("k")
                    CP("pool", vb[:], v_f[:], r=["v_f"], w=["vb"])
                    for bi in range(NBT):
                        TR(psT[0][:, bi * 128:(bi + 1) * 128], vb[:, bi * 128:(bi + 1) * 128], ident, r=["vb", "cstb"], w=[pkT(0)])
                    CP("dve", vT[:, :, fc * 128:(fc + 1) * 128], psT[0][:, 0:NBT * 128].rearrange("p (b c) -> p b c", b=NBT), r=[pkT(0)], w=[("vT", fc)])
                    for d in dirs:
                        dn = "AB"[d]
                        a_f, b_f, kd_f, t1_f, pb, E_f, G_f, sgfc = a_fD[d], b_fD[d], kd_fD[d], t1_fD[d], pbD[d], E_fD[d], G_fD[d], sgfcD[d]
                        b1 = nb()
                        MM(psA[b1][:, 0:TT], a2t[64:128, d, fc * 128:(fc + 1) * 128], alx[64:128, :], r=["a2t", "alx"], w=[pk(b1)])
                        ACT(a_f[:], psA[b1][:, 0:TT], AF.Tanh, bias=vc("ha0" + dn, fc), scale=0.5, r=[pk(b1), "vec2"], w=[("a_f", d)])
                        STT("dve", b_f[:], a_f[:], 1.0, nkk[:], ALU.add, ALU.mult, r=["nkk", ("a_f", d)], w=[("b_f", d)])
                        TS("dve", t1_f[:], a_f[:], vc("hka", fc), vc("c2ka", fc), ALU.mult, ALU.add, r=[("a_f", d), "vec2"], w=[("t1_f", d)])
                        TTo("pool", kd_f[:], k_f[:], t1_f[:], ALU.mult, r=["k_f", ("t1_f", d)], w=[("kd_f", d)])
                        if own:
                            STT("dve", pb[:], r_f[:], vc("rk" + dn, fc), kd_f[:], ALU.mult, ALU.mult, r=["r_f", ("kd_f", d), "vec"], w=[("pb", d)])
                            MM(psA[2][:, 0:TT], onesblk, pb[:], start=(d == 0), stop=(d == 1), r=["cstb", ("pb", d)], w=[pk(2)])
                        b2 = nb()
                        for bi in range(NBT):
                            MM(psA[b2][:, bi * 128:(bi + 1) * 128], twl[:, bi * 128:(bi + 1) * 128], w2aug[:, d, fc * 128:(fc + 1) * 128],
                               r=["twl", "twl1", "w2aug"], w=[pk(b2)])
                        ACT(sgfc[:, :, :], psA[b2][:, 0:NBT * 128].rearrange("p (b c) -> p b c", b=NBT), AF.Tanh, scale=0.5, r=[pk(b2)], w=[("sgfc", d)])
                        b3 = nb()
                        for bi in range(NBT):
                            MM(psA[b3][:, bi * 128:(bi + 1) * 128], sgfc[:, bi, :], tri[:, d, :], start=True, stop=False, r=[("sgfc", d), "tri"], w=[pk(b3)])
                            MM(psA[b3][:, bi * 128:(bi + 1) * 128], cstf[0:1, 2, :], crow[0:1, d, :], start=False, stop=True, r=["cstf", "crow"], w=[pk(b3)])
                        ACT(E_f[:], psA[b3][:, 0:TT], AF.Exp, r=[pk(b3)], w=[("E_f", d)])
                        ACT(G_f[:], psA[b3][:, 0:TT], AF.Exp, scale=-1.0, r=[pk(b3)], w=[("G_f", d)])
                        Ev = E_f[:].rearrange("p (c t) -> p c t", t=64)
                        pcol = 63 if d == 0 else 0
                        CP("pool", PCs[:, d, pcidx0:pcidx0 + TT // 64, fc], Ev[:, :, pcol], r=[("E_f", d)], w=["PCs"])
                        ARv = AR[:, d, fc, :, :]
                        if own:
                            TTo("dve", ARv[:, :, 128:256], r_f[:].rearrange("p (b t) -> p b t", t=128), E_f[:].rearrange("p (b t) -> p b t", t=128),
                                ALU.mult, r=["r_f", ("E_f", d)], w=[("AR", d, fc)])
                        ARa = AR[:, d, fc, :, 0:128].rearrange("p b (c t) -> p b c t", t=64)
                        nk4 = nkk[:].rearrange("p (b c t) -> p b c t", c=2, t=64)
                        E4 = E_f[:].rearrange("p (b c t) -> p b c t", c=2, t=64)
                        if d == 0:
                            TTo("pool", ARa[:, :, :, 1:64], nk4[:, :, :, 1:64], E4[:, :, :, 0:63], ALU.mult, r=["nkk", ("E_f", d)], w=[("AR", d, fc)])
                            CP("pool", ARa[:, :, :, 0:1], nk4[:, :, :, 0:1], r=["nkk"], w=[("AR", d, fc)])
                        else:
                            TTo("pool", ARa[:, :, :, 0:63], nk4[:, :, :, 0:63], E4[:, :, :, 1:64], ALU.mult, r=["nkk", ("E_f", d)], w=[("AR", d, fc)])
                            CP("pool", ARa[:, :, :, 63:64], nk4[:, :, :, 63:64], r=["nkk"], w=[("AR", d, fc)])
                        STT("dve", bt[:, d, fc, :], b_f[:], -0.5, G_f[:], ALU.mult, ALU.mult, r=[("b_f", d), ("G_f", d)], w=[("bt", d, fc)])
                        TTo("pool", kt[:, d, fc, :], kd_f[:], G_f[:], ALU.mult, r=[("kd_f", d), ("G_f", d)], w=[("kt", d, fc)])
                        for which, (srck, dst, dkey) in enumerate(((("bt", d, fc), btT, "btT"), (("kt", d, fc), ktT, "ktT"), (("AR", d, fc), atT, "atT"))):
                            pb_i = which % 2
                            for bi in range(NBT):
                                if which == 0:
                                    src = bt[:, d, fc, bi * 128:(bi + 1) * 128]
                                elif which == 1:
                                    src = kt[:, d, fc, bi * 128:(bi + 1) * 128]
                                else:
                                    src = AR[:, d, fc, bi, 0:128]
                                TR(psT[pb_i][:, bi * 128:(bi + 1) * 128], src, ident, r=[srck, "cstb"], w=[pkT(pb_i)])
                            pv = psT[pb_i][:, 0:NBT * 128].rearrange("p (b c) -> p b c", b=NBT)
                            if which != 1:
                                CP("dve", dst[:, d, :, fc * 128:(fc + 1) * 128], pv, r=[pkT(pb_i)], w=[(dkey, d, fc)])
                            else:
                                ACT(dst[:, d, :, fc * 128:(fc + 1) * 128], pv, AF.Copy, r=[pkT(pb_i)], w=[(dkey, d, fc)])
                    if own:
                        TTo("dve", bv[:, fc, :], psA[2][:, 0:TT], v_f[:], ALU.mult, r=[pk(2), "v_f"], w=["bv"])
                        DMA("sp", bv_s[:, fc, t0:t0 + TT], bv[:, fc, :], r=["bv"], w=[("bv_s", t0 // TT, fc)])
                ck("f")
                for d in dirs:
                    ms, mi = (4, 5) if d == 0 else (6, 7)
                    for bi in range(NBT):
                        P.tag = f"t{t0}-hl{d}{bi}"
                        ins_ = hlcount[0] % 2
                        hlcount[0] += 1
                        Ap, ATp, Zp = ApI[ins_], ATpI[ins_], ZpI[ins_]
                        B = (lambda p_: (lambda b_: (b_ + 3 * p_) % 6))(ins_)
                        M0st, N0st, Qst, Y0st = M0stI[ins_], N0stI[ins_], QstI[ins_], Y0stI[ins_]
                        gblk = (t0 // 128 + bi) if own else None
                        def amat(bank0, lh, rh, keys):
                            for h in range(8):
                                fc = h // 2; rows = slice((h % 2) * 64, (h % 2) * 64 + 64)
                                MM(psA[B(bank0 + h % 2)][:, (h // 2) * 128:(h // 2) * 128 + 128], lh(rows, fc), rh(rows, fc),
                                   r=[(k_, d, fc) for k_ in keys], w=[pk(B(bank0 + h % 2))])

                        def aevac(bank0, dst, dkey, mslot):
                            dv = dst[:].rearrange("p (i two) t -> p two i t", two=2)
                            for par in range(2):
                                TTo("dve", dv[:, par, :, :], psA[B(bank0 + par)][:, :].rearrange("p (i t) -> p i t", i=4), cstm[:, mslot - 4, :, :], ALU.mult,
                                    r=[pk(B(bank0 + par)), "cstm"], w=[dkey])
                        tokc = slice(bi * 128, (bi + 1) * 128)
                        amat(0, lambda rows, fc: bt[rows, d, fc, tokc], lambda rows, fc: AR[rows, d, fc, bi, 0:128], ["bt", "AR"])
                        amat(2, lambda rows, fc: kt[rows, d, fc, tokc], lambda rows, fc: AR[rows, d, fc, bi, 0:128], ["kt", "AR"])
                        amat(4, lambda rows, fc: AR[rows, d, fc, bi, 0:128], lambda rows, fc: bt[rows, d, fc, tokc], ["bt", "AR"])
                        aevac(0, Aab, "Aab", ms)
                        aevac(2, Aak, "Aak", ms)
                        aevac(4, AabT, "AabT", (6 if d == 0 else 4))
                        ck("a")
                        Z = Zp[0]
                        CP("pool", Z[:, :, 0, :, :], atT[:, d, bi, :].rearrange("p (f h j) -> p f h j", f=4, h=2), r=[("atT", d, f_) for f_ in range(4)], w=[("Z", ins_, 0, f_) for f_ in range(4)])
                        for h in range(8):
                            MM(psA[B(3)][:, h * 64:(h + 1) * 64], Aak[:, h, :], vT[:, bi, h * 64:(h + 1) * 64], r=["Aak", ("vT", h // 2)], w=[pk(B(3))])
                        CP("dve", Z[:, :, 1, :, :], psA[B(3)][:, 0:512].rearrange("p (f h j) -> p f h j", f=4, h=2), r=[pk(B(3))], w=[("Z", ins_, 0, f_) for f_ in range(4)])
                        curA, curAT, ak, atk = Aab, AabT, ["Aab", "Aab"], ["AabT", "AabT"]
                        zi = 0
                        for lev in range(6):
                            Zc, Zn = Zp[zi], Zp[1 - zi]
                            for hh in range(2):
                                bank = 4 + hh
                                for q4 in range(4):
                                    h = 4 * hh + q4; fc = h // 2; h2 = h % 2
                                    o = psA[B(bank)][:, q4 * 128:(q4 + 1) * 128].rearrange("p (a j) -> p a j", a=2)
                                    MM(o, curA[:, h, :], Zc[:, fc, :, h2, :], start=True, stop=False, r=[ak[hh], ("Z", ins_, zi, fc)], w=[pk(B(bank))])
                                    MM(o, ident, Zc[:, fc, :, h2, :], start=False, stop=True, r=["cstb", ("Z", ins_, zi, fc)], w=[pk(B(bank))])
                                for f in range(2):
                                    src = psA[B(bank)][:, f * 256:(f + 1) * 256].rearrange("p (h a j) -> p a h j", h=2, a=2)
                                    if hh == 0:
                                        ACT(Zn[:, 2 * hh + f, :, :, :], src, AF.Copy, r=[pk(B(bank))], w=[("Z", ins_, 1 - zi, 2 * hh + f)])
                                    else:
                                        CP("dve", Zn[:, 2 * hh + f, :, :, :], src, r=[pk(B(bank))], w=[("Z", ins_, 1 - zi, 2 * hh + f)])
                            zi = 1 - zi
                            if lev < 5:
                                nA, nAT = Ap[lev % 2], ATp[lev % 2]
                                nak = [("Ap", ins_, lev % 2, 0), ("Ap", ins_, lev % 2, 1)]
                                natk = [("ATp", ins_, lev % 2, 0), ("ATp", ins_, lev % 2, 1)]
                                for hh in range(2):
                                    for q4 in range(4):
                                        h = 4 * hh + q4
                                        MM(psA[B(0 + hh)][:, q4 * 128:(q4 + 1) * 128], curAT[:, h, :], curA[:, h, :], r=[ak[hh], atk[hh]], w=[pk(B(0 + hh))])
                                        MM(psA[B(2 + hh)][:, q4 * 128:(q4 + 1) * 128], curA[:, h, :], curAT[:, h, :], r=[ak[hh], atk[hh]], w=[pk(B(2 + hh))])
                                    ACT(nA[:, 4 * hh:4 * hh + 4, :], psA[B(0 + hh)][:, :].rearrange("p (h t) -> p h t", h=4), AF.Copy, r=[pk(B(0 + hh))], w=[nak[hh]])
                                    CP("dve", nAT[:, 4 * hh:4 * hh + 4, :], psA[B(2 + hh)][:, :].rearrange("p (h t) -> p h t", h=4), r=[pk(B(2 + hh))], w=[natk[hh]])
                                curA, curAT, ak, atk = nA, nAT, nak, natk
                        Zf = Zp[zi]; zkf = lambda f_: ("Z", ins_, zi, f_)
                        ck("d")
                        for c2 in range(2):
                            rs_ = slice(c2 * 64, c2 * 64 + 64)
                            for fc in range(4):
                                MM(psA[B(c2)][:, fc * 128:(fc + 1) * 128], Zf[rs_, fc, 0, :, :], btT[rs_, d, bi, fc * 128:(fc + 1) * 128], r=[zkf(fc), ("btT", d, fc)], w=[pk(B(c2))])
                            for fc in range(4):
                                TTo("dve", M0st[:, c2, fc, :], psA[B(c2)][:, fc * 128:(fc + 1) * 128], bdm, ALU.mult, r=[pk(B(c2)), "cstb"], w=[("M0st", ins_)])
                            for h in range(8):
                                fc = h // 2; h2 = h % 2
                                o = psA[B(2 + c2)][h2 * 64:h2 * 64 + 64, fc * 64:fc * 64 + 64]
                                MM(o, btT[rs_, d, bi, h * 64:(h + 1) * 64], Zf[rs_, fc, 1, h2, :], start=True, stop=False, r=[("btT", d, fc), zkf(fc)], w=[pk(B(2 + c2))])
                                MM(o, ktT[rs_, d, bi, h * 64:(h + 1) * 64], vT[rs_, bi, h * 64:(h + 1) * 64], start=False, stop=True, r=[("ktT", d, fc), ("vT", fc)], w=[pk(B(2 + c2))])
                        for c2 in range(2):
                            ACT(N0st[:, c2, :, :], psA[B(2 + c2)][:, 0:256].rearrange("p (f i) -> p f i", f=4), AF.Copy, r=[pk(B(2 + c2))], w=[("N0st", ins_)])
                        cidx = pcidx0 + 2 * bi
                        DMA("sp", m0_s[d, cidx:cidx + 2].rearrange("c p f j -> p c f j"), M0st[:], r=[("M0st", ins_)], w=[("m0_s", d, cidx)])
                        DMA("sp", n0_s[d, cidx:cidx + 2].rearrange("c p f i -> p c f i"), N0st[:], r=[("N0st", ins_)], w=[("n0_s", d, cidx)])
                        if own:
                            amat(0, lambda rows, fc: bt[rows, d, fc, tokc], lambda rows, fc: AR[rows, d, fc, bi, 128:256], ["bt", "AR"])
                            amat(4, lambda rows, fc: kt[rows, d, fc, tokc], lambda rows, fc: AR[rows, d, fc, bi, 128:256], ["kt", "AR"])
                            aevac(0, Arb, "Arb", mi)
                            aevac(4, Ark, "Ark", mi)
                            for fc in range(4):
                                o = psA[B(2)][:, fc * 128:(fc + 1) * 128]
                                MM(o, ident, AR[:, d, fc, bi, 128:256], start=True, stop=False, r=["cstb", ("AR", d, fc)], w=[pk(B(2))])
                                for h2 in range(2):
                                    h = 2 * fc + h2
                                    MM(psA[B(2)][h2 * 64:h2 * 64 + 64, fc * 128:(fc + 1) * 128], Zf[:, fc, 0, h2, :], Arb[:, h, :],
                                       start=False, stop=(h2 == 1), r=[zkf(fc), "Arb"], w=[pk(B(2))])
                                for h2 in range(2):
                                    h = 2 * fc + h2
                                    o2 = psA[B(3)][h2 * 64:h2 * 64 + 64, fc * 128:(fc + 1) * 128]
                                    MM(o2, Zf[:, fc, 1, h2, :], Arb[:, h, :], start=True, stop=False, r=[zkf(fc), "Arb"], w=[pk(B(3))])
                                    MM(o2, vT[:, bi, h * 64:(h + 1) * 64], Ark[:, h, :], start=False, stop=True, r=[("vT", fc), "Ark"], w=[pk(B(3))])
                            ACT(Qst[:], psA[B(2)][:, :].rearrange("p (f t) -> p f t", f=4), AF.Copy, r=[pk(B(2))], w=[("Qst", ins_)])
                            DMA("sp", q_s[d, gblk], Qst[:], r=[("Qst", ins_)], w=[("q_s", d, gblk)])
                            CP("dve", Y0st[:], psA[B(3)][:, :].rearrange("p (f t) -> p f t", f=4), r=[pk(B(3))], w=[("Y0st", ins_)])
                            DMA("sp", y0_s[d, :, :, t0 + bi * 128:t0 + bi * 128 + 128], Y0st[:], r=[("Y0st", ins_)], w=[("y0_s", d, gblk)])

            def attention_tile(ti):
                P.tag = f"attn{ti}"
                qs = ti % 2
                for qb in range(NBT):
                    n = NBT * ti + qb
                    kbs = [kb for kb in (n - 1, n, n + 1) if kb >= 0]
                    for m in range(4):
                        for h2 in range(2):
                            h = 2 * m + h2
                            c = h % 4; g = h // 4
                            rows = slice(g * 64, g * 64 + 64)
                            pi = h2
                            for kb in kbs:
                                slot = kb - (n - 1)
                                MM(psTf[0][:, slot * 128:(slot + 1) * 128], kT[rows, kb * 128:(kb + 1) * 128], qT[rows, qs, c, qb * 128:(qb + 1) * 128],
                                   r=["kT", ("qT", qs)], w=[pkT(0)])
                            lo = (kbs[0] - (n - 1)) * 128
                            ACT(probs[:, pi, lo:384], psTf[0][:, lo:384], AF.Exp, scale=0.125, r=[pkT(0)], w=[("probs", pi)])
                            if n - 1 >= 0:
                                TTo("pool", probs[:, pi, 0:128], probs[:, pi, 0:128], cstb[:, 10, :], ALU.mult, r=[("probs", pi), "cstb"], w=[("probs", pi)])
                            TTo("pool", probs[:, pi, 256:384], probs[:, pi, 256:384], cstb[:, 11, :], ALU.mult, r=[("probs", pi), "cstb"], w=[("probs", pi)])
                            orow = slice(h2 * 64, h2 * 64 + 64)
                            for i, kb in enumerate(kbs):
                                slot = kb - (n - 1)
                                MM(psTf[1][orow, 0:128], vaug[:, kb, g, :], probs[:, pi, slot * 128:(slot + 1) * 128], start=(i == 0), stop=(i == len(kbs) - 1),
                                   r=["vaug", ("probs", pi)], w=[pkT(1)])
                            for i, kb in enumerate(kbs):
                                slot = kb - (n - 1)
                                MM(psTf[1][orow, 128:256], vones[:], probs[:, pi, slot * 128:(slot + 1) * 128], start=(i == 0), stop=(i == len(kbs) - 1),
                                   r=["vaug1", ("probs", pi)], w=[pkT(1)])
                        TS("dve", den[:], psTf[1][:, 128:256], vc("esink", m), None, ALU.add, r=[pkT(1), "vec2"], w=["den"])
                        RCP(den[:], den[:], r=["den"], w=["den"])
                        TTo("dve", attn_o[:, m, qb * 128:(qb + 1) * 128], psTf[1][:, 0:128], den[:], ALU.mult, r=[pkT(1), "den"], w=["attn_o"])

            def attn_and_store(ti):
                attention_tile(ti)
                DMA("sp", ao_s[:, :, ti * TT:(ti + 1) * TT], attn_o[:], r=["attn_o"], w=[("ao_s", ti)])

            oth = list(range(2 * NTH - 1, NTH - 1, -1))
            if ntiles_other is not None:
                oth = oth[len(oth) - ntiles_other:] if ntiles_other > 0 else []
            for ti in oth:
                t0 = ti * TT
                sweep_tile(t0, own=False, first_other=(ti == NTH), pcidx0=t0 // 64)
            nown = NTH if ntiles_own is None else ntiles_own
            for ti in range(nown):
                sweep_tile(ti * TT, own=True, first_other=False, pcidx0=(ti * TT) // 64)
                if ti >= 1:
                    attn_and_store(ti - 1)
            if nown == NTH:
                attn_and_store(NTH - 1)
            if dbg is not None and upto == 1:
                dbg(locals(), P, DMA, dbg_out)
            if upto == 1:
                P.disabled = True
        barrier()

        with ExitStack() as s23:
            Yb = sb("Yb", [128, 4, HALF], F32, s23)
            if True:
                s2 = s23
                S32 = sb("S32", [128, 2, 4, 64], F32, s2)
                Sb = sb("Sb", [128, 2, 4, 64], BF16, s2)
                St = sb("St", [128, 2, 4, 64], F32, s2)
                M0l = sb("M0l", [128, 4, 4, 128], BF16, s2)
                N0l = sb("N0l", [128, 4, 4, 64], BF16, s2)
                Ql = sb("Ql", [128, 4, 4, 128], BF16, s2)
                ytmp = sb("ytmp", [128, 2, 4, 512], F32, s2)
                for j in range(4):
                    s = j % 2
                    gs_ = range(4 * j, 4 * j + 4)
                    DMA("sp", Yb[:, :, j * 512:(j + 1) * 512], y0_s[0, :, :, j * 512:(j + 1) * 512], r=[("y0_s", 0, g) for g in gs_], w=[("Yb", 2 * j), ("Yb", 2 * j + 1)])
                    DMA("sp", ytmp[:, s, :, :], y0_s[1, :, :, j * 512:(j + 1) * 512], r=[("y0_s", 1, g) for g in gs_], w=[("ytmp", s)])
                    TTo("pool", Yb[:, :, j * 512:(j + 1) * 512], Yb[:, :, j * 512:(j + 1) * 512], ytmp[:, s, :, :], ALU.add,
                        r=[("ytmp", s), ("Yb", 2 * j), ("Yb", 2 * j + 1)], w=[("Yb", 2 * j), ("Yb", 2 * j + 1)])
                MEMSET("dve", S32[:], 0.0, w=[("S32", 0), ("S32", 1)])
                MEMSET("pool", Sb[:], 0.0, w=[("Sb", 0), ("Sb", 1)])
                stepn = [0]

                def chain_step(d, cidx, own):
                    sl = stepn[0] % 4
                    stepn[0] += 1
                    DMA("sp", M0l[:, sl, :, :], m0_s[d, cidx], r=[("m0_s", d, cidx - cidx % 2)], w=[("M0l", sl)])
                    DMA("sp", N0l[:, sl, :, :], n0_s[d, cidx], r=[("n0_s", d, cidx - cidx % 2)], w=[("N0l", sl)])
                    bank = 4 + d
                    if own:
                        blk = cidx // 2; c2 = cidx % 2
                        qsl = (blk % 2) * 2 + d
                        if (d == 0 and c2 == 0) or (d == 1 and c2 == 1):
                            DMA("sp", Ql[:, qsl, :, :], q_s[d, blk], r=[("q_s", d, blk)], w=[("Ql", qsl)])
                        yb = 0 + d
                        for h in range(8):
                            fc = h // 2; h2 = h % 2; rows = slice(h2 * 64, h2 * 64 + 64)
                            MM(psA[yb][rows, fc * 64:(fc + 1) * 64], Sb[rows, d, fc, :], Ql[rows, qsl, fc, c2 * 64:c2 * 64 + 64],
                               r=[("Sb", d), ("Ql", qsl)], w=[pk(yb)])
                        yv = Yb[:, :, cidx * 64:(cidx + 1) * 64]
                        TTo("dve", yv, yv, psA[yb][:, 0:256].rearrange("p (f t) -> p f t", f=4), ALU.add, r=[pk(yb), ("Yb", cidx // 4)], w=[("Yb", cidx // 4)])
                    for fc in range(4):
                        o = psA[bank][:, fc * 64:(fc + 1) * 64]
                        MM(o, ident, N0l[:, sl, fc, :], start=True, stop=False, r=["cstb", ("N0l", sl)], w=[pk(bank)])
                        MM(o, M0l[:, sl, fc, :], Sb[:, d, fc, :], start=False, stop=True, r=[("M0l", sl), ("Sb", d)], w=[pk(bank)])
                    TTo("dve", St[:, d, :, :], psA[bank][:, 0:256].rearrange("p (f i) -> p f i", f=4), S32[:, d, :, :], ALU.add,
                        r=[pk(bank), ("S32", d)], w=[("St", d)])
                    for fc in range(4):
                        TS("dve", S32[:, d, fc, :], St[:, d, fc, :], PCs[:, d, cidx, fc:fc + 1], None, ALU.mult, r=[("St", d), "PCs"], w=[("S32", d)])
                    ACT(Sb[:, d, :, :], S32[:, d, :, :], AF.Copy, r=[("S32", d)], w=[("Sb", d)])

                for cidx in range(63, 31, -1):
                    chain_step(1, cidx, own=False)
                for i in range(32):
                    chain_step(0, i, own=True)
                    chain_step(1, 31 - i, own=True)
            if upto == 2:
                P.disabled = True

            if True:
                s3 = s23
                wg = sb("wg", [128, 8, 2048], BF16, s3)
                wua = sb("wua", [128, 4, D], BF16, s3); wur = sb("wur", [128, 4, D], BF16, s3); wout = sb("wout", [128, 8, D], BF16, s3)
                for kc in range(8):
                    DMA("pool", wg[:, kc, :], wgt[kc * 128:(kc + 1) * 128, :], w=["wg"])
                    DMA("pool", wout[:, kc, :], wout_d[kc * 128:(kc + 1) * 128, :], w=["wout"])
                for kc in range(4):
                    DMA("pool", wua[:, kc, :], wua_d[kc * 128:(kc + 1) * 128, :], w=["wua"])
                    DMA("pool", wur[:, kc, :], wur_d[kc * 128:(kc + 1) * 128, :], w=["wur"])
                xt = sb("xt3", [128, 2, D], F32, s3); hb = sb("hb3", [128, 1, D], BF16, s3)
                ssq = sb("ssq3", [128, 2], F32, s3); rstd = sb("rstd3", [128, 2], F32, s3)
                hT = sb("hT3", [128, 8, TT], BF16, s3)
                ycD = [sb(f"yc{i}", [128, TT], F32, s3) for i in range(2)]; ysqD = [sb(f"ysq{i}", [128, TT], F32, s3) for i in range(2)]
                yrD = [sb(f"yr{i}", [128, TT], F32, s3) for i in range(2)]
                rwT = sb("rwT", [128, 4, TT], BF16, s3)
                gates = sb("gates", [128, 16, TT], BF16, s3)
                mg = sb("mg", [128, 8, TT], BF16, s3)
                m1D = [sb(f"m1{i}", [128, TT], F32, s3) for i in range(2)]; m2D = [sb(f"m2{i}", [128, TT], F32, s3) for i in range(2)]
                x1b = sb("x1b", [128, 2, D], F32, s3)
                xres = sb("xres", [128, 2, D], F32, s3)
                ao_l = sb("ao_l", [128, 4, TT], BF16, s3); bv_l = sb("bv_l", [128, 4, TT], BF16, s3); sgl_l = sb("sgl_l", [128, TT], BF16, s3)
                onesf = cstf[:, 1, :]
                for ti in (3, 4, 2, 5, 1, 6, 0, 7):
                    t0 = ti * TT
                    DMA("sp", ao_l[:], ao_s[:, :, t0:t0 + TT], r=[("ao_s", ti)], w=["ao_l"])
                    DMA("sp", bv_l[:], bv_s[:, :, t0:t0 + TT], r=[("bv_s", ti, fc) for fc in range(4)], w=["bv_l"])
                    DMA("sp", sgl_l[:], sgl_s[:, t0:t0 + TT], r=[("sgl_s", ti)], w=["sgl_l"])
                    for fc in range(4):
                        Yv = Yb[:, fc, t0:t0 + TT]
                        yc, ysq, yr = ycD[fc % 2], ysqD[fc % 2], yrD[fc % 2]
                        b0_ = 2 * (fc % 2)
                        MM(psA[b0_][:, 0:TT], onesf, Yv, r=["cstf", ("Yb", ti)], w=[pk(b0_)])
                        TTo("dve", yc[:], Yv, psA[b0_][:, 0:TT], ALU.subtract, r=[("Yb", ti), pk(b0_)], w=[("yc", fc % 2)])
                        ACT(ysq[:], yc[:], AF.Square, r=[("yc", fc % 2)], w=[("ysq", fc % 2)])
                        MM(psA[b0_ + 1][:, 0:TT], onesf, ysq[:], r=["cstf", ("ysq", fc % 2)], w=[pk(b0_ + 1)])
                        RSQ(yr[:], psA[b0_ + 1][:, 0:TT], 1.0, eps_c[:, 2:3], TT, r=[pk(b0_ + 1)], w=[("yr", fc % 2)])
                        TTo("pool", yc[:], yc[:], yr[:], ALU.mult, r=[("yc", fc % 2), ("yr", fc % 2)], w=[("yc", fc % 2)])
                        TS("dve", yc[:], yc[:], vc("lnw", fc), vc("lnb", fc), ALU.mult, ALU.add, r=[("yc", fc % 2), "vec"], w=[("yc", fc % 2)])
                        TTo("pool", yc[:], yc[:], bv_l[:, fc, :], ALU.add, r=[("yc", fc % 2), "bv_l"], w=[("yc", fc % 2)])
                        MM(psA[4 + fc % 2][:, 0:TT], g2t[:, fc * 128:(fc + 1) * 128], sgl_l[:, :], r=["g2t", "sgl_l"], w=[pk(4 + fc % 2)])
                        TTo("dve", rwT[:, fc, :], yc[:], psA[4 + fc % 2][:, 0:TT], ALU.mult, r=[("yc", fc % 2), pk(4 + fc % 2)], w=["rwT"])
                    xaps = [(xs[1 + t0 + 128 * i: 1 + t0 + 128 * i + 128, :], 128, 128 * i) for i in range(NBT)]
                    make_hT("hT3", xaps, None, hT, "nmix", "hT3", xt, hb, ssq, rstd)
                    for gc in range(16):
                        bank = gc % 2
                        for kc in range(8):
                            MM(psA[bank][:, 0:TT], wg[:, kc, gc * 128:(gc + 1) * 128], hT[:, kc, :], start=(kc == 0), stop=(kc == 7), r=["wg", ("hT3", kc)], w=[pk(bank)])
                        ACT(gates[:, gc, :], psA[bank][:, 0:TT], AF.Tanh, scale=0.5, r=[pk(bank)], w=["gates"])
                    for oc in range(8):
                        m1, m2 = m1D[oc % 2], m2D[oc % 2]
                        ba_ = 2 + 2 * (oc % 2)
                        for kc in range(4):
                            MM(psA[ba_][:, 0:TT], wua[:, kc, oc * 128:(oc + 1) * 128], ao_l[:, kc, :], start=(kc == 0), stop=(kc == 3), r=["wua", "ao_l"], w=[pk(ba_)])
                        for kc in range(4):
                            MM(psA[ba_ + 1][:, 0:TT], wur[:, kc, oc * 128:(oc + 1) * 128], rwT[:, kc, :], start=(kc == 0), stop=(kc == 3), r=["wur", "rwT"], w=[pk(ba_ + 1)])
                        STT("dve", m1[:], gates[:, oc, :], 1.0, psA[ba_][:, 0:TT], ALU.add, ALU.mult, r=[pk(ba_), "gates"], w=[("m1", oc % 2)])
                        STT("dve", m2[:], gates[:, 8 + oc, :], 1.0, psA[ba_ + 1][:, 0:TT], ALU.add, ALU.mult, r=[pk(ba_ + 1), "gates"], w=[("m2", oc % 2)])
                        TTo("pool", mg[:, oc, :], m1[:], m2[:], ALU.add, r=[("m1", oc % 2), ("m2", oc % 2)], w=["mg"])
                    for bi in range(NBT):
                        s = bi % 2
                        tok = t0 + bi * 128
                        DMA("sp", xres[:, s, :], xs[1 + tok:1 + tok + 128, :], w=[("xres", s)])
                        for hf in range(2):
                            bank = 4 + hf
                            for kc in range(8):
                                MM(psA[bank][:, :], mg[:, kc, bi * 128:(bi + 1) * 128], wout[:, kc, hf * 512:(hf + 1) * 512], start=(kc == 0), stop=(kc == 7), r=["mg", "wout"], w=[pk(bank)])
                            STT("dve", x1b[:, s, hf * 512:(hf + 1) * 512], psA[bank][:, :], 0.5, xres[:, s, hf * 512:(hf + 1) * 512], ALU.mult, ALU.add, r=[pk(bank), ("xres", s)], w=[("x1b", s)])
                        DMA("sp", x1_s[tok:tok + 128, :], x1b[:, s, :], r=[("x1b", s)], w=[("x1_s", tok // 128)])
        if upto == 3:
            P.disabled = True
        barrier()

        with ExitStack() as s4:
            wf1 = sb("wf1", [128, 8, 4096], BF16, s4); wf2 = sb("wf2", [128, 32, D], BF16, s4)
            wpg = sb("wpg", [128, 8, D], BF16, s4); wple = sb("wple", [128, 2, D], BF16, s4)
            for cg in range(4):
                for kc in range(8):
                    DMA("pool", wf1[:, kc, cg * 1024:(cg + 1) * 1024], wff1_d[kc * 128:(kc + 1) * 128, cg * 1024:(cg + 1) * 1024], w=[("wf1", cg)])
            for kc in range(32):
                DMA("pool", wf2[:, kc, :], wff2_d[kc * 128:(kc + 1) * 128, :], w=[("wf2", kc // 8)])
            for kc in range(8):
                DMA("pool", wpg[:, kc, :], wpg_d[kc * 128:(kc + 1) * 128, :], w=["wpg"])
            for kc in range(2):
                DMA("pool", wple[:, kc, :], wple_d[kc * 128:(kc + 1) * 128, :], w=["wple"])
            xt = sb("xt4", [128, 2, D], F32, s4); hb = sb("hb4", [128, 1, D], BF16, s4)
            ssq = sb("ssq4", [128, 4], F32, s4); rstd = sb("rstd4", [128, 4], F32, s4)
            hTf = sb("hT4", [128, 2, 8, 128], BF16, s4)
            hTp = sb("hT4p", [128, 1, 8, 128], BF16, s4)
            act = sb("act", [128, 2, 32, 128], BF16, s4)
            rl = sb("rl", [128, 4, 128], F32, s4)
            x2 = sb("x2", [128, 1, D], F32, s4)
            pt = sb("pt", [128, 2, 256], F32, s4); pb16 = sb("pb16", [128, 2, 256], BF16, s4); pT = sb("pT", [128, 2, 2, 128], BF16, s4)
            sgp = sb("sgp", [128, 1, 512], F32, s4)
            nhb = [0]

            def norm_T(src, gain, skey, hTt, okey, idx):
                hs = 0
                nhb[0] += 1
                ACT(junk[:], src, AF.Square, accum=ssq[:, idx:idx + 1], r=[skey], w=["junk", ("ssq4", idx)])
                RSQ(rstd[:, idx:idx + 1], ssq[:, idx:idx + 1], 1.0 / D, eps_c[:, 0:1], 1, r=[("ssq4", idx)], w=[("rstd4", idx)])
                TS("dve", hb[:, hs, :], src, rstd[:, idx:idx + 1], None, ALU.mult, r=[skey, ("rstd4", idx)], w=[("hb4o", hs)])
                for kc in range(8):
                    TR(psT[kc % 2][:, (kc // 2) * 128:(kc // 2) * 128 + 128], hb[:, hs, kc * 128:(kc + 1) * 128], ident, r=[("hb4o", hs), "cstb"], w=[pkT(kc % 2)])
                for kc in range(8):
                    srcp = psT[kc % 2][:, (kc // 2) * 128:(kc // 2) * 128 + 128]
                    if kc % 2 == 0:
                        ACT(hTt[:, kc, :], srcp, AF.Copy, scale=vc(gain, kc), r=[pkT(kc % 2), "vec"], w=[(okey, kc)])
                    else:
                        TS("dve", hTt[:, kc, :], srcp, vc(gain, kc), None, ALU.mult, r=[pkT(kc % 2), "vec"], w=[(okey, kc)])

            for blk in range(NB_OWN):
                tok = blk * 128
                s = blk % 2
                DMA("sp", xt[:, s, :], x1_s[tok:tok + 128, :], r=[("x1_s", blk)], w=[("xt4", s)])
                DMA("sp", pt[:, s, :], pp[tok:tok + 128, :], w=[("pt", s)])
                hk = ("hT4", s)
                norm_T(xt[:, s, :], "nffn", ("xt4", s), hTf[:, s], hk, s)
                for hc in range(32):
                    bank = hc % 4
                    for kc in range(8):
                        MM(psA[bank][:, 0:128], wf1[:, kc, hc * 128:(hc + 1) * 128], hTf[:, s, kc, :], start=(kc == 0), stop=(kc == 7),
                           r=[("wf1", hc // 8), (hk, kc)], w=[pk(bank)])
                    ACT(rl[:, bank, :], psA[bank][:, 0:128], AF.Relu, r=[pk(bank)], w=[("rl", bank)])
                    TTo("dve", act[:, s, hc, :], rl[:, bank, :], rl[:, bank, :], ALU.mult, r=[("rl", bank)], w=[("act", s)])
                for hf in range(2):
                    bank = 4 + hf
                    for hc in range(32):
                        MM(psA[bank][:, :], act[:, s, hc, :], wf2[:, hc, hf * 512:(hf + 1) * 512], start=(hc == 0), stop=(hc == 31),
                           r=[("act", s), ("wf2", hc // 8)], w=[pk(bank)])
                    TTo("dve", x2[:, 0, hf * 512:(hf + 1) * 512], psA[bank][:, :], xt[:, s, hf * 512:(hf + 1) * 512], ALU.add, r=[pk(bank), ("xt4", s)], w=["x2"])
                hkp = "hT4p"
                norm_T(x2[:, 0, :], "nple", "x2", hTp[:, 0], hkp, 2 + s)
                CP("pool", pb16[:, s, :], pt[:, s, :], r=[("pt", s)], w=[("pb16", s)])
                for kc in range(2):
                    TR(psT[0][:, 512 + kc * 128:512 + (kc + 1) * 128], pb16[:, s, kc * 128:(kc + 1) * 128], ident, r=[("pb16", s), "cstb"], w=[pkT(0)])
                CP("dve", pT[:, s, :, :], psT[0][:, 512:768].rearrange("p (k t) -> p k t", k=2), r=[pkT(0)], w=[("pT", s)])
                for hf in range(2):
                    gb = 0 + hf
                    pbk = 2 + hf
                    for kc in range(8):
                        MM(psA[gb][:, :], hTp[:, 0, kc, :], wpg[:, kc, hf * 512:(hf + 1) * 512], start=(kc == 0), stop=(kc == 7), r=[(hkp, kc), "wpg"], w=[pk(gb)])
                    for kc in range(2):
                        MM(psA[pbk][:, :], pT[:, s, kc, :], wple[:, kc, hf * 512:(hf + 1) * 512], start=(kc == 0), stop=(kc == 1), r=[("pT", s), "wple"], w=[pk(pbk)])
                    ACT(sgp[:, 0, :], psA[gb][:, :], AF.Tanh, scale=0.5, r=[pk(gb)], w=["sgp"])
                    STT("dve", sgp[:, 0, :], sgp[:, 0, :], 1.0, psA[pbk][:, :], ALU.add, ALU.mult, r=[pk(pbk), "sgp"], w=["sgp"])
                    STT("dve", xt[:, s, hf * 512:(hf + 1) * 512], sgp[:, 0, :], 0.5, x2[:, 0, hf * 512:(hf + 1) * 512], ALU.mult, ALU.add,
                        r=["sgp", "x2", ("xt4", s)], w=[("ob", s)])
                DMA("sp", out_d[tok:tok + 128, :], xt[:, s, :], r=[("ob", s), ("xt4", s)], w=[("out", blk), ("xt4", s)])
            P.add("sp", None, [("out", i) for i in range(NB_OWN)], ())
        P.emit(nc, gs)
    return nc


def _consts():
    cst = np.zeros((128, 12, 128), np.float32)
    i = np.arange(128)
    cst[:, 0, :] = np.eye(128)
    cst[:, 1, :] = (i[:, None] // 64 == i[None, :] // 64)
    perm = np.zeros((128, 128), np.float32)
    for m in range(128):
        j = m % 64
        if j < 8:
            perm[m + 8, m] = 1.0
        elif j < 16:
            perm[m - 8, m] = 1.0
    cst[:, 2, :] = perm
    cst[:, 3, :] = cst[:, 1, :]
    same = (i[:, None] // 64 == i[None, :] // 64)
    cst[:, 4, :] = same & (i[:, None] < i[None, :])
    cst[:, 5, :] = same & (i[:, None] <= i[None, :])
    cst[:, 6, :] = same & (i[:, None] > i[None, :])
    cst[:, 7, :] = same & (i[:, None] >= i[None, :])
    cst[:, 8, :] = 1.0
    cst[:, 9, :] = cst[:, 1, :] / 64.0
    cst[:, 10, :] = (i[:, None] >= i[None, :])
    cst[:, 11, :] = (i[:, None] <= i[None, :])
    tri = np.zeros((128, 2, 128), np.float32)
    tri[:, 0, :] = 0.5 * LWS * cst[:, 5, :]
    tri[:, 1, :] = 0.5 * LWS * cst[:, 7, :]
    return cst, tri


def _rope_tables(pos):
    half = 8
    inv = np.power(np.float32(500000.0), -np.arange(half, dtype=np.float32) * 2.0 / 16.0).astype(np.float32)
    ang = pos.astype(np.float32)[None, :] * inv[:, None]
    cos = np.cos(ang).astype(np.float32); sin = np.sin(ang).astype(np.float32)
    n = pos.shape[0]
    tab = np.zeros((128, 2, n), np.float32)
    for hb in range(2):
        b = hb * 64
        tab[b:b + 64, 0, :] = 1.0
        tab[b:b + 8, 0, :] = cos; tab[b + 8:b + 16, 0, :] = cos
        tab[b:b + 8, 1, :] = -sin; tab[b + 8:b + 16, 1, :] = sin
    return tab


def _col128(v):
    return np.ascontiguousarray(np.asarray(v, np.float32).reshape(-1, 128).T)


_NC_CACHE = {}


def kernel(x, p, norm_mix, w_in, shift_mu, q_norm, k_norm, sink, w0, w2, a0, a2, g2, k_k, k_a, r_k,
           lnx_w, lnx_b, w_up_attn, w_up_rwkv, w_out, norm_ffn, w_ff1, w_ff2, norm_ple, w_ple_gate, w_ple):
    f = lambda a: np.asarray(a, np.float32)
    x = f(x); p = f(p)[0]; w_in = f(w_in)[0]
    qcols = []
    for c in range(4):
        qcols += list(range(c * 64, c * 64 + 64)) + list(range((4 + c) * 64, (4 + c) * 64 + 64))
    O_K, O_V, O_R = 512, 640, 768
    cols = qcols + list(range(O_K, O_K + 128)) + list(range(O_V, O_V + 128))
    rw = O_R
    wa = list(range(rw + 1536, rw + 1536 + 128)); gl = list(range(rw + 1664, rw + 1792))
    cols += wa + gl
    mu_idx = [12, 13]
    for fc in range(4):
        for part in range(3):
            cols += list(range(rw + part * 512 + fc * 128, rw + part * 512 + fc * 128 + 128))
            mu_idx.append(part * 4 + fc)
    wsw = np.ascontiguousarray(w_in[:, cols])
    wgt = np.ascontiguousarray(w_in[:, O_R + 1792:])
    mu_c = _col128(f(shift_mu)[0])[:, mu_idx]
    cst, tri = _consts()
    cst_in = cst.copy()
    cst_in_f = cst.copy()
    cst_in[:, 1, :] = cst[:, 1, :]
    sink_ = f(sink)[0]
    sinkcols = np.zeros((128, 4), np.float32)
    for m in range(4):
        sinkcols[0:64, m] = sink_[2 * m]; sinkcols[64:128, m] = sink_[2 * m + 1]
    in_maps = []
    for c in range(8):
        b, hf = c // 2, c % 2
        xb = x[b]; pb_ = p[b]
        if hf == 0:
            xl = xb; pl = pb_[:HALF]; pos = np.arange(0, HALF + 512); dA, dB = 0, 1
        else:
            xl = xb[::-1]; pl = pb_[HALF:][::-1]; pos = SEQ - 1 - np.arange(0, HALF + 512); dA, dB = 1, 0
        xs = np.zeros((SEQ + 2, D), np.float32); xs[1:SEQ + 1] = xl
        dd = [dA, dB]
        vec = np.zeros((128, NVEC_IN), np.float32)
        def put(n, arr):
            vec[:, VC[n]:VC[n] + arr.shape[1]] = arr
        put("mu", mu_c); put("k_k", _col128(f(k_k)[0])); put("k_a", _col128(f(k_a)[0]))
        put("rkA", _col128(f(r_k)[0, dA].reshape(-1))); put("rkB", _col128(f(r_k)[0, dB].reshape(-1)))
        put("lnw", _col128(f(lnx_w)[0])); put("lnb", _col128(f(lnx_b)[0]))
        put("a0A", _col128(f(a0)[0, dA])); put("a0B", _col128(f(a0)[0, dB]))
        put("qg", np.tile(f(q_norm)[0], 2)[:, None]); put("kg", np.tile(f(k_norm)[0], 2)[:, None])
        put("nmix", _col128(f(norm_mix)[0])); put("nffn", _col128(f(norm_ffn)[0])); put("nple", _col128(f(norm_ple)[0]))
        put("sink", sinkcols)
        w2aug = np.stack([np.concatenate([f(w2)[0, d_], f(w0)[0, d_][None, :]], 0) for d_ in dd])
        a2s = np.stack([f(a2)[0, d_] for d_ in dd])
        cst_c = cst.copy()
        m = {"xs": xs, "pp": np.ascontiguousarray(pl), "wsw": wsw, "wgt": wgt, "vec": vec, "w2aug": np.ascontiguousarray(w2aug),
             "a2": np.ascontiguousarray(a2s), "g2": f(g2)[0], "wua": f(w_up_attn)[0], "wur": f(w_up_rwkv)[0], "wout": f(w_out)[0],
             "wff1": f(w_ff1)[0], "wff2": f(w_ff2)[0], "wpg": f(w_ple_gate)[0], "wple": f(w_ple)[0],
             "rope": _rope_tables(pos), "cst": cst_c, "tri": tri, "crow": np.ascontiguousarray(tri.sum(0)[None])}
        in_maps.append(m)
    if _NC_CACHE.get("prep_only"):
        return in_maps
    if "nc" not in _NC_CACHE:
        _NC_CACHE["nc"] = build_program()
    res = run_bass_kernel_spmd(_NC_CACHE["nc"], in_maps, core_ids=list(range(8)))
    out = np.zeros((4, SEQ, D), np.float32)
    for c in range(8):
        b, hf = c // 2, c % 2
        o = res.results[c]["out"]
        if hf == 0:
            out[b, :HALF] = o
        else:
            out[b, HALF:] = o[::-1]
    return out
```

```python
import os
import numpy as np
from contextlib import ExitStack
import concourse.bass as bass
import concourse.mybir as mybir
from concourse.bass_utils import run_bass_kernel_spmd

F32 = mybir.dt.float32
BF16 = mybir.dt.bfloat16
ALU = mybir.AluOpType
AF = mybir.ActivationFunctionType

SEM_EPOCH = 3000
DMA_SLOTS = 8

D = 1024; SEQ = 4096; HALF = 2048; NB_OWN = 16
TT = 256; NBT = 2; NTH = 8
LWS = -0.6065306597126334


class Op:
    __slots__ = ("eng", "fn", "reads", "writes", "dma", "deps", "flag", "cnt", "slot", "slotval", "idx", "xdeps",
                 "dur", "rg", "tbl", "seg", "alld", "pos")

    def __init__(self, eng, fn, reads, writes, dma):
        self.eng = eng; self.fn = fn; self.reads = reads; self.writes = writes; self.dma = dma
        self.deps = set(); self.flag = False; self.cnt = None; self.slot = None; self.slotval = None
        self.xdeps = (); self.dur = 0.2; self.rg = None; self.tbl = None; self.seg = 0


SYNC_LAT = 0.35


class Prog:
    def __init__(self):
        self.ops = []
        self.tags = []
        self.last_barrier = 0
        self.seg = 0
        self.reorder = True

    def add(self, eng, fn, r=(), w=(), dma=False, dur=0.2, rg=None, tbl=None):
        op = Op(eng, fn, tuple(r), tuple(w), dma)
        op.dur = dur; op.rg = rg; op.tbl = tbl; op.seg = self.seg
        self.tags.append(getattr(self, "tag", None)) if not getattr(self, "disabled", False) else None
        if getattr(self, "disabled", False):
            op.idx = -1
            return op
        op.idx = len(self.ops)
        self.ops.append(op)
        return op

    def all_deps(self):
        last_w = {}
        readers = {}
        ops = self.ops
        for op in ops:
            d = {}
            for j in op.xdeps:
                d[j] = "raw"
            for k in op.reads:
                j = last_w.get(k)
                if j is not None:
                    d[j] = "raw"
            for k in op.writes:
                j = last_w.get(k)
                if j is not None and j not in d:
                    d[j] = "waw"
                for j in readers.get(k, ()):
                    if j != op.idx and j not in d:
                        d[j] = "war"
            for k in op.reads:
                if isinstance(k, tuple) and k[0] in ("ps", "psT"):
                    for j in readers.get(k, ()):
                        if ops[j].eng != op.eng and j not in d:
                            d[j] = "raw"
            for k in op.reads:
                readers.setdefault(k, []).append(op.idx)
            for k in op.writes:
                last_w[k] = op.idx
                readers[k] = []
            op.alld = d

    def schedule(self):
        ops = self.ops
        n = len(ops)
        if not self.reorder:
            return list(range(n))
        fin = [0.0] * n
        self.fin = fin
        self.bind = {}; self._last_on = {}; self.start = {}
        succ = [[] for _ in range(n)]
        ndep = [0] * n
        for op in ops:
            for j in op.alld:
                succ[j].append(op.idx)
            ndep[op.idx] = len(op.alld)
        order = []
        free_t = {}
        cur_tbl = [None]
        segs = {}
        for op in ops:
            segs.setdefault(op.seg, []).append(op.idx)
        done = [False] * n
        ksegs = os.environ.get("KREORD")
        for sg in sorted(segs):
            members = segs[sg]
            if ksegs is not None and str(sg) not in ksegs.split(","):
                for i in members:
                    t0_ = max([free_t.get(ops[i].eng, 0.0)] + [fin[j] for j in ops[i].alld])
                    fin[i] = t0_ + ops[i].dur
                    free_t[ops[i].eng] = fin[i]
                    done[i] = True
                    order.append(i)
                continue
            inseg = set(members)
            ready = {}
            rt = {}
            remaining = len(members)
            cnt_un = {}
            for i in members:
                c = sum(1 for j in ops[i].alld if not done[j])
                cnt_un[i] = c
                if c == 0:
                    ready.setdefault(ops[i].eng, []).append(i)
            def ready_time(i):
                op = ops[i]
                t = 0.0
                for j, kind in op.alld.items():
                    p = ops[j]
                    lat = SYNC_LAT if (p.dma or p.eng != op.eng or kind == "raw") else 0.0
                    if p.eng == op.eng and op.eng == "pe":
                        lat = 0.0
                    t = max(t, fin[j] + lat)
                return t
            for e in ready:
                for i in ready[e]:
                    rt[i] = ready_time(i)
            while remaining:
                best = None
                for e, lst in ready.items():
                    if not lst:
                        continue
                    ft = free_t.get(e, 0.0)
                    bi = None
                    for i in lst:
                        st = max(ft, rt[i])
                        if e == "act" and ops[i].tbl is not None and cur_tbl[0] is not None and ops[i].tbl != cur_tbl[0]:
                            st += 1.3
                        key = (st, i)
                        if bi is None or key < bi:
                            bi = key
                    if best is None or bi < best[0]:
                        best = (bi, e)
                (st, i), e = best
                op = ops[i]
                ready[e].remove(i)
                if e == "act" and op.tbl is not None:
                    cur_tbl[0] = op.tbl
                fin[i] = st + op.dur
                if getattr(self, "trace_bind", False):
                    bj = None; bt_ = -1.0
                    for j in op.alld:
                        if fin[j] > bt_:
                            bt_ = fin[j]; bj = j
                    prev_e = self._last_on.get(e)
                    if prev_e is not None and free_t.get(e, 0.0) >= rt[i] - 1e-9:
                        self.bind[i] = ("eng", prev_e)
                    else:
                        self.bind[i] = ("dep", bj)
                    self._last_on[e] = i
                    self.start[i] = st
                if op.dma:
                    free_t[e] = st + 0.1
                else:
                    free_t[e] = fin[i]
                done[i] = True
                order.append(i)
                remaining -= 1
                for k in succ[i]:
                    if k in inseg:
                        cnt_un[k] -= 1
                        if cnt_un[k] == 0:
                            rt[k] = ready_time(k)
                            ready.setdefault(ops[k].eng, []).append(k)
        self.est_total = max(fin) if fin else 0.0
        self.seg_end = {sg: max(fin[i] for i in segs[sg]) for sg in segs}
        return order

    def analyse(self):
        self.all_deps()
        order = self.schedule()
        ops = self.ops
        for pos, i in enumerate(order):
            ops[i].pos = pos
        self.order = order
        pe = [i for i in order if ops[i].eng == "pe"]
        extra = {}
        for a in range(len(pe)):
            oa = ops[pe[a]]
            if oa.rg is None:
                continue
            for b in range(a + 1, a + 3):
                if b >= len(pe):
                    break
                ob = ops[pe[b]]
                if ob.rg is None:
                    continue
                if (oa.rg[1] < ob.rg[0] or ob.rg[1] < oa.rg[0]) and set(oa.writes) & set(ob.writes):
                    extra.setdefault(pe[b], set()).add(pe[a])
        for op in ops:
            need = set()
            for j, kind in op.alld.items():
                p = ops[j]
                if p.dma:
                    need.add(j)
                elif p.eng == op.eng and not op.dma:
                    if kind == "raw" and op.eng != "pe":
                        need.add(j); p.flag = True
                else:
                    need.add(j); p.flag = True
            for j in extra.get(op.idx, ()):
                need.add(j); ops[j].flag = True
            op.deps = need
        cnt = {}
        nd = {}
        for i in order:
            op = ops[i]
            if op.dma:
                k = nd.get(op.eng, 0)
                nd[op.eng] = k + 1
                op.slot = (op.eng, k % DMA_SLOTS)
                op.slotval = 16 * (k // DMA_SLOTS + 1)
            elif op.fn is not None and op.flag:
                cnt[op.eng] = cnt.get(op.eng, 0) + 1
                op.cnt = cnt[op.eng]
        self.nflag = cnt
        self.ndma = nd

    def emit(self, nc, stack):
        self.analyse()
        engs = ["pe", "act", "dve", "pool", "sp"]
        sems = {}
        for e in engs:
            n = self.nflag.get(e, 0)
            for ep in range(n // SEM_EPOCH + 1):
                sems[(e, ep)] = stack.enter_context(nc.semaphore(f"s_{e}_{ep}"))
        dsem = {}
        for e, n in self.ndma.items():
            for s in range(min(n, DMA_SLOTS)):
                dsem[(e, s)] = stack.enter_context(nc.semaphore(f"d_{e}_{s}"))
        block = stack.enter_context(nc.Block())
        ops = self.ops
        per_eng = {e: [ops[i] for i in self.order if ops[i].eng == e] for e in engs}

        def sem_of(p):
            if p.dma:
                return dsem[p.slot], p.slotval
            c = p.cnt - 1
            return sems[(p.eng, c // SEM_EPOCH)], (c % SEM_EPOCH) + 1

        def run(engobj, lst):
            seen = {}
            dma_hist = {}
            for op in lst:
                waits = {}
                for j in op.deps:
                    s, v = sem_of(ops[j])
                    key = id(s)
                    if seen.get(key, 0) >= v:
                        continue
                    if key not in waits or waits[key][1] < v:
                        waits[key] = (s, v)
                if op.dma:
                    prev = dma_hist.get(op.slot)
                    if prev is not None:
                        s, v = dsem[op.slot], prev
                        key = id(s)
                        if seen.get(key, 0) < v and (key not in waits or waits[key][1] < v):
                            waits[key] = (s, v)
                    dma_hist[op.slot] = op.slotval
                for key, (s, v) in waits.items():
                    engobj.wait_ge(s, v)
                    seen[key] = v
                if op.fn is None:
                    continue
                ins = op.fn(engobj)
                if op.dma:
                    ins.then_inc(dsem[op.slot], 16)
                elif op.flag:
                    s, v = sem_of(op)
                    ins.then_inc(s, 1)

        if per_eng["pe"]:
            @block.tensor
            def _(e):
                run(e, per_eng["pe"])
        if per_eng["act"]:
            @block.scalar
            def _(e):
                run(e, per_eng["act"])
        if per_eng["dve"]:
            @block.vector
            def _(e):
                run(e, per_eng["dve"])
        if per_eng["pool"]:
            @block.gpsimd
            def _(e):
                run(e, per_eng["pool"])
        if per_eng["sp"]:
            @block.sync
            def _(e):
                run(e, per_eng["sp"])


VC = {}
_o = 0
for _n, _c in [("mu", 14), ("k_k", 4), ("k_a", 4), ("rkA", 4), ("rkB", 4), ("lnw", 4), ("lnb", 4),
               ("a0A", 4), ("a0B", 4), ("qg", 1), ("kg", 1), ("nmix", 8), ("nffn", 8), ("nple", 8),
               ("sink", 4), ("omm", 14), ("hmu", 14), ("nkk", 4), ("omka", 4), ("esink", 4), ("ha0A", 4), ("ha0B", 4), ("hka", 4), ("c2ka", 4)]:
    VC[_n] = _o
    _o += _c
NVEC_IN = VC["omm"]
NVEC = _o

N_SW = 20


def build_program(dbg=None, upto=99, ntiles_other=None, ntiles_own=None):
    nc = bass.Bass("TRN2", target_bir_lowering=False)
    P = Prog()

    def din(name, shape):
        return nc.dram_tensor(name, list(shape), F32, kind="ExternalInput").ap()

    xs = din("xs", [SEQ + 2, D])
    pp = din("pp", [HALF, 256])
    wsw = din("wsw", [D, N_SW * 128])
    wgt = din("wgt", [D, 2048])
    vec_d = din("vec", [128, NVEC_IN])
    w2aug_d = din("w2aug", [2, 65, 512])
    a2_d = din("a2", [2, 64, 512])
    g2_d = din("g2", [128, 512])
    wua_d = din("wua", [512, D]); wur_d = din("wur", [512, D]); wout_d = din("wout", [D, D])
    wff1_d = din("wff1", [D, 4096]); wff2_d = din("wff2", [4096, D])
    wpg_d = din("wpg", [D, D]); wple_d = din("wple", [256, D])
    rope_d = din("rope", [128, 2, HALF + 512])
    cst_d = din("cst", [128, 12, 128])
    tri_d = din("tri", [128, 2, 128])
    crow_d = din("crow", [1, 2, 128])
    out_d = nc.dram_tensor("out", [HALF, D], F32, kind="ExternalOutput").ap()
    m0_s = nc.dram_tensor("m0_s", [2, 64, 128, 4, 128], BF16, kind="Internal").ap()
    n0_s = nc.dram_tensor("n0_s", [2, 64, 128, 4, 64], BF16, kind="Internal").ap()
    q_s = nc.dram_tensor("q_s", [2, 16, 128, 4, 128], BF16, kind="Internal").ap()
    x1_s = nc.dram_tensor("x1_s", [HALF, D], F32, kind="Internal").ap()
    y0_s = nc.dram_tensor("y0_s", [2, 128, 4, HALF], F32, kind="Internal").ap()
    ao_s = nc.dram_tensor("ao_s", [128, 4, HALF], BF16, kind="Internal").ap()
    bv_s = nc.dram_tensor("bv_s", [128, 4, HALF], BF16, kind="Internal").ap()
    sgl_s = nc.dram_tensor("sgl_s", [128, HALF], BF16, kind="Internal").ap()
    dbg_out = None
    if dbg is not None:
        dbg_out = nc.dram_tensor("dbg", [128, 16384], F32, kind="ExternalOutput").ap()

    def fsz(ap):
        n = 1
        for d_ in ap.shape[1:]:
            n *= d_
        return n

    def rgof(ap):
        b = ap.base_partition()
        return (b // 32, (b + ap.shape[0] - 1) // 32)

    def MM(out, lhsT, rhs, start=True, stop=True, r=(), w=()):
        dur = max(0.064, 0.02 + fsz(out) / 2000.0) * (4 if lhsT.dtype == F32 else 1)
        P.add("pe", lambda e: e.matmul(out, lhsT, rhs, start=start, stop=stop), r, w, dur=dur, rg=rgof(lhsT))

    def TR(out, in_, idn, r=(), w=()):
        P.add("pe", lambda e: e.transpose(out, in_, idn), r, w, dur=0.064, rg=rgof(in_))

    def ACT(out, in_, func, bias=None, scale=None, accum=None, r=(), w=()):
        kw = {}
        if bias is not None:
            kw["bias"] = bias
        if scale is not None:
            kw["scale"] = scale
        if accum is not None:
            kw["accum_out"] = accum
        tbl = "ln" if func == AF.Ln else ("tanh" if func == AF.Tanh else None)
        P.add("act", lambda e: e.activation(out=out, in_=in_, func=func, **kw), r, w, dur=0.2 + fsz(out) / 1400.0, tbl=tbl)

    def vdur(eng, out):
        return (0.1 + fsz(out) / 960.0) if eng == "dve" else (0.3 + fsz(out) / 500.0)

    def TTo(eng, out, a, b, op, r=(), w=()):
        P.add(eng, lambda e: e.tensor_tensor(out, a, b, op), r, w, dur=vdur(eng, out))

    def TS(eng, out, a, s1, s2, op0, op1=None, r=(), w=()):
        if op1 is None:
            P.add(eng, lambda e: e.tensor_scalar(out, a, s1, None, op0), r, w, dur=vdur(eng, out))
        else:
            P.add(eng, lambda e: e.tensor_scalar(out, a, s1, s2, op0, op1), r, w, dur=vdur(eng, out))

    def STT(eng, out, in0, sc, in1, op0, op1, r=(), w=()):
        P.add(eng, lambda e: e.scalar_tensor_tensor(out, in0, sc, in1, op0, op1), r, w, dur=vdur(eng, out))

    def CP(eng, out, in_, r=(), w=()):
        P.add(eng, lambda e: e.tensor_copy(out, in_), r, w, dur=vdur(eng, out))

    def RCP(out, in_, r=(), w=()):
        P.add("dve", lambda e: e.reciprocal(out, in_), r, w, dur=vdur("dve", out) * 2)

    def MEMSET(eng, ap, val, w=()):
        P.add(eng, lambda e: e.memset(ap, val), (), w, dur=vdur(eng, ap))

    def DMA(eng, out, in_, r=(), w=()):
        nbytes = fsz(out) * out.shape[0] * (4 if out.dtype == F32 else 2)
        P.add(eng, lambda e: e.dma_start(out=out, in_=in_), r, w, dma=True, dur=2.0 + nbytes / 100000.0)

    with ExitStack() as gs:
        def sb(name, shape, dt=F32, st=None):
            return (st or gs).enter_context(nc.sbuf_tensor("sb_" + name, list(shape), dt))

        psA = [gs.enter_context(nc.psum_tensor(f"psA{i}", [128, 512], F32)) for i in range(6)]
        psT = [gs.enter_context(nc.psum_tensor(f"psT{i}", [128, 1024], BF16)) for i in range(2)]
        psTf = [psT[i][:, :].bitcast(F32) for i in range(2)]
        pk = lambda i: ("ps", i)
        pkT = lambda i: ("psT", i)

        vec = sb("vec", [128, NVEC])
        cstb = sb("cstb", [128, 12, 128], BF16)
        cstf = sb("cstf", [128, 3, 128])
        crow = sb("crow", [1, 2, 128])
        tri = sb("tri", [128, 2, 128])
        eps_c = sb("eps_c", [128, 4])
        junk = sb("junk", [128, D], BF16)
        DMA("sp", vec[:, 0:NVEC_IN], vec_d, w=["vec"])
        DMA("pool", cstb[:], cst_d, w=["cstb"])
        DMA("sp", cstf[:, 0, :], cst_d[:, 0, :], w=["cstf"])
        DMA("sp", cstf[:, 1, :], cst_d[:, 9, :], w=["cstf"])
        DMA("sp", cstf[:, 2, :], cst_d[:, 8, :], w=["cstf"])
        DMA("sp", crow[:], crow_d, w=["crow"])
        DMA("sp", tri[:], tri_d, w=["tri"])
        MEMSET("dve", eps_c[:, 0:1], 1e-6, w=["eps"])
        MEMSET("dve", eps_c[:, 1:2], 1e-24, w=["eps"])
        MEMSET("dve", eps_c[:, 2:3], 64e-5, w=["eps"])
        ident = cstb[:, 0, :]; onesblk = cstb[:, 1, :]; perm = cstb[:, 2, :]; bdm = cstb[:, 3, :]
        identf = cstf[:, 0, :]
        vc = lambda n, i=0: vec[:, VC[n] + i:VC[n] + i + 1]
        TS("dve", vec[:, VC["omm"]:VC["omm"] + 14], vec[:, VC["mu"]:VC["mu"] + 14], -1.0, 1.0, ALU.mult, ALU.add, r=["vec"], w=["vec2"])
        TS("dve", vec[:, VC["hmu"]:VC["hmu"] + 14], vec[:, VC["mu"]:VC["mu"] + 14], 0.5, None, ALU.mult, r=["vec"], w=["vec2"])
        TS("dve", vec[:, VC["nkk"]:VC["nkk"] + 4], vec[:, VC["k_k"]:VC["k_k"] + 4], -1.0, None, ALU.mult, r=["vec"], w=["vec2"])
        TS("dve", vec[:, VC["omka"]:VC["omka"] + 4], vec[:, VC["k_a"]:VC["k_a"] + 4], -1.0, 1.0, ALU.mult, ALU.add, r=["vec"], w=["vec2"])
        ACT(vec[:, VC["esink"]:VC["esink"] + 4], vec[:, VC["sink"]:VC["sink"] + 4], AF.Exp, r=["vec"], w=["vec2"])
        TS("dve", vec[:, VC["ha0A"]:VC["ha0A"] + 8], vec[:, VC["a0A"]:VC["a0A"] + 8], 0.5, None, ALU.mult, r=["vec"], w=["vec2"])
        TS("dve", vec[:, VC["hka"]:VC["hka"] + 4], vec[:, VC["k_a"]:VC["k_a"] + 4], 0.5, None, ALU.mult, r=["vec"], w=["vec2"])
        TS("dve", vec[:, VC["c2ka"]:VC["c2ka"] + 4], vec[:, VC["k_a"]:VC["k_a"] + 4], -0.5, 1.0, ALU.mult, ALU.add, r=["vec"], w=["vec2"])
        VK = ["vec", "vec2", "eps", "cstb", "cstf", "tri"]

        def ck(name):
            if os.environ.get("KSTOP") == name:
                P.disabled = True
        if os.environ.get("KSTOP") == "c":
            P.disabled = True

        bar_t = {e: sb(f"bar_{e}", [1, 8]) for e in ["act", "dve", "pool"]}

        def RSQ(dst, src, scale, bias_ap, n, r=(), w=()):
            ACT(dst, src, AF.Ln, bias=bias_ap, scale=scale, r=list(r) + ["eps"], w=list(w))
            ACT(dst, dst, AF.Exp, scale=-0.5, r=list(w), w=list(w))

        def barrier():
            if getattr(P, "disabled", False):
                return
            P.seg += 1
            n = len(P.ops)
            pend = [op.idx for op in P.ops[P.last_barrier:n]]
            marks = []
            for e in ["act", "dve", "pool"]:
                t = bar_t[e]
                marks.append(P.add(e, (lambda tt, ee: (lambda en: en.memzero(tt[:]) if ee == 'act' else en.memset(tt[:], 0.0)))(t, e), (), [f"bar_{e}_{n}"]).idx)
            marks.append(P.add("pe", lambda e: e.matmul(psA[0][0:1, 0:1], cstb[0:1, 8, 0:1], cstb[0:1, 8, 0:1], start=True, stop=True),
                               ["cstb", pk(0)], [pk(0), f"bar_pe_{n}"]).idx)
            for e in ["pe", "act", "dve", "pool", "sp"]:
                f = P.add(e, None, (), ())
                f.xdeps = tuple(marks) + tuple(i for i in pend if P.ops[i].dma)
            P.last_barrier = len(P.ops)
            P.seg += 1

        def make_hT(st_key, x_aps, nrows, hT, gain, wk, xt, hb, ssq, rstd):
            nblk = len(x_aps)
            for bi, (xap, rows, col0) in enumerate(x_aps):
                s = bi % xt.shape[1]
                DMA("sp", xt[0:rows, s, :], xap, w=[("xt", s)])
                ACT(junk[0:rows, :], xt[0:rows, s, :], AF.Square, accum=ssq[0:rows, s:s + 1], r=[("xt", s)], w=["junk", ("ssq", s)])
                RSQ(rstd[0:rows, s:s + 1], ssq[0:rows, s:s + 1], 1.0 / D, eps_c[0:rows, 0:1], 1, r=[("ssq", s)], w=[("rstd", s)])
                TS("dve", hb[0:rows, 0, :], xt[0:rows, s, :], rstd[0:rows, s:s + 1], None, ALU.mult, r=[("xt", s), ("rstd", s)], w=["hbo"])
                for kc in range(8):
                    TR(psT[kc % 2][:, (kc // 2) * 128:(kc // 2) * 128 + rows], hb[0:rows, 0, kc * 128:(kc + 1) * 128], ident[0:rows, 0:rows],
                       r=["hbo", "cstb"], w=[pkT(kc % 2)])
                for kc in range(8):
                    eng_act = (kc % 2 == 0)
                    src = psT[kc % 2][:, (kc // 2) * 128:(kc // 2) * 128 + rows]
                    dst = hT[:, kc, col0:col0 + rows]
                    if eng_act:
                        ACT(dst, src, AF.Copy, scale=vc(gain, kc), r=[pkT(kc % 2), "vec"], w=[(wk, kc)])
                    else:
                        TS("dve", dst, src, vc(gain, kc), None, ALU.mult, r=[pkT(kc % 2), "vec"], w=[(wk, kc)])

        g2t = sb("g2t", [128, 512], BF16)
        PCs = sb("PCs", [128, 2, 64, 4], F32)
        with ExitStack() as s1:
            win = sb("win", [128, 8, N_SW * 128], BF16, s1)
            cstm = sb("cstm", [128, 4, 4, 128], BF16, s1)
            for _m in range(4):
                for _r in range(4):
                    DMA("pool", cstm[:, _m, _r, :], cst_d[:, 4 + _m, :], w=["cstm"])
            for (c0_, c1_) in ((768, 1664), (1664, 2560), (0, 768)):
                for kc in range(8):
                    DMA("pool", win[:, kc, c0_:c1_], wsw[kc * 128:(kc + 1) * 128, c0_:c1_], w=[("win", c0_)])
            w2aug = sb("w2aug", [65, 2, 512], BF16, s1)
            a2t = sb("a2t", [128, 2, 512], BF16, s1)
            rope = sb("rope", [128, 2, HALF + 256], BF16, s1)
            for d in range(2):
                DMA("pool", w2aug[:, d, :], w2aug_d[d], w=["w2aug"])
                DMA("pool", a2t[64:128, d, :], a2_d[d], w=["a2t"])
            DMA("pool", g2t[:], g2_d, w=["g2t"])
            DMA("pool", rope[:], rope_d[:, :, 0:HALF + 256], w=["rope"])
            if os.environ.get("KSTOP") == "w":
                P.disabled = True
            kT = sb("kT", [128, HALF + 128], BF16, s1)
            vaug = sb("vaug", [128, NB_OWN + 1, 2, 64], BF16, s1)
            vones = sb("vones", [128, 64], BF16, s1)
            attn_o = sb("attn_o", [128, 4, TT], BF16, s1)
            sgl = sb("sgl", [128, TT], BF16, s1)
            bv = sb("bv", [128, 4, TT], BF16, s1)
            MEMSET("pool", vones[:], 1.0, w=["vaug1"])
            xt = sb("xt", [128, 1, D], F32, s1); hb = sb("hb", [128, 1, D], BF16, s1)
            ssq = sb("ssq", [128, 2], F32, s1); rstd = sb("rstd", [128, 2], F32, s1)
            hT = sb("hT", [128, 8, TT + 2], BF16, s1)
            zr = sb("zr", [128, 2, TT + 2], F32, s1)
            zs = sb("zs", [128, 2, TT], F32, s1)
            ztmp = sb("ztmp", [128, 2, TT], F32, s1)
            zq = sb("zq", [128, TT], F32, s1)
            qsq = sb("qsq", [128, TT], BF16, s1)
            qrs = sb("qrs", [128, TT], F32, s1)
            qn = sb("qn", [128, TT], BF16, s1)
            qt1 = sb("qt1", [128, TT], F32, s1); qt2 = sb("qt2", [128, TT], F32, s1)
            qT = sb("qT", [128, 2, 4, TT], BF16, s1)
            vb = sb("vb", [128, TT], BF16, s1)
            twl = sb("twl", [65, TT], BF16, s1); alx = sb("alx", [128, TT], BF16, s1)
            MEMSET("pool", twl[64:65, :], 1.0, w=["twl1"])
            if os.environ.get("KSTOP") == "m":
                P.disabled = True
            sgfcD = [sb(f"sgfc{i}", [128, NBT, 128], F32, s1) for i in range(2)]
            r_f = sb("r_f", [128, TT], F32, s1); k_f = sb("k_f", [128, TT], F32, s1); v_f = sb("v_f", [128, TT], F32, s1)
            nkk = sb("nkk", [128, TT], F32, s1)
            a_fD = [sb(f"a_f{i}", [128, TT], F32, s1) for i in range(2)]; b_fD = [sb(f"b_f{i}", [128, TT], F32, s1) for i in range(2)]
            kd_fD = [sb(f"kd_f{i}", [128, TT], F32, s1) for i in range(2)]; t1_fD = [sb(f"t1_f{i}", [128, TT], F32, s1) for i in range(2)]
            pbD = [sb(f"pb{i}", [128, TT], BF16, s1) for i in range(2)]
            E_fD = [sb(f"E_f{i}", [128, TT], F32, s1) for i in range(2)]; G_fD = [sb(f"G_f{i}", [128, TT], F32, s1) for i in range(2)]
            AR = sb("AR", [128, 2, 4, NBT, 256], BF16, s1)
            bt = sb("bt", [128, 2, 4, TT], BF16, s1); kt = sb("kt", [128, 2, 4, TT], BF16, s1)
            btT = sb("btT", [128, 2, NBT, 512], BF16, s1); ktT = sb("ktT", [128, 2, NBT, 512], BF16, s1)
            atT = sb("atT", [128, 2, NBT, 512], BF16, s1)
            vT = sb("vT", [128, NBT, 512], BF16, s1)
            Aab = sb("Aab", [128, 8, 128], BF16, s1); AabT = sb("AabT", [128, 8, 128], BF16, s1)
            Aak = sb("Aak", [128, 8, 128], BF16, s1); Arb = sb("Arb", [128, 8, 128], BF16, s1); Ark = sb("Ark", [128, 8, 128], BF16, s1)
            ApI = [[sb(f"Ap{j}_{i}", [128, 8, 128], BF16, s1) for i in range(2)] for j in range(2)]
            ATpI = [[sb(f"ATp{j}_{i}", [128, 8, 128], BF16, s1) for i in range(2)] for j in range(2)]
            ZpI = [[sb(f"Zp{j}_{i}", [128, 4, 2, 2, 64], BF16, s1) for i in range(2)] for j in range(2)]
            M0stI = [sb(f"M0st{j}", [128, 2, 4, 128], BF16, s1) for j in range(2)]
            N0stI = [sb(f"N0st{j}", [128, 2, 4, 64], BF16, s1) for j in range(2)]
            QstI = [sb(f"Qst{j}", [128, 4, 128], BF16, s1) for j in range(2)]
            Y0stI = [sb(f"Y0st{j}", [128, 4, 128], F32, s1) for j in range(2)]
            hlcount = [0]
            probs = sb("probs", [128, 2, 384], BF16, s1)
            den = sb("den", [128, 128], F32, s1)

            def proj_chunk(ci, bank, halo_bank=None):
                if halo_bank is None:
                    for kc in range(8):
                        MM(psA[bank][:, 0:TT], win[:, kc, ci * 128:(ci + 1) * 128], hT[:, kc, 1:TT + 1], start=(kc == 0), stop=(kc == 7),
                           r=[("win", 0 if ci < 6 else (768 if ci < 13 else 1664)), ("hT", kc)], w=[pk(bank)])
                else:
                    for kc in range(8):
                        MM(psA[bank][:, 0:TT + 2], win[:, kc, ci * 128:(ci + 1) * 128], hT[:, kc, 0:TT + 2], start=(kc == 0), stop=(kc == 7),
                           r=[("win", 0 if ci < 6 else (768 if ci < 13 else 1664)), ("hT", kc)], w=[pk(bank)])

            zcount = [0]
            mbc = [0]

            def nb():
                mbc[0] += 1
                return 3 + (mbc[0] % 3)

            def rwkv_z(ci, mui, out_ap, wkey, post=None):
                s = zcount[0] % 2
                zcount[0] += 1
                bank = s
                proj_chunk(ci, bank, 2)
                ACT(zr[:, s, :], psA[bank][:, 0:TT + 2], AF.Copy, r=[pk(bank)], w=[("zr", s)])
                ACT(zs[:, s, :], psA[bank][:, 1:TT + 1], AF.Copy, scale=vc("omm", mui), r=[pk(bank), "vec2"], w=[("zs", s)])
                TTo("pool", ztmp[:, s, :], zr[:, s, 0:TT], zr[:, s, 2:TT + 2], ALU.add, r=[("zr", s)], w=[("ztmp", s)])
                STT("dve", out_ap, ztmp[:, s, :], vc("hmu", mui), zs[:, s, :], ALU.mult, ALU.add, r=[("ztmp", s), ("zs", s), "vec2"], w=[wkey])

            def sweep_tile(t0, own, first_other, pcidx0):
                dirs = [0, 1] if own else [1]
                blk0 = t0 // 128
                xaps = [(xs[t0 + 128 * i: t0 + 128 * i + 128, :], 128, 128 * i) for i in range(NBT)]
                xaps.append((xs[t0 + TT: t0 + TT + 2, :], 2, TT))
                P.tag = f"t{t0}-hT"
                make_hT("hT", xaps, None, hT, "nmix", "hT", xt, hb, ssq, rstd)
                ck("h")
                P.tag = f"t{t0}-qkv"
                qs = (t0 // TT) % 2
                if own or first_other:
                    for ci in ([0, 1, 2, 3, 4] if own else [4]):
                        bank = ci % 2
                        proj_chunk(ci, bank)
                        gcol = "kg" if ci == 4 else "qg"
                        ACT(qsq[:], psA[bank][:, 0:TT], AF.Square, r=[pk(bank)], w=["qsq"])
                        b1 = nb()
                        MM(psA[b1][:, 0:TT], onesblk, qsq[:], r=["cstb", "qsq"], w=[pk(b1)])
                        RSQ(qrs[:], psA[b1][:, 0:TT], 1.0 / 64, eps_c[:, 0:1], TT, r=[pk(b1)], w=["qrs"])
                        STT("dve", qn[:], psA[bank][:, 0:TT], vc(gcol), qrs[:], ALU.mult, ALU.mult, r=[pk(bank), "qrs", "vec"], w=["qn"])
                        b2 = nb()
                        MM(psA[b2][:, 0:TT], perm, qn[:], r=["cstb", "qn"], w=[pk(b2)])
                        TTo("pool", qt1[:], qn[:], rope[:, 0, t0:t0 + TT], ALU.mult, r=["qn", "rope"], w=["qt1"])
                        TTo("dve", qt2[:], psA[b2][:, 0:TT], rope[:, 1, t0:t0 + TT], ALU.mult, r=[pk(b2), "rope"], w=["qt2"])
                        if ci < 4:
                            TTo("pool", qT[:, qs, ci, :], qt1[:], qt2[:], ALU.add, r=["qt1", "qt2"], w=[("qT", qs)])
                        else:
                            n = TT if own else 128
                            TTo("pool", kT[:, t0:t0 + n], qt1[:, 0:n], qt2[:, 0:n], ALU.add, r=["qt1", "qt2"], w=["kT"])
                    proj_chunk(5, 0)
                    ACT(vb[:], psA[0][:, 0:TT], AF.Copy, r=[pk(0)], w=["vb"])
                    nbv = NBT if own else 1
                    for bi in range(nbv):
                        TR(psT[0][:, bi * 128:(bi + 1) * 128], vb[:, bi * 128:(bi + 1) * 128], ident, r=["vb", "cstb"], w=[pkT(0)])
                    for bi in range(nbv):
                        CP("dve", vaug[:, blk0 + bi, :, :], psT[0][:, bi * 128:(bi + 1) * 128].rearrange("p (g d) -> p g d", g=2),
                           r=[pkT(0)], w=["vaug"])
                P.tag = f"t{t0}-wa"
                ck("q")
                rwkv_z(6, 0, zq[:], "zq")
                ACT(twl[0:64, :], zq[0:64, :], AF.Tanh, r=["zq"], w=["twl"])
                CP("pool", alx[64:128, :], zq[64:128, :], r=["zq"], w=["alx"])
                ck("z")
                if own:
                    rwkv_z(7, 1, zq[:], "zq")
                    ACT(zq[:], zq[:], AF.Tanh, scale=0.5, r=["zq"], w=["zq"])
                    TS("pool", sgl[:, :], zq[:], 0.5, 0.5, ALU.mult, ALU.add, r=["zq"], w=["sgl"])
                    DMA("sp", sgl_s[:, t0:t0 + TT], sgl[:, :], r=["sgl"], w=[("sgl_s", t0 // TT)])
                for fc in range(4):
                    P.tag = f"t{t0}-fc{fc}"
                    if own:
                        rwkv_z(8 + 3 * fc, 2 + 3 * fc, r_f[:], "r_f")
                    rwkv_z(9 + 3 * fc, 3 + 3 * fc, k_f[:], "k_f")
                    rwkv_z(10 + 3 * fc, 4 + 3 * fc, v_f[:], "v_f")
                    ACT(qsq[:], k_f[:], AF.Square, scale=vc("k_k", fc), r=["k_f", "vec"], w=["qsq"])
                    b1 = nb()
                    MM(psA[b1][:, 0:TT], onesblk, qsq[:], r=["cstb", "qsq"], w=[pk(b1)])
                    RSQ(qrs[:], psA[b1][:, 0:TT], 1.0, eps_c[:, 1:2], TT, r=[pk(b1)], w=["qrs"])
                    STT("dve", nkk[:], k_f[:], vc("nkk", fc), qrs[:], ALU.mult, ALU.mult, r=["k_f", "qrs", "vec2"], w=["nkk"])
                    ck("k")
                    CP("pool", vb[:], v_f[:], r=["v_f"], w=["vb"])
                    for bi in range(NBT):
                        TR(psT[0][:, bi * 128:(bi + 1) * 128], vb[:, bi * 128:(bi + 1) * 128], ident, r=["vb", "cstb"], w=[pkT(0)])
                    CP("dve", vT[:, :, fc * 128:(fc + 1) * 128], psT[0][:, 0:NBT * 128].rearrange("p (b c) -> p b c", b=NBT), r=[pkT(0)], w=[("vT", fc)])
                    for d in dirs:
                        dn = "AB"[d]
                        a_f, b_f, kd_f, t1_f, pb, E_f, G_f, sgfc = a_fD[d], b_fD[d], kd_fD[d], t1_fD[d], pbD[d], E_fD[d], G_fD[d], sgfcD[d]
                        b1 = nb()
                        MM(psA[b1][:, 0:TT], a2t[64:128, d, fc * 128:(fc + 1) * 128], alx[64:128, :], r=["a2t", "alx"], w=[pk(b1)])
                        ACT(a_f[:], psA[b1][:, 0:TT], AF.Tanh, bias=vc("ha0" + dn, fc), scale=0.5, r=[pk(b1), "vec2"], w=[("a_f", d)])
                        STT("dve", b_f[:], a_f[:], 1.0, nkk[:], ALU.add, ALU.mult, r=["nkk", ("a_f", d)], w=[("b_f", d)])
                        TS("dve", t1_f[:], a_f[:], vc("hka", fc), vc("c2ka", fc), ALU.mult, ALU.add, r=[("a_f", d), "vec2"], w=[("t1_f", d)])
                        TTo("pool", kd_f[:], k_f[:], t1_f[:], ALU.mult, r=["k_f", ("t1_f", d)], w=[("kd_f", d)])
                        if own:
                            STT("dve", pb[:], r_f[:], vc("rk" + dn, fc), kd_f[:], ALU.mult, ALU.mult, r=["r_f", ("kd_f", d), "vec"], w=[("pb", d)])
                            MM(psA[2][:, 0:TT], onesblk, pb[:], start=(d == 0), stop=(d == 1), r=["cstb", ("pb", d)], w=[pk(2)])
                        b2 = nb()
                        for bi in range(NBT):
                            MM(psA[b2][:, bi * 128:(bi + 1) * 128], twl[:, bi * 128:(bi + 1) * 128], w2aug[:, d, fc * 128:(fc + 1) * 128],
                               r=["twl", "twl1", "w2aug"], w=[pk(b2)])
                        ACT(sgfc[:, :, :], psA[b2][:, 0:NBT * 128].rearrange("p (b c) -> p b c", b=NBT), AF.Tanh, scale=0.5, r=[pk(b2)], w=[("sgfc", d)])
                        b3 = nb()
                        for bi in range(NBT):
                            MM(psA[b3][:, bi * 128:(bi + 1) * 128], sgfc[:, bi, :], tri[:, d, :], start=True, stop=False, r=[("sgfc", d), "tri"], w=[pk(b3)])
                            MM(psA[b3][:, bi * 128:(bi + 1) * 128], cstf[0:1, 2, :], crow[0:1, d, :], start=False, stop=True, r=["cstf", "crow"], w=[pk(b3)])
                        ACT(E_f[:], psA[b3][:, 0:TT], AF.Exp, r=[pk(b3)], w=[("E_f", d)])
                        ACT(G_f[:], psA[b3][:, 0:TT], AF.Exp, scale=-1.0, r=[pk(b3)], w=[("G_f", d)])
                        Ev = E_f[:].rearrange("p (c t) -> p c t", t=64)
                        pcol = 63 if d == 0 else 0
                        CP("pool", PCs[:, d, pcidx0:pcidx0 + TT // 64, fc], Ev[:, :, pcol], r=[("E_f", d)], w=["PCs"])
                        ARv = AR[:, d, fc, :, :]
                        if own:
                            TTo("dve", ARv[:, :, 128:256], r_f[:].rearrange("p (b t) -> p b t", t=128), E_f[:].rearrange("p (b t) -> p b t", t=128),
                                ALU.mult, r=["r_f", ("E_f", d)], w=[("AR", d, fc)])
                        ARa = AR[:, d, fc, :, 0:128].rearrange("p b (c t) -> p b c t", t=64)
                        nk4 = nkk[:].rearrange("p (b c t) -> p b c t", c=2, t=64)
                        E4 = E_f[:].rearrange("p (b c t) -> p b c t", c=2, t=64)
                        if d == 0:
                            TTo("pool", ARa[:, :, :, 1:64], nk4[:, :, :, 1:64], E4[:, :, :, 0:63], ALU.mult, r=["nkk", ("E_f", d)], w=[("AR", d, fc)])
                            CP("pool", ARa[:, :, :, 0:1], nk4[:, :, :, 0:1], r=["nkk"], w=[("AR", d, fc)])
                        else:
                            TTo("pool", ARa[:, :, :, 0:63], nk4[:, :, :, 0:63], E4[:, :, :, 1:64], ALU.mult, r=["nkk", ("E_f", d)], w=[("AR", d, fc)])
                            CP("pool", ARa[:, :, :, 63:64], nk4[:, :, :, 63:64], r=["nkk"], w=[("AR", d, fc)])
                        STT("dve", bt[:, d, fc, :], b_f[:], -0.5, G_f[:], ALU.mult, ALU.mult, r=[("b_f", d), ("G_f", d)], w=[("bt", d, fc)])
                        TTo("pool", kt[:, d, fc, :], kd_f[:], G_f[:], ALU.mult, r=[("kd_f", d), ("G_f", d)], w=[("kt", d, fc)])
                        for which, (srck, dst, dkey) in enumerate(((("bt", d, fc), btT, "btT"), (("kt", d, fc), ktT, "ktT"), (("AR", d, fc), atT, "atT"))):
                            pb_i = which % 2
                            for bi in range(NBT):
                                if which == 0:
                                    src = bt[:, d, fc, bi * 128:(bi + 1) * 128]
                                elif which == 1:
                                    src = kt[:, d, fc, bi * 128:(bi + 1) * 128]
                                else:
                                    src = AR[:, d, fc, bi, 0:128]
                                TR(psT[pb_i][:, bi * 128:(bi + 1) * 128], src, ident, r=[srck, "cstb"], w=[pkT(pb_i)])
                            pv = psT[pb_i][:, 0:NBT * 128].rearrange("p (b c) -> p b c", b=NBT)
                            if which != 1:
                                CP("dve", dst[:, d, :, fc * 128:(fc + 1) * 128], pv, r=[pkT(pb_i)], w=[(dkey, d, fc)])
                            else:
                                ACT(dst[:, d, :, fc * 128:(fc + 1) * 128], pv, AF.Copy, r=[pkT(pb_i)], w=[(dkey, d, fc)])
                    if own:
                        TTo("dve", bv[:, fc, :], psA[2][:, 0:TT], v_f[:], ALU.mult, r=[pk(2), "v_f"], w=["bv"])
                        DMA("sp", bv_s[:, fc, t0:t0 + TT], bv[:, fc, :], r=["bv"], w=[("bv_s", t0 // TT, fc)])
                ck("f")
                for d in dirs:
                    ms, mi = (4, 5) if d == 0 else (6, 7)
                    for bi in range(NBT):
                        P.tag = f"t{t0}-hl{d}{bi}"
                        ins_ = hlcount[0] % 2
                        hlcount[0] += 1
                        Ap, ATp, Zp = ApI[ins_], ATpI[ins_], ZpI[ins_]
                        B = (lambda p_: (lambda b_: 3 * p_ + (b_ % 3)))(ins_)
                        M0st, N0st, Qst, Y0st = M0stI[ins_], N0stI[ins_], QstI[ins_], Y0stI[ins_]
                        gblk = (t0 // 128 + bi) if own else None
                        def amat(bank0, lh, rh, keys):
                            for h in range(8):
                                fc = h // 2; rows = slice((h % 2) * 64, (h % 2) * 64 + 64)
                                MM(psA[B(bank0 + h % 2)][:, (h // 2) * 128:(h // 2) * 128 + 128], lh(rows, fc), rh(rows, fc),
                                   r=[(k_, d, fc) for k_ in keys], w=[pk(B(bank0 + h % 2))])

                        def aevac(bank0, dst, dkey, mslot):
                            dv = dst[:].rearrange("p (i two) t -> p two i t", two=2)
                            for par in range(2):
                                TTo("dve", dv[:, par, :, :], psA[B(bank0 + par)][:, :].rearrange("p (i t) -> p i t", i=4), cstm[:, mslot - 4, :, :], ALU.mult,
                                    r=[pk(B(bank0 + par)), "cstm"], w=[dkey])
                        tokc = slice(bi * 128, (bi + 1) * 128)
                        amat(0, lambda rows, fc: bt[rows, d, fc, tokc], lambda rows, fc: AR[rows, d, fc, bi, 0:128], ["bt", "AR"])
                        aevac(0, Aab, "Aab", ms)
                        amat(4, lambda rows, fc: AR[rows, d, fc, bi, 0:128], lambda rows, fc: bt[rows, d, fc, tokc], ["bt", "AR"])
                        aevac(4, AabT, "AabT", (6 if d == 0 else 4))
                        amat(2, lambda rows, fc: kt[rows, d, fc, tokc], lambda rows, fc: AR[rows, d, fc, bi, 0:128], ["kt", "AR"])
                        aevac(2, Aak, "Aak", ms)
                        ck("a")
                        Z = Zp[0]
                        CP("pool", Z[:, :, 0, :, :], atT[:, d, bi, :].rearrange("p (f h j) -> p f h j", f=4, h=2), r=[("atT", d, f_) for f_ in range(4)], w=[("Z", ins_, 0, f_) for f_ in range(4)])
                        for h in range(8):
                            MM(psA[B(3)][:, h * 64:(h + 1) * 64], Aak[:, h, :], vT[:, bi, h * 64:(h + 1) * 64], r=["Aak", ("vT", h // 2)], w=[pk(B(3))])
                        CP("dve", Z[:, :, 1, :, :], psA[B(3)][:, 0:512].rearrange("p (f h j) -> p f h j", f=4, h=2), r=[pk(B(3))], w=[("Z", ins_, 0, f_) for f_ in range(4)])
                        curA, curAT, ak, atk = Aab, AabT, ["Aab", "Aab"], ["AabT", "AabT"]
                        zi = 0
                        for lev in range(6):
                            Zc, Zn = Zp[zi], Zp[1 - zi]
                            for hh in range(2):
                                bank = 4 + hh
                                for q4 in range(4):
                                    h = 4 * hh + q4; fc = h // 2; h2 = h % 2
                                    o = psA[B(bank)][:, q4 * 128:(q4 + 1) * 128].rearrange("p (a j) -> p a j", a=2)
                                    MM(o, curA[:, h, :], Zc[:, fc, :, h2, :], start=True, stop=False, r=[ak[hh], ("Z", ins_, zi, fc)], w=[pk(B(bank))])
                                    MM(o, ident, Zc[:, fc, :, h2, :], start=False, stop=True, r=["cstb", ("Z", ins_, zi, fc)], w=[pk(B(bank))])
                                for f in range(2):
                                    src = psA[B(bank)][:, f * 256:(f + 1) * 256].rearrange("p (h a j) -> p a h j", h=2, a=2)
                                    if hh == 0:
                                        ACT(Zn[:, 2 * hh + f, :, :, :], src, AF.Copy, r=[pk(B(bank))], w=[("Z", ins_, 1 - zi, 2 * hh + f)])
                                    else:
                                        CP("dve", Zn[:, 2 * hh + f, :, :, :], src, r=[pk(B(bank))], w=[("Z", ins_, 1 - zi, 2 * hh + f)])
                            zi = 1 - zi
                            if lev < 5:
                                nA, nAT = Ap[lev % 2], ATp[lev % 2]
                                nak = [("Ap", ins_, lev % 2, 0), ("Ap", ins_, lev % 2, 1)]
                                natk = [("ATp", ins_, lev % 2, 0), ("ATp", ins_, lev % 2, 1)]
                                for hh in range(2):
                                    for q4 in range(4):
                                        h = 4 * hh + q4
                                        MM(psA[B(0 + hh)][:, q4 * 128:(q4 + 1) * 128], curAT[:, h, :], curA[:, h, :], r=[ak[hh], atk[hh]], w=[pk(B(0 + hh))])
                                        MM(psA[B(2 + hh)][:, q4 * 128:(q4 + 1) * 128], curA[:, h, :], curAT[:, h, :], r=[ak[hh], atk[hh]], w=[pk(B(2 + hh))])
                                    ACT(nA[:, 4 * hh:4 * hh + 4, :], psA[B(0 + hh)][:, :].rearrange("p (h t) -> p h t", h=4), AF.Copy, r=[pk(B(0 + hh))], w=[nak[hh]])
                                    CP("dve", nAT[:, 4 * hh:4 * hh + 4, :], psA[B(2 + hh)][:, :].rearrange("p (h t) -> p h t", h=4), r=[pk(B(2 + hh))], w=[natk[hh]])
                                curA, curAT, ak, atk = nA, nAT, nak, natk
                        Zf = Zp[zi]; zkf = lambda f_: ("Z", ins_, zi, f_)
                        ck("d")
                        for c2 in range(2):
                            rs_ = slice(c2 * 64, c2 * 64 + 64)
                            for fc in range(4):
                                MM(psA[B(c2)][:, fc * 128:(fc + 1) * 128], Zf[rs_, fc, 0, :, :], btT[rs_, d, bi, fc * 128:(fc + 1) * 128], r=[zkf(fc), ("btT", d, fc)], w=[pk(B(c2))])
                            for fc in range(4):
                                TTo("dve", M0st[:, c2, fc, :], psA[B(c2)][:, fc * 128:(fc + 1) * 128], bdm, ALU.mult, r=[pk(B(c2)), "cstb"], w=[("M0st", ins_)])
                            for h in range(8):
                                fc = h // 2; h2 = h % 2
                                o = psA[B(2 + c2)][h2 * 64:h2 * 64 + 64, fc * 64:fc * 64 + 64]
                                MM(o, btT[rs_, d, bi, h * 64:(h + 1) * 64], Zf[rs_, fc, 1, h2, :], start=True, stop=False, r=[("btT", d, fc), zkf(fc)], w=[pk(B(2 + c2))])
                                MM(o, ktT[rs_, d, bi, h * 64:(h + 1) * 64], vT[rs_, bi, h * 64:(h + 1) * 64], start=False, stop=True, r=[("ktT", d, fc), ("vT", fc)], w=[pk(B(2 + c2))])
                        for c2 in range(2):
                            ACT(N0st[:, c2, :, :], psA[B(2 + c2)][:, 0:256].rearrange("p (f i) -> p f i", f=4), AF.Copy, r=[pk(B(2 + c2))], w=[("N0st", ins_)])
                        cidx = pcidx0 + 2 * bi
                        DMA("sp", m0_s[d, cidx:cidx + 2].rearrange("c p f j -> p c f j"), M0st[:], r=[("M0st", ins_)], w=[("m0_s", d, cidx)])
                        DMA("sp", n0_s[d, cidx:cidx + 2].rearrange("c p f i -> p c f i"), N0st[:], r=[("N0st", ins_)], w=[("n0_s", d, cidx)])
                        if own:
                            amat(0, lambda rows, fc: bt[rows, d, fc, tokc], lambda rows, fc: AR[rows, d, fc, bi, 128:256], ["bt", "AR"])
                            aevac(0, Arb, "Arb", mi)
                            amat(4, lambda rows, fc: kt[rows, d, fc, tokc], lambda rows, fc: AR[rows, d, fc, bi, 128:256], ["kt", "AR"])
                            aevac(4, Ark, "Ark", mi)
                            for fc in range(4):
                                o = psA[B(2)][:, fc * 128:(fc + 1) * 128]
                                MM(o, ident, AR[:, d, fc, bi, 128:256], start=True, stop=False, r=["cstb", ("AR", d, fc)], w=[pk(B(2))])
                                for h2 in range(2):
                                    h = 2 * fc + h2
                                    MM(psA[B(2)][h2 * 64:h2 * 64 + 64, fc * 128:(fc + 1) * 128], Zf[:, fc, 0, h2, :], Arb[:, h, :],
                                       start=False, stop=(h2 == 1), r=[zkf(fc), "Arb"], w=[pk(B(2))])
                                for h2 in range(2):
                                    h = 2 * fc + h2
                                    o2 = psA[B(3)][h2 * 64:h2 * 64 + 64, fc * 128:(fc + 1) * 128]
                                    MM(o2, Zf[:, fc, 1, h2, :], Arb[:, h, :], start=True, stop=False, r=[zkf(fc), "Arb"], w=[pk(B(3))])
                                    MM(o2, vT[:, bi, h * 64:(h + 1) * 64], Ark[:, h, :], start=False, stop=True, r=[("vT", fc), "Ark"], w=[pk(B(3))])
                            ACT(Qst[:], psA[B(2)][:, :].rearrange("p (f t) -> p f t", f=4), AF.Copy, r=[pk(B(2))], w=[("Qst", ins_)])
                            DMA("sp", q_s[d, gblk], Qst[:], r=[("Qst", ins_)], w=[("q_s", d, gblk)])
                            CP("dve", Y0st[:], psA[B(3)][:, :].rearrange("p (f t) -> p f t", f=4), r=[pk(B(3))], w=[("Y0st", ins_)])
                            DMA("sp", y0_s[d, :, :, t0 + bi * 128:t0 + bi * 128 + 128], Y0st[:], r=[("Y0st", ins_)], w=[("y0_s", d, gblk)])

            def attention_tile(ti):
                P.tag = f"attn{ti}"
                qs = ti % 2
                for qb in range(NBT):
                    n = NBT * ti + qb
                    kbs = [kb for kb in (n - 1, n, n + 1) if kb >= 0]
                    for m in range(4):
                        for h2 in range(2):
                            h = 2 * m + h2
                            c = h % 4; g = h // 4
                            rows = slice(g * 64, g * 64 + 64)
                            pi = h2
                            for kb in kbs:
                                slot = kb - (n - 1)
                                MM(psTf[0][:, slot * 128:(slot + 1) * 128], kT[rows, kb * 128:(kb + 1) * 128], qT[rows, qs, c, qb * 128:(qb + 1) * 128],
                                   r=["kT", ("qT", qs)], w=[pkT(0)])
                            lo = (kbs[0] - (n - 1)) * 128
                            ACT(probs[:, pi, lo:384], psTf[0][:, lo:384], AF.Exp, scale=0.125, r=[pkT(0)], w=[("probs", pi)])
                            if n - 1 >= 0:
                                TTo("pool", probs[:, pi, 0:128], probs[:, pi, 0:128], cstb[:, 10, :], ALU.mult, r=[("probs", pi), "cstb"], w=[("probs", pi)])
                            TTo("pool", probs[:, pi, 256:384], probs[:, pi, 256:384], cstb[:, 11, :], ALU.mult, r=[("probs", pi), "cstb"], w=[("probs", pi)])
                            orow = slice(h2 * 64, h2 * 64 + 64)
                            for i, kb in enumerate(kbs):
                                slot = kb - (n - 1)
                                MM(psTf[1][orow, 0:128], vaug[:, kb, g, :], probs[:, pi, slot * 128:(slot + 1) * 128], start=(i == 0), stop=(i == len(kbs) - 1),
                                   r=["vaug", ("probs", pi)], w=[pkT(1)])
                            for i, kb in enumerate(kbs):
                                slot = kb - (n - 1)
                                MM(psTf[1][orow, 128:256], vones[:], probs[:, pi, slot * 128:(slot + 1) * 128], start=(i == 0), stop=(i == len(kbs) - 1),
                                   r=["vaug1", ("probs", pi)], w=[pkT(1)])
                        TS("dve", den[:], psTf[1][:, 128:256], vc("esink", m), None, ALU.add, r=[pkT(1), "vec2"], w=["den"])
                        RCP(den[:], den[:], r=["den"], w=["den"])
                        TTo("dve", attn_o[:, m, qb * 128:(qb + 1) * 128], psTf[1][:, 0:128], den[:], ALU.mult, r=[pkT(1), "den"], w=["attn_o"])

            def attn_and_store(ti):
                attention_tile(ti)
                DMA("sp", ao_s[:, :, ti * TT:(ti + 1) * TT], attn_o[:], r=["attn_o"], w=[("ao_s", ti)])

            oth = list(range(2 * NTH - 1, NTH - 1, -1))
            if ntiles_other is not None:
                oth = oth[len(oth) - ntiles_other:] if ntiles_other > 0 else []
            for ti in oth:
                t0 = ti * TT
                sweep_tile(t0, own=False, first_other=(ti == NTH), pcidx0=t0 // 64)
            nown = NTH if ntiles_own is None else ntiles_own
            for ti in range(nown):
                sweep_tile(ti * TT, own=True, first_other=False, pcidx0=(ti * TT) // 64)
                if ti >= 1:
                    attn_and_store(ti - 1)
            if nown == NTH:
                attn_and_store(NTH - 1)
            if dbg is not None and upto == 1:
                dbg(locals(), P, DMA, dbg_out)
            if upto == 1:
                P.disabled = True
        barrier()

        with ExitStack() as s23:
            Yb = sb("Yb", [128, 4, HALF], F32, s23)
            if True:
                s2 = s23
                S32 = sb("S32", [128, 2, 4, 64], F32, s2)
                Sb = sb("Sb", [128, 2, 4, 64], BF16, s2)
                St = sb("St", [128, 2, 4, 64], F32, s2)
                M0l = sb("M0l", [128, 4, 4, 128], BF16, s2)
                N0l = sb("N0l", [128, 4, 4, 64], BF16, s2)
                Ql = sb("Ql", [128, 4, 4, 128], BF16, s2)
                ytmp = sb("ytmp", [128, 2, 4, 512], F32, s2)
                for j in range(4):
                    s = j % 2
                    gs_ = range(4 * j, 4 * j + 4)
                    DMA("sp", Yb[:, :, j * 512:(j + 1) * 512], y0_s[0, :, :, j * 512:(j + 1) * 512], r=[("y0_s", 0, g) for g in gs_], w=[("Yb", 2 * j), ("Yb", 2 * j + 1)])
                    DMA("sp", ytmp[:, s, :, :], y0_s[1, :, :, j * 512:(j + 1) * 512], r=[("y0_s", 1, g) for g in gs_], w=[("ytmp", s)])
                    TTo("pool", Yb[:, :, j * 512:(j + 1) * 512], Yb[:, :, j * 512:(j + 1) * 512], ytmp[:, s, :, :], ALU.add,
                        r=[("ytmp", s), ("Yb", 2 * j), ("Yb", 2 * j + 1)], w=[("Yb", 2 * j), ("Yb", 2 * j + 1)])
                MEMSET("dve", S32[:], 0.0, w=[("S32", 0), ("S32", 1)])
                MEMSET("pool", Sb[:], 0.0, w=[("Sb", 0), ("Sb", 1)])
                stepn = [0]

                def chain_step(d, cidx, own):
                    sl = stepn[0] % 4
                    stepn[0] += 1
                    DMA("sp", M0l[:, sl, :, :], m0_s[d, cidx], r=[("m0_s", d, cidx - cidx % 2)], w=[("M0l", sl)])
                    DMA("sp", N0l[:, sl, :, :], n0_s[d, cidx], r=[("n0_s", d, cidx - cidx % 2)], w=[("N0l", sl)])
                    bank = 4 + d
                    if own:
                        blk = cidx // 2; c2 = cidx % 2
                        qsl = (blk % 2) * 2 + d
                        if (d == 0 and c2 == 0) or (d == 1 and c2 == 1):
                            DMA("sp", Ql[:, qsl, :, :], q_s[d, blk], r=[("q_s", d, blk)], w=[("Ql", qsl)])
                        yb = 0 + d
                        for h in range(8):
                            fc = h // 2; h2 = h % 2; rows = slice(h2 * 64, h2 * 64 + 64)
                            MM(psA[yb][rows, fc * 64:(fc + 1) * 64], Sb[rows, d, fc, :], Ql[rows, qsl, fc, c2 * 64:c2 * 64 + 64],
                               r=[("Sb", d), ("Ql", qsl)], w=[pk(yb)])
                        yv = Yb[:, :, cidx * 64:(cidx + 1) * 64]
                        TTo("dve", yv, yv, psA[yb][:, 0:256].rearrange("p (f t) -> p f t", f=4), ALU.add, r=[pk(yb), ("Yb", cidx // 4)], w=[("Yb", cidx // 4)])
                    for fc in range(4):
                        o = psA[bank][:, fc * 64:(fc + 1) * 64]
                        MM(o, ident, N0l[:, sl, fc, :], start=True, stop=False, r=["cstb", ("N0l", sl)], w=[pk(bank)])
                        MM(o, M0l[:, sl, fc, :], Sb[:, d, fc, :], start=False, stop=True, r=[("M0l", sl), ("Sb", d)], w=[pk(bank)])
                    TTo("dve", St[:, d, :, :], psA[bank][:, 0:256].rearrange("p (f i) -> p f i", f=4), S32[:, d, :, :], ALU.add,
                        r=[pk(bank), ("S32", d)], w=[("St", d)])
                    for fc in range(4):
                        TS("dve", S32[:, d, fc, :], St[:, d, fc, :], PCs[:, d, cidx, fc:fc + 1], None, ALU.mult, r=[("St", d), "PCs"], w=[("S32", d)])
                    ACT(Sb[:, d, :, :], S32[:, d, :, :], AF.Copy, r=[("S32", d)], w=[("Sb", d)])

                for cidx in range(63, 31, -1):
                    chain_step(1, cidx, own=False)
                for i in range(32):
                    chain_step(0, i, own=True)
                    chain_step(1, 31 - i, own=True)
            if upto == 2:
                P.disabled = True

            if True:
                s3 = s23
                wg = sb("wg", [128, 8, 2048], BF16, s3)
                wua = sb("wua", [128, 4, D], BF16, s3); wur = sb("wur", [128, 4, D], BF16, s3); wout = sb("wout", [128, 8, D], BF16, s3)
                for kc in range(8):
                    DMA("pool", wg[:, kc, :], wgt[kc * 128:(kc + 1) * 128, :], w=["wg"])
                    DMA("pool", wout[:, kc, :], wout_d[kc * 128:(kc + 1) * 128, :], w=["wout"])
                for kc in range(4):
                    DMA("pool", wua[:, kc, :], wua_d[kc * 128:(kc + 1) * 128, :], w=["wua"])
                    DMA("pool", wur[:, kc, :], wur_d[kc * 128:(kc + 1) * 128, :], w=["wur"])
                xt = sb("xt3", [128, 2, D], F32, s3); hb = sb("hb3", [128, 1, D], BF16, s3)
                ssq = sb("ssq3", [128, 2], F32, s3); rstd = sb("rstd3", [128, 2], F32, s3)
                hT = sb("hT3", [128, 8, TT], BF16, s3)
                ycD = [sb(f"yc{i}", [128, TT], F32, s3) for i in range(2)]; ysqD = [sb(f"ysq{i}", [128, TT], F32, s3) for i in range(2)]
                yrD = [sb(f"yr{i}", [128, TT], F32, s3) for i in range(2)]
                rwT = sb("rwT", [128, 4, TT], BF16, s3)
                gates = sb("gates", [128, 16, TT], BF16, s3)
                mg = sb("mg", [128, 8, TT], BF16, s3)
                m1D = [sb(f"m1{i}", [128, TT], F32, s3) for i in range(2)]; m2D = [sb(f"m2{i}", [128, TT], F32, s3) for i in range(2)]
                x1b = sb("x1b", [128, 2, D], F32, s3)
                xres = sb("xres", [128, 2, D], F32, s3)
                ao_l = sb("ao_l", [128, 4, TT], BF16, s3); bv_l = sb("bv_l", [128, 4, TT], BF16, s3); sgl_l = sb("sgl_l", [128, TT], BF16, s3)
                onesf = cstf[:, 1, :]
                for ti in (3, 4, 2, 5, 1, 6, 0, 7):
                    t0 = ti * TT
                    DMA("sp", ao_l[:], ao_s[:, :, t0:t0 + TT], r=[("ao_s", ti)], w=["ao_l"])
                    DMA("sp", bv_l[:], bv_s[:, :, t0:t0 + TT], r=[("bv_s", ti, fc) for fc in range(4)], w=["bv_l"])
                    DMA("sp", sgl_l[:], sgl_s[:, t0:t0 + TT], r=[("sgl_s", ti)], w=["sgl_l"])
                    for fc in range(4):
                        Yv = Yb[:, fc, t0:t0 + TT]
                        yc, ysq, yr = ycD[fc % 2], ysqD[fc % 2], yrD[fc % 2]
                        b0_ = 2 * (fc % 2)
                        MM(psA[b0_][:, 0:TT], onesf, Yv, r=["cstf", ("Yb", ti)], w=[pk(b0_)])
                        TTo("dve", yc[:], Yv, psA[b0_][:, 0:TT], ALU.subtract, r=[("Yb", ti), pk(b0_)], w=[("yc", fc % 2)])
                        ACT(ysq[:], yc[:], AF.Square, r=[("yc", fc % 2)], w=[("ysq", fc % 2)])
                        MM(psA[b0_ + 1][:, 0:TT], onesf, ysq[:], r=["cstf", ("ysq", fc % 2)], w=[pk(b0_ + 1)])
                        RSQ(yr[:], psA[b0_ + 1][:, 0:TT], 1.0, eps_c[:, 2:3], TT, r=[pk(b0_ + 1)], w=[("yr", fc % 2)])
                        TTo("pool", yc[:], yc[:], yr[:], ALU.mult, r=[("yc", fc % 2), ("yr", fc % 2)], w=[("yc", fc % 2)])
                        TS("dve", yc[:], yc[:], vc("lnw", fc), vc("lnb", fc), ALU.mult, ALU.add, r=[("yc", fc % 2), "vec"], w=[("yc", fc % 2)])
                        TTo("pool", yc[:], yc[:], bv_l[:, fc, :], ALU.add, r=[("yc", fc % 2), "bv_l"], w=[("yc", fc % 2)])
                        MM(psA[4 + fc % 2][:, 0:TT], g2t[:, fc * 128:(fc + 1) * 128], sgl_l[:, :], r=["g2t", "sgl_l"], w=[pk(4 + fc % 2)])
                        TTo("dve", rwT[:, fc, :], yc[:], psA[4 + fc % 2][:, 0:TT], ALU.mult, r=[("yc", fc % 2), pk(4 + fc % 2)], w=["rwT"])
                    xaps = [(xs[1 + t0 + 128 * i: 1 + t0 + 128 * i + 128, :], 128, 128 * i) for i in range(NBT)]
                    make_hT("hT3", xaps, None, hT, "nmix", "hT3", xt, hb, ssq, rstd)
                    for gc in range(16):
                        bank = gc % 2
                        for kc in range(8):
                            MM(psA[bank][:, 0:TT], wg[:, kc, gc * 128:(gc + 1) * 128], hT[:, kc, :], start=(kc == 0), stop=(kc == 7), r=["wg", ("hT3", kc)], w=[pk(bank)])
                        ACT(gates[:, gc, :], psA[bank][:, 0:TT], AF.Tanh, scale=0.5, r=[pk(bank)], w=["gates"])
                    for oc in range(8):
                        m1, m2 = m1D[oc % 2], m2D[oc % 2]
                        ba_ = 2 + 2 * (oc % 2)
                        for kc in range(4):
                            MM(psA[ba_][:, 0:TT], wua[:, kc, oc * 128:(oc + 1) * 128], ao_l[:, kc, :], start=(kc == 0), stop=(kc == 3), r=["wua", "ao_l"], w=[pk(ba_)])
                        for kc in range(4):
                            MM(psA[ba_ + 1][:, 0:TT], wur[:, kc, oc * 128:(oc + 1) * 128], rwT[:, kc, :], start=(kc == 0), stop=(kc == 3), r=["wur", "rwT"], w=[pk(ba_ + 1)])
                        STT("dve", m1[:], gates[:, oc, :], 1.0, psA[ba_][:, 0:TT], ALU.add, ALU.mult, r=[pk(ba_), "gates"], w=[("m1", oc % 2)])
                        STT("dve", m2[:], gates[:, 8 + oc, :], 1.0, psA[ba_ + 1][:, 0:TT], ALU.add, ALU.mult, r=[pk(ba_ + 1), "gates"], w=[("m2", oc % 2)])
                        TTo("pool", mg[:, oc, :], m1[:], m2[:], ALU.add, r=[("m1", oc % 2), ("m2", oc % 2)], w=["mg"])
                    for bi in range(NBT):
                        s = bi % 2
                        tok = t0 + bi * 128
                        DMA("sp", xres[:, s, :], xs[1 + tok:1 + tok + 128, :], w=[("xres", s)])
                        for hf in range(2):
                            bank = 4 + hf
                            for kc in range(8):
                                MM(psA[bank][:, :], mg[:, kc, bi * 128:(bi + 1) * 128], wout[:, kc, hf * 512:(hf + 1) * 512], start=(kc == 0), stop=(kc == 7), r=["mg", "wout"], w=[pk(bank)])
                            STT("dve", x1b[:, s, hf * 512:(hf + 1) * 512], psA[bank][:, :], 0.5, xres[:, s, hf * 512:(hf + 1) * 512], ALU.mult, ALU.add, r=[pk(bank), ("xres", s)], w=[("x1b", s)])
                        DMA("sp", x1_s[tok:tok + 128, :], x1b[:, s, :], r=[("x1b", s)], w=[("x1_s", tok // 128)])
        if upto == 3:
            P.disabled = True
        barrier()

        with ExitStack() as s4:
            wf1 = sb("wf1", [128, 8, 4096], BF16, s4); wf2 = sb("wf2", [128, 32, D], BF16, s4)
            wpg = sb("wpg", [128, 8, D], BF16, s4); wple = sb("wple", [128, 2, D], BF16, s4)
            for cg in range(4):
                for kc in range(8):
                    DMA("pool", wf1[:, kc, cg * 1024:(cg + 1) * 1024], wff1_d[kc * 128:(kc + 1) * 128, cg * 1024:(cg + 1) * 1024], w=[("wf1", cg)])
            for kc in range(32):
                DMA("pool", wf2[:, kc, :], wff2_d[kc * 128:(kc + 1) * 128, :], w=[("wf2", kc // 8)])
            for kc in range(8):
                DMA("pool", wpg[:, kc, :], wpg_d[kc * 128:(kc + 1) * 128, :], w=["wpg"])
            for kc in range(2):
                DMA("pool", wple[:, kc, :], wple_d[kc * 128:(kc + 1) * 128, :], w=["wple"])
            xt = sb("xt4", [128, 2, D], F32, s4); hb = sb("hb4", [128, 1, D], BF16, s4)
            ssq = sb("ssq4", [128, 4], F32, s4); rstd = sb("rstd4", [128, 4], F32, s4)
            hTf = sb("hT4", [128, 2, 8, 128], BF16, s4)
            hTp = sb("hT4p", [128, 1, 8, 128], BF16, s4)
            act = sb("act", [128, 2, 32, 128], BF16, s4)
            rl = sb("rl", [128, 4, 128], F32, s4)
            x2 = sb("x2", [128, 1, D], F32, s4)
            pt = sb("pt", [128, 2, 256], F32, s4); pb16 = sb("pb16", [128, 2, 256], BF16, s4); pT = sb("pT", [128, 2, 2, 128], BF16, s4)
            sgp = sb("sgp", [128, 1, 512], F32, s4)
            nhb = [0]

            def norm_T(src, gain, skey, hTt, okey, idx):
                hs = 0
                nhb[0] += 1
                ACT(junk[:], src, AF.Square, accum=ssq[:, idx:idx + 1], r=[skey], w=["junk", ("ssq4", idx)])
                RSQ(rstd[:, idx:idx + 1], ssq[:, idx:idx + 1], 1.0 / D, eps_c[:, 0:1], 1, r=[("ssq4", idx)], w=[("rstd4", idx)])
                TS("dve", hb[:, hs, :], src, rstd[:, idx:idx + 1], None, ALU.mult, r=[skey, ("rstd4", idx)], w=[("hb4o", hs)])
                for kc in range(8):
                    TR(psT[kc % 2][:, (kc // 2) * 128:(kc // 2) * 128 + 128], hb[:, hs, kc * 128:(kc + 1) * 128], ident, r=[("hb4o", hs), "cstb"], w=[pkT(kc % 2)])
                for kc in range(8):
                    srcp = psT[kc % 2][:, (kc // 2) * 128:(kc // 2) * 128 + 128]
                    if kc % 2 == 0:
                        ACT(hTt[:, kc, :], srcp, AF.Copy, scale=vc(gain, kc), r=[pkT(kc % 2), "vec"], w=[(okey, kc)])
                    else:
                        TS("dve", hTt[:, kc, :], srcp, vc(gain, kc), None, ALU.mult, r=[pkT(kc % 2), "vec"], w=[(okey, kc)])

            for blk in range(NB_OWN):
                tok = blk * 128
                s = blk % 2
                DMA("sp", xt[:, s, :], x1_s[tok:tok + 128, :], r=[("x1_s", blk)], w=[("xt4", s)])
                DMA("sp", pt[:, s, :], pp[tok:tok + 128, :], w=[("pt", s)])
                hk = ("hT4", s)
                norm_T(xt[:, s, :], "nffn", ("xt4", s), hTf[:, s], hk, s)
                for hc in range(32):
                    bank = hc % 4
                    for kc in range(8):
                        MM(psA[bank][:, 0:128], wf1[:, kc, hc * 128:(hc + 1) * 128], hTf[:, s, kc, :], start=(kc == 0), stop=(kc == 7),
                           r=[("wf1", hc // 8), (hk, kc)], w=[pk(bank)])
                    ACT(rl[:, bank, :], psA[bank][:, 0:128], AF.Relu, r=[pk(bank)], w=[("rl", bank)])
                    TTo("dve", act[:, s, hc, :], rl[:, bank, :], rl[:, bank, :], ALU.mult, r=[("rl", bank)], w=[("act", s)])
                for hf in range(2):
                    bank = 4 + hf
                    for hc in range(32):
                        MM(psA[bank][:, :], act[:, s, hc, :], wf2[:, hc, hf * 512:(hf + 1) * 512], start=(hc == 0), stop=(hc == 31),
                           r=[("act", s), ("wf2", hc // 8)], w=[pk(bank)])
                    TTo("dve", x2[:, 0, hf * 512:(hf + 1) * 512], psA[bank][:, :], xt[:, s, hf * 512:(hf + 1) * 512], ALU.add, r=[pk(bank), ("xt4", s)], w=["x2"])
                hkp = "hT4p"
                norm_T(x2[:, 0, :], "nple", "x2", hTp[:, 0], hkp, 2 + s)
                CP("pool", pb16[:, s, :], pt[:, s, :], r=[("pt", s)], w=[("pb16", s)])
                for kc in range(2):
                    TR(psT[0][:, 512 + kc * 128:512 + (kc + 1) * 128], pb16[:, s, kc * 128:(kc + 1) * 128], ident, r=[("pb16", s), "cstb"], w=[pkT(0)])
                CP("dve", pT[:, s, :, :], psT[0][:, 512:768].rearrange("p (k t) -> p k t", k=2), r=[pkT(0)], w=[("pT", s)])
                for hf in range(2):
                    gb = 0 + hf
                    pbk = 2 + hf
                    for kc in range(8):
                        MM(psA[gb][:, :], hTp[:, 0, kc, :], wpg[:, kc, hf * 512:(hf + 1) * 512], start=(kc == 0), stop=(kc == 7), r=[(hkp, kc), "wpg"], w=[pk(gb)])
                    for kc in range(2):
                        MM(psA[pbk][:, :], pT[:, s, kc, :], wple[:, kc, hf * 512:(hf + 1) * 512], start=(kc == 0), stop=(kc == 1), r=[("pT", s), "wple"], w=[pk(pbk)])
                    ACT(sgp[:, 0, :], psA[gb][:, :], AF.Tanh, scale=0.5, r=[pk(gb)], w=["sgp"])
                    STT("dve", sgp[:, 0, :], sgp[:, 0, :], 1.0, psA[pbk][:, :], ALU.add, ALU.mult, r=[pk(pbk), "sgp"], w=["sgp"])
                    STT("dve", xt[:, s, hf * 512:(hf + 1) * 512], sgp[:, 0, :], 0.5, x2[:, 0, hf * 512:(hf + 1) * 512], ALU.mult, ALU.add,
                        r=["sgp", "x2", ("xt4", s)], w=[("ob", s)])
                DMA("sp", out_d[tok:tok + 128, :], xt[:, s, :], r=[("ob", s), ("xt4", s)], w=[("out", blk), ("xt4", s)])
            P.add("sp", None, [("out", i) for i in range(NB_OWN)], ())
        P.emit(nc, gs)
    return nc


def _consts():
    cst = np.zeros((128, 12, 128), np.float32)
    i = np.arange(128)
    cst[:, 0, :] = np.eye(128)
    cst[:, 1, :] = (i[:, None] // 64 == i[None, :] // 64)
    perm = np.zeros((128, 128), np.float32)
    for m in range(128):
        j = m % 64
        if j < 8:
            perm[m + 8, m] = 1.0
        elif j < 16:
            perm[m - 8, m] = 1.0
    cst[:, 2, :] = perm
    cst[:, 3, :] = cst[:, 1, :]
    same = (i[:, None] // 64 == i[None, :] // 64)
    cst[:, 4, :] = same & (i[:, None] < i[None, :])
    cst[:, 5, :] = same & (i[:, None] <= i[None, :])
    cst[:, 6, :] = same & (i[:, None] > i[None, :])
    cst[:, 7, :] = same & (i[:, None] >= i[None, :])
    cst[:, 8, :] = 1.0
    cst[:, 9, :] = cst[:, 1, :] / 64.0
    cst[:, 10, :] = (i[:, None] >= i[None, :])
    cst[:, 11, :] = (i[:, None] <= i[None, :])
    tri = np.zeros((128, 2, 128), np.float32)
    tri[:, 0, :] = 0.5 * LWS * cst[:, 5, :]
    tri[:, 1, :] = 0.5 * LWS * cst[:, 7, :]
    return cst, tri


def _rope_tables(pos):
    half = 8
    inv = np.power(np.float32(500000.0), -np.arange(half, dtype=np.float32) * 2.0 / 16.0).astype(np.float32)
    ang = pos.astype(np.float32)[None, :] * inv[:, None]
    cos = np.cos(ang).astype(np.float32); sin = np.sin(ang).astype(np.float32)
    n = pos.shape[0]
    tab = np.zeros((128, 2, n), np.float32)
    for hb in range(2):
        b = hb * 64
        tab[b:b + 64, 0, :] = 1.0
        tab[b:b + 8, 0, :] = cos; tab[b + 8:b + 16, 0, :] = cos
        tab[b:b + 8, 1, :] = -sin; tab[b + 8:b + 16, 1, :] = sin
    return tab


def _col128(v):
    return np.ascontiguousarray(np.asarray(v, np.float32).reshape(-1, 128).T)


_NC_CACHE = {}


def kernel(x, p, norm_mix, w_in, shift_mu, q_norm, k_norm, sink, w0, w2, a0, a2, g2, k_k, k_a, r_k,
           lnx_w, lnx_b, w_up_attn, w_up_rwkv, w_out, norm_ffn, w_ff1, w_ff2, norm_ple, w_ple_gate, w_ple):
    f = lambda a: np.asarray(a, np.float32)
    x = f(x); p = f(p)[0]; w_in = f(w_in)[0]
    qcols = []
    for c in range(4):
        qcols += list(range(c * 64, c * 64 + 64)) + list(range((4 + c) * 64, (4 + c) * 64 + 64))
    O_K, O_V, O_R = 512, 640, 768
    cols = qcols + list(range(O_K, O_K + 128)) + list(range(O_V, O_V + 128))
    rw = O_R
    wa = list(range(rw + 1536, rw + 1536 + 128)); gl = list(range(rw + 1664, rw + 1792))
    cols += wa + gl
    mu_idx = [12, 13]
    for fc in range(4):
        for part in range(3):
            cols += list(range(rw + part * 512 + fc * 128, rw + part * 512 + fc * 128 + 128))
            mu_idx.append(part * 4 + fc)
    wsw = np.ascontiguousarray(w_in[:, cols])
    wgt = np.ascontiguousarray(w_in[:, O_R + 1792:])
    mu_c = _col128(f(shift_mu)[0])[:, mu_idx]
    cst, tri = _consts()
    cst_in = cst.copy()
    cst_in_f = cst.copy()
    cst_in[:, 1, :] = cst[:, 1, :]
    sink_ = f(sink)[0]
    sinkcols = np.zeros((128, 4), np.float32)
    for m in range(4):
        sinkcols[0:64, m] = sink_[2 * m]; sinkcols[64:128, m] = sink_[2 * m + 1]
    in_maps = []
    for c in range(8):
        b, hf = c // 2, c % 2
        xb = x[b]; pb_ = p[b]
        if hf == 0:
            xl = xb; pl = pb_[:HALF]; pos = np.arange(0, HALF + 512); dA, dB = 0, 1
        else:
            xl = xb[::-1]; pl = pb_[HALF:][::-1]; pos = SEQ - 1 - np.arange(0, HALF + 512); dA, dB = 1, 0
        xs = np.zeros((SEQ + 2, D), np.float32); xs[1:SEQ + 1] = xl
        dd = [dA, dB]
        vec = np.zeros((128, NVEC_IN), np.float32)
        def put(n, arr):
            vec[:, VC[n]:VC[n] + arr.shape[1]] = arr
        put("mu", mu_c); put("k_k", _col128(f(k_k)[0])); put("k_a", _col128(f(k_a)[0]))
        put("rkA", _col128(f(r_k)[0, dA].reshape(-1))); put("rkB", _col128(f(r_k)[0, dB].reshape(-1)))
        put("lnw", _col128(f(lnx_w)[0])); put("lnb", _col128(f(lnx_b)[0]))
        put("a0A", _col128(f(a0)[0, dA])); put("a0B", _col128(f(a0)[0, dB]))
        put("qg", np.tile(f(q_norm)[0], 2)[:, None]); put("kg", np.tile(f(k_norm)[0], 2)[:, None])
        put("nmix", _col128(f(norm_mix)[0])); put("nffn", _col128(f(norm_ffn)[0])); put("nple", _col128(f(norm_ple)[0]))
        put("sink", sinkcols)
        w2aug = np.stack([np.concatenate([f(w2)[0, d_], f(w0)[0, d_][None, :]], 0) for d_ in dd])
        a2s = np.stack([f(a2)[0, d_] for d_ in dd])
        cst_c = cst.copy()
        m = {"xs": xs, "pp": np.ascontiguousarray(pl), "wsw": wsw, "wgt": wgt, "vec": vec, "w2aug": np.ascontiguousarray(w2aug),
             "a2": np.ascontiguousarray(a2s), "g2": f(g2)[0], "wua": f(w_up_attn)[0], "wur": f(w_up_rwkv)[0], "wout": f(w_out)[0],
             "wff1": f(w_ff1)[0], "wff2": f(w_ff2)[0], "wpg": f(w_ple_gate)[0], "wple": f(w_ple)[0],
             "rope": _rope_tables(pos), "cst": cst_c, "tri": tri, "crow": np.ascontiguousarray(tri.sum(0)[None])}
        in_maps.append(m)
    if _NC_CACHE.get("prep_only"):
        return in_maps
    if "nc" not in _NC_CACHE:
        _NC_CACHE["nc"] = build_program()
    res = run_bass_kernel_spmd(_NC_CACHE["nc"], in_maps, core_ids=list(range(8)))
    out = np.zeros((4, SEQ, D), np.float32)
    for c in range(8):
        b, hf = c // 2, c % 2
        o = res.results[c]["out"]
        if hf == 0:
            out[b, :HALF] = o
        else:
            out[b, HALF:] = o[::-1]
    return out
```
